# Optimizing a Trainium2 kernel written in Bass

```python
import math
import jax
import jax.numpy as jnp
from jax import lax
import numpy as np

D_MODEL = 1024
BATCH = 8
SEQ = 8192
DEPTH = 4
DEC_BATCH = 16
DEC_SEQ = 16
PAST_LEN = 1024

CHUNK = 64
N_EVEN = (DEPTH + 1) // 2
N_ODD = DEPTH // 2
EPS = 1e-6
H_ATT = 8
D_HEAD = D_MODEL // H_ATT // 2
D_V = 2 * D_HEAD
D_QK = H_ATT * 2 * D_HEAD
D_ATT = H_ATT * D_V
Q_BLOCK = 128
D_INNER = D_MODEL
SSM_HEADDIM = 64
H_SSM = D_INNER // SSM_HEADDIM
D_STATE = 128
N_GROUPS = 2
SSM_CONV = 4
D_XBC = D_INNER + 2 * N_GROUPS * D_STATE
HYB_SPLITS = (D_QK, 2 * D_QK, 2 * D_QK + D_ATT, 2 * D_QK + D_ATT + D_INNER,
              2 * D_QK + D_ATT + D_INNER + D_XBC)
D_IN_HYB = 2 * D_QK + D_ATT + D_INNER + D_XBC + H_SSM
D_MIX_OUT = D_ATT + D_INNER
CONF_KW = 31
D_FF = -(-8 * D_MODEL // (3 * 256)) * 256

kernel_name = "hybrid_diffattn_ssd_conformer_stream_step"


def _rmsnorm(x, g):
    xf = x.astype(jnp.float32)
    y = xf * lax.rsqrt(jnp.mean(xf * xf, axis=-1, keepdims=True) + EPS)
    return (y * g.astype(jnp.float32)).astype(x.dtype)


def _layernorm(x, g, b):
    xf = x.astype(jnp.float32)
    mu = jnp.mean(xf, axis=-1, keepdims=True)
    var = jnp.mean(jnp.square(xf - mu), axis=-1, keepdims=True)
    y = (xf - mu) * lax.rsqrt(var + EPS) * g.astype(jnp.float32) + b.astype(jnp.float32)
    return y.astype(x.dtype)


def _alibi_slopes():
    return 2.0 ** (-8.0 * jnp.arange(1, H_ATT + 1, dtype=jnp.float32) / H_ATT)


def _causal_dwconv(u, past, w, b):
    full = jnp.concatenate([past.astype(u.dtype), u], axis=1)
    y = lax.conv_general_dilated(full, w[:, None, :].astype(u.dtype), (1,), "VALID",
                                 dimension_numbers=("NWC", "WIO", "NWC"),
                                 feature_group_count=u.shape[-1])
    return y + b.astype(u.dtype), full[:, full.shape[1] - (w.shape[0] - 1):]


def _diff_attn_block(qb, qpos, k, v, kpos, lam, slopes):
    s = jnp.einsum("bqhmd,bkhmd->bhmqk", qb, k).astype(jnp.float32) * (D_HEAD ** -0.5)
    dist = jnp.abs(qpos[:, None] - kpos[None, :]).astype(jnp.float32)
    bias = -slopes[:, None, None] * dist
    visible = (kpos[None, :] // CHUNK) <= (qpos[:, None] // CHUNK)
    s = jnp.where(visible, s + bias[None, :, None], -jnp.inf)
    p = jax.nn.softmax(s, axis=-1)
    a = p[:, :, 0] - lam * p[:, :, 1]
    return jnp.einsum("bhqk,bkhe->bqhe", a.astype(v.dtype), v)


def _ssd_scan(x, dt, a, bm, cm, h0, block):
    B, T, H, P = x.shape
    nc = T // block

    def to_chunks(t):
        return jnp.moveaxis(t.reshape((B, nc, block) + t.shape[2:]), 1, 0)

    xs = (to_chunks(x.astype(jnp.float32)), to_chunks(dt.astype(jnp.float32)),
          to_chunks(bm.astype(jnp.float32)), to_chunks(cm.astype(jnp.float32)))
    causal = jnp.tril(jnp.ones((block, block), dtype=bool))

    def step(h, inp):
        xc, dtc, bc, cc = inp
        bc = jnp.repeat(bc, H // N_GROUPS, axis=2)
        cc = jnp.repeat(cc, H // N_GROUPS, axis=2)
        acum = jnp.cumsum(dtc * a, axis=1)
        seg = jnp.where(causal[None, :, :, None],
                        acum[:, :, None, :] - acum[:, None, :, :], -jnp.inf)
        xdt = xc * dtc[..., None]
        scores = jnp.einsum("blhn,bshn->blsh", cc, bc) * jnp.exp(seg)
        y_diag = jnp.einsum("blsh,bshp->blhp", scores, xdt)
        y_off = jnp.einsum("blhn,bhpn->blhp", cc, h) * jnp.exp(acum)[..., None]
        decay_end = jnp.exp(acum[:, -1:, :] - acum)
        h = h * jnp.exp(acum[:, -1, :])[:, :, None, None] + jnp.einsum(
            "blhn,blhp->bhpn", bc * decay_end[..., None], xdt)
        return h, y_diag + y_off

    h, ys = lax.scan(step, h0.astype(jnp.float32), xs)
    return jnp.moveaxis(ys, 0, 1).reshape(B, T, H, P), h


def _hybrid_mixer(h, k_past, v_past, conv_past, ssm_past, w_in, lam_vec, lam_init, subln_g,
                  conv_w, conv_b, dt_bias, a_log, d_skip, norm_g, w_out):
    B, T, _ = h.shape
    pos0 = 0 if k_past is None else k_past.shape[1]
    q, k, v, z, xbc, dt_raw = jnp.split(h @ w_in, HYB_SPLITS, axis=-1)
    q = q.reshape(B, T, H_ATT, 2, D_HEAD)
    k = k.reshape(B, T, H_ATT, 2, D_HEAD)
    v = v.reshape(B, T, H_ATT, D_V)
    lv = lam_vec.astype(jnp.float32)
    lam = jnp.exp(jnp.sum(lv[0] * lv[1])) - jnp.exp(jnp.sum(lv[2] * lv[3])) + lam_init
    slopes = _alibi_slopes()
    qpos = pos0 + jnp.arange(T, dtype=jnp.int32)
    if k_past is None:
        nblk = T // Q_BLOCK
        q_blocks = jnp.moveaxis(q.reshape(B, nblk, Q_BLOCK, H_ATT, 2, D_HEAD), 1, 0)
        pos_blocks = qpos.reshape(nblk, Q_BLOCK)
        o = lax.map(lambda qp: _diff_attn_block(qp[0], qp[1], k, v, qpos, lam, slopes),
                    (q_blocks, pos_blocks))
        o = jnp.moveaxis(o, 0, 1).reshape(B, T, H_ATT, D_V)
        conv_past = jnp.zeros((B, SSM_CONV - 1, D_XBC), h.dtype)
        ssm_past = jnp.zeros((B, H_SSM, SSM_HEADDIM, D_STATE), jnp.float32)
    else:
        k_all = jnp.concatenate([k_past.reshape(B, pos0, H_ATT, 2, D_HEAD).astype(k.dtype), k], axis=1)
        v_all = jnp.concatenate([v_past.astype(v.dtype), v], axis=1)
        kpos = jnp.arange(pos0 + T, dtype=jnp.int32)
        o = _diff_attn_block(q, qpos, k_all, v_all, kpos, lam, slopes)
    o = (_rmsnorm(o, subln_g) * (1.0 - lam_init)).reshape(B, T, D_ATT)
    xbc, conv_new = _causal_dwconv(xbc, conv_past, conv_w, conv_b)
    xbc = jax.nn.silu(xbc)
    xs, bm, cm = jnp.split(xbc, [D_INNER, D_INNER + N_GROUPS * D_STATE], axis=-1)
    xs = xs.reshape(B, T, H_SSM, SSM_HEADDIM)
    dt = jax.nn.softplus(dt_raw.astype(jnp.float32) + dt_bias.astype(jnp.float32))
    a = -jnp.exp(a_log.astype(jnp.float32))
    y, ssm_new = _ssd_scan(xs, dt, a, bm.reshape(B, T, N_GROUPS, D_STATE),
                           cm.reshape(B, T, N_GROUPS, D_STATE), ssm_past, min(T, CHUNK))
    y = y + d_skip.astype(jnp.float32)[:, None] * xs.astype(jnp.float32)
    y = y.reshape(B, T, D_INNER) * jax.nn.silu(z.astype(jnp.float32))
    y = _rmsnorm(y.reshape(B, T, N_GROUPS, D_INNER // N_GROUPS),
                 norm_g.reshape(N_GROUPS, D_INNER // N_GROUPS)).reshape(B, T, D_INNER).astype(h.dtype)
    out = jnp.concatenate([o, y], axis=-1) @ w_out
    return out, k.reshape(B, T, H_ATT, 2 * D_HEAD), v, conv_new, ssm_new


def _conformer_conv(h, conv_past, w_in, b_in, dw_w, dw_b, ln_g, ln_b, w_out, b_out):
    B = h.shape[0]
    if conv_past is None:
        conv_past = jnp.zeros((B, CONF_KW - 1, D_MODEL), h.dtype)
    u = h @ w_in + b_in
    u = u[..., :D_MODEL] * jax.nn.sigmoid(u[..., D_MODEL:])
    y, conv_new = _causal_dwconv(u, conv_past, dw_w, dw_b)
    y = jax.nn.silu(_layernorm(y, ln_g, ln_b))
    return y @ w_out + b_out, conv_new


def _swiglu(h, w_up, w_down):
    g, u = jnp.split(h @ w_up, 2, axis=-1)
    return (jax.nn.silu(g) * u) @ w_down


def _trunk(x, c, past, weights):
    (ada_w, ada_b, norm_g, ffn_w_up, ffn_w_down,
     hyb_w_in, attn_lambda, attn_subln_g, ssm_conv_w, ssm_conv_b,
     ssm_dt_bias, ssm_a_log, ssm_d, ssm_norm_g, hyb_w_out,
     conf_w_in, conf_b_in, conf_dw_w, conf_dw_b, conf_ln_g, conf_ln_b,
     conf_w_out, conf_b_out) = weights
    mod = jnp.einsum("bd,lde->lbe", jax.nn.silu(c), ada_w) + ada_b[:, None, :]
    ks, vs, sconvs, ssms, cconvs = [], [], [], [], []
    for l in range(DEPTH):
        shift_m, scale_m, gate_m, shift_f, scale_f, gate_f = jnp.split(mod[l][:, None, :], 6, axis=-1)
        h = _rmsnorm(x, norm_g[l, 0]) * (1.0 + scale_m) + shift_m
        j = l // 2
        if l % 2 == 0:
            if past is None:
                k_past = v_past = conv_past = ssm_past = None
            else:
                k_past, v_past, conv_past, ssm_past = past[0][j], past[1][j], past[2][j], past[3][j]
            lam_init = 0.8 - 0.6 * math.exp(-0.3 * l)
            out, k_rows, v_rows, conv_new, ssm_new = _hybrid_mixer(
                h, k_past, v_past, conv_past, ssm_past, hyb_w_in[j], attn_lambda[j], lam_init,
                attn_subln_g[j], ssm_conv_w[j], ssm_conv_b[j], ssm_dt_bias[j], ssm_a_log[j],
                ssm_d[j], ssm_norm_g[j], hyb_w_out[j])
            ks.append(k_rows)
            vs.append(v_rows)
            sconvs.append(conv_new)
            ssms.append(ssm_new)
        else:
            conv_past = None if past is None else past[4][j]
            out, conv_new = _conformer_conv(h, conv_past, conf_w_in[j], conf_b_in[j], conf_dw_w[j],
                                            conf_dw_b[j], conf_ln_g[j], conf_ln_b[j],
                                            conf_w_out[j], conf_b_out[j])
            cconvs.append(conv_new)
        x = x + gate_m * _rmsnorm(out, norm_g[l, 1])
        h = _rmsnorm(x, norm_g[l, 2]) * (1.0 + scale_f) + shift_f
        x = x + gate_f * _rmsnorm(_swiglu(h, ffn_w_up[l], ffn_w_down[l]), norm_g[l, 3])
    return x, jnp.stack(ks), jnp.stack(vs), jnp.stack(sconvs), jnp.stack(ssms), jnp.stack(cconvs)


def setup_inputs(seed: int = 0) -> dict:
    key = jax.random.key(seed)
    ks = iter(jax.random.split(key, 40))

    def nrm(shape, scale):
        return scale * jax.random.normal(next(ks), shape, jnp.float32)

    dt0 = jnp.exp(jax.random.uniform(next(ks), (N_EVEN, H_SSM), jnp.float32,
                                     minval=math.log(1e-3), maxval=math.log(1e-1)))
    return {
        "x_prompt": nrm((BATCH, SEQ, D_MODEL), 1.0),
        "x_sample": nrm((DEC_BATCH, DEC_SEQ, D_MODEL), 1.0),
        "c_prompt": nrm((BATCH, D_MODEL), 1.0),
        "c_sample": nrm((DEC_BATCH, D_MODEL), 1.0),
        "cache_attn_k": nrm((N_EVEN, DEC_BATCH, PAST_LEN, H_ATT, 2 * D_HEAD), 1.0),
        "cache_attn_v": nrm((N_EVEN, DEC_BATCH, PAST_LEN, H_ATT, D_V), 1.0),
        "state_ssm_conv": nrm((N_EVEN, DEC_BATCH, SSM_CONV - 1, D_XBC), 1.0),
        "state_ssm": nrm((N_EVEN, DEC_BATCH, H_SSM, SSM_HEADDIM, D_STATE), 0.3),
        "state_conf_conv": nrm((N_ODD, DEC_BATCH, CONF_KW - 1, D_MODEL), 0.5),
        "ada_w": nrm((DEPTH, D_MODEL, 6 * D_MODEL), D_MODEL ** -0.5),
        "ada_b": nrm((DEPTH, 6 * D_MODEL), 0.01),
        "norm_g": 1.0 + nrm((DEPTH, 4, D_MODEL), 0.05),
        "ffn_w_up": nrm((DEPTH, D_MODEL, 2 * D_FF), D_MODEL ** -0.5),
        "ffn_w_down": nrm((DEPTH, D_FF, D_MODEL), D_FF ** -0.5),
        "hyb_w_in": nrm((N_EVEN, D_MODEL, D_IN_HYB), D_MODEL ** -0.5),
        "attn_lambda": nrm((N_EVEN, 4, D_HEAD), 0.1),
        "attn_subln_g": 1.0 + nrm((N_EVEN, D_V), 0.05),
        "ssm_conv_w": nrm((N_EVEN, SSM_CONV, D_XBC), SSM_CONV ** -0.5),
        "ssm_conv_b": nrm((N_EVEN, D_XBC), 0.01),
        "ssm_dt_bias": dt0 + jnp.log(-jnp.expm1(-dt0)),
        "ssm_a_log": jnp.log(jax.random.uniform(next(ks), (N_EVEN, H_SSM), jnp.float32, minval=1.0, maxval=16.0)),
        "ssm_d": 1.0 + nrm((N_EVEN, H_SSM), 0.1),
        "ssm_norm_g": 1.0 + nrm((N_EVEN, D_INNER), 0.05),
        "hyb_w_out": nrm((N_EVEN, D_MIX_OUT, D_MODEL), D_MIX_OUT ** -0.5),
        "conf_w_in": nrm((N_ODD, D_MODEL, 2 * D_MODEL), D_MODEL ** -0.5),
        "conf_b_in": nrm((N_ODD, 2 * D_MODEL), 0.01),
        "conf_dw_w": nrm((N_ODD, CONF_KW, D_MODEL), CONF_KW ** -0.5),
        "conf_dw_b": nrm((N_ODD, D_MODEL), 0.01),
        "conf_ln_g": 1.0 + nrm((N_ODD, D_MODEL), 0.05),
        "conf_ln_b": nrm((N_ODD, D_MODEL), 0.01),
        "conf_w_out": nrm((N_ODD, D_MODEL, D_MODEL), D_MODEL ** -0.5),
        "conf_b_out": nrm((N_ODD, D_MODEL), 0.01),
    }


def reference(x_prompt, x_sample, c_prompt, c_sample, cache_attn_k, cache_attn_v,
              state_ssm_conv, state_ssm, state_conf_conv, ada_w, ada_b, norm_g, ffn_w_up,
              ffn_w_down, hyb_w_in, attn_lambda, attn_subln_g, ssm_conv_w, ssm_conv_b,
              ssm_dt_bias, ssm_a_log, ssm_d, ssm_norm_g, hyb_w_out, conf_w_in, conf_b_in,
              conf_dw_w, conf_dw_b, conf_ln_g, conf_ln_b, conf_w_out, conf_b_out):
    weights = (ada_w, ada_b, norm_g, ffn_w_up, ffn_w_down,
               hyb_w_in, attn_lambda, attn_subln_g, ssm_conv_w, ssm_conv_b,
               ssm_dt_bias, ssm_a_log, ssm_d, ssm_norm_g, hyb_w_out,
               conf_w_in, conf_b_in, conf_dw_w, conf_dw_b, conf_ln_g, conf_ln_b,
               conf_w_out, conf_b_out)
    y_prompt, k_p, v_p, sconv_p, ssm_p, cconv_p = _trunk(x_prompt, c_prompt, None, weights)
    past = (cache_attn_k, cache_attn_v, state_ssm_conv, state_ssm, state_conf_conv)
    y_sample, k_s, v_s, sconv_s, ssm_s, cconv_s = _trunk(x_sample, c_sample, past, weights)
    return (y_prompt, y_sample, k_p, v_p, sconv_p, ssm_p, cconv_p, k_s, v_s, sconv_s, ssm_s, cconv_s)
```

```python
import math
import numpy as np
import concourse.bass as bass
import concourse.mybir as mybir
from concourse.bass_utils import run_bass_kernel_spmd
from contextlib import ExitStack

F32 = mybir.dt.float32
BF16 = mybir.dt.bfloat16
AF = mybir.ActivationFunctionType
ALU = mybir.AluOpType
AX = mybir.AxisListType

D = 1024
DFF = 2816
DIN = 5648
EPS = 1e-6
NCORES = 8
SKIP = set()
DEBUG = False
CONF_STOP = 99
E1B_STOP = 99
CONF_SEQS = None


class Buf:
    __slots__ = ("name", "w", "r", "excl")

    def __init__(self, name="", excl=False):
        self.name = name
        self.w = None
        self.r = {}
        self.excl = excl


class KB:
    def __init__(self, nc, es, n_lanes=8):
        self.nc = nc
        self.eng = {"pe": nc.tensor, "act": nc.scalar, "dve": nc.vector, "pool": nc.gpsimd, "sp": nc.sync}
        self.sem, self.cnt, self.mult = {}, {}, {}
        for e in self.eng:
            self.sem[e] = es.enter_context(nc.semaphore("s_" + e))
            self.cnt[e] = 0
            self.mult[e] = 1
        self.lanes = {}
        for q in ("sp", "pool"):
            ls = []
            for i in range(n_lanes):
                key = "L%s%d" % (q, i)
                self.sem[key] = es.enter_context(nc.semaphore("s_" + key))
                self.cnt[key] = 0
                self.mult[key] = 16
                ls.append(key)
            self.lanes[q] = ls
        self.lane_rr = {q: 0 for q in self.lanes}
        self.known = {e: {} for e in self.eng}
        self.nins = 0
        self.nwait = 0

    def _need(self, deps, key, k):
        if deps.get(key, 0) < k:
            deps[key] = k

    def _collect(self, R, W, eng=None):
        deps = {}
        for b in R:
            if b.w is not None:
                self._need(deps, b.w[0], b.w[1])
            if b.excl:
                for e, k in b.r.items():
                    if e != eng:
                        self._need(deps, e, k)
        for b in W:
            if b.w is not None:
                self._need(deps, b.w[0], b.w[1])
            for e, k in b.r.items():
                self._need(deps, e, k)
        return deps

    def _emit_waits(self, e, deps):
        kn = self.known[e]
        for key, k in deps.items():
            if key == e and e == "pe":
                continue
            if kn.get(key, 0) >= k:
                continue
            self.eng[e].wait_ge(self.sem[key], k * self.mult[key])
            kn[key] = k
            self.nwait += 1

    def _mark(self, key, k, R, W):
        for b in R:
            if b.r.get(key, 0) < k:
                b.r[key] = k
        for b in W:
            b.w = (key, k)
            b.r = {}

    def op(self, e, fn, R=(), W=()):
        self._emit_waits(e, self._collect(R, W, e))
        ins = fn(self.eng[e])
        self.cnt[e] += 1
        ins.then_inc(self.sem[e], 1)
        self._mark(e, self.cnt[e], R, W)
        self.nins += 1

    def dma(self, q, out, in_, R=(), W=()):
        ls = self.lanes[q]
        lane = ls[self.lane_rr[q] % len(ls)]
        self.lane_rr[q] += 1
        deps = self._collect(R, W)
        if self.cnt[lane] > 0:
            self._need(deps, lane, self.cnt[lane])
        self._emit_waits(q, deps)
        ins = self.eng[q].dma_start(out=out, in_=in_)
        self.cnt[lane] += 1
        ins.then_inc(self.sem[lane], 16)
        self._mark(lane, self.cnt[lane], R, W)
        self.nins += 1

    def barrier(self):
        for e in self.eng:
            deps = {k2: self.cnt[k2] for k2 in self.cnt if self.cnt[k2] > 0 and k2 != e}
            self._emit_waits(e, deps)


def _consts():
    c = {}
    i = np.arange(128)
    c["ident"] = np.eye(128, dtype=np.float32)
    c["ones"] = np.ones((128, 128), np.float32)
    c["trile"] = (i[:, None] <= i[None, :]).astype(np.float32)
    c["ugt"] = (i[:, None] > i[None, :]).astype(np.float32)
    cf = np.stack([c["ident"], c["ones"], c["trile"], c["ugt"]], 0)
    pos = np.arange(8192)
    kaug = np.zeros((8, 5, 8192), np.float32)
    qaug = np.zeros((8, 5, 8192), np.float32)
    corr = np.zeros((8, 128, 128), np.float32)
    for h in range(8):
        s8 = 8.0 * 2.0 ** (-(h + 1))
        ph, pl = (pos // 128).astype(np.float32), (pos % 128).astype(np.float32)
        qaug[h, 0] = -s8 * 128.0 * ph
        qaug[h, 1] = -s8 * pl
        qaug[h, 2] = 0.0
        qaug[h, 3] = 1.0
        qaug[h, 4] = 1.0
        kaug[h, 0] = 1.0
        kaug[h, 1] = 1.0
        kaug[h, 2] = 1.0
        kaug[h, 3] = s8 * 128.0 * ph
        kaug[h, 4] = s8 * pl
        kk, qq = i[:, None], i[None, :]
        cm = np.where(kk > qq, -2.0 * s8 * (kk - qq), 0.0)
        cm = np.where((kk // 64) > (qq // 64), -240000.0, cm)
        corr[h] = cm
    return cf, kaug, qaug, corr


class Prog:
    def __init__(self, T=8192, depth=4, TS=16, PAST=1024):
        self.T, self.depth, self.TS, self.PAST = T, depth, TS, PAST
        self.NE = (depth + 1) // 2
        self.NO = depth // 2
        self.nc = bass.Bass("TRN2", target_bir_lowering=False)
        self.build()

    def din(self, name, shape, dt=F32):
        return self.nc.dram_tensor(name, list(shape), dt, kind="ExternalInput").ap()

    def dout(self, name, shape, dt=F32):
        return self.nc.dram_tensor(name, list(shape), dt, kind="ExternalOutput").ap()

    def dscr(self, name, shape, dt=F32):
        return self.nc.dram_tensor(name, list(shape), dt, kind="Internal").ap()

    def sb(self, es, name, shape, dt):
        self._uid += 1
        return es.enter_context(self.nc.sbuf_tensor("%s_%d" % (name, self._uid), list(shape), dt))

    def bank(self):
        i = self.bank_list[self.bank_rr % len(self.bank_list)]
        self.bank_rr += 1
        return self.PP[i // 2][:, (i % 2) * 512:(i % 2) * 512 + 512], [self.pb[i]]

    def pair(self):
        i = self.pair_list[self.pair_rr % len(self.pair_list)]
        self.pair_rr += 1
        return self.PP[i], [self.pb[2 * i], self.pb[2 * i + 1]]

    def rot(self, key, es, shape, dt, n=2):
        tiles = [(self.sb(es, key, shape, dt), Buf(key)) for _ in range(n)]
        st = {"i": 0}

        def nxt():
            t = tiles[st["i"] % n]
            st["i"] += 1
            return t
        return nxt

    def load_fm(self, es, dst_fn, src, R, nch, bdst, q="sp", pad=0):
        kb = self.kb
        Rp = R + pad
        for b0 in range(0, nch, 8):
            nb = min(8, nch - b0)
            st, bst = self.wstage()
            if pad:
                kb.op("pool", lambda e, st=st: e.memset(st[0:Rp, :], 0.0), W=[bst])
            kb.dma(q, st[pad:Rp, 0:nb * 128], src[:, b0 * 128:(b0 + nb) * 128], W=[bst])
            for c0 in range(0, nb, 4):
                pb, bb = self.bank()
                n4 = min(4, nb - c0)
                for c in range(c0, c0 + n4):
                    kb.op("pe", lambda e, c=c, pb=pb, st=st: e.transpose(out=pb[:, (c - c0) * 32:(c - c0) * 32 + Rp],
                                                            in_=st[0:Rp, c * 128:(c + 1) * 128],
                                                            identity=self.identf[0:Rp, 0:Rp]), R=[bst, self.bconst], W=bb)
                for c in range(c0, c0 + n4):
                    kb.op("act", lambda e, c=c, pb=pb: e.activation(out=dst_fn(b0 + c), in_=pb[:, (c - c0) * 32:(c - c0) * 32 + Rp],
                                                             func=AF.Copy), R=bb, W=[bdst])

    def load_w(self, dst, src3, nk, ncols, bdst, c0=0):
        kb = self.kb
        for k in range(nk):
            for a in range(0, ncols, 1024):
                w = min(1024, ncols - a)
                st, bst = self.wstage()
                kb.dma("sp", st[:, 0:w], src3[k, :, c0 + a:c0 + a + w], W=[bst])
                kb.op("pool", lambda e, st=st, k=k, a=a, w=w: e.tensor_copy(out=dst[:, k, a:a + w], in_=st[:, 0:w]),
                      R=[bst], W=[bdst])

    def prelude(self, xsrc, P, hT, bhT, col0, gs, sh, bbc):
        kb = self.kb
        xt, bx = self.xrot()
        kb.dma("sp", xt[0:P, :], xsrc, W=[bx])
        junk, bj = self.frot()
        ssq, bs = self.srot()
        kb.op("act", lambda e: e.activation(out=junk[0:P, :], in_=xt[0:P, :], func=AF.Square, accum_out=ssq[0:P, 0:1]),
              R=[bx], W=[bj, bs])
        kb.op("act", lambda e: e.activation(out=ssq[0:P, 1:2], in_=ssq[0:P, 0:1], func=AF.Sqrt, scale=1.0 / D, bias=self.epsb[0:P, 0:1]),
              R=[bs, self.bconst], W=[bs])
        kb.op("dve", lambda e: e.reciprocal(out=ssq[0:P, 1:2], in_=ssq[0:P, 1:2]), R=[bs], W=[bs])
        kb.op("dve", lambda e: e.scalar_tensor_tensor(out=junk[0:P, :], in0=xt[0:P, :], scalar=ssq[0:P, 1:2], in1=gs[0:P, :],
                                                       op0=ALU.mult, op1=ALU.mult), R=[bx, bs, bbc], W=[bj])
        hb, bh = self.hrot()
        kb.op("pool", lambda e: e.tensor_tensor(out=hb[0:P, :], in0=junk[0:P, :], in1=sh[0:P, :], op=ALU.add),
              R=[bj, bbc], W=[bh])
        self.transpose_to(hb, bh, P, 8, lambda k: hT[:, k, col0:col0 + P], bhT)

    def transpose_to(self, src, bsrc, P, nch, dst_fn, bdst):
        kb = self.kb
        for k0 in range(0, nch, 8):
            pb, bb = self.bank()
            pbb = pb.bitcast(BF16)
            n8 = min(8, nch - k0)
            for k in range(k0, k0 + n8):
                kb.op("pe", lambda e, k=k: e.transpose(out=pbb[:, (k - k0) * 128:(k - k0) * 128 + P],
                                                        in_=src[0:P, k * 128:(k + 1) * 128],
                                                        identity=self.identb[0:P, 0:P]), R=[bsrc, self.bconst], W=bb)
            for k in range(k0, k0 + n8):
                kb.op("act", lambda e, k=k: e.activation(out=dst_fn(k), in_=pbb[:, (k - k0) * 128:(k - k0) * 128 + P],
                                                         func=AF.Copy), R=bb, W=[bdst])

    def post(self, pp, bpp, P, xsrc, xdst, gg, bbc, bdram):
        kb = self.kb
        junk, bj = self.frot()
        ssq, bs = self.srot()
        kb.op("act", lambda e: e.activation(out=junk[0:P, :], in_=pp[0:P, :], func=AF.Square, accum_out=ssq[0:P, 0:1]),
              R=bpp, W=[bj, bs])
        kb.op("act", lambda e: e.activation(out=ssq[0:P, 1:2], in_=ssq[0:P, 0:1], func=AF.Sqrt, scale=1.0 / D, bias=self.epsb[0:P, 0:1]),
              R=[bs, self.bconst], W=[bs])
        kb.op("dve", lambda e: e.reciprocal(out=ssq[0:P, 1:2], in_=ssq[0:P, 1:2]), R=[bs], W=[bs])
        kb.op("dve", lambda e: e.scalar_tensor_tensor(out=junk[0:P, :], in0=pp[0:P, :], scalar=ssq[0:P, 1:2], in1=gg[0:P, :],
                                                       op0=ALU.mult, op1=ALU.mult), R=bpp + [bs, bbc], W=[bj])
        xt, bx = self.xrot()
        kb.dma("sp", xt[0:P, :], xsrc, R=[bdram], W=[bx])
        kb.op("pool", lambda e: e.tensor_tensor(out=xt[0:P, :], in0=xt[0:P, :], in1=junk[0:P, :], op=ALU.add),
              R=[bx, bj], W=[bx])
        kb.dma("pool", xdst, xt[0:P, :], R=[bx], W=[bdram])

    def make_bc(self, l, sub, ci):
        kb = self.kb
        gs, sh, gg = self.bc_gs, self.bc_sh, self.bc_gg
        bbc = self.bbc
        tmp, btmp = self.wstage()
        tmp2, btmp2 = self.wstage()
        base = sub * 3 * D
        mrow = lambda a: self.MOD[l, ci:ci + 1, base + a * D: base + (a + 1) * D].partition_broadcast(128)
        grow = lambda i: self.norm_g[l, i:i + 1, :].partition_broadcast(128)
        kb.dma("sp", sh[:], mrow(0), R=[self.bmod], W=[bbc])
        kb.dma("sp", gs[:], mrow(1), R=[self.bmod], W=[bbc])
        kb.dma("sp", gg[:], mrow(2), R=[self.bmod], W=[bbc])
        kb.dma("sp", tmp[:], grow(2 * sub), W=[btmp])
        kb.op("dve", lambda e: e.scalar_tensor_tensor(out=gs[:], in0=gs[:], scalar=1.0, in1=tmp[:], op0=ALU.add, op1=ALU.mult),
              R=[bbc, btmp], W=[bbc])
        kb.dma("sp", tmp2[:], grow(2 * sub + 1), W=[btmp2])
        kb.op("dve", lambda e: e.tensor_tensor(out=gg[:], in0=gg[:], in1=tmp2[:], op=ALU.mult), R=[bbc, btmp2], W=[bbc])
        return gs, sh, gg, bbc

    def seqs(self):
        T, TS = self.T, self.TS
        S = []
        S.append(dict(name="p", ci=0, T=T, P=128, GN=min(512, T), x0=self.x_p, xb=self.xb_p, y=self.y_p, s=None))
        for s in range(2):
            S.append(dict(name="s%d" % s, ci=1 + s, T=TS, P=TS, GN=TS, x0=self.x_s[s], xb=self.xb_s[s], y=self.y_s[s], s=s))
        return S

    def xio(self, sq, stage):
        act = self.active_stages
        src = sq["x0"] if stage == act[0] else sq["xb"]
        dst = sq["y"] if stage == act[-1] else sq["xb"]
        return src, dst

    def build(self):
        nc = self.nc
        T, TS, PAST, NE, NO, depth = self.T, self.TS, self.PAST, self.NE, self.NO, self.depth
        TK = PAST + TS
        self._uid = 0
        self.x_p = self.din("x_p", [T, D])
        self.x_s = self.din("x_s", [2, TS, D])
        self.c_all = self.din("c_all", [3, D])
        self.cache_k = self.din("cache_k", [NE, 2, PAST, D])
        self.cache_v = self.din("cache_v", [NE, 2, PAST, D])
        self.st_sconv = self.din("st_sconv", [NE, 2, 3, 1536])
        self.st_ssm = self.din("st_ssm", [NE, 2, 1024, 128])
        self.st_cconv = self.din("st_cconv", [max(NO, 1), 2, 30, D])
        self.ada_w = self.din("ada_w", [depth, D, 6 * D])
        self.ada_b = self.din("ada_b", [depth, 6 * D])
        self.norm_g = self.din("norm_g", [depth, 4, D])
        self.ffn_w_up = self.din("ffn_w_up", [depth, D, 2 * DFF])
        self.ffn_w_down = self.din("ffn_w_down", [depth, DFF, D])
        self.hyb_w_in = self.din("hyb_w_in", [NE, D, DIN])
        self.attn_lambda = self.din("attn_lambda", [NE, 256])
        self.attn_subln_g = self.din("attn_subln_g", [NE, 128])
        self.ssm_conv_w = self.din("ssm_conv_w", [NE, 4, 1536])
        self.ssm_conv_b = self.din("ssm_conv_b", [NE, 1536])
        self.ssm_dt_bias = self.din("ssm_dt_bias", [NE, 16])
        self.ssm_a_log = self.din("ssm_a_log", [NE, 16])
        self.ssm_d = self.din("ssm_d", [NE, 16])
        self.ssm_norm_g = self.din("ssm_norm_g", [NE, D])
        self.hyb_w_out = self.din("hyb_w_out", [NE, 2 * D, D])
        self.conf_w_in = self.din("conf_w_in", [max(NO, 1), D, 2 * D])
        self.conf_b_in = self.din("conf_b_in", [max(NO, 1), 2 * D])
        self.conf_dw_w = self.din("conf_dw_w", [max(NO, 1), 31, D])
        self.conf_dw_b = self.din("conf_dw_b", [max(NO, 1), D])
        self.conf_ln_g = self.din("conf_ln_g", [max(NO, 1), D])
        self.conf_ln_b = self.din("conf_ln_b", [max(NO, 1), D])
        self.conf_w_out = self.din("conf_w_out", [max(NO, 1), D, D])
        self.conf_b_out = self.din("conf_b_out", [max(NO, 1), D])
        self.cst_f = self.din("cst_f", [4, 128, 128])
        self.cst_kaug = self.din("cst_kaug", [8, 5, 8192])
        self.cst_qaug = self.din("cst_qaug", [8, 5, 8192])
        self.cst_corr = self.din("cst_corr", [8, 128, 128])
        self.y_p = self.dout("y_p", [T, D])
        self.y_s = self.dout("y_s", [2, TS, D])
        self.k_p = self.dout("k_p", [NE, T, D])
        self.v_p = self.dout("v_p", [NE, T, D])
        self.sconv_p = self.dout("sconv_p", [NE, 3, 1536])
        self.ssm_p = self.dout("ssm_p", [NE, 1024, 128])
        self.cconv_p = self.dout("cconv_p", [max(NO, 1), 30, D])
        self.k_s = self.dout("k_s", [NE, 2, TS, D])
        self.v_s = self.dout("v_s", [NE, 2, TS, D])
        self.sconv_s = self.dout("sconv_s", [NE, 2, 3, 1536])
        self.ssm_s = self.dout("ssm_s", [NE, 2, 1024, 128])
        self.cconv_s = self.dout("cconv_s", [max(NO, 1), 2, 30, D])
        self.xb_p = self.dscr("xb_p", [T, D])
        self.xb_s = self.dscr("xb_s", [2, TS, D])
        self.MOD = self.dscr("modrows", [depth, 3, 6 * D])
        self.QT = [self.dscr("qt_p", [8, 128, T], BF16)] + [self.dscr("qt_s%d" % s, [8, 128, TS], BF16) for s in range(2)]
        self.KT = [self.dscr("kt_p", [8, 128, T], BF16)] + [self.dscr("kt_s%d" % s, [8, 128, TK], BF16) for s in range(2)]
        self.VB = [self.dscr("vb_p", [T, D], BF16)] + [self.dscr("vb_s%d" % s, [TK, D], BF16) for s in range(2)]
        self.OT = [self.dscr("ot_p", [D, T], BF16)] + [self.dscr("ot_s%d" % s, [D, TS], BF16) for s in range(2)]
        self.YT = [self.dscr("yt_p", [D, T], BF16)] + [self.dscr("yt_s%d" % s, [D, TS], BF16) for s in range(2)]
        self.bscr = [dict(q=Buf(), k=Buf(), v=Buf(), o=Buf(), y=Buf(), x=Buf()) for _ in range(3)]
        self.bmod = Buf()
        self.bout = Buf()

        with ExitStack() as es:
            self.kb = kb = KB(nc, es)
            self.PP = [es.enter_context(nc.psum_tensor("pp%d" % i, [128, 1024], F32)) for i in range(4)]
            self.pb = [Buf("bank%d" % i, excl=True) for i in range(8)]
            self.bank_list, self.bank_rr = list(range(8)), 0
            self.pair_list, self.pair_rr = list(range(4)), 0
            self.bconst = Buf("const")
            cf = self.sb(es, "cf", [128, 4, 128], F32)
            kb.dma("sp", cf[:], self.cst_f.rearrange("a p c -> p a c"), W=[self.bconst])
            self.identf, self.onesf, self.trilef, self.ugtf = cf[:, 0, :], cf[:, 1, :], cf[:, 2, :], cf[:, 3, :]
            cb = self.sb(es, "cb", [128, 4, 128], BF16)
            kb.op("pool", lambda e: e.tensor_copy(out=cb[:], in_=cf[:]), R=[self.bconst], W=[self.bconst])
            self.identb, self.onesb, self.trileb, self.ugtb = cb[:, 0, :], cb[:, 1, :], cb[:, 2, :], cb[:, 3, :]
            self.epsb = self.sb(es, "epsb", [128, 1], F32)
            kb.op("pool", lambda e: e.memset(self.epsb[:], EPS), W=[self.bconst])
            self.m1024 = self.sb(es, "m1024", [128, 128], F32)
            kb.op("pool", lambda e: e.memset(self.m1024[:], 1.0 / 1024), W=[self.bconst])
            self.m128 = self.sb(es, "m128", [128, 128], F32)
            kb.op("pool", lambda e: e.memset(self.m128[:], 1.0 / 128), W=[self.bconst])
            self.xrot = self.rot("xt", es, [128, D], F32, 2)
            self.frot = self.rot("ft", es, [128, D], F32, 2)
            self.hrot = self.rot("hb", es, [128, D], BF16, 1)
            self.srot = self.rot("ssq", es, [128, 4], F32, 4)
            self.wstage = self.rot("wst", es, [128, 1024], F32, 2)
            self.bc_gs = self.sb(es, "bcgs", [128, D], F32)
            self.bc_sh = self.sb(es, "bcsh", [128, D], F32)
            self.bc_gg = self.sb(es, "bcgg", [128, D], F32)
            self.bbc = Buf("bc")

            self.active_stages = []
            for l in range(depth):
                if (l % 2 == 0 and "hyb" not in SKIP) or (l % 2 == 1 and "conf" not in SKIP):
                    self.active_stages.append(2 * l)
                if "ffn" not in SKIP:
                    self.active_stages.append(2 * l + 1)
            self.phase_adaln()
            S = self.seqs()
            for l in range(depth):
                j = l // 2
                if l % 2 == 0 and "hyb" not in SKIP:
                    for ph in (self.phase_e1a, self.phase_e1b, self.phase_e2, self.phase_e3):
                        if DEBUG:
                            print("phase", ph.__name__, "l", l, "next_id", self.nc.next_id(), flush=True)
                        if ph.__name__[6:] not in SKIP:
                            ph(l, j, S)
                if l % 2 == 1 and "conf" not in SKIP:
                    self.phase_conf(l, j, S)
                if "ffn" not in SKIP:
                    self.phase_ffn(l, S)
            kb.barrier()
            self.stats = (kb.nins, kb.nwait)

    def phase_adaln(self):
        kb, depth = self.kb, self.depth
        with ExitStack() as es:
            ct = self.sb(es, "ct", [4, D], F32); bct = Buf()
            kb.dma("sp", ct[0:3, :], self.c_all, W=[bct])
            kb.op("act", lambda e: e.activation(out=ct[0:3, :], in_=ct[0:3, :], func=AF.Silu), R=[bct], W=[bct])
            cT = self.sb(es, "cT", [128, 8, 4], F32); bcT = Buf()
            pb, bb = self.bank()
            for k in range(8):
                kb.op("pe", lambda e, k=k: e.transpose(out=pb[:, k * 4:k * 4 + 3], in_=ct[0:3, k * 128:(k + 1) * 128],
                                                        identity=self.identf[0:3, 0:3]), R=[bct, self.bconst], W=bb)
            for k in range(8):
                kb.op("act", lambda e, k=k: e.activation(out=cT[:, k, 0:3], in_=pb[:, k * 4:k * 4 + 3], func=AF.Copy), R=bb, W=[bcT])
            ab = self.sb(es, "ab", [4, 6 * D], F32); bab = Buf()
            mrow = self.sb(es, "mrow", [4, 6 * D], F32); bmr = Buf()
            wrot = self.rot("adaw", es, [128, 3072], F32, 3)
            for l in range(depth):
                kb.dma("sp", ab[0:3, :], self.ada_b[l:l + 1, :].partition_broadcast(3), R=[bab], W=[bab])
                for half in range(2):
                    banks = [self.bank() for _ in range(6)]
                    for k in range(8):
                        wt, bw = wrot()
                        kb.dma("sp" if k % 2 == 0 else "pool", wt[:], self.ada_w[l, k * 128:(k + 1) * 128, half * 3072:(half + 1) * 3072], W=[bw])
                        for n in range(6):
                            pbn, bbn = banks[n]
                            kb.op("pe", lambda e, k=k, n=n, pbn=pbn, wt=wt: e.matmul(pbn[0:3, :], lhsT=cT[:, k, 0:3], rhs=wt[:, n * 512:(n + 1) * 512],
                                                                                  start=(k == 0), stop=(k == 7)), R=[bcT, bw], W=bbn)
                    for n in range(6):
                        pbn, bbn = banks[n]
                        c0 = half * 3072 + n * 512
                        kb.op("dve", lambda e, pbn=pbn, c0=c0: e.tensor_tensor(out=mrow[0:3, c0:c0 + 512], in0=pbn[0:3, :], in1=ab[0:3, c0:c0 + 512], op=ALU.add),
                              R=bbn + [bab], W=[bmr])
                kb.dma("sp", self.MOD[l], mrow[0:3, :], R=[bmr], W=[self.bmod])
            kb.barrier()

    def phase_ffn(self, l, S):
        kb = self.kb
        stage = 2 * l + 1
        with ExitStack() as es:
            wup = self.sb(es, "wup", [128, 8, 2 * DFF], BF16); bwu = Buf()
            wdn = self.sb(es, "wdn", [128, 22, D], BF16); bwd = Buf()
            self.load_w(wup, self.ffn_w_up[l].rearrange("(k p) n -> k p n", p=128), 8, 2 * DFF, bwu)
            self.load_w(wdn, self.ffn_w_down[l].rearrange("(k p) n -> k p n", p=128), 22, D, bwd)
            GNmax = S[0]["GN"]
            hT = self.sb(es, "hT", [128, 8, GNmax], BF16); bhT = Buf()
            aT = self.sb(es, "aT", [128, 22, GNmax], BF16); baT = Buf()
            sgr = self.rot("sg", es, [128, GNmax], F32, 1)
            kb.op("pool", lambda e: e.memset(hT[:], 0.0), W=[bhT])
            for si, sq in enumerate(S):
                gs, sh, gg, bbc = self.make_bc(l, 1, sq["ci"])
                src, dst = self.xio(sq, stage)
                bx = self.bscr[si]["x"]
                P, GN = sq["P"], sq["GN"]
                GNp = max(GN, 128)
                for g0 in range(0, sq["T"], GN):
                    nt = GN // P
                    for m in range(nt):
                        self.prelude_x(src[g0 + m * P:g0 + (m + 1) * P, :], bx, P, hT, bhT, m * P, gs, sh, bbc)
                    for jf in range(22):
                        pg, bg = self.bank()
                        pu, bu = self.bank()
                        for k in range(8):
                            kb.op("pe", lambda e, k=k, pg=pg: e.matmul(pg[:, 0:GNp], lhsT=wup[:, k, jf * 128:(jf + 1) * 128], rhs=hT[:, k, 0:GNp],
                                                                      start=(k == 0), stop=(k == 7)), R=[bwu, bhT], W=bg)
                        for k in range(8):
                            kb.op("pe", lambda e, k=k, pu=pu: e.matmul(pu[:, 0:GNp], lhsT=wup[:, k, DFF + jf * 128:DFF + (jf + 1) * 128], rhs=hT[:, k, 0:GNp],
                                                                      start=(k == 0), stop=(k == 7)), R=[bwu, bhT], W=bu)
                        sg, bsg = sgr()
                        kb.op("act", lambda e, sg=sg, pg=pg: e.activation(out=sg[:, 0:GN], in_=pg[:, 0:GN], func=AF.Silu), R=bg, W=[bsg])
                        kb.op("dve", lambda e, sg=sg, pu=pu, jf=jf: e.tensor_tensor(out=aT[:, jf, 0:GN], in0=sg[:, 0:GN], in1=pu[:, 0:GN], op=ALU.mult),
                              R=[bsg] + bu, W=[baT])
                    for m in range(nt):
                        pp, bpp = self.pair()
                        for nh in range(2):
                            for k in range(22):
                                kb.op("pe", lambda e, k=k, nh=nh, m=m, pp=pp: e.matmul(pp[0:P, nh * 512:(nh + 1) * 512], lhsT=aT[:, k, m * P:(m + 1) * P],
                                                                                     rhs=wdn[:, k, nh * 512:(nh + 1) * 512], start=(k == 0), stop=(k == 21)),
                                      R=[baT, bwd], W=bpp)
                        r0 = g0 + m * P
                        self.post(pp, bpp, P, src[r0:r0 + P, :], dst[r0:r0 + P, :], gg, bbc, bx)
            kb.barrier()

    def prelude_x(self, xsrc, bdram, P, hT, bhT, col0, gs, sh, bbc):
        kb = self.kb
        xt, bx = self.xrot()
        kb.dma("sp", xt[0:P, :], xsrc, R=[bdram], W=[bx])
        junk, bj = self.frot()
        ssq, bs = self.srot()
        kb.op("act", lambda e: e.activation(out=junk[0:P, :], in_=xt[0:P, :], func=AF.Square, accum_out=ssq[0:P, 0:1]),
              R=[bx], W=[bj, bs])
        kb.op("act", lambda e: e.activation(out=ssq[0:P, 1:2], in_=ssq[0:P, 0:1], func=AF.Sqrt, scale=1.0 / D, bias=self.epsb[0:P, 0:1]),
              R=[bs, self.bconst], W=[bs])
        kb.op("dve", lambda e: e.reciprocal(out=ssq[0:P, 1:2], in_=ssq[0:P, 1:2]), R=[bs], W=[bs])
        kb.op("dve", lambda e: e.scalar_tensor_tensor(out=junk[0:P, :], in0=xt[0:P, :], scalar=ssq[0:P, 1:2], in1=gs[0:P, :],
                                                       op0=ALU.mult, op1=ALU.mult), R=[bx, bs, bbc], W=[bj])
        hb, bh = self.hrot()
        kb.op("pool", lambda e: e.tensor_tensor(out=hb[0:P, :], in0=junk[0:P, :], in1=sh[0:P, :], op=ALU.add),
              R=[bj, bbc], W=[bh])
        self.transpose_to(hb, bh, P, 8, lambda k: hT[:, k, col0:col0 + P], bhT)

    def evac(self, out, in_, R, W):
        self._ev = getattr(self, "_ev", 0) + 1
        if self._ev % 2 == 0:
            self.kb.op("act", lambda e: e.activation(out=out, in_=in_, func=AF.Copy), R=R, W=W)
        else:
            self.kb.op("dve", lambda e: e.tensor_copy(out=out, in_=in_), R=R, W=W)

    def phase_e1a(self, l, j, S):
        kb = self.kb
        stage = 2 * l
        PAST, TS = self.PAST, self.TS
        with ExitStack() as es:
            w = self.sb(es, "w1a", [128, 8, 3072], BF16); bw = Buf()
            self.load_w(w, self.hyb_w_in[j].rearrange("(k p) n -> k p n", p=128), 8, 3072, bw, c0=0)
            GNmax = S[0]["GN"]
            hT = self.sb(es, "ahT", [128, 8, GNmax], BF16); bhT = Buf()
            kvt = self.rot("kvt", es, [128, 2048], F32, 2)
            vbr = self.rot("vbr", es, [128, D], BF16, 2)
            fmr = self.rot("fmr", es, [128, GNmax], BF16, 3)
            kfr = self.rot("kfr", es, [128, 8, 128], BF16, 2)
            kb.op("pool", lambda e: e.memset(hT[:], 0.0), W=[bhT])
            for si, sq in enumerate(S):
                gs, sh, gg, bbc = self.make_bc(l, 0, sq["ci"])
                src, _ = self.xio(sq, stage)
                bx = self.bscr[si]["x"]
                bsc = self.bscr[si]
                P, GN, Tq = sq["P"], sq["GN"], sq["T"]
                kp0 = 0
                if sq["s"] is not None:
                    s_ = sq["s"]
                    kp0 = PAST
                    for t in range(PAST // 128):
                        ck, bck = kvt()
                        kb.dma("sp", ck[:, 0:1024], self.cache_k[j, s_, t * 128:(t + 1) * 128, :], W=[bck])
                        kb.dma("sp", ck[:, 1024:2048], self.cache_v[j, s_, t * 128:(t + 1) * 128, :], W=[bck])
                        kf, bkf = kfr()
                        for h0 in (0, 4):
                            pb, bb = self.bank()
                            for h in range(h0, h0 + 4):
                                kb.op("pe", lambda e, h=h, pb=pb, ck=ck: e.transpose(out=pb[:, (h - h0) * 128:(h - h0 + 1) * 128], in_=ck[:, h * 128:(h + 1) * 128],
                                                                                     identity=self.identf[:, :]), R=[bck, self.bconst], W=bb)
                            self.evac(kf[:, h0:h0 + 4, :], pb.rearrange("p (a b) -> p a b", a=4), bb, [bkf])
                        kb.dma("pool", self.KT[si][:, :, t * 128:(t + 1) * 128].rearrange("h r t -> r h t"), kf[:], R=[bkf], W=[bsc["k"]])
                        vb, bvb = vbr()
                        kb.op("pool", lambda e, vb=vb, ck=ck: e.tensor_copy(out=vb[:], in_=ck[:, 1024:2048]), R=[bck], W=[bvb])
                        kb.dma("pool", self.VB[si][t * 128:(t + 1) * 128, :], vb[:], R=[bvb], W=[bsc["v"]])
                for g0 in range(0, Tq, GN):
                    nt = GN // P
                    for m in range(nt):
                        self.prelude_x(src[g0 + m * P:g0 + (m + 1) * P, :], bx, P, hT, bhT, m * P, gs, sh, bbc)
                    for m in range(nt):
                        kv, bkv = kvt()
                        for n4 in range(4):
                            pb, bb = self.bank()
                            for k in range(8):
                                kb.op("pe", lambda e, k=k, pb=pb, n4=n4, m=m: e.matmul(pb[0:P, :], lhsT=hT[:, k, m * P:(m + 1) * P], rhs=w[:, k, 1024 + n4 * 512:1024 + (n4 + 1) * 512],
                                                                                     start=(k == 0), stop=(k == 7)), R=[bhT, bw], W=bb)
                            self.evac(kv[0:P, n4 * 512:(n4 + 1) * 512], pb[0:P, :], bb, [bkv])
                        r0 = g0 + m * P
                        if sq["s"] is None:
                            kb.dma("pool", self.k_p[j, r0:r0 + P, :], kv[0:P, 0:1024], R=[bkv], W=[self.bout])
                            kb.dma("pool", self.v_p[j, r0:r0 + P, :], kv[0:P, 1024:2048], R=[bkv], W=[self.bout])
                        else:
                            kb.dma("pool", self.k_s[j, sq["s"], r0:r0 + P, :], kv[0:P, 0:1024], R=[bkv], W=[self.bout])
                            kb.dma("pool", self.v_s[j, sq["s"], r0:r0 + P, :], kv[0:P, 1024:2048], R=[bkv], W=[self.bout])
                        vb, bvb = vbr()
                        kb.op("pool", lambda e, vb=vb, kv=kv: e.tensor_copy(out=vb[0:P, :], in_=kv[0:P, 1024:2048]), R=[bkv], W=[bvb])
                        kb.dma("pool", self.VB[si][kp0 + r0:kp0 + r0 + P, :], vb[0:P, :], R=[bvb], W=[bsc["v"]])
                    for c in range(16):
                        pb, bb = self.bank()
                        for k in range(8):
                            kb.op("pe", lambda e, k=k, pb=pb, c=c: e.matmul(pb[:, 0:max(GN, 128)], lhsT=w[:, k, c * 128:(c + 1) * 128], rhs=hT[:, k, 0:max(GN, 128)],
                                                                           start=(k == 0), stop=(k == 7)), R=[bhT, bw], W=bb)
                        fm, bfm = fmr()
                        self.evac(fm[:, 0:GN], pb[:, 0:GN], bb, [bfm])
                        if c < 8:
                            kb.dma("pool", self.QT[si][c, :, g0:g0 + GN], fm[:, 0:GN], R=[bfm], W=[bsc["q"]])
                        else:
                            kb.dma("pool", self.KT[si][c - 8, :, kp0 + g0:kp0 + g0 + GN], fm[:, 0:GN], R=[bfm], W=[bsc["k"]])
            kb.barrier()

    def phase_e1b(self, l, j, S):
        kb = self.kb
        stage = 2 * l
        with ExitStack() as es:
            w = self.sb(es, "w1b", [128, 8, 2576], BF16); bw = Buf()
            self.load_w(w, self.hyb_w_in[j].rearrange("(k p) n -> k p n", p=128), 8, 2576, bw, c0=3072)
            bsm = Buf("e1bsmall")
            cwf = self.sb(es, "cwf", [128, 12, 4], F32)
            self.load_fm(es, lambda c: cwf[:, c, :], self.ssm_conv_w[j], 4, 12, bsm)
            cbf = self.sb(es, "cbf", [128, 12, 1], F32)
            self.load_fm(es, lambda c: cbf[:, c, :], self.ssm_conv_b[j:j + 1, :], 1, 12, bsm)
            diag4 = self.sb(es, "diag4", [128, 48, 128], BF16)
            kb.op("dve", lambda e: e.tensor_tensor(out=diag4[:], in0=self.identf.unsqueeze(1).to_broadcast([128, 48, 128]),
                                                   in1=cwf[:].rearrange("p c k -> p (c k)").unsqueeze(2).to_broadcast([128, 48, 128]),
                                                   op=ALU.mult), R=[bsm, self.bconst], W=[bsm])
            cbr = self.sb(es, "cbr", [1, 1536], F32)
            kb.dma("sp", cbr[:], self.ssm_conv_b[j:j + 1, :], W=[bsm])
            cbrb = self.sb(es, "cbrb", [1, 1536], BF16)
            kb.op("pool", lambda e: e.tensor_copy(out=cbrb[:], in_=cbr[:]), R=[bsm], W=[bsm])
            sm = self.sb(es, "ssmsm", [128, 4, 16], F32)
            kb.dma("sp", sm[:, 0, :], self.ssm_a_log[j:j + 1, :].partition_broadcast(128), W=[bsm])
            kb.dma("sp", sm[:, 1, :], self.ssm_d[j:j + 1, :].partition_broadcast(128), W=[bsm])
            kb.dma("sp", sm[:, 2, :], self.ssm_dt_bias[j:j + 1, :].partition_broadcast(128), W=[bsm])
            kb.op("act", lambda e: e.activation(out=sm[:, 0, :], in_=sm[:, 0, :], func=AF.Exp), R=[bsm], W=[bsm])
            kb.op("dve", lambda e: e.tensor_scalar(out=sm[:, 0, :], in0=sm[:, 0, :], scalar1=-1.0, scalar2=None, op0=ALU.mult), R=[bsm], W=[bsm])
            a_b, D_b, dtb_b = sm[:, 0, :], sm[:, 1, :], sm[:, 2, :]
            ngb = self.sb(es, "ngb", [128, D], F32)
            kb.dma("sp", ngb[:], self.ssm_norm_g[j:j + 1, :].partition_broadcast(128), W=[bsm])
            if E1B_STOP <= 1:
                kb.barrier()
                return
            GNmax = S[0]["GN"]
            hT = self.sb(es, "bhT", [128, 8, GNmax], BF16); bhT = Buf()
            xbcT = self.sb(es, "xbcT", [128, 12, 4 + GNmax], BF16); bxb = Buf()
            szt = self.sb(es, "szt", [128, D], F32); bsz = Buf()
            xst = self.sb(es, "xst", [128, D], F32); bxs = Buf()
            Rf = self.sb(es, "Rf", [128, 2048], BF16); bR = Buf()
            exf = self.sb(es, "exf", [128, 1024], F32); bex = Buf()
            scf = self.sb(es, "scf", [128, 2048], BF16); bscf = Buf()
            xdt = self.sb(es, "xdt", [128, D], BF16); bxdt = Buf()
            xdtw = self.sb(es, "xdtw", [128, D], BF16); bxdtw = Buf()
            t1 = self.sb(es, "t1", [128, D], F32); bt1 = Buf()
            t3 = self.sb(es, "t3", [128, D], F32); bt3 = Buf()
            ynb = self.sb(es, "ynb", [128, D], BF16); bynb = Buf()
            ynT = self.sb(es, "ynT", [128, 8, 128], BF16); bynT = Buf()
            hst = self.sb(es, "hst", [128, D], F32); bhst = Buf()
            hbf = self.sb(es, "hbf", [128, D], BF16); bhbf = Buf()
            bct = self.sb(es, "bct", [128, 4, 128], BF16); bbct = Buf()
            btok = self.sb(es, "btok", [128, 256], BF16); bbtok = Buf()
            cbm = self.sb(es, "cbm", [128, 2, 128], F32); bcbm = Buf()
            d16 = self.sb(es, "d16", [128, 12, 16], F32); bd = Buf()
            d16b = self.sb(es, "d16b", [128, 16], BF16)
            xraw = self.sb(es, "xraw", [128, 1536], F32); bxr = Buf()
            kb.op("pool", lambda e: e.memset(hT[:], 0.0), W=[bhT])
            kb.op("pool", lambda e: e.memset(xbcT[:], 0.0), W=[bxb])
            for si, sq in enumerate(S):
                if CONF_SEQS is not None and si not in CONF_SEQS:
                    continue
                gs, sh, gg, bbc = self.make_bc(l, 0, sq["ci"])
                src, _ = self.xio(sq, stage)
                bx = self.bscr[si]["x"]
                bsc = self.bscr[si]
                P, GN, Tq = sq["P"], sq["GN"], sq["T"]
                Rv = Rf[0:P, 0:16 * P].rearrange("p (h l) -> p h l", h=16)
                scv = scf[0:P, 0:16 * P].rearrange("p (h l) -> p h l", h=16)
                if sq["s"] is None:
                    kb.op("pool", lambda e: e.memset(xbcT[:, :, 0:4], 0.0), W=[bxb])
                    kb.op("pool", lambda e: e.memset(hst[:], 0.0), W=[bhst])
                    kb.op("pool", lambda e: e.memset(hbf[:], 0.0), W=[bhbf])
                else:
                    s_ = sq["s"]
                    self.load_fm(es, lambda c: xbcT[:, c, 0:4], self.st_sconv[j, s_], 3, 12, bxb, pad=1)
                    st, bst = self.wstage()
                    kb.dma("sp", st[:].rearrange("p (c n) -> p c n", c=8), self.st_ssm[j, s_].rearrange("(c p) n -> p c n", p=128), W=[bst])
                    pp, bpp = self.pair()
                    for c in range(8):
                        kb.op("pe", lambda e, c=c, pp=pp, st=st: e.transpose(out=pp[:, c * 128:(c + 1) * 128], in_=st[:, c * 128:(c + 1) * 128], identity=self.identf[:, :]),
                              R=[bst, self.bconst], W=bpp)
                    kb.op("act", lambda e, pp=pp: e.activation(out=hst[:], in_=pp[:], func=AF.Copy), R=bpp, W=[bhst])
                    kb.op("dve", lambda e: e.tensor_copy(out=hbf[:], in_=hst[:]), R=[bhst], W=[bhbf])
                ngr = Tq // GN
                for gi in range(ngr):
                    g0 = gi * GN
                    nt = GN // P
                    for m in range(nt):
                        self.prelude_x(src[g0 + m * P:g0 + (m + 1) * P, :], bx, P, hT, bhT, m * P, gs, sh, bbc)
                    for c in range(12):
                        pb, bb = self.bank()
                        for k in range(8):
                            kb.op("pe", lambda e, k=k, pb=pb, c=c: e.matmul(pb[:, 0:max(GN, 128)], lhsT=w[:, k, 1024 + c * 128:1024 + (c + 1) * 128], rhs=hT[:, k, 0:max(GN, 128)],
                                                                           start=(k == 0), stop=(k == 7)), R=[bhT, bw], W=bb)
                        self.evac(xbcT[:, c, 4:4 + GN], pb[:, 0:GN], bb, [bxb])
                    for m in range(nt):
                        if E1B_STOP <= 2:
                            continue
                        t0 = m * P
                        r0 = g0 + t0
                        pz, bz = self.pair()
                        for nh in range(2):
                            for k in range(8):
                                kb.op("pe", lambda e, k=k, nh=nh, pz=pz: e.matmul(pz[0:P, nh * 512:(nh + 1) * 512], lhsT=hT[:, k, t0:t0 + P], rhs=w[:, k, nh * 512:(nh + 1) * 512],
                                                                                start=(k == 0), stop=(k == 7)), R=[bhT, bw], W=bz)
                        kb.op("act", lambda e, pz=pz: e.activation(out=szt[0:P, :], in_=pz[0:P, :], func=AF.Silu), R=bz, W=[bsz])
                        pd, bpd = self.bank()
                        for k in range(8):
                            kb.op("pe", lambda e, k=k, pd=pd: e.matmul(pd[0:P, 0:128], lhsT=hT[:, k, t0:t0 + P], rhs=w[:, k, 2448:2576], start=(k == 0), stop=(k == 7)),
                                  R=[bhT, bw], W=bpd)
                        X, AXv, EX, LG, DT, DTA, WL, DTW, EXPA, EL = [d16[0:P, i, :] for i in range(10)]
                        EL = d16[:, 9, :]
                        kb.op("dve", lambda e, pd=pd: e.tensor_tensor(out=X, in0=pd[0:P, 112:128], in1=dtb_b[0:P, :], op=ALU.add), R=bpd + [bsm], W=[bd])
                        kb.op("dve", lambda e: e.scalar_tensor_tensor(out=AXv, in0=X, scalar=-1.0, in1=X, op0=ALU.mult, op1=ALU.max), R=[bd], W=[bd])
                        kb.op("act", lambda e: e.activation(out=EX, in_=AXv, func=AF.Exp, scale=-1.0), R=[bd], W=[bd])
                        kb.op("act", lambda e: e.activation(out=LG, in_=EX, func=AF.Ln, bias=1.0), R=[bd], W=[bd])
                        kb.op("dve", lambda e: e.scalar_tensor_tensor(out=DT, in0=X, scalar=0.0, in1=LG, op0=ALU.max, op1=ALU.add), R=[bd], W=[bd])
                        kb.op("dve", lambda e: e.tensor_tensor(out=DTA, in0=DT, in1=a_b[0:P, :], op=ALU.mult), R=[bd, bsm], W=[bd])
                        kb.op("dve", lambda e: e.tensor_copy(out=d16b[0:P, :], in_=DTA), R=[bd], W=[bd])
                        px, bpx = self.pair()
                        for k in range(5):
                            for c in range(8):
                                st_, sp_ = (k == 0 and c % 4 == 0), (k == 4 and c % 4 == 3)
                                if k < 4:
                                    kb.op("pe", lambda e, c=c, k=k, px=px, st_=st_, sp_=sp_: e.matmul(px[0:P, c * 128:(c + 1) * 128], lhsT=xbcT[:, c, 1 + t0 + k:1 + t0 + k + P], rhs=diag4[:, c * 4 + k, :],
                                                                                                   start=st_, stop=sp_), R=[bxb, bsm], W=bpx)
                                else:
                                    kb.op("pe", lambda e, c=c, px=px, st_=st_, sp_=sp_: e.matmul(px[0:P, c * 128:(c + 1) * 128], lhsT=self.onesb[0:1, 0:P], rhs=cbrb[0:1, c * 128:(c + 1) * 128],
                                                                                              start=st_, stop=sp_), R=[bsm, self.bconst], W=bpx)
                        kb.op("act", lambda e, px=px: e.activation(out=xst[0:P, :], in_=px[0:P, :], func=AF.Silu), R=bpx, W=[bxs])
                        pk, bpk = self.bank()
                        for k in range(5):
                            for c in (8, 9):
                                st_, sp_ = (k == 0 and c == 8), (k == 4 and c == 9)
                                if k < 4:
                                    kb.op("pe", lambda e, c=c, k=k, pk=pk, st_=st_, sp_=sp_: e.matmul(pk[0:P, (c - 8) * 128:(c - 7) * 128], lhsT=xbcT[:, c, 1 + t0 + k:1 + t0 + k + P], rhs=diag4[:, c * 4 + k, :],
                                                                                                   start=st_, stop=sp_), R=[bxb, bsm], W=bpk)
                                else:
                                    kb.op("pe", lambda e, c=c, pk=pk, st_=st_, sp_=sp_: e.matmul(pk[0:P, (c - 8) * 128:(c - 7) * 128], lhsT=self.onesb[0:1, 0:P], rhs=cbrb[0:1, c * 128:(c + 1) * 128],
                                                                                              start=st_, stop=sp_), R=[bsm, self.bconst], W=bpk)
                        kb.op("act", lambda e, pk=pk: e.activation(out=btok[0:P, :], in_=pk[0:P, 0:256], func=AF.Silu), R=bpk, W=[bbtok])
                        pf, bpf = self.bank()
                        for k in range(4):
                            for i, c in enumerate((8, 9, 10, 11)):
                                st_, sp_ = (k == 0 and i == 0), (k == 3 and i == 3)
                                kb.op("pe", lambda e, c=c, k=k, i=i, pf=pf, st_=st_, sp_=sp_: e.matmul(pf[:, i * 128:(i + 1) * 128], lhsT=diag4[:, c * 4 + k, :], rhs=xbcT[:, c, 1 + t0 + k:1 + t0 + k + 128],
                                                                                                    start=st_, stop=sp_), R=[bxb, bsm], W=bpf)
                        for i, c in enumerate((8, 9, 10, 11)):
                            kb.op("act", lambda e, c=c, i=i, pf=pf: e.activation(out=bct[:, i, 0:P], in_=pf[:, i * 128:i * 128 + P], func=AF.Silu, bias=cbf[:, c, :]),
                                  R=bpf + [bsm], W=[bbct])
                        if E1B_STOP <= 3:
                            continue
                        pcb, bpcb = self.bank()
                        for g in range(2):
                            kb.op("pe", lambda e, g=g, pcb=pcb: e.matmul(pcb[0:P, g * P:(g + 1) * P], lhsT=bct[:, g, 0:P], rhs=bct[:, 2 + g, 0:P], start=True, stop=True),
                                  R=[bbct], W=bpcb)
                        kb.op("dve", lambda e, pcb=pcb: e.tensor_tensor(out=cbm[0:P, :, 0:P], in0=pcb[0:P, 0:2 * P].rearrange("p (g l) -> p g l", g=2),
                                                                       in1=self.trilef[0:P, 0:P].unsqueeze(1).to_broadcast([P, 2, P]), op=ALU.mult),
                              R=bpcb + [self.bconst], W=[bcbm])
                        kb.op("dve", lambda e: e.tensor_tensor(out=Rv, in0=self.trilef[0:P, 0:P].unsqueeze(1).to_broadcast([P, 16, P]),
                                                               in1=DTA.unsqueeze(2).to_broadcast([P, 16, P]), op=ALU.mult), R=[bd, self.bconst], W=[bR])
                        for half in range(2):
                            psg, bsg = self.pair()
                            for q4 in range(2):
                                h0 = half * 8 + q4 * 4
                                kb.op("pe", lambda e, q4=q4, h0=h0, psg=psg: e.matmul(psg[0:P, q4 * 4 * P:(q4 + 1) * 4 * P], lhsT=self.ugtb[0:P, 0:P],
                                                                                     rhs=Rf[0:P, h0 * P:(h0 + 4) * P], start=True, stop=True), R=[bR, self.bconst], W=bsg)
                            kb.op("act", lambda e, psg=psg: e.activation(out=exf[0:P, 0:8 * P], in_=psg[0:P, 0:8 * P], func=AF.Exp), R=bsg, W=[bex])
                            exv = exf[0:P, 0:8 * P].rearrange("p (h l) -> p h l", h=8)
                            kb.op("dve", lambda e, half=half, exv=exv: e.tensor_copy(out=WL[:, half * 8:(half + 1) * 8], in_=exv[:, :, P - 1]), R=[bex], W=[bd])
                            kb.op("dve", lambda e, half=half, exv=exv: e.tensor_tensor(out=scv[:, half * 8:(half + 1) * 8, :], in0=exv,
                                                                                      in1=cbm[0:P, half, 0:P].unsqueeze(1).to_broadcast([P, 8, P]), op=ALU.mult),
                                  R=[bex, bcbm], W=[bscf])
                        if E1B_STOP <= 4:
                            continue
                        pa, bpa = self.bank()
                        kb.op("pe", lambda e, pa=pa: e.matmul(pa[0:P, 0:16], lhsT=self.trileb[0:P, 0:P], rhs=d16b[0:P, :], start=True, stop=True), R=[bd, self.bconst], W=bpa)
                        kb.op("pe", lambda e, pa=pa: e.matmul(pa[:, 16:32], lhsT=self.onesb[0:P, :], rhs=d16b[0:P, :], start=True, stop=True), R=[bd, self.bconst], W=bpa)
                        kb.op("act", lambda e, pa=pa: e.activation(out=EXPA, in_=pa[0:P, 0:16], func=AF.Exp), R=bpa, W=[bd])
                        kb.op("act", lambda e, pa=pa: e.activation(out=EL, in_=pa[:, 16:32], func=AF.Exp), R=bpa, W=[bd])
                        kb.op("dve", lambda e: e.tensor_tensor(out=DTW, in0=DT, in1=WL, op=ALU.mult), R=[bd], W=[bd])
                        xs3 = xst[0:P, :].rearrange("p (h q) -> p h q", h=16)
                        kb.op("dve", lambda e: e.tensor_tensor(out=xdt[0:P, :].rearrange("p (h q) -> p h q", h=16), in0=xs3, in1=DT.unsqueeze(2).to_broadcast([P, 16, 64]), op=ALU.mult),
                              R=[bxs, bd], W=[bxdt])
                        kb.op("pool", lambda e: e.tensor_tensor(out=xdtw[0:P, :].rearrange("p (h q) -> p h q", h=16), in0=xs3, in1=DTW.unsqueeze(2).to_broadcast([P, 16, 64]), op=ALU.mult),
                              R=[bxs, bd], W=[bxdtw])
                        py, bpy = self.pair()
                        for h in range(16):
                            kb.op("pe", lambda e, h=h, py=py: e.matmul(py[0:P, h * 64:(h + 1) * 64], lhsT=scf[0:P, h * P:(h + 1) * P], rhs=xdt[0:P, h * 64:(h + 1) * 64], start=True, stop=True),
                                  R=[bscf, bxdt], W=bpy)
                        po, bpo = self.pair()
                        for g in range(2):
                            kb.op("pe", lambda e, g=g, po=po: e.matmul(po[0:P, g * 512:(g + 1) * 512], lhsT=bct[:, 2 + g, 0:P], rhs=hbf[:, g * 512:(g + 1) * 512], start=True, stop=True),
                                  R=[bbct, bhbf], W=bpo)
                        kb.op("dve", lambda e, po=po: e.tensor_tensor(out=t1[0:P, :].rearrange("p (h q) -> p h q", h=16), in0=po[0:P, :].rearrange("p (h q) -> p h q", h=16),
                                                                     in1=EXPA.unsqueeze(2).to_broadcast([P, 16, 64]), op=ALU.mult), R=bpo + [bd], W=[bt1])
                        kb.op("dve", lambda e, py=py: e.tensor_tensor(out=t1[0:P, :], in0=t1[0:P, :], in1=py[0:P, :], op=ALU.add), R=bpy + [bt1], W=[bt1])
                        kb.op("pool", lambda e: e.tensor_tensor(out=t3[0:P, :].rearrange("p (h q) -> p h q", h=16), in0=xs3, in1=D_b[0:P, :].unsqueeze(2).to_broadcast([P, 16, 64]), op=ALU.mult),
                              R=[bxs, bsm], W=[bt3])
                        kb.op("pool", lambda e: e.tensor_tensor(out=t1[0:P, :], in0=t1[0:P, :], in1=t3[0:P, :], op=ALU.add), R=[bt1, bt3], W=[bt1])
                        kb.op("pool", lambda e: e.tensor_tensor(out=t1[0:P, :], in0=t1[0:P, :], in1=szt[0:P, :], op=ALU.mult), R=[bt1, bsz], W=[bt1])
                        ssq, bs = self.srot()
                        for g in range(2):
                            kb.op("act", lambda e, g=g, ssq=ssq: e.activation(out=t3[0:P, g * 512:(g + 1) * 512], in_=t1[0:P, g * 512:(g + 1) * 512], func=AF.Square, accum_out=ssq[0:P, g:g + 1]),
                                  R=[bt1], W=[bt3, bs])
                        kb.op("act", lambda e, ssq=ssq: e.activation(out=ssq[0:P, 2:4], in_=ssq[0:P, 0:2], func=AF.Sqrt, scale=1.0 / 512, bias=self.epsb[0:P, 0:1]), R=[bs, self.bconst], W=[bs])
                        kb.op("dve", lambda e, ssq=ssq: e.reciprocal(out=ssq[0:P, 2:4], in_=ssq[0:P, 2:4]), R=[bs], W=[bs])
                        for g in range(2):
                            kb.op("dve", lambda e, g=g, ssq=ssq: e.scalar_tensor_tensor(out=ynb[0:P, g * 512:(g + 1) * 512], in0=t1[0:P, g * 512:(g + 1) * 512], scalar=ssq[0:P, 2 + g:3 + g],
                                                                                       in1=ngb[0:P, g * 512:(g + 1) * 512], op0=ALU.mult, op1=ALU.mult), R=[bt1, bs, bsm], W=[bynb])
                        self.transpose_to(ynb, bynb, P, 8, lambda k: ynT[:, k, 0:P], bynT)
                        kb.dma("pool", self.YT[si].rearrange("(c p) t -> p c t", p=128)[:, :, r0:r0 + P], ynT[:, :, 0:P], R=[bynT], W=[bsc["y"]])
                        if E1B_STOP <= 5:
                            continue
                        ps2, bps2 = self.pair()
                        for g in range(2):
                            kb.op("pe", lambda e, g=g, ps2=ps2: e.matmul(ps2[:, g * 512:(g + 1) * 512], lhsT=btok[0:P, g * 128:(g + 1) * 128], rhs=xdtw[0:P, g * 512:(g + 1) * 512], start=True, stop=True),
                                  R=[bbtok, bxdtw], W=bps2)
                        kb.op("dve", lambda e: e.tensor_tensor(out=hst[:].rearrange("p (h q) -> p h q", h=16), in0=hst[:].rearrange("p (h q) -> p h q", h=16),
                                                               in1=EL.unsqueeze(2).to_broadcast([128, 16, 64]), op=ALU.mult), R=[bhst, bd], W=[bhst])
                        kb.op("dve", lambda e, ps2=ps2: e.tensor_tensor(out=hst[:], in0=hst[:], in1=ps2[:], op=ALU.add), R=bps2 + [bhst], W=[bhst])
                        kb.op("act", lambda e: e.activation(out=hbf[:], in_=hst[:], func=AF.Copy), R=[bhst], W=[bhbf])
                        if E1B_STOP <= 6:
                            continue
                        if gi == ngr - 1 and m == nt - 1:
                            for n3 in range(3):
                                pb, bb = self.bank()
                                for k in range(8):
                                    kb.op("pe", lambda e, k=k, pb=pb, n3=n3: e.matmul(pb[0:P, :], lhsT=hT[:, k, t0:t0 + P], rhs=w[:, k, 1024 + n3 * 512:1024 + (n3 + 1) * 512],
                                                                                     start=(k == 0), stop=(k == 7)), R=[bhT, bw], W=bb)
                                self.evac(xraw[0:P, n3 * 512:(n3 + 1) * 512], pb[0:P, :], bb, [bxr])
                            dsto = self.sconv_p[j] if sq["s"] is None else self.sconv_s[j, sq["s"]]
                            kb.dma("pool", dsto, xraw[P - 3:P, :], R=[bxr], W=[self.bout])
                    if gi < ngr - 1:
                        kb.op("pool", lambda e: e.tensor_copy(out=xbcT[:, :, 0:4], in_=xbcT[:, :, GN:GN + 4]), R=[bxb], W=[bxb])
                if E1B_STOP <= 6:
                    continue
                pp, bpp = self.pair()
                for c in range(8):
                    kb.op("pe", lambda e, c=c, pp=pp: e.transpose(out=pp[:, c * 128:(c + 1) * 128], in_=hst[:, c * 128:(c + 1) * 128], identity=self.identf[:, :]),
                          R=[bhst, self.bconst], W=bpp)
                so, bso = self.xrot()
                kb.op("act", lambda e, pp=pp, so=so: e.activation(out=so[:], in_=pp[:], func=AF.Copy), R=bpp, W=[bso])
                dsts = self.ssm_p[j] if sq["s"] is None else self.ssm_s[j, sq["s"]]
                kb.dma("pool", dsts.rearrange("(c p) n -> p c n", p=128), so[:].rearrange("p (c n) -> p c n", c=8), R=[bso], W=[self.bout])
            kb.barrier()

    def phase_e2(self, l, j, S):
        kb = self.kb
        PAST, TS, T = self.PAST, self.TS, self.T
        lam_init = 0.8 - 0.6 * math.exp(-0.3 * l)
        TKmax = max(T, PAST + TS)
        with ExitStack() as es:
            save_banks = self.bank_list
            self.bank_list = [4, 5, 6, 7]
            bsm = Buf("e2small")
            lamt = self.sb(es, "lamt", [128, 256], F32)
            kb.dma("sp", lamt[:], self.attn_lambda[j:j + 1, :].partition_broadcast(128), W=[bsm])
            lsm = self.sb(es, "lsm", [128, 8], F32)
            lpr = self.sb(es, "lpr", [128, 128], F32)
            kb.op("dve", lambda e: e.tensor_tensor(out=lpr[:, 0:64], in0=lamt[:, 0:64], in1=lamt[:, 64:128], op=ALU.mult), R=[bsm], W=[bsm])
            kb.op("dve", lambda e: e.tensor_tensor(out=lpr[:, 64:128], in0=lamt[:, 128:192], in1=lamt[:, 192:256], op=ALU.mult), R=[bsm], W=[bsm])
            kb.op("dve", lambda e: e.reduce_sum(out=lsm[:, 0:2], in_=lpr[:].rearrange("p (a b) -> p a b", a=2), axis=AX.X), R=[bsm], W=[bsm])
            kb.op("act", lambda e: e.activation(out=lsm[:, 2:4], in_=lsm[:, 0:2], func=AF.Exp), R=[bsm], W=[bsm])
            kb.op("dve", lambda e: e.scalar_tensor_tensor(out=lsm[:, 4:5], in0=lsm[:, 3:4], scalar=-lam_init, in1=lsm[:, 2:3], op0=ALU.add, op1=ALU.subtract),
                  R=[bsm], W=[bsm])
            neglam = lsm[:, 4:5]
            subg = self.sb(es, "subg", [128, 1, 1], F32)
            self.load_fm(es, lambda c: subg[:, c, :], self.attn_subln_g[j:j + 1, :], 1, 1, bsm)
            kb.op("dve", lambda e: e.tensor_scalar(out=subg[:, 0, :], in0=subg[:, 0, :], scalar1=(1.0 - lam_init), scalar2=None, op0=ALU.mult), R=[bsm], W=[bsm])
            NKT = (TKmax + 127) // 128
            kvset = []
            for i in range(2):
                kvset.append(dict(kT=[self.sb(es, "kT%d_%d" % (m, i), [69, TKmax], BF16) for m in range(2)], bkT=Buf(),
                                  vh=self.sb(es, "vh%d" % i, [128, NKT, 128], BF16), bvh=Buf(),
                                  corrb=self.sb(es, "corrb%d" % i, [128, 128], BF16), bcorr=Buf()))
            GNmax = S[0]["GN"]
            qTr = self.rot("qT", es, [69, 2, GNmax], BF16, 2)
            ptr = self.rot("pt", es, [128, GNmax], BF16, 4)
            accr = self.rot("accs", es, [128, 4, GNmax], F32, 2)
            lacc = [self.sb(es, "lacc%d" % m, [128, GNmax], F32) for m in range(2)]
            blacc = [Buf(), Buf()]
            for m in range(2):
                kb.op("pool", lambda e, m=m: e.memset(lacc[m][:], 0.0), W=[blacc[m]])
            ot = self.sb(es, "e2o", [128, GNmax], F32); bot = Buf()
            o1 = self.sb(es, "e2o1", [128, GNmax], F32); bo1 = Buf()
            rr = self.sb(es, "e2r", [128, GNmax], F32); brr = Buf()
            kb.op("pool", lambda e: e.memset(o1[:], 0.0), W=[bo1])
            onr = self.rot("e2on", es, [128, GNmax], BF16, 2)
            acc = [self.PP[0][:, 0:512], self.PP[0][:, 512:1024], self.PP[1][:, 0:512], self.PP[1][:, 512:1024]]
            bacc = [[self.pb[i]] for i in range(4)]
            for _ in range(2):
                qT, bqT = qTr()
                kb.op("pool", lambda e, qT=qT: e.memset(qT[:], 0.0), W=[bqT])

            def load_head(si, sq, h, ks):
                bsc = self.bscr[si]
                Tq = sq["T"]
                kp0 = 0 if sq["s"] is None else PAST
                Tk = kp0 + Tq
                kT, bkT, vh, bvh, corrb, bcorr = ks["kT"], ks["bkT"], ks["vh"], ks["bvh"], ks["corrb"], ks["bcorr"]
                for m in range(2):
                    kb.dma("sp", kT[m][0:64, 0:Tk], self.KT[si][h, m * 64:(m + 1) * 64, 0:Tk], R=[bsc["k"]], W=[bkT])
                for a in range(0, Tk, 1024):
                    wd = min(1024, Tk - a)
                    st, bst = self.wstage()
                    kb.dma("sp", st[64:69, 0:wd], self.cst_kaug[h, :, a:a + wd], W=[bst])
                    for m in range(2):
                        kb.op("pool", lambda e, m=m, st=st, a=a, wd=wd: e.tensor_copy(out=kT[m][64:69, a:a + wd], in_=st[64:69, 0:wd]), R=[bst], W=[bkT])
                nfull = Tk // 128
                kb.dma("sp", vh[:, 0:nfull, :], self.VB[si][0:nfull * 128, h * 128:(h + 1) * 128].rearrange("(j p) e -> p j e", p=128), R=[bsc["v"]], W=[bvh])
                if Tk % 128:
                    kb.dma("sp", vh[0:Tk % 128, nfull, :], self.VB[si][nfull * 128:Tk, h * 128:(h + 1) * 128], R=[bsc["v"]], W=[bvh])
                st, bst = self.wstage()
                kb.dma("sp", st[:, 0:128], self.cst_corr[h], W=[bst])
                kb.op("pool", lambda e, st=st: e.tensor_copy(out=corrb[:], in_=st[:, 0:128]), R=[bst], W=[bcorr])

            pending = [None]
            heads = [(si, sq, h) for si, sq in enumerate(S) for h in range(8)]
            load_head(heads[0][0], heads[0][1], heads[0][2], kvset[0])
            for hi, (si, sq, h) in enumerate(heads):
                ks = kvset[hi % 2]
                kT, bkT, vh, bvh, corrb, bcorr = ks["kT"], ks["bkT"], ks["vh"], ks["bvh"], ks["corrb"], ks["bcorr"]
                bsc = self.bscr[si]
                P, GN, Tq = sq["P"], sq["GN"], sq["T"]
                kp0 = 0 if sq["s"] is None else PAST
                Tk = kp0 + Tq
                for gidx, g0 in enumerate(range(0, Tq, GN)):
                    qT, bqT = qTr()
                    if GN < 128:
                        kb.op("pool", lambda e, qT=qT: e.memset(qT[:, :, GN:128], 0.0), W=[bqT])
                    for m in range(2):
                        kb.dma("sp", qT[0:64, m, 0:GN], self.QT[si][h, m * 64:(m + 1) * 64, g0:g0 + GN], R=[bsc["q"]], W=[bqT])
                    st, bst = self.wstage()
                    kb.dma("sp", st[64:69, 0:GN], self.cst_qaug[h, :, kp0 + g0:kp0 + g0 + GN], W=[bst])
                    for m in range(2):
                        kb.op("pool", lambda e, m=m, st=st, qT=qT: e.tensor_copy(out=qT[64:69, m, 0:GN], in_=st[64:69, 0:GN]), R=[bst], W=[bqT])
                    if gidx == 0 and hi + 1 < len(heads):
                        load_head(heads[hi + 1][0], heads[hi + 1][1], heads[hi + 1][2], kvset[(hi + 1) % 2])
                    GNq = max(GN, 128)
                    tiles = []
                    if sq["s"] is None:
                        i0, nt = g0 // 128, GN // 128
                        for jt in range(i0 + nt):
                            if jt < i0:
                                tiles.append((jt, 128, [(0, GN, False)]))
                            else:
                                c0 = (jt - i0) * 128
                                rg = [(c0, c0 + 128, True)]
                                if c0 + 128 < GN:
                                    rg.append((c0 + 128, GN, False))
                                tiles.append((jt, 128, rg))
                    else:
                        for jt in range(PAST // 128):
                            tiles.append((jt, 128, [(0, GNq, False)]))
                        tiles.append((PAST // 128, Tq, [(0, GNq, True)]))

                    def st_exp(ti):
                        jt, nk, rg = tiles[ti]
                        k0 = jt * 128
                        c0 = rg[0][0]
                        pts = []
                        for m in range(2):
                            ps, bps = self.bank()
                            for (a, b, isd) in rg:
                                kb.op("pe", lambda e, ps=ps, m=m, a=a, b=b, isd=isd: e.matmul(ps[0:nk, a:b], lhsT=kT[m][:, k0:k0 + nk], rhs=qT[:, m, a:b],
                                                                                         start=True, stop=(not isd)), R=[bkT, bqT], W=bps)
                                if isd:
                                    kb.op("pe", lambda e, ps=ps, a=a, b=b: e.matmul(ps[0:nk, a:b], lhsT=self.identb[0:nk, 0:nk], rhs=corrb[0:nk, 0:b - a],
                                                                                 start=False, stop=True), R=[bcorr, self.bconst], W=bps)
                            pt, bpt = ptr()
                            kb.op("act", lambda e, pt=pt, ps=ps: e.activation(out=pt[0:nk, c0:GNq], in_=ps[0:nk, c0:GNq], func=AF.Exp, scale=0.125), R=bps, W=[bpt])
                            pts.append((pt, bpt))
                        return pts

                    def pv(ti, pts):
                        jt, nk, rg = tiles[ti]
                        for m in range(2):
                            pt, bpt = pts[m]
                            for ri, (a, b, isd) in enumerate(rg):
                                first = (ti == 0 and ri == 0)
                                lastm = (ti == len(tiles) - 1 and ri == len(rg) - 1)
                                kb.op("pe", lambda e, pt=pt, m=m, a=a, b=b: e.matmul(acc[m][:, a:b], lhsT=vh[0:nk, jt, :], rhs=pt[0:nk, a:b],
                                                                                         start=first, stop=lastm), R=[bvh, bpt], W=bacc[m])
                            c0 = rg[0][0]
                            eng = "dve" if m == 0 else "pool"
                            if ti == 0:
                                kb.op(eng, lambda e, pt=pt, m=m: e.tensor_copy(out=lacc[m][0:nk, c0:GNq], in_=pt[0:nk, c0:GNq]), R=[bpt], W=[blacc[m]])
                            else:
                                kb.op(eng, lambda e, pt=pt, m=m: e.tensor_tensor(out=lacc[m][0:nk, c0:GNq], in0=lacc[m][0:nk, c0:GNq], in1=pt[0:nk, c0:GNq], op=ALU.add),
                                      R=[bpt, blacc[m]], W=[blacc[m]])

                    cur = st_exp(0)
                    for ti in range(len(tiles)):
                        nxt = st_exp(ti + 1) if ti + 1 < len(tiles) else None
                        pv(ti, cur)
                        cur = nxt
                        if ti == 2 and pending[0] is not None:
                            pending[0]()
                            pending[0] = None
                    if pending[0] is not None:
                        pending[0]()
                        pending[0] = None
                    for m in range(2):
                        kb.op("pe", lambda e, m=m: e.matmul(acc[2 + m][:, 0:GNq], lhsT=self.onesf[:, :], rhs=lacc[m][:, 0:GNq], start=True, stop=True),
                              R=[self.bconst, blacc[m]], W=bacc[2 + m])
                    ac, bac = accr()
                    for i in range(4):
                        if i < 2:
                            kb.op("act", lambda e, i=i, ac=ac: e.activation(out=ac[:, i, 0:GN], in_=acc[i][:, 0:GN], func=AF.Copy), R=bacc[i], W=[bac])
                        else:
                            kb.op("dve", lambda e, i=i, ac=ac: e.tensor_copy(out=ac[:, i, 0:GN], in_=acc[i][:, 0:GN]), R=bacc[i], W=[bac])
                    kb.op("dve", lambda e, ac=ac: e.reciprocal(out=rr[:, 0:GN], in_=ac[:, 2, 0:GN]), R=[bac], W=[brr])
                    kb.op("dve", lambda e, ac=ac: e.tensor_tensor(out=ot[:, 0:GN], in0=ac[:, 0, 0:GN], in1=rr[:, 0:GN], op=ALU.mult), R=[bac, brr], W=[bot])
                    kb.op("dve", lambda e, ac=ac: e.reciprocal(out=rr[:, 0:GN], in_=ac[:, 3, 0:GN]), R=[bac], W=[brr])
                    kb.op("dve", lambda e, ac=ac: e.tensor_tensor(out=o1[:, 0:GN], in0=ac[:, 1, 0:GN], in1=rr[:, 0:GN], op=ALU.mult), R=[bac, brr], W=[bo1])
                    kb.op("dve", lambda e: e.scalar_tensor_tensor(out=ot[:, 0:GN], in0=o1[:, 0:GN], scalar=neglam, in1=ot[:, 0:GN], op0=ALU.mult, op1=ALU.add),
                          R=[bo1, bot, bsm], W=[bot])
                    kb.op("pool", lambda e: e.tensor_tensor(out=o1[:, 0:GN], in0=ot[:, 0:GN], in1=ot[:, 0:GN], op=ALU.mult), R=[bot], W=[bo1])

                    def part_b(si=si, h=h, g0=g0, GN=GN, GNq=GNq, bsc=bsc):
                        pm, bm = self.bank()
                        kb.op("pe", lambda e, pm=pm: e.matmul(pm[:, 0:GNq], lhsT=self.m128[:], rhs=o1[:, 0:GNq], start=True, stop=True), R=[bo1, self.bconst], W=bm)
                        kb.op("act", lambda e, pm=pm: e.activation(out=rr[:, 0:GN], in_=pm[:, 0:GN], func=AF.Ln, bias=self.epsb[:, 0:1]), R=bm + [self.bconst], W=[brr])
                        kb.op("act", lambda e: e.activation(out=rr[:, 0:GN], in_=rr[:, 0:GN], func=AF.Exp, scale=-0.5), R=[brr], W=[brr])
                        on, bon = onr()
                        kb.op("dve", lambda e, on=on: e.scalar_tensor_tensor(out=on[:, 0:GN], in0=ot[:, 0:GN], scalar=subg[:, 0, :], in1=rr[:, 0:GN], op0=ALU.mult, op1=ALU.mult),
                              R=[bot, brr, bsm], W=[bon])
                        kb.dma("pool", self.OT[si][h * 128:(h + 1) * 128, g0:g0 + GN], on[:, 0:GN], R=[bon], W=[bsc["o"]])
                    pending[0] = part_b
            if pending[0] is not None:
                pending[0]()
                pending[0] = None
            self.bank_list = save_banks
            kb.barrier()

    def phase_e3(self, l, j, S):
        kb = self.kb
        stage = 2 * l
        with ExitStack() as es:
            wo = self.sb(es, "wo", [128, 16, D], BF16); bwo = Buf()
            self.load_w(wo, self.hyb_w_out[j].rearrange("(k p) n -> k p n", p=128), 16, D, bwo)
            GNmax = S[0]["GN"]
            oyr = self.rot("oy", es, [128, 16, GNmax], BF16, 2)
            for si, sq in enumerate(S):
                gs, sh, gg, bbc = self.make_bc(l, 0, sq["ci"])
                src, dst = self.xio(sq, stage)
                bx = self.bscr[si]["x"]
                bsc = self.bscr[si]
                P, GN, Tq = sq["P"], sq["GN"], sq["T"]
                for g0 in range(0, Tq, GN):
                    nt = GN // P
                    t, bt = oyr()
                    kb.dma("sp", t[:, 0:8, 0:GN], self.OT[si].rearrange("(c p) t -> p c t", p=128)[:, :, g0:g0 + GN], R=[bsc["o"]], W=[bt])
                    kb.dma("sp", t[:, 8:16, 0:GN], self.YT[si].rearrange("(c p) t -> p c t", p=128)[:, :, g0:g0 + GN], R=[bsc["y"]], W=[bt])
                    for m in range(nt):
                        pp, bpp = self.pair()
                        for nh in range(2):
                            for c in range(16):
                                kb.op("pe", lambda e, c=c, nh=nh, m=m, pp=pp, t=t: e.matmul(pp[0:P, nh * 512:(nh + 1) * 512], lhsT=t[:, c, m * P:(m + 1) * P],
                                                                                          rhs=wo[:, c, nh * 512:(nh + 1) * 512], start=(c == 0), stop=(c == 15)),
                                      R=[bt, bwo], W=bpp)
                        r0 = g0 + m * P
                        self.post(pp, bpp, P, src[r0:r0 + P, :], dst[r0:r0 + P, :], gg, bbc, bx)
            kb.barrier()

    def phase_conf(self, l, j, S):
        kb = self.kb
        stage = 2 * l
        with ExitStack() as es:
            win = self.sb(es, "cwin", [128, 8, 2 * D], BF16); bwin = Buf()
            wout = self.sb(es, "cwout", [128, 8, D], BF16); bwout = Buf()
            self.load_w(win, self.conf_w_in[j].rearrange("(k p) n -> k p n", p=128), 8, 2 * D, bwin)
            self.load_w(wout, self.conf_w_out[j].rearrange("(k p) n -> k p n", p=128), 8, D, bwout)
            bsm = Buf("confsmall")
            dwf = self.sb(es, "dwf", [128, 8, 31], F32)
            self.load_fm(es, lambda c: dwf[:, c, :], self.conf_dw_w[j], 31, 8, bsm)
            bin_ = self.sb(es, "binf", [128, 16, 1], F32)
            self.load_fm(es, lambda c: bin_[:, c, :], self.conf_b_in[j:j + 1, :], 1, 16, bsm)
            vecs = self.sb(es, "cvecs", [128, 3, 8, 1], F32)
            for i, src in enumerate((self.conf_dw_b, self.conf_ln_g, self.conf_ln_b)):
                self.load_fm(es, lambda c, i=i: vecs[:, i, c, :], src[j:j + 1, :], 1, 8, bsm)
            diag = self.sb(es, "cdiag", [128, 8 * 31, 128], BF16)
            kb.op("dve", lambda e: e.tensor_tensor(out=diag[:], in0=self.identf.unsqueeze(1).to_broadcast([128, 248, 128]),
                                                   in1=dwf[:].rearrange("p c k -> p (c k)").unsqueeze(2).to_broadcast([128, 248, 128]),
                                                   op=ALU.mult), R=[bsm, self.bconst], W=[bsm])
            bor = self.sb(es, "bor", [1, D], F32)
            kb.dma("sp", bor[:], self.conf_b_out[j:j + 1, :], W=[bsm])
            borb = self.sb(es, "borb", [1, D], BF16)
            kb.op("pool", lambda e: e.tensor_copy(out=borb[:], in_=bor[:]), R=[bsm], W=[bsm])
            if CONF_STOP <= 1:
                kb.barrier()
                return
            GNmax = S[0]["GN"]
            hT = self.sb(es, "chT", [128, 8, GNmax], BF16); bhT = Buf()
            uT = self.sb(es, "uT", [128, 8, 30 + GNmax], BF16); buT = Buf()
            uF = self.sb(es, "uF", [128, 8, 32], F32); buF = Buf()
            yT = self.sb(es, "yT", [128, 8, GNmax], F32); byT = Buf()
            ynT = self.sb(es, "ynT", [128, 8, GNmax], BF16); bynT = Buf()
            sgr = self.rot("csg", es, [128, GNmax], F32, 1)
            mu = self.sb(es, "cmu", [128, GNmax], F32); bmu = Buf()
            rs = self.sb(es, "crs", [128, GNmax], F32); brs = Buf()
            kb.op("pool", lambda e: e.memset(hT[:], 0.0), W=[bhT])
            kb.op("pool", lambda e: e.memset(uT[:], 0.0), W=[buT])
            kb.op("pool", lambda e: e.memset(yT[:], 0.0), W=[byT])
            kb.op("pool", lambda e: e.memset(ynT[:], 0.0), W=[bynT])
            for si, sq in enumerate(S):
                if CONF_SEQS is not None and si not in CONF_SEQS:
                    continue
                gs, sh, gg, bbc = self.make_bc(l, 0, sq["ci"])
                src, dst = self.xio(sq, stage)
                bx = self.bscr[si]["x"]
                P, GN, Tq = sq["P"], sq["GN"], sq["T"]
                GNp = max(GN, 128)
                if sq["s"] is None:
                    kb.op("pool", lambda e: e.memset(uT[:, :, 0:30], 0.0), W=[buT])
                else:
                    self.load_fm(es, lambda c: uT[:, c, 0:30], self.st_cconv[j, sq["s"]], 30, 8, buT)
                ngr = Tq // GN
                nt = GN // P
                nl = min(30, GN)

                def stage_a(gi):
                    g0 = gi * GN
                    last = gi == ngr - 1
                    for m in range(nt):
                        self.prelude_x(src[g0 + m * P:g0 + (m + 1) * P, :], bx, P, hT, bhT, m * P, gs, sh, bbc)
                    nl = min(30, GN)
                    for c in range(8):
                        pa, ba = self.bank()
                        pg, bg = self.bank()
                        for k in range(8):
                            kb.op("pe", lambda e, k=k, pa=pa, c=c: e.matmul(pa[:, 0:GNp], lhsT=win[:, k, c * 128:(c + 1) * 128], rhs=hT[:, k, 0:GNp],
                                                                           start=(k == 0), stop=(k == 7)), R=[bwin, bhT], W=ba)
                        for k in range(8):
                            kb.op("pe", lambda e, k=k, pg=pg, c=c: e.matmul(pg[:, 0:GNp], lhsT=win[:, k, D + c * 128:D + (c + 1) * 128], rhs=hT[:, k, 0:GNp],
                                                                           start=(k == 0), stop=(k == 7)), R=[bwin, bhT], W=bg)
                        sg, bsg = sgr()
                        kb.op("act", lambda e, sg=sg, pg=pg, c=c: e.activation(out=sg[:, 0:GN], in_=pg[:, 0:GN], func=AF.Sigmoid, bias=bin_[:, 8 + c, :]),
                              R=bg + [bsm], W=[bsg])
                        kb.op("dve", lambda e, sg=sg, pa=pa, c=c: e.scalar_tensor_tensor(out=uT[:, c, 30:30 + GN], in0=pa[:, 0:GN], scalar=bin_[:, c, :],
                                                                                        in1=sg[:, 0:GN], op0=ALU.add, op1=ALU.mult),
                              R=ba + [bsg, bsm], W=[buT])
                        if last:
                            kb.op("dve", lambda e, sg=sg, pa=pa, c=c: e.scalar_tensor_tensor(out=uF[:, c, 0:nl], in0=pa[:, GN - nl:GN], scalar=bin_[:, c, :],
                                                                                            in1=sg[:, GN - nl:GN], op0=ALU.add, op1=ALU.mult),
                                  R=ba + [bsg, bsm], W=[buF])

                def stage_conv(gi):
                    for c in range(8):
                        py, by_ = self.bank()
                        for k in range(31):
                            kb.op("pe", lambda e, k=k, py=py, c=c: e.matmul(py[:, 0:GNp], lhsT=diag[:, c * 31 + k, :], rhs=uT[:, c, k:k + GNp],
                                                                           start=(k == 0), stop=(k == 30)), R=[bsm, buT], W=by_)
                        kb.op("act", lambda e, py=py, c=c: e.activation(out=yT[:, c, 0:GN], in_=py[:, 0:GN], func=AF.Identity, bias=vecs[:, 0, c, :]),
                              R=by_ + [bsm], W=[byT])

                def stage_c(gi):
                    g0 = gi * GN
                    pm, bm = self.bank()
                    for c in range(8):
                        kb.op("pe", lambda e, c=c, pm=pm: e.matmul(pm[:, 0:GNp], lhsT=self.m1024[:], rhs=yT[:, c, 0:GNp], start=(c == 0), stop=(c == 7)),
                              R=[byT, self.bconst], W=bm)
                    kb.op("act", lambda e, pm=pm: e.activation(out=mu[:, 0:GN], in_=pm[:, 0:GN], func=AF.Copy), R=bm, W=[bmu])
                    kb.op("dve", lambda e: e.tensor_tensor(out=yT[:, :, 0:GN], in0=yT[:, :, 0:GN], in1=mu[:, 0:GN].unsqueeze(1).to_broadcast([128, 8, GN]),
                                                           op=ALU.subtract), R=[byT, bmu], W=[byT])
                    kb.op("pool", lambda e: e.tensor_tensor(out=ynT[:, :, 0:GN], in0=yT[:, :, 0:GN], in1=yT[:, :, 0:GN], op=ALU.mult), R=[byT], W=[bynT])
                    pv, bv = self.bank()
                    for c in range(8):
                        kb.op("pe", lambda e, c=c, pv=pv: e.matmul(pv[:, 0:GNp], lhsT=self.onesb[:, :], rhs=ynT[:, c, 0:GNp], start=(c == 0), stop=(c == 7)),
                              R=[bynT, self.bconst], W=bv)
                    kb.op("act", lambda e, pv=pv: e.activation(out=rs[:, 0:GN], in_=pv[:, 0:GN], func=AF.Sqrt, scale=1.0 / 1024, bias=self.epsb[:, 0:1]), R=bv + [self.bconst], W=[brs])
                    kb.op("dve", lambda e: e.reciprocal(out=rs[:, 0:GN], in_=rs[:, 0:GN]), R=[brs], W=[brs])
                    kb.op("dve", lambda e: e.tensor_tensor(out=yT[:, :, 0:GN], in0=yT[:, :, 0:GN], in1=rs[:, 0:GN].unsqueeze(1).to_broadcast([128, 8, GN]),
                                                           op=ALU.mult), R=[byT, brs], W=[byT])
                    for c in range(8):
                        kb.op("act", lambda e, c=c: e.activation(out=ynT[:, c, 0:GN], in_=yT[:, c, 0:GN], func=AF.Silu, scale=vecs[:, 1, c, :], bias=vecs[:, 2, c, :]),
                              R=[byT, bsm], W=[bynT])
                    for m in range(nt):
                        pp, bpp = self.pair()
                        for nh in range(2):
                            for c in range(8):
                                kb.op("pe", lambda e, c=c, nh=nh, m=m, pp=pp: e.matmul(pp[0:P, nh * 512:(nh + 1) * 512], lhsT=ynT[:, c, m * P:(m + 1) * P],
                                                                                     rhs=wout[:, c, nh * 512:(nh + 1) * 512], start=(c == 0), stop=False),
                                      R=[bynT, bwout], W=bpp)
                            kb.op("pe", lambda e, nh=nh, pp=pp: e.matmul(pp[0:P, nh * 512:(nh + 1) * 512], lhsT=self.onesb[0:1, 0:P],
                                                                        rhs=borb[0:1, nh * 512:(nh + 1) * 512], start=False, stop=True),
                                  R=[bsm, self.bconst], W=bpp)
                        r0 = g0 + m * P
                        self.post(pp, bpp, P, src[r0:r0 + P, :], dst[r0:r0 + P, :], gg, bbc, bx)

                stage_a(0)
                for gi in range(ngr):
                    stage_conv(gi)
                    if gi + 1 < ngr:
                        kb.op("pool", lambda e: e.tensor_copy(out=uT[:, :, 0:30], in_=uT[:, :, GN:GN + 30]), R=[buT], W=[buT])
                        stage_a(gi + 1)
                    stage_c(gi)
                if CONF_STOP <= 5:
                    continue
                nl = min(30, GN)
                pp, bpp = self.pair()
                for c in range(8):
                    kb.op("pe", lambda e, c=c, pp=pp: e.transpose(out=pp[0:nl, c * 128:(c + 1) * 128], in_=uF[:, c, 0:nl], identity=self.identf[:, :]),
                          R=[buF, self.bconst], W=bpp)
                ot, bo = self.xrot()
                kb.op("act", lambda e, pp=pp, ot=ot: e.activation(out=ot[0:nl, :], in_=pp[0:nl, :], func=AF.Copy), R=bpp, W=[bo])
                if sq["s"] is None:
                    kb.dma("pool", self.cconv_p[j], ot[0:30, :], R=[bo], W=[self.bout])
                else:
                    s_ = sq["s"]
                    kb.dma("pool", self.cconv_s[j, s_, 30 - nl:30, :], ot[0:nl, :], R=[bo], W=[self.bout])
                    if nl < 30:
                        kb.dma("pool", self.cconv_s[j, s_, 0:30 - nl, :], self.st_cconv[j, s_, nl:30, :], W=[self.bout])
            kb.barrier()


_PROG = {}


def _get_prog(T, depth):
    key = (T, depth)
    if key not in _PROG:
        _PROG[key] = Prog(T=T, depth=depth)
    return _PROG[key]


def make_in_maps(inp, T, depth):
    NE, NO = (depth + 1) // 2, depth // 2
    cf, kaug, qaug, corr = _consts()
    f = lambda a: np.ascontiguousarray(np.asarray(a, dtype=np.float32))
    shared = {}
    for nm in ("ada_w", "ada_b", "norm_g", "ffn_w_up", "ffn_w_down", "hyb_w_in", "attn_subln_g", "ssm_conv_w", "ssm_conv_b",
               "ssm_dt_bias", "ssm_a_log", "ssm_d", "ssm_norm_g", "hyb_w_out", "conf_w_in", "conf_b_in", "conf_dw_w",
               "conf_dw_b", "conf_ln_g", "conf_ln_b", "conf_w_out", "conf_b_out"):
        shared[nm] = f(inp[nm])
    shared["attn_lambda"] = f(inp["attn_lambda"]).reshape(NE, 256)
    shared["cst_f"], shared["cst_kaug"], shared["cst_qaug"], shared["cst_corr"] = cf, kaug, qaug, corr
    maps = []
    for c in range(NCORES):
        m = dict(shared)
        m["x_p"] = f(inp["x_prompt"][c])
        m["x_s"] = f(inp["x_sample"][2 * c:2 * c + 2])
        m["c_all"] = f(np.concatenate([inp["c_prompt"][c:c + 1], inp["c_sample"][2 * c:2 * c + 2]], 0))
        m["cache_k"] = f(np.asarray(inp["cache_attn_k"])[:, 2 * c:2 * c + 2].reshape(NE, 2, -1, D))
        m["cache_v"] = f(np.asarray(inp["cache_attn_v"])[:, 2 * c:2 * c + 2].reshape(NE, 2, -1, D))
        m["st_sconv"] = f(np.asarray(inp["state_ssm_conv"])[:, 2 * c:2 * c + 2])
        m["st_ssm"] = f(np.asarray(inp["state_ssm"])[:, 2 * c:2 * c + 2].reshape(NE, 2, 1024, 128))
        m["st_cconv"] = f(np.asarray(inp["state_conf_conv"])[:, 2 * c:2 * c + 2])
        maps.append(m)
    return maps


def gather(res, T, depth, TS=16):
    NE, NO = (depth + 1) // 2, depth // 2
    R = res.results
    st = lambda k, ax=0: np.stack([np.asarray(r[k]) for r in R], ax)
    cat = lambda k, ax: np.concatenate([np.asarray(r[k]) for r in R], ax)
    y_p = st("y_p")
    y_s = cat("y_s", 0)
    k_p = st("k_p", 1).reshape(NE, NCORES, T, 8, 128)
    v_p = st("v_p", 1).reshape(NE, NCORES, T, 8, 128)
    sconv_p = st("sconv_p", 1)
    ssm_p = st("ssm_p", 1).reshape(NE, NCORES, 16, 64, 128)
    cconv_p = st("cconv_p", 1)
    k_s = cat("k_s", 1).reshape(NE, 2 * NCORES, TS, 8, 128)
    v_s = cat("v_s", 1).reshape(NE, 2 * NCORES, TS, 8, 128)
    sconv_s = cat("sconv_s", 1)
    ssm_s = cat("ssm_s", 1).reshape(NE, 2 * NCORES, 16, 64, 128)
    cconv_s = cat("cconv_s", 1)
    return (y_p, y_s, k_p, v_p, sconv_p, ssm_p, cconv_p, k_s, v_s, sconv_s, ssm_s, cconv_s)


def kernel(**inputs):
    T = int(np.asarray(inputs["x_prompt"]).shape[1])
    depth = int(np.asarray(inputs["ada_w"]).shape[0])
    prog = _get_prog(T, depth)
    maps = make_in_maps(inputs, T, depth)
    res = run_bass_kernel_spmd(prog.nc, maps, core_ids=list(range(NCORES)))
    outs = gather(res, T, depth)
    return tuple(np.ascontiguousarray(o, dtype=np.float32) for o in outs)
```

```python
import math
import numpy as np
import concourse.bass as bass
import concourse.mybir as mybir
from concourse.bass_utils import run_bass_kernel_spmd
from contextlib import ExitStack

F32 = mybir.dt.float32
BF16 = mybir.dt.bfloat16
AF = mybir.ActivationFunctionType
ALU = mybir.AluOpType
AX = mybir.AxisListType

D = 1024
DFF = 2816
DIN = 5648
EPS = 1e-6
NCORES = 8
SKIP = set()
DEBUG = False
CONF_STOP = 99
E1B_STOP = 99
CONF_SEQS = None


class Buf:
    __slots__ = ("name", "w", "r", "excl")

    def __init__(self, name="", excl=False):
        self.name = name
        self.w = None
        self.r = {}
        self.excl = excl


class KB:
    def __init__(self, nc, es, n_lanes=8):
        self.nc = nc
        self.eng = {"pe": nc.tensor, "act": nc.scalar, "dve": nc.vector, "pool": nc.gpsimd, "sp": nc.sync}
        self.sem, self.cnt, self.mult = {}, {}, {}
        for e in self.eng:
            self.sem[e] = es.enter_context(nc.semaphore("s_" + e))
            self.cnt[e] = 0
            self.mult[e] = 1
        self.lanes = {}
        for q in ("sp", "pool"):
            ls = []
            for i in range(n_lanes):
                key = "L%s%d" % (q, i)
                self.sem[key] = es.enter_context(nc.semaphore("s_" + key))
                self.cnt[key] = 0
                self.mult[key] = 16
                ls.append(key)
            self.lanes[q] = ls
        self.lane_rr = {q: 0 for q in self.lanes}
        self.known = {e: {} for e in self.eng}
        self.nins = 0
        self.nwait = 0

    def _need(self, deps, key, k):
        if deps.get(key, 0) < k:
            deps[key] = k

    def _collect(self, R, W, eng=None):
        deps = {}
        for b in R:
            if b.w is not None:
                self._need(deps, b.w[0], b.w[1])
            if b.excl:
                for e, k in b.r.items():
                    if e != eng:
                        self._need(deps, e, k)
        for b in W:
            if b.w is not None:
                self._need(deps, b.w[0], b.w[1])
            for e, k in b.r.items():
                self._need(deps, e, k)
        return deps

    def _emit_waits(self, e, deps):
        kn = self.known[e]
        for key, k in deps.items():
            if key == e and e == "pe":
                continue
            if kn.get(key, 0) >= k:
                continue
            self.eng[e].wait_ge(self.sem[key], k * self.mult[key])
            kn[key] = k
            self.nwait += 1

    def _mark(self, key, k, R, W):
        for b in R:
            if b.r.get(key, 0) < k:
                b.r[key] = k
        for b in W:
            b.w = (key, k)
            b.r = {}

    def op(self, e, fn, R=(), W=()):
        self._emit_waits(e, self._collect(R, W, e))
        ins = fn(self.eng[e])
        self.cnt[e] += 1
        ins.then_inc(self.sem[e], 1)
        self._mark(e, self.cnt[e], R, W)
        self.nins += 1

    def dma(self, q, out, in_, R=(), W=()):
        ls = self.lanes[q]
        lane = ls[self.lane_rr[q] % len(ls)]
        self.lane_rr[q] += 1
        deps = self._collect(R, W)
        if self.cnt[lane] > 0:
            self._need(deps, lane, self.cnt[lane])
        self._emit_waits(q, deps)
        ins = self.eng[q].dma_start(out=out, in_=in_)
        self.cnt[lane] += 1
        ins.then_inc(self.sem[lane], 16)
        self._mark(lane, self.cnt[lane], R, W)
        self.nins += 1

    def barrier(self):
        for e in self.eng:
            deps = {k2: self.cnt[k2] for k2 in self.cnt if self.cnt[k2] > 0 and k2 != e}
            self._emit_waits(e, deps)


def _consts():
    c = {}
    i = np.arange(128)
    c["ident"] = np.eye(128, dtype=np.float32)
    c["ones"] = np.ones((128, 128), np.float32)
    c["trile"] = (i[:, None] <= i[None, :]).astype(np.float32)
    c["ugt"] = (i[:, None] > i[None, :]).astype(np.float32)
    cf = np.stack([c["ident"], c["ones"], c["trile"], c["ugt"]], 0)
    pos = np.arange(8192)
    kaug = np.zeros((8, 5, 8192), np.float32)
    qaug = np.zeros((8, 5, 8192), np.float32)
    corr = np.zeros((8, 128, 128), np.float32)
    for h in range(8):
        s8 = 8.0 * 2.0 ** (-(h + 1))
        ph, pl = (pos // 128).astype(np.float32), (pos % 128).astype(np.float32)
        qaug[h, 0] = -s8 * 128.0 * ph
        qaug[h, 1] = -s8 * pl
        qaug[h, 2] = 0.0
        qaug[h, 3] = 1.0
        qaug[h, 4] = 1.0
        kaug[h, 0] = 1.0
        kaug[h, 1] = 1.0
        kaug[h, 2] = 1.0
        kaug[h, 3] = s8 * 128.0 * ph
        kaug[h, 4] = s8 * pl
        kk, qq = i[:, None], i[None, :]
        cm = np.where(kk > qq, -2.0 * s8 * (kk - qq), 0.0)
        cm = np.where((kk // 64) > (qq // 64), -240000.0, cm)
        corr[h] = cm
    return cf, kaug, qaug, corr


class Prog:
    def __init__(self, T=8192, depth=4, TS=16, PAST=1024):
        self.T, self.depth, self.TS, self.PAST = T, depth, TS, PAST
        self.NE = (depth + 1) // 2
        self.NO = depth // 2
        self.nc = bass.Bass("TRN2", target_bir_lowering=False)
        self.build()

    def din(self, name, shape, dt=F32):
        return self.nc.dram_tensor(name, list(shape), dt, kind="ExternalInput").ap()

    def dout(self, name, shape, dt=F32):
        return self.nc.dram_tensor(name, list(shape), dt, kind="ExternalOutput").ap()

    def dscr(self, name, shape, dt=F32):
        return self.nc.dram_tensor(name, list(shape), dt, kind="Internal").ap()

    def sb(self, es, name, shape, dt):
        self._uid += 1
        return es.enter_context(self.nc.sbuf_tensor("%s_%d" % (name, self._uid), list(shape), dt))

    def bank(self):
        i = self.bank_list[self.bank_rr % len(self.bank_list)]
        self.bank_rr += 1
        return self.PP[i // 2][:, (i % 2) * 512:(i % 2) * 512 + 512], [self.pb[i]]

    def pair(self):
        i = self.pair_list[self.pair_rr % len(self.pair_list)]
        self.pair_rr += 1
        return self.PP[i], [self.pb[2 * i], self.pb[2 * i + 1]]

    def rot(self, key, es, shape, dt, n=2):
        tiles = [(self.sb(es, key, shape, dt), Buf(key)) for _ in range(n)]
        st = {"i": 0}

        def nxt():
            t = tiles[st["i"] % n]
            st["i"] += 1
            return t
        return nxt

    def load_fm(self, es, dst_fn, src, R, nch, bdst, q="sp", pad=0):
        kb = self.kb
        Rp = R + pad
        for b0 in range(0, nch, 8):
            nb = min(8, nch - b0)
            st, bst = self.wstage()
            if pad:
                kb.op("pool", lambda e, st=st: e.memset(st[0:Rp, :], 0.0), W=[bst])
            kb.dma(q, st[pad:Rp, 0:nb * 128], src[:, b0 * 128:(b0 + nb) * 128], W=[bst])
            for c0 in range(0, nb, 4):
                pb, bb = self.bank()
                n4 = min(4, nb - c0)
                for c in range(c0, c0 + n4):
                    kb.op("pe", lambda e, c=c, pb=pb, st=st: e.transpose(out=pb[:, (c - c0) * 32:(c - c0) * 32 + Rp],
                                                            in_=st[0:Rp, c * 128:(c + 1) * 128],
                                                            identity=self.identf[0:Rp, 0:Rp]), R=[bst, self.bconst], W=bb)
                for c in range(c0, c0 + n4):
                    kb.op("act", lambda e, c=c, pb=pb: e.activation(out=dst_fn(b0 + c), in_=pb[:, (c - c0) * 32:(c - c0) * 32 + Rp],
                                                             func=AF.Copy), R=bb, W=[bdst])

    def load_w(self, dst, src3, nk, ncols, bdst, c0=0):
        kb = self.kb
        for k in range(nk):
            for a in range(0, ncols, 1024):
                w = min(1024, ncols - a)
                st, bst = self.wstage()
                kb.dma("sp", st[:, 0:w], src3[k, :, c0 + a:c0 + a + w], W=[bst])
                kb.op("pool", lambda e, st=st, k=k, a=a, w=w: e.tensor_copy(out=dst[:, k, a:a + w], in_=st[:, 0:w]),
                      R=[bst], W=[bdst])

    def prelude(self, xsrc, P, hT, bhT, col0, gs, sh, bbc):
        kb = self.kb
        xt, bx = self.xrot()
        kb.dma("sp", xt[0:P, :], xsrc, W=[bx])
        junk, bj = self.frot()
        ssq, bs = self.srot()
        kb.op("act", lambda e: e.activation(out=junk[0:P, :], in_=xt[0:P, :], func=AF.Square, accum_out=ssq[0:P, 0:1]),
              R=[bx], W=[bj, bs])
        kb.op("act", lambda e: e.activation(out=ssq[0:P, 1:2], in_=ssq[0:P, 0:1], func=AF.Sqrt, scale=1.0 / D, bias=self.epsb[0:P, 0:1]),
              R=[bs, self.bconst], W=[bs])
        kb.op("dve", lambda e: e.reciprocal(out=ssq[0:P, 1:2], in_=ssq[0:P, 1:2]), R=[bs], W=[bs])
        kb.op("dve", lambda e: e.scalar_tensor_tensor(out=junk[0:P, :], in0=xt[0:P, :], scalar=ssq[0:P, 1:2], in1=gs[0:P, :],
                                                       op0=ALU.mult, op1=ALU.mult), R=[bx, bs, bbc], W=[bj])
        hb, bh = self.hrot()
        kb.op("pool", lambda e: e.tensor_tensor(out=hb[0:P, :], in0=junk[0:P, :], in1=sh[0:P, :], op=ALU.add),
              R=[bj, bbc], W=[bh])
        self.transpose_to(hb, bh, P, 8, lambda k: hT[:, k, col0:col0 + P], bhT)

    def transpose_to(self, src, bsrc, P, nch, dst_fn, bdst):
        kb = self.kb
        for k0 in range(0, nch, 8):
            pb, bb = self.bank()
            pbb = pb.bitcast(BF16)
            n8 = min(8, nch - k0)
            for k in range(k0, k0 + n8):
                kb.op("pe", lambda e, k=k: e.transpose(out=pbb[:, (k - k0) * 128:(k - k0) * 128 + P],
                                                        in_=src[0:P, k * 128:(k + 1) * 128],
                                                        identity=self.identb[0:P, 0:P]), R=[bsrc, self.bconst], W=bb)
            for k in range(k0, k0 + n8):
                kb.op("act", lambda e, k=k: e.activation(out=dst_fn(k), in_=pbb[:, (k - k0) * 128:(k - k0) * 128 + P],
                                                         func=AF.Copy), R=bb, W=[bdst])

    def post(self, pp, bpp, P, xsrc, xdst, gg, bbc, bdram):
        kb = self.kb
        junk, bj = self.frot()
        ssq, bs = self.srot()
        kb.op("act", lambda e: e.activation(out=junk[0:P, :], in_=pp[0:P, :], func=AF.Square, accum_out=ssq[0:P, 0:1]),
              R=bpp, W=[bj, bs])
        kb.op("act", lambda e: e.activation(out=ssq[0:P, 1:2], in_=ssq[0:P, 0:1], func=AF.Sqrt, scale=1.0 / D, bias=self.epsb[0:P, 0:1]),
              R=[bs, self.bconst], W=[bs])
        kb.op("dve", lambda e: e.reciprocal(out=ssq[0:P, 1:2], in_=ssq[0:P, 1:2]), R=[bs], W=[bs])
        kb.op("dve", lambda e: e.scalar_tensor_tensor(out=junk[0:P, :], in0=pp[0:P, :], scalar=ssq[0:P, 1:2], in1=gg[0:P, :],
                                                       op0=ALU.mult, op1=ALU.mult), R=bpp + [bs, bbc], W=[bj])
        xt, bx = self.xrot()
        kb.dma("sp", xt[0:P, :], xsrc, R=[bdram], W=[bx])
        kb.op("dve", lambda e: e.tensor_tensor(out=xt[0:P, :], in0=xt[0:P, :], in1=junk[0:P, :], op=ALU.add),
              R=[bx, bj], W=[bx])
        kb.dma("pool", xdst, xt[0:P, :], R=[bx], W=[bdram])

    def make_bc(self, l, sub, ci):
        kb = self.kb
        gs, sh, gg = self.bc_gs, self.bc_sh, self.bc_gg
        bbc = self.bbc
        tmp, btmp = self.wstage()
        tmp2, btmp2 = self.wstage()
        base = sub * 3 * D
        mrow = lambda a: self.MOD[l, ci:ci + 1, base + a * D: base + (a + 1) * D].partition_broadcast(128)
        grow = lambda i: self.norm_g[l, i:i + 1, :].partition_broadcast(128)
        kb.dma("sp", sh[:], mrow(0), R=[self.bmod], W=[bbc])
        kb.dma("sp", gs[:], mrow(1), R=[self.bmod], W=[bbc])
        kb.dma("sp", gg[:], mrow(2), R=[self.bmod], W=[bbc])
        kb.dma("sp", tmp[:], grow(2 * sub), W=[btmp])
        kb.op("dve", lambda e: e.scalar_tensor_tensor(out=gs[:], in0=gs[:], scalar=1.0, in1=tmp[:], op0=ALU.add, op1=ALU.mult),
              R=[bbc, btmp], W=[bbc])
        kb.dma("sp", tmp2[:], grow(2 * sub + 1), W=[btmp2])
        kb.op("dve", lambda e: e.tensor_tensor(out=gg[:], in0=gg[:], in1=tmp2[:], op=ALU.mult), R=[bbc, btmp2], W=[bbc])
        return gs, sh, gg, bbc

    def seqs(self):
        T, TS = self.T, self.TS
        S = []
        S.append(dict(name="p", ci=0, T=T, P=128, GN=min(512, T), x0=self.x_p, xb=self.xb_p, y=self.y_p, s=None))
        for s in range(2):
            S.append(dict(name="s%d" % s, ci=1 + s, T=TS, P=TS, GN=TS, x0=self.x_s[s], xb=self.xb_s[s], y=self.y_s[s], s=s))
        return S

    def xio(self, sq, stage):
        act = self.active_stages
        src = sq["x0"] if stage == act[0] else sq["xb"]
        dst = sq["y"] if stage == act[-1] else sq["xb"]
        return src, dst

    def build(self):
        nc = self.nc
        T, TS, PAST, NE, NO, depth = self.T, self.TS, self.PAST, self.NE, self.NO, self.depth
        TK = PAST + TS
        self._uid = 0
        self.x_p = self.din("x_p", [T, D])
        self.x_s = self.din("x_s", [2, TS, D])
        self.c_all = self.din("c_all", [3, D])
        self.cache_k = self.din("cache_k", [NE, 2, PAST, D])
        self.cache_v = self.din("cache_v", [NE, 2, PAST, D])
        self.st_sconv = self.din("st_sconv", [NE, 2, 3, 1536])
        self.st_ssm = self.din("st_ssm", [NE, 2, 1024, 128])
        self.st_cconv = self.din("st_cconv", [max(NO, 1), 2, 30, D])
        self.ada_w = self.din("ada_w", [depth, D, 6 * D])
        self.ada_b = self.din("ada_b", [depth, 6 * D])
        self.norm_g = self.din("norm_g", [depth, 4, D])
        self.ffn_w_up = self.din("ffn_w_up", [depth, D, 2 * DFF])
        self.ffn_w_down = self.din("ffn_w_down", [depth, DFF, D])
        self.hyb_w_in = self.din("hyb_w_in", [NE, D, DIN])
        self.attn_lambda = self.din("attn_lambda", [NE, 256])
        self.attn_subln_g = self.din("attn_subln_g", [NE, 128])
        self.ssm_conv_w = self.din("ssm_conv_w", [NE, 4, 1536])
        self.ssm_conv_b = self.din("ssm_conv_b", [NE, 1536])
        self.ssm_dt_bias = self.din("ssm_dt_bias", [NE, 16])
        self.ssm_a_log = self.din("ssm_a_log", [NE, 16])
        self.ssm_d = self.din("ssm_d", [NE, 16])
        self.ssm_norm_g = self.din("ssm_norm_g", [NE, D])
        self.hyb_w_out = self.din("hyb_w_out", [NE, 2 * D, D])
        self.conf_w_in = self.din("conf_w_in", [max(NO, 1), D, 2 * D])
        self.conf_b_in = self.din("conf_b_in", [max(NO, 1), 2 * D])
        self.conf_dw_w = self.din("conf_dw_w", [max(NO, 1), 31, D])
        self.conf_dw_b = self.din("conf_dw_b", [max(NO, 1), D])
        self.conf_ln_g = self.din("conf_ln_g", [max(NO, 1), D])
        self.conf_ln_b = self.din("conf_ln_b", [max(NO, 1), D])
        self.conf_w_out = self.din("conf_w_out", [max(NO, 1), D, D])
        self.conf_b_out = self.din("conf_b_out", [max(NO, 1), D])
        self.cst_f = self.din("cst_f", [4, 128, 128])
        self.cst_kaug = self.din("cst_kaug", [8, 5, 8192])
        self.cst_qaug = self.din("cst_qaug", [8, 5, 8192])
        self.cst_corr = self.din("cst_corr", [8, 128, 128])
        self.y_p = self.dout("y_p", [T, D])
        self.y_s = self.dout("y_s", [2, TS, D])
        self.k_p = self.dout("k_p", [NE, T, D])
        self.v_p = self.dout("v_p", [NE, T, D])
        self.sconv_p = self.dout("sconv_p", [NE, 3, 1536])
        self.ssm_p = self.dout("ssm_p", [NE, 1024, 128])
        self.cconv_p = self.dout("cconv_p", [max(NO, 1), 30, D])
        self.k_s = self.dout("k_s", [NE, 2, TS, D])
        self.v_s = self.dout("v_s", [NE, 2, TS, D])
        self.sconv_s = self.dout("sconv_s", [NE, 2, 3, 1536])
        self.ssm_s = self.dout("ssm_s", [NE, 2, 1024, 128])
        self.cconv_s = self.dout("cconv_s", [max(NO, 1), 2, 30, D])
        self.xb_p = self.dscr("xb_p", [T, D])
        self.xb_s = self.dscr("xb_s", [2, TS, D])
        self.MOD = self.dscr("modrows", [depth, 3, 6 * D])
        self.QT = [self.dscr("qt_p", [8, 128, T], BF16)] + [self.dscr("qt_s%d" % s, [8, 128, TS], BF16) for s in range(2)]
        self.KT = [self.dscr("kt_p", [8, 128, T], BF16)] + [self.dscr("kt_s%d" % s, [8, 128, TK], BF16) for s in range(2)]
        self.VB = [self.dscr("vb_p", [T, D], BF16)] + [self.dscr("vb_s%d" % s, [TK, D], BF16) for s in range(2)]
        self.OT = [self.dscr("ot_p", [D, T], BF16)] + [self.dscr("ot_s%d" % s, [D, TS], BF16) for s in range(2)]
        self.YT = [self.dscr("yt_p", [D, T], BF16)] + [self.dscr("yt_s%d" % s, [D, TS], BF16) for s in range(2)]
        self.bscr = [dict(q=Buf(), k=Buf(), v=Buf(), o=Buf(), y=Buf(), x=Buf()) for _ in range(3)]
        self.bmod = Buf()
        self.bout = Buf()

        with ExitStack() as es:
            self.kb = kb = KB(nc, es)
            self.PP = [es.enter_context(nc.psum_tensor("pp%d" % i, [128, 1024], F32)) for i in range(4)]
            self.pb = [Buf("bank%d" % i, excl=True) for i in range(8)]
            self.bank_list, self.bank_rr = list(range(8)), 0
            self.pair_list, self.pair_rr = list(range(4)), 0
            self.bconst = Buf("const")
            cf = self.sb(es, "cf", [128, 4, 128], F32)
            kb.dma("sp", cf[:], self.cst_f.rearrange("a p c -> p a c"), W=[self.bconst])
            self.identf, self.onesf, self.trilef, self.ugtf = cf[:, 0, :], cf[:, 1, :], cf[:, 2, :], cf[:, 3, :]
            cb = self.sb(es, "cb", [128, 4, 128], BF16)
            kb.op("pool", lambda e: e.tensor_copy(out=cb[:], in_=cf[:]), R=[self.bconst], W=[self.bconst])
            self.identb, self.onesb, self.trileb, self.ugtb = cb[:, 0, :], cb[:, 1, :], cb[:, 2, :], cb[:, 3, :]
            self.epsb = self.sb(es, "epsb", [128, 1], F32)
            kb.op("pool", lambda e: e.memset(self.epsb[:], EPS), W=[self.bconst])
            self.m1024 = self.sb(es, "m1024", [128, 128], F32)
            kb.op("pool", lambda e: e.memset(self.m1024[:], 1.0 / 1024), W=[self.bconst])
            self.m128 = self.sb(es, "m128", [128, 128], F32)
            kb.op("pool", lambda e: e.memset(self.m128[:], 1.0 / 128), W=[self.bconst])
            self.xrot = self.rot("xt", es, [128, D], F32, 2)
            self.frot = self.rot("ft", es, [128, D], F32, 2)
            self.hrot = self.rot("hb", es, [128, D], BF16, 1)
            self.srot = self.rot("ssq", es, [128, 4], F32, 4)
            self.wstage = self.rot("wst", es, [128, 1024], F32, 2)
            self.bc_gs = self.sb(es, "bcgs", [128, D], F32)
            self.bc_sh = self.sb(es, "bcsh", [128, D], F32)
            self.bc_gg = self.sb(es, "bcgg", [128, D], F32)
            self.bbc = Buf("bc")

            self.active_stages = []
            for l in range(depth):
                if (l % 2 == 0 and "hyb" not in SKIP) or (l % 2 == 1 and "conf" not in SKIP):
                    self.active_stages.append(2 * l)
                if "ffn" not in SKIP:
                    self.active_stages.append(2 * l + 1)
            self.phase_adaln()
            S = self.seqs()
            for l in range(depth):
                j = l // 2
                if l % 2 == 0 and "hyb" not in SKIP:
                    for ph in (self.phase_e1a, self.phase_e1b, self.phase_e2, self.phase_e3):
                        if DEBUG:
                            print("phase", ph.__name__, "l", l, "next_id", self.nc.next_id(), flush=True)
                        if ph.__name__[6:] not in SKIP:
                            ph(l, j, S)
                if l % 2 == 1 and "conf" not in SKIP:
                    self.phase_conf(l, j, S)
                if "ffn" not in SKIP:
                    self.phase_ffn(l, S)
            kb.barrier()
            self.stats = (kb.nins, kb.nwait)

    def phase_adaln(self):
        kb, depth = self.kb, self.depth
        with ExitStack() as es:
            ct = self.sb(es, "ct", [4, D], F32); bct = Buf()
            kb.dma("sp", ct[0:3, :], self.c_all, W=[bct])
            kb.op("act", lambda e: e.activation(out=ct[0:3, :], in_=ct[0:3, :], func=AF.Silu), R=[bct], W=[bct])
            cT = self.sb(es, "cT", [128, 8, 4], F32); bcT = Buf()
            pb, bb = self.bank()
            for k in range(8):
                kb.op("pe", lambda e, k=k: e.transpose(out=pb[:, k * 4:k * 4 + 3], in_=ct[0:3, k * 128:(k + 1) * 128],
                                                        identity=self.identf[0:3, 0:3]), R=[bct, self.bconst], W=bb)
            for k in range(8):
                kb.op("act", lambda e, k=k: e.activation(out=cT[:, k, 0:3], in_=pb[:, k * 4:k * 4 + 3], func=AF.Copy), R=bb, W=[bcT])
            ab = self.sb(es, "ab", [4, 6 * D], F32); bab = Buf()
            mrow = self.sb(es, "mrow", [4, 6 * D], F32); bmr = Buf()
            wrot = self.rot("adaw", es, [128, 3072], F32, 3)
            for l in range(depth):
                kb.dma("sp", ab[0:3, :], self.ada_b[l:l + 1, :].partition_broadcast(3), R=[bab], W=[bab])
                for half in range(2):
                    banks = [self.bank() for _ in range(6)]
                    for k in range(8):
                        wt, bw = wrot()
                        kb.dma("sp" if k % 2 == 0 else "pool", wt[:], self.ada_w[l, k * 128:(k + 1) * 128, half * 3072:(half + 1) * 3072], W=[bw])
                        for n in range(6):
                            pbn, bbn = banks[n]
                            kb.op("pe", lambda e, k=k, n=n, pbn=pbn, wt=wt: e.matmul(pbn[0:3, :], lhsT=cT[:, k, 0:3], rhs=wt[:, n * 512:(n + 1) * 512],
                                                                                  start=(k == 0), stop=(k == 7)), R=[bcT, bw], W=bbn)
                    for n in range(6):
                        pbn, bbn = banks[n]
                        c0 = half * 3072 + n * 512
                        kb.op("dve", lambda e, pbn=pbn, c0=c0: e.tensor_tensor(out=mrow[0:3, c0:c0 + 512], in0=pbn[0:3, :], in1=ab[0:3, c0:c0 + 512], op=ALU.add),
                              R=bbn + [bab], W=[bmr])
                kb.dma("sp", self.MOD[l], mrow[0:3, :], R=[bmr], W=[self.bmod])
            kb.barrier()

    def phase_ffn(self, l, S):
        kb = self.kb
        stage = 2 * l + 1
        with ExitStack() as es:
            wup = self.sb(es, "wup", [128, 8, 2 * DFF], BF16); bwu = Buf()
            wdn = self.sb(es, "wdn", [128, 22, D], BF16); bwd = Buf()
            self.load_w(wup, self.ffn_w_up[l].rearrange("(k p) n -> k p n", p=128), 8, 2 * DFF, bwu)
            self.load_w(wdn, self.ffn_w_down[l].rearrange("(k p) n -> k p n", p=128), 22, D, bwd)
            GNmax = S[0]["GN"]
            hT = self.sb(es, "hT", [128, 8, GNmax], BF16); bhT = Buf()
            aT = self.sb(es, "aT", [128, 22, GNmax], BF16); baT = Buf()
            sgr = self.rot("sg", es, [128, GNmax], F32, 1)
            kb.op("pool", lambda e: e.memset(hT[:], 0.0), W=[bhT])
            for si, sq in enumerate(S):
                gs, sh, gg, bbc = self.make_bc(l, 1, sq["ci"])
                src, dst = self.xio(sq, stage)
                bx = self.bscr[si]["x"]
                P, GN = sq["P"], sq["GN"]
                GNp = max(GN, 128)
                for g0 in range(0, sq["T"], GN):
                    nt = GN // P
                    for m in range(nt):
                        self.prelude_x(src[g0 + m * P:g0 + (m + 1) * P, :], bx, P, hT, bhT, m * P, gs, sh, bbc)
                    for jf in range(22):
                        pg, bg = self.bank()
                        pu, bu = self.bank()
                        for k in range(8):
                            kb.op("pe", lambda e, k=k, pg=pg: e.matmul(pg[:, 0:GNp], lhsT=wup[:, k, jf * 128:(jf + 1) * 128], rhs=hT[:, k, 0:GNp],
                                                                      start=(k == 0), stop=(k == 7)), R=[bwu, bhT], W=bg)
                        for k in range(8):
                            kb.op("pe", lambda e, k=k, pu=pu: e.matmul(pu[:, 0:GNp], lhsT=wup[:, k, DFF + jf * 128:DFF + (jf + 1) * 128], rhs=hT[:, k, 0:GNp],
                                                                      start=(k == 0), stop=(k == 7)), R=[bwu, bhT], W=bu)
                        sg, bsg = sgr()
                        kb.op("act", lambda e, sg=sg, pg=pg: e.activation(out=sg[:, 0:GN], in_=pg[:, 0:GN], func=AF.Silu), R=bg, W=[bsg])
                        kb.op("dve", lambda e, sg=sg, pu=pu, jf=jf: e.tensor_tensor(out=aT[:, jf, 0:GN], in0=sg[:, 0:GN], in1=pu[:, 0:GN], op=ALU.mult),
                              R=[bsg] + bu, W=[baT])
                    for m in range(nt):
                        pp, bpp = self.pair()
                        for nh in range(2):
                            for k in range(22):
                                kb.op("pe", lambda e, k=k, nh=nh, m=m, pp=pp: e.matmul(pp[0:P, nh * 512:(nh + 1) * 512], lhsT=aT[:, k, m * P:(m + 1) * P],
                                                                                     rhs=wdn[:, k, nh * 512:(nh + 1) * 512], start=(k == 0), stop=(k == 21)),
                                      R=[baT, bwd], W=bpp)
                        r0 = g0 + m * P
                        self.post(pp, bpp, P, src[r0:r0 + P, :], dst[r0:r0 + P, :], gg, bbc, bx)
            kb.barrier()

    def prelude_x(self, xsrc, bdram, P, hT, bhT, col0, gs, sh, bbc):
        kb = self.kb
        xt, bx = self.xrot()
        kb.dma("sp", xt[0:P, :], xsrc, R=[bdram], W=[bx])
        junk, bj = self.frot()
        ssq, bs = self.srot()
        kb.op("act", lambda e: e.activation(out=junk[0:P, :], in_=xt[0:P, :], func=AF.Square, accum_out=ssq[0:P, 0:1]),
              R=[bx], W=[bj, bs])
        kb.op("act", lambda e: e.activation(out=ssq[0:P, 1:2], in_=ssq[0:P, 0:1], func=AF.Sqrt, scale=1.0 / D, bias=self.epsb[0:P, 0:1]),
              R=[bs, self.bconst], W=[bs])
        kb.op("dve", lambda e: e.reciprocal(out=ssq[0:P, 1:2], in_=ssq[0:P, 1:2]), R=[bs], W=[bs])
        kb.op("dve", lambda e: e.scalar_tensor_tensor(out=junk[0:P, :], in0=xt[0:P, :], scalar=ssq[0:P, 1:2], in1=gs[0:P, :],
                                                       op0=ALU.mult, op1=ALU.mult), R=[bx, bs, bbc], W=[bj])
        hb, bh = self.hrot()
        kb.op("pool", lambda e: e.tensor_tensor(out=hb[0:P, :], in0=junk[0:P, :], in1=sh[0:P, :], op=ALU.add),
              R=[bj, bbc], W=[bh])
        self.transpose_to(hb, bh, P, 8, lambda k: hT[:, k, col0:col0 + P], bhT)

    def evac(self, out, in_, R, W):
        self._ev = getattr(self, "_ev", 0) + 1
        if self._ev % 2 == 0:
            self.kb.op("act", lambda e: e.activation(out=out, in_=in_, func=AF.Copy), R=R, W=W)
        else:
            self.kb.op("dve", lambda e: e.tensor_copy(out=out, in_=in_), R=R, W=W)

    def phase_e1a(self, l, j, S):
        kb = self.kb
        stage = 2 * l
        PAST, TS = self.PAST, self.TS
        with ExitStack() as es:
            w = self.sb(es, "w1a", [128, 8, 3072], BF16); bw = Buf()
            self.load_w(w, self.hyb_w_in[j].rearrange("(k p) n -> k p n", p=128), 8, 3072, bw, c0=0)
            GNmax = S[0]["GN"]
            hT = self.sb(es, "ahT", [128, 8, GNmax], BF16); bhT = Buf()
            kvt = self.rot("kvt", es, [128, 2048], F32, 2)
            vbr = self.rot("vbr", es, [128, D], BF16, 2)
            fmr = self.rot("fmr", es, [128, GNmax], BF16, 3)
            kfr = self.rot("kfr", es, [128, 8, 128], BF16, 2)
            kb.op("pool", lambda e: e.memset(hT[:], 0.0), W=[bhT])
            for si, sq in enumerate(S):
                gs, sh, gg, bbc = self.make_bc(l, 0, sq["ci"])
                src, _ = self.xio(sq, stage)
                bx = self.bscr[si]["x"]
                bsc = self.bscr[si]
                P, GN, Tq = sq["P"], sq["GN"], sq["T"]
                kp0 = 0
                if sq["s"] is not None:
                    s_ = sq["s"]
                    kp0 = PAST
                    for t in range(PAST // 128):
                        ck, bck = kvt()
                        kb.dma("sp", ck[:, 0:1024], self.cache_k[j, s_, t * 128:(t + 1) * 128, :], W=[bck])
                        kb.dma("sp", ck[:, 1024:2048], self.cache_v[j, s_, t * 128:(t + 1) * 128, :], W=[bck])
                        kf, bkf = kfr()
                        for h0 in (0, 4):
                            pb, bb = self.bank()
                            for h in range(h0, h0 + 4):
                                kb.op("pe", lambda e, h=h, pb=pb, ck=ck: e.transpose(out=pb[:, (h - h0) * 128:(h - h0 + 1) * 128], in_=ck[:, h * 128:(h + 1) * 128],
                                                                                     identity=self.identf[:, :]), R=[bck, self.bconst], W=bb)
                            self.evac(kf[:, h0:h0 + 4, :], pb.rearrange("p (a b) -> p a b", a=4), bb, [bkf])
                        kb.dma("pool", self.KT[si][:, :, t * 128:(t + 1) * 128].rearrange("h r t -> r h t"), kf[:], R=[bkf], W=[bsc["k"]])
                        vb, bvb = vbr()
                        kb.op("pool", lambda e, vb=vb, ck=ck: e.tensor_copy(out=vb[:], in_=ck[:, 1024:2048]), R=[bck], W=[bvb])
                        kb.dma("pool", self.VB[si][t * 128:(t + 1) * 128, :], vb[:], R=[bvb], W=[bsc["v"]])
                for g0 in range(0, Tq, GN):
                    nt = GN // P
                    for m in range(nt):
                        self.prelude_x(src[g0 + m * P:g0 + (m + 1) * P, :], bx, P, hT, bhT, m * P, gs, sh, bbc)
                    for m in range(nt):
                        kv, bkv = kvt()
                        for n4 in range(4):
                            pb, bb = self.bank()
                            for k in range(8):
                                kb.op("pe", lambda e, k=k, pb=pb, n4=n4, m=m: e.matmul(pb[0:P, :], lhsT=hT[:, k, m * P:(m + 1) * P], rhs=w[:, k, 1024 + n4 * 512:1024 + (n4 + 1) * 512],
                                                                                     start=(k == 0), stop=(k == 7)), R=[bhT, bw], W=bb)
                            self.evac(kv[0:P, n4 * 512:(n4 + 1) * 512], pb[0:P, :], bb, [bkv])
                        r0 = g0 + m * P
                        if sq["s"] is None:
                            kb.dma("pool", self.k_p[j, r0:r0 + P, :], kv[0:P, 0:1024], R=[bkv], W=[self.bout])
                            kb.dma("pool", self.v_p[j, r0:r0 + P, :], kv[0:P, 1024:2048], R=[bkv], W=[self.bout])
                        else:
                            kb.dma("pool", self.k_s[j, sq["s"], r0:r0 + P, :], kv[0:P, 0:1024], R=[bkv], W=[self.bout])
                            kb.dma("pool", self.v_s[j, sq["s"], r0:r0 + P, :], kv[0:P, 1024:2048], R=[bkv], W=[self.bout])
                        vb, bvb = vbr()
                        kb.op("pool", lambda e, vb=vb, kv=kv: e.tensor_copy(out=vb[0:P, :], in_=kv[0:P, 1024:2048]), R=[bkv], W=[bvb])
                        kb.dma("pool", self.VB[si][kp0 + r0:kp0 + r0 + P, :], vb[0:P, :], R=[bvb], W=[bsc["v"]])
                    for c in range(16):
                        pb, bb = self.bank()
                        for k in range(8):
                            kb.op("pe", lambda e, k=k, pb=pb, c=c: e.matmul(pb[:, 0:max(GN, 128)], lhsT=w[:, k, c * 128:(c + 1) * 128], rhs=hT[:, k, 0:max(GN, 128)],
                                                                           start=(k == 0), stop=(k == 7)), R=[bhT, bw], W=bb)
                        fm, bfm = fmr()
                        self.evac(fm[:, 0:GN], pb[:, 0:GN], bb, [bfm])
                        if c < 8:
                            kb.dma("pool", self.QT[si][c, :, g0:g0 + GN], fm[:, 0:GN], R=[bfm], W=[bsc["q"]])
                        else:
                            kb.dma("pool", self.KT[si][c - 8, :, kp0 + g0:kp0 + g0 + GN], fm[:, 0:GN], R=[bfm], W=[bsc["k"]])
            kb.barrier()

    def phase_e1b(self, l, j, S):
        kb = self.kb
        stage = 2 * l
        with ExitStack() as es:
            w = self.sb(es, "w1b", [128, 8, 2576], BF16); bw = Buf()
            self.load_w(w, self.hyb_w_in[j].rearrange("(k p) n -> k p n", p=128), 8, 2576, bw, c0=3072)
            bsm = Buf("e1bsmall")
            cwf = self.sb(es, "cwf", [128, 12, 4], F32)
            self.load_fm(es, lambda c: cwf[:, c, :], self.ssm_conv_w[j], 4, 12, bsm)
            cbf = self.sb(es, "cbf", [128, 12, 1], F32)
            self.load_fm(es, lambda c: cbf[:, c, :], self.ssm_conv_b[j:j + 1, :], 1, 12, bsm)
            diag4 = self.sb(es, "diag4", [128, 48, 128], BF16)
            kb.op("dve", lambda e: e.tensor_tensor(out=diag4[:], in0=self.identf.unsqueeze(1).to_broadcast([128, 48, 128]),
                                                   in1=cwf[:].rearrange("p c k -> p (c k)").unsqueeze(2).to_broadcast([128, 48, 128]),
                                                   op=ALU.mult), R=[bsm, self.bconst], W=[bsm])
            cbr = self.sb(es, "cbr", [1, 1536], F32)
            kb.dma("sp", cbr[:], self.ssm_conv_b[j:j + 1, :], W=[bsm])
            cbrb = self.sb(es, "cbrb", [1, 1536], BF16)
            kb.op("pool", lambda e: e.tensor_copy(out=cbrb[:], in_=cbr[:]), R=[bsm], W=[bsm])
            sm = self.sb(es, "ssmsm", [128, 4, 16], F32)
            kb.dma("sp", sm[:, 0, :], self.ssm_a_log[j:j + 1, :].partition_broadcast(128), W=[bsm])
            kb.dma("sp", sm[:, 1, :], self.ssm_d[j:j + 1, :].partition_broadcast(128), W=[bsm])
            kb.dma("sp", sm[:, 2, :], self.ssm_dt_bias[j:j + 1, :].partition_broadcast(128), W=[bsm])
            kb.op("act", lambda e: e.activation(out=sm[:, 0, :], in_=sm[:, 0, :], func=AF.Exp), R=[bsm], W=[bsm])
            kb.op("dve", lambda e: e.tensor_scalar(out=sm[:, 0, :], in0=sm[:, 0, :], scalar1=-1.0, scalar2=None, op0=ALU.mult), R=[bsm], W=[bsm])
            a_b, D_b, dtb_b = sm[:, 0, :], sm[:, 1, :], sm[:, 2, :]
            ngb = self.sb(es, "ngb", [128, D], F32)
            kb.dma("sp", ngb[:], self.ssm_norm_g[j:j + 1, :].partition_broadcast(128), W=[bsm])
            if E1B_STOP <= 1:
                kb.barrier()
                return
            GNmax = S[0]["GN"]
            hT = self.sb(es, "bhT", [128, 8, GNmax], BF16); bhT = Buf()
            xbcT = self.sb(es, "xbcT", [128, 12, 4 + GNmax], BF16); bxb = Buf()
            t1 = self.sb(es, "t1", [128, D], F32); bt1 = Buf()
            t3 = self.sb(es, "t3", [128, D], F32); bt3 = Buf()
            ynb = self.sb(es, "ynb", [128, D], BF16); bynb = Buf()
            ynT = self.sb(es, "ynT", [128, 8, 128], BF16); bynT = Buf()
            hst = self.sb(es, "hst", [128, D], F32); bhst = Buf()
            hbf = self.sb(es, "hbf", [128, D], BF16); bhbf = Buf()
            xraw = self.sb(es, "xraw", [128, 1536], F32); bxr = Buf()
            TB = []
            for i in range(2):
                TB.append((self.sb(es, "szt", [128, D], F32), Buf(), self.sb(es, "xst", [128, D], F32), Buf(),
                           self.sb(es, "Rf", [128, 2048], BF16), Buf(), self.sb(es, "exf", [128, 1024], F32), Buf(),
                           self.sb(es, "scf", [128, 2048], BF16), Buf(), self.sb(es, "xdt", [128, D], BF16), Buf(),
                           self.sb(es, "xdtw", [128, D], BF16), Buf(), self.sb(es, "bct", [128, 4, 128], BF16), Buf(),
                           self.sb(es, "btok", [128, 256], BF16), Buf(), self.sb(es, "cbm", [128, 2, 128], F32), Buf(),
                           self.sb(es, "d16", [128, 12, 16], F32), Buf(), self.sb(es, "d16b", [128, 16], BF16)))
            tbi = [0]
            kb.op("pool", lambda e: e.memset(hT[:], 0.0), W=[bhT])
            kb.op("pool", lambda e: e.memset(xbcT[:], 0.0), W=[bxb])
            for si, sq in enumerate(S):
                if CONF_SEQS is not None and si not in CONF_SEQS:
                    continue
                gs, sh, gg, bbc = self.make_bc(l, 0, sq["ci"])
                src, _ = self.xio(sq, stage)
                bx = self.bscr[si]["x"]
                bsc = self.bscr[si]
                P, GN, Tq = sq["P"], sq["GN"], sq["T"]
                if sq["s"] is None:
                    kb.op("pool", lambda e: e.memset(xbcT[:, :, 0:4], 0.0), W=[bxb])
                    kb.op("pool", lambda e: e.memset(hst[:], 0.0), W=[bhst])
                    kb.op("pool", lambda e: e.memset(hbf[:], 0.0), W=[bhbf])
                else:
                    s_ = sq["s"]
                    self.load_fm(es, lambda c: xbcT[:, c, 0:4], self.st_sconv[j, s_], 3, 12, bxb, pad=1)
                    st, bst = self.wstage()
                    kb.dma("sp", st[:].rearrange("p (c n) -> p c n", c=8), self.st_ssm[j, s_].rearrange("(c p) n -> p c n", p=128), W=[bst])
                    pp, bpp = self.pair()
                    for c in range(8):
                        kb.op("pe", lambda e, c=c, pp=pp, st=st: e.transpose(out=pp[:, c * 128:(c + 1) * 128], in_=st[:, c * 128:(c + 1) * 128], identity=self.identf[:, :]),
                              R=[bst, self.bconst], W=bpp)
                    kb.op("act", lambda e, pp=pp: e.activation(out=hst[:], in_=pp[:], func=AF.Copy), R=bpp, W=[bhst])
                    kb.op("dve", lambda e: e.tensor_copy(out=hbf[:], in_=hst[:]), R=[bhst], W=[bhbf])
                ngr = Tq // GN
                for gi in range(ngr):
                    g0 = gi * GN
                    nt = GN // P
                    for m in range(nt):
                        self.prelude_x(src[g0 + m * P:g0 + (m + 1) * P, :], bx, P, hT, bhT, m * P, gs, sh, bbc)
                    for c in range(12):
                        pb, bb = self.bank()
                        for k in range(8):
                            kb.op("pe", lambda e, k=k, pb=pb, c=c: e.matmul(pb[:, 0:max(GN, 128)], lhsT=w[:, k, 1024 + c * 128:1024 + (c + 1) * 128], rhs=hT[:, k, 0:max(GN, 128)],
                                                                           start=(k == 0), stop=(k == 7)), R=[bhT, bw], W=bb)
                        self.evac(xbcT[:, c, 4:4 + GN], pb[:, 0:GN], bb, [bxb])
                    def tile_gen(m, gi=gi, g0=g0):
                        (szt, bsz, xst, bxs, Rf, bR, exf, bex, scf, bscf, xdt, bxdt, xdtw, bxdtw, bct, bbct, btok, bbtok, cbm, bcbm, d16, bd, d16b) = TB[tbi[0] % 2]
                        tbi[0] += 1
                        Rv = Rf[0:P, 0:16 * P].rearrange("p (h l) -> p h l", h=16)
                        scv = scf[0:P, 0:16 * P].rearrange("p (h l) -> p h l", h=16)
                        t0 = m * P
                        r0 = g0 + t0
                        pz, bz = self.pair()
                        for nh in range(2):
                            for k in range(8):
                                kb.op("pe", lambda e, k=k, nh=nh, pz=pz: e.matmul(pz[0:P, nh * 512:(nh + 1) * 512], lhsT=hT[:, k, t0:t0 + P], rhs=w[:, k, nh * 512:(nh + 1) * 512],
                                                                                start=(k == 0), stop=(k == 7)), R=[bhT, bw], W=bz)
                        kb.op("act", lambda e, pz=pz: e.activation(out=szt[0:P, :], in_=pz[0:P, :], func=AF.Silu), R=bz, W=[bsz])
                        pd, bpd = self.bank()
                        for k in range(8):
                            kb.op("pe", lambda e, k=k, pd=pd: e.matmul(pd[0:P, 0:128], lhsT=hT[:, k, t0:t0 + P], rhs=w[:, k, 2448:2576], start=(k == 0), stop=(k == 7)),
                                  R=[bhT, bw], W=bpd)
                        X, AXv, EX, LG, DT, DTA, WL, DTW, EXPA, EL = [d16[0:P, i, :] for i in range(10)]
                        EL = d16[:, 9, :]
                        kb.op("dve", lambda e, pd=pd: e.tensor_tensor(out=X, in0=pd[0:P, 112:128], in1=dtb_b[0:P, :], op=ALU.add), R=bpd + [bsm], W=[bd])
                        kb.op("dve", lambda e: e.scalar_tensor_tensor(out=AXv, in0=X, scalar=-1.0, in1=X, op0=ALU.mult, op1=ALU.max), R=[bd], W=[bd])
                        kb.op("act", lambda e: e.activation(out=EX, in_=AXv, func=AF.Exp, scale=-1.0), R=[bd], W=[bd])
                        kb.op("act", lambda e: e.activation(out=LG, in_=EX, func=AF.Ln, bias=1.0), R=[bd], W=[bd])
                        kb.op("dve", lambda e: e.scalar_tensor_tensor(out=DT, in0=X, scalar=0.0, in1=LG, op0=ALU.max, op1=ALU.add), R=[bd], W=[bd])
                        kb.op("dve", lambda e: e.tensor_tensor(out=DTA, in0=DT, in1=a_b[0:P, :], op=ALU.mult), R=[bd, bsm], W=[bd])
                        kb.op("dve", lambda e: e.tensor_copy(out=d16b[0:P, :], in_=DTA), R=[bd], W=[bd])
                        px, bpx = self.pair()
                        for k in range(5):
                            for c in range(8):
                                st_, sp_ = (k == 0 and c % 4 == 0), (k == 4 and c % 4 == 3)
                                if k < 4:
                                    kb.op("pe", lambda e, c=c, k=k, px=px, st_=st_, sp_=sp_: e.matmul(px[0:P, c * 128:(c + 1) * 128], lhsT=xbcT[:, c, 1 + t0 + k:1 + t0 + k + P], rhs=diag4[:, c * 4 + k, :],
                                                                                                   start=st_, stop=sp_), R=[bxb, bsm], W=bpx)
                                else:
                                    kb.op("pe", lambda e, c=c, px=px, st_=st_, sp_=sp_: e.matmul(px[0:P, c * 128:(c + 1) * 128], lhsT=self.onesb[0:1, 0:P], rhs=cbrb[0:1, c * 128:(c + 1) * 128],
                                                                                              start=st_, stop=sp_), R=[bsm, self.bconst], W=bpx)
                        kb.op("act", lambda e, px=px: e.activation(out=xst[0:P, :], in_=px[0:P, :], func=AF.Silu), R=bpx, W=[bxs])
                        pk, bpk = self.bank()
                        for k in range(5):
                            for c in (8, 9):
                                st_, sp_ = (k == 0 and c == 8), (k == 4 and c == 9)
                                if k < 4:
                                    kb.op("pe", lambda e, c=c, k=k, pk=pk, st_=st_, sp_=sp_: e.matmul(pk[0:P, (c - 8) * 128:(c - 7) * 128], lhsT=xbcT[:, c, 1 + t0 + k:1 + t0 + k + P], rhs=diag4[:, c * 4 + k, :],
                                                                                                   start=st_, stop=sp_), R=[bxb, bsm], W=bpk)
                                else:
                                    kb.op("pe", lambda e, c=c, pk=pk, st_=st_, sp_=sp_: e.matmul(pk[0:P, (c - 8) * 128:(c - 7) * 128], lhsT=self.onesb[0:1, 0:P], rhs=cbrb[0:1, c * 128:(c + 1) * 128],
                                                                                              start=st_, stop=sp_), R=[bsm, self.bconst], W=bpk)
                        kb.op("act", lambda e, pk=pk: e.activation(out=btok[0:P, :], in_=pk[0:P, 0:256], func=AF.Silu), R=bpk, W=[bbtok])
                        pf, bpf = self.bank()
                        for k in range(4):
                            for i, c in enumerate((8, 9, 10, 11)):
                                st_, sp_ = (k == 0 and i == 0), (k == 3 and i == 3)
                                kb.op("pe", lambda e, c=c, k=k, i=i, pf=pf, st_=st_, sp_=sp_: e.matmul(pf[:, i * 128:(i + 1) * 128], lhsT=diag4[:, c * 4 + k, :], rhs=xbcT[:, c, 1 + t0 + k:1 + t0 + k + 128],
                                                                                                    start=st_, stop=sp_), R=[bxb, bsm], W=bpf)
                        for i, c in enumerate((8, 9, 10, 11)):
                            kb.op("act", lambda e, c=c, i=i, pf=pf: e.activation(out=bct[:, i, 0:P], in_=pf[:, i * 128:i * 128 + P], func=AF.Silu, bias=cbf[:, c, :]),
                                  R=bpf + [bsm], W=[bbct])
                        pcb, bpcb = self.bank()
                        for g in range(2):
                            kb.op("pe", lambda e, g=g, pcb=pcb: e.matmul(pcb[0:P, g * P:(g + 1) * P], lhsT=bct[:, g, 0:P], rhs=bct[:, 2 + g, 0:P], start=True, stop=True),
                                  R=[bbct], W=bpcb)
                        kb.op("dve", lambda e, pcb=pcb: e.tensor_tensor(out=cbm[0:P, :, 0:P], in0=pcb[0:P, 0:2 * P].rearrange("p (g l) -> p g l", g=2),
                                                                       in1=self.trilef[0:P, 0:P].unsqueeze(1).to_broadcast([P, 2, P]), op=ALU.mult),
                              R=bpcb + [self.bconst], W=[bcbm])
                        kb.op("dve", lambda e: e.tensor_tensor(out=Rv, in0=self.trilef[0:P, 0:P].unsqueeze(1).to_broadcast([P, 16, P]),
                                                               in1=DTA.unsqueeze(2).to_broadcast([P, 16, P]), op=ALU.mult), R=[bd, self.bconst], W=[bR])
                        for half in range(2):
                            psg, bsg = self.pair()
                            for q4 in range(2):
                                h0 = half * 8 + q4 * 4
                                kb.op("pe", lambda e, q4=q4, h0=h0, psg=psg: e.matmul(psg[0:P, q4 * 4 * P:(q4 + 1) * 4 * P], lhsT=self.ugtb[0:P, 0:P],
                                                                                     rhs=Rf[0:P, h0 * P:(h0 + 4) * P], start=True, stop=True), R=[bR, self.bconst], W=bsg)
                            kb.op("act", lambda e, psg=psg: e.activation(out=exf[0:P, 0:8 * P], in_=psg[0:P, 0:8 * P], func=AF.Exp), R=bsg, W=[bex])
                            exv = exf[0:P, 0:8 * P].rearrange("p (h l) -> p h l", h=8)
                            kb.op("dve", lambda e, half=half, exv=exv: e.tensor_copy(out=WL[:, half * 8:(half + 1) * 8], in_=exv[:, :, P - 1]), R=[bex], W=[bd])
                            kb.op("dve", lambda e, half=half, exv=exv: e.tensor_tensor(out=scv[:, half * 8:(half + 1) * 8, :], in0=exv,
                                                                                      in1=cbm[0:P, half, 0:P].unsqueeze(1).to_broadcast([P, 8, P]), op=ALU.mult),
                                  R=[bex, bcbm], W=[bscf])
                        pa, bpa = self.bank()
                        kb.op("pe", lambda e, pa=pa: e.matmul(pa[0:P, 0:16], lhsT=self.trileb[0:P, 0:P], rhs=d16b[0:P, :], start=True, stop=True), R=[bd, self.bconst], W=bpa)
                        kb.op("pe", lambda e, pa=pa: e.matmul(pa[:, 16:32], lhsT=self.onesb[0:P, :], rhs=d16b[0:P, :], start=True, stop=True), R=[bd, self.bconst], W=bpa)
                        kb.op("act", lambda e, pa=pa: e.activation(out=EXPA, in_=pa[0:P, 0:16], func=AF.Exp), R=bpa, W=[bd])
                        kb.op("act", lambda e, pa=pa: e.activation(out=EL, in_=pa[:, 16:32], func=AF.Exp), R=bpa, W=[bd])
                        kb.op("dve", lambda e: e.tensor_tensor(out=DTW, in0=DT, in1=WL, op=ALU.mult), R=[bd], W=[bd])
                        xs3 = xst[0:P, :].rearrange("p (h q) -> p h q", h=16)
                        kb.op("dve", lambda e: e.tensor_tensor(out=xdt[0:P, :].rearrange("p (h q) -> p h q", h=16), in0=xs3, in1=DT.unsqueeze(2).to_broadcast([P, 16, 64]), op=ALU.mult),
                              R=[bxs, bd], W=[bxdt])
                        kb.op("pool", lambda e: e.tensor_tensor(out=xdtw[0:P, :].rearrange("p (h q) -> p h q", h=16), in0=xs3, in1=DTW.unsqueeze(2).to_broadcast([P, 16, 64]), op=ALU.mult),
                              R=[bxs, bd], W=[bxdtw])
                        yield
                        py, bpy = self.pair()
                        for h in range(16):
                            kb.op("pe", lambda e, h=h, py=py: e.matmul(py[0:P, h * 64:(h + 1) * 64], lhsT=scf[0:P, h * P:(h + 1) * P], rhs=xdt[0:P, h * 64:(h + 1) * 64], start=True, stop=True),
                                  R=[bscf, bxdt], W=bpy)
                        po, bpo = self.pair()
                        for g in range(2):
                            kb.op("pe", lambda e, g=g, po=po: e.matmul(po[0:P, g * 512:(g + 1) * 512], lhsT=bct[:, 2 + g, 0:P], rhs=hbf[:, g * 512:(g + 1) * 512], start=True, stop=True),
                                  R=[bbct, bhbf], W=bpo)
                        kb.op("dve", lambda e, po=po: e.tensor_tensor(out=t1[0:P, :].rearrange("p (h q) -> p h q", h=16), in0=po[0:P, :].rearrange("p (h q) -> p h q", h=16),
                                                                     in1=EXPA.unsqueeze(2).to_broadcast([P, 16, 64]), op=ALU.mult), R=bpo + [bd], W=[bt1])
                        kb.op("dve", lambda e, py=py: e.tensor_tensor(out=t1[0:P, :], in0=t1[0:P, :], in1=py[0:P, :], op=ALU.add), R=bpy + [bt1], W=[bt1])
                        kb.op("pool", lambda e: e.tensor_tensor(out=t3[0:P, :].rearrange("p (h q) -> p h q", h=16), in0=xs3, in1=D_b[0:P, :].unsqueeze(2).to_broadcast([P, 16, 64]), op=ALU.mult),
                              R=[bxs, bsm], W=[bt3])
                        kb.op("dve", lambda e: e.tensor_tensor(out=t1[0:P, :], in0=t1[0:P, :], in1=t3[0:P, :], op=ALU.add), R=[bt1, bt3], W=[bt1])
                        kb.op("dve", lambda e: e.tensor_tensor(out=t1[0:P, :], in0=t1[0:P, :], in1=szt[0:P, :], op=ALU.mult), R=[bt1, bsz], W=[bt1])
                        ssq, bs = self.srot()
                        for g in range(2):
                            kb.op("act", lambda e, g=g, ssq=ssq: e.activation(out=t3[0:P, g * 512:(g + 1) * 512], in_=t1[0:P, g * 512:(g + 1) * 512], func=AF.Square, accum_out=ssq[0:P, g:g + 1]),
                                  R=[bt1], W=[bt3, bs])
                        kb.op("act", lambda e, ssq=ssq: e.activation(out=ssq[0:P, 2:4], in_=ssq[0:P, 0:2], func=AF.Sqrt, scale=1.0 / 512, bias=self.epsb[0:P, 0:1]), R=[bs, self.bconst], W=[bs])
                        kb.op("dve", lambda e, ssq=ssq: e.reciprocal(out=ssq[0:P, 2:4], in_=ssq[0:P, 2:4]), R=[bs], W=[bs])
                        for g in range(2):
                            kb.op("dve", lambda e, g=g, ssq=ssq: e.scalar_tensor_tensor(out=ynb[0:P, g * 512:(g + 1) * 512], in0=t1[0:P, g * 512:(g + 1) * 512], scalar=ssq[0:P, 2 + g:3 + g],
                                                                                       in1=ngb[0:P, g * 512:(g + 1) * 512], op0=ALU.mult, op1=ALU.mult), R=[bt1, bs, bsm], W=[bynb])
                        self.transpose_to(ynb, bynb, P, 8, lambda k: ynT[:, k, 0:P], bynT)
                        kb.dma("pool", self.YT[si].rearrange("(c p) t -> p c t", p=128)[:, :, r0:r0 + P], ynT[:, :, 0:P], R=[bynT], W=[bsc["y"]])
                        ps2, bps2 = self.pair()
                        for g in range(2):
                            kb.op("pe", lambda e, g=g, ps2=ps2: e.matmul(ps2[:, g * 512:(g + 1) * 512], lhsT=btok[0:P, g * 128:(g + 1) * 128], rhs=xdtw[0:P, g * 512:(g + 1) * 512], start=True, stop=True),
                                  R=[bbtok, bxdtw], W=bps2)
                        kb.op("dve", lambda e: e.tensor_tensor(out=hst[:].rearrange("p (h q) -> p h q", h=16), in0=hst[:].rearrange("p (h q) -> p h q", h=16),
                                                               in1=EL.unsqueeze(2).to_broadcast([128, 16, 64]), op=ALU.mult), R=[bhst, bd], W=[bhst])
                        kb.op("dve", lambda e, ps2=ps2: e.tensor_tensor(out=hst[:], in0=hst[:], in1=ps2[:], op=ALU.add), R=bps2 + [bhst], W=[bhst])
                        kb.op("act", lambda e: e.activation(out=hbf[:], in_=hst[:], func=AF.Copy), R=[bhst], W=[bhbf])
                        if gi == ngr - 1 and m == nt - 1:
                            for n3 in range(3):
                                pb, bb = self.bank()
                                for k in range(8):
                                    kb.op("pe", lambda e, k=k, pb=pb, n3=n3: e.matmul(pb[0:P, :], lhsT=hT[:, k, t0:t0 + P], rhs=w[:, k, 1024 + n3 * 512:1024 + (n3 + 1) * 512],
                                                                                     start=(k == 0), stop=(k == 7)), R=[bhT, bw], W=bb)
                                self.evac(xraw[0:P, n3 * 512:(n3 + 1) * 512], pb[0:P, :], bb, [bxr])
                            dsto = self.sconv_p[j] if sq["s"] is None else self.sconv_s[j, sq["s"]]
                            kb.dma("pool", dsto, xraw[P - 3:P, :], R=[bxr], W=[self.bout])
                    gens = [tile_gen(m) for m in range(nt)]
                    next(gens[0])
                    for m in range(nt):
                        if m + 1 < nt:
                            next(gens[m + 1])
                        next(gens[m], None)
                    if gi < ngr - 1:
                        kb.op("pool", lambda e: e.tensor_copy(out=xbcT[:, :, 0:4], in_=xbcT[:, :, GN:GN + 4]), R=[bxb], W=[bxb])
                if E1B_STOP <= 6:
                    continue
                pp, bpp = self.pair()
                for c in range(8):
                    kb.op("pe", lambda e, c=c, pp=pp: e.transpose(out=pp[:, c * 128:(c + 1) * 128], in_=hst[:, c * 128:(c + 1) * 128], identity=self.identf[:, :]),
                          R=[bhst, self.bconst], W=bpp)
                so, bso = self.xrot()
                kb.op("act", lambda e, pp=pp, so=so: e.activation(out=so[:], in_=pp[:], func=AF.Copy), R=bpp, W=[bso])
                dsts = self.ssm_p[j] if sq["s"] is None else self.ssm_s[j, sq["s"]]
                kb.dma("pool", dsts.rearrange("(c p) n -> p c n", p=128), so[:].rearrange("p (c n) -> p c n", c=8), R=[bso], W=[self.bout])
            kb.barrier()

    def phase_e2(self, l, j, S):
        kb = self.kb
        PAST, TS, T = self.PAST, self.TS, self.T
        lam_init = 0.8 - 0.6 * math.exp(-0.3 * l)
        TKmax = max(T, PAST + TS)
        with ExitStack() as es:
            save_banks = self.bank_list
            self.bank_list = [4, 5, 6, 7]
            bsm = Buf("e2small")
            lamt = self.sb(es, "lamt", [128, 256], F32)
            kb.dma("sp", lamt[:], self.attn_lambda[j:j + 1, :].partition_broadcast(128), W=[bsm])
            lsm = self.sb(es, "lsm", [128, 8], F32)
            lpr = self.sb(es, "lpr", [128, 128], F32)
            kb.op("dve", lambda e: e.tensor_tensor(out=lpr[:, 0:64], in0=lamt[:, 0:64], in1=lamt[:, 64:128], op=ALU.mult), R=[bsm], W=[bsm])
            kb.op("dve", lambda e: e.tensor_tensor(out=lpr[:, 64:128], in0=lamt[:, 128:192], in1=lamt[:, 192:256], op=ALU.mult), R=[bsm], W=[bsm])
            kb.op("dve", lambda e: e.reduce_sum(out=lsm[:, 0:2], in_=lpr[:].rearrange("p (a b) -> p a b", a=2), axis=AX.X), R=[bsm], W=[bsm])
            kb.op("act", lambda e: e.activation(out=lsm[:, 2:4], in_=lsm[:, 0:2], func=AF.Exp), R=[bsm], W=[bsm])
            kb.op("dve", lambda e: e.scalar_tensor_tensor(out=lsm[:, 4:5], in0=lsm[:, 3:4], scalar=-lam_init, in1=lsm[:, 2:3], op0=ALU.add, op1=ALU.subtract),
                  R=[bsm], W=[bsm])
            neglam = lsm[:, 4:5]
            subg = self.sb(es, "subg", [128, 1, 1], F32)
            self.load_fm(es, lambda c: subg[:, c, :], self.attn_subln_g[j:j + 1, :], 1, 1, bsm)
            kb.op("dve", lambda e: e.tensor_scalar(out=subg[:, 0, :], in0=subg[:, 0, :], scalar1=(1.0 - lam_init), scalar2=None, op0=ALU.mult), R=[bsm], W=[bsm])
            NKT = (TKmax + 127) // 128
            kvset = []
            for i in range(2):
                kvset.append(dict(kT=[self.sb(es, "kT%d_%d" % (m, i), [69, TKmax], BF16) for m in range(2)], bkT=Buf(),
                                  vh=self.sb(es, "vh%d" % i, [128, NKT, 128], BF16), bvh=Buf(),
                                  corrb=self.sb(es, "corrb%d" % i, [128, 128], BF16), bcorr=Buf()))
            GNmax = S[0]["GN"]
            qTr = self.rot("qT", es, [69, 2, GNmax], BF16, 2)
            ptr = self.rot("pt", es, [128, GNmax], BF16, 4)
            accr = self.rot("accs", es, [128, 4, GNmax], F32, 2)
            ot = self.sb(es, "e2o", [128, GNmax], F32); bot = Buf()
            o1 = self.sb(es, "e2o1", [128, GNmax], F32); bo1 = Buf()
            rr = self.sb(es, "e2r", [128, GNmax], F32); brr = Buf()
            kb.op("pool", lambda e: e.memset(o1[:], 0.0), W=[bo1])
            onr = self.rot("e2on", es, [128, GNmax], BF16, 2)
            acc = [self.PP[0][:, 0:512], self.PP[0][:, 512:1024], self.PP[1][:, 0:512], self.PP[1][:, 512:1024]]
            bacc = [[self.pb[i]] for i in range(4)]
            for _ in range(2):
                qT, bqT = qTr()
                kb.op("pool", lambda e, qT=qT: e.memset(qT[:], 0.0), W=[bqT])

            def load_head(si, sq, h, ks):
                bsc = self.bscr[si]
                Tq = sq["T"]
                kp0 = 0 if sq["s"] is None else PAST
                Tk = kp0 + Tq
                kT, bkT, vh, bvh, corrb, bcorr = ks["kT"], ks["bkT"], ks["vh"], ks["bvh"], ks["corrb"], ks["bcorr"]
                for m in range(2):
                    kb.dma("sp", kT[m][0:64, 0:Tk], self.KT[si][h, m * 64:(m + 1) * 64, 0:Tk], R=[bsc["k"]], W=[bkT])
                for a in range(0, Tk, 1024):
                    wd = min(1024, Tk - a)
                    st, bst = self.wstage()
                    kb.dma("sp", st[64:69, 0:wd], self.cst_kaug[h, :, a:a + wd], W=[bst])
                    for m in range(2):
                        kb.op("pool", lambda e, m=m, st=st, a=a, wd=wd: e.tensor_copy(out=kT[m][64:69, a:a + wd], in_=st[64:69, 0:wd]), R=[bst], W=[bkT])
                nfull = Tk // 128
                kb.dma("sp", vh[:, 0:nfull, :], self.VB[si][0:nfull * 128, h * 128:(h + 1) * 128].rearrange("(j p) e -> p j e", p=128), R=[bsc["v"]], W=[bvh])
                if Tk % 128:
                    kb.dma("sp", vh[0:Tk % 128, nfull, :], self.VB[si][nfull * 128:Tk, h * 128:(h + 1) * 128], R=[bsc["v"]], W=[bvh])
                st, bst = self.wstage()
                kb.dma("sp", st[:, 0:128], self.cst_corr[h], W=[bst])
                kb.op("pool", lambda e, st=st: e.tensor_copy(out=corrb[:], in_=st[:, 0:128]), R=[bst], W=[bcorr])

            pending = [None]
            heads = [(si, sq, h) for si, sq in enumerate(S) for h in range(8)]
            load_head(heads[0][0], heads[0][1], heads[0][2], kvset[0])
            for hi, (si, sq, h) in enumerate(heads):
                ks = kvset[hi % 2]
                kT, bkT, vh, bvh, corrb, bcorr = ks["kT"], ks["bkT"], ks["vh"], ks["bvh"], ks["corrb"], ks["bcorr"]
                bsc = self.bscr[si]
                P, GN, Tq = sq["P"], sq["GN"], sq["T"]
                kp0 = 0 if sq["s"] is None else PAST
                Tk = kp0 + Tq
                for gidx, g0 in enumerate(range(0, Tq, GN)):
                    qT, bqT = qTr()
                    if GN < 128:
                        kb.op("pool", lambda e, qT=qT: e.memset(qT[:, :, GN:128], 0.0), W=[bqT])
                    for m in range(2):
                        kb.dma("sp", qT[0:64, m, 0:GN], self.QT[si][h, m * 64:(m + 1) * 64, g0:g0 + GN], R=[bsc["q"]], W=[bqT])
                    st, bst = self.wstage()
                    kb.dma("sp", st[64:69, 0:GN], self.cst_qaug[h, :, kp0 + g0:kp0 + g0 + GN], W=[bst])
                    for m in range(2):
                        kb.op("pool", lambda e, m=m, st=st, qT=qT: e.tensor_copy(out=qT[64:69, m, 0:GN], in_=st[64:69, 0:GN]), R=[bst], W=[bqT])
                    if gidx == 0 and hi + 1 < len(heads):
                        load_head(heads[hi + 1][0], heads[hi + 1][1], heads[hi + 1][2], kvset[(hi + 1) % 2])
                    GNq = max(GN, 128)
                    tiles = []
                    if sq["s"] is None:
                        i0, nt = g0 // 128, GN // 128
                        for jt in range(i0 + nt):
                            if jt < i0:
                                tiles.append((jt, 128, [(0, GN, False)]))
                            else:
                                c0 = (jt - i0) * 128
                                rg = [(c0, c0 + 128, True)]
                                if c0 + 128 < GN:
                                    rg.append((c0 + 128, GN, False))
                                tiles.append((jt, 128, rg))
                    else:
                        for jt in range(PAST // 128):
                            tiles.append((jt, 128, [(0, GNq, False)]))
                        tiles.append((PAST // 128, Tq, [(0, GNq, True)]))

                    def st_exp(ti):
                        jt, nk, rg = tiles[ti]
                        k0 = jt * 128
                        c0 = rg[0][0]
                        pts = []
                        for m in range(2):
                            ps, bps = self.bank()
                            for (a, b, isd) in rg:
                                kb.op("pe", lambda e, ps=ps, m=m, a=a, b=b, isd=isd: e.matmul(ps[0:nk, a:b], lhsT=kT[m][:, k0:k0 + nk], rhs=qT[:, m, a:b],
                                                                                         start=True, stop=(not isd)), R=[bkT, bqT], W=bps)
                                if isd:
                                    kb.op("pe", lambda e, ps=ps, a=a, b=b: e.matmul(ps[0:nk, a:b], lhsT=self.identb[0:nk, 0:nk], rhs=corrb[0:nk, 0:b - a],
                                                                                 start=False, stop=True), R=[bcorr, self.bconst], W=bps)
                            pt, bpt = ptr()
                            kb.op("act", lambda e, pt=pt, ps=ps: e.activation(out=pt[0:nk, c0:GNq], in_=ps[0:nk, c0:GNq], func=AF.Exp, scale=0.125), R=bps, W=[bpt])
                            pts.append((pt, bpt))
                        return pts

                    def pv(ti, pts):
                        jt, nk, rg = tiles[ti]
                        for m in range(2):
                            pt, bpt = pts[m]
                            for ri, (a, b, isd) in enumerate(rg):
                                first = (ti == 0 and ri == 0)
                                lastm = (ti == len(tiles) - 1 and ri == len(rg) - 1)
                                kb.op("pe", lambda e, pt=pt, m=m, a=a, b=b: e.matmul(acc[m][:, a:b], lhsT=vh[0:nk, jt, :], rhs=pt[0:nk, a:b],
                                                                                         start=first, stop=lastm), R=[bvh, bpt], W=bacc[m])
                                kb.op("pe", lambda e, pt=pt, m=m, a=a, b=b: e.matmul(acc[2 + m][:, a:b], lhsT=self.onesb[0:nk, :], rhs=pt[0:nk, a:b],
                                                                                         start=first, stop=lastm), R=[self.bconst, bpt], W=bacc[2 + m])

                    cur = st_exp(0)
                    for ti in range(len(tiles)):
                        nxt = st_exp(ti + 1) if ti + 1 < len(tiles) else None
                        pv(ti, cur)
                        cur = nxt
                        if ti == 2 and pending[0] is not None:
                            pending[0]()
                            pending[0] = None
                    if pending[0] is not None:
                        pending[0]()
                        pending[0] = None
                    ac, bac = accr()
                    for i in range(4):
                        if i < 2:
                            kb.op("act", lambda e, i=i, ac=ac: e.activation(out=ac[:, i, 0:GN], in_=acc[i][:, 0:GN], func=AF.Copy), R=bacc[i], W=[bac])
                        else:
                            kb.op("dve", lambda e, i=i, ac=ac: e.tensor_copy(out=ac[:, i, 0:GN], in_=acc[i][:, 0:GN]), R=bacc[i], W=[bac])
                    kb.op("dve", lambda e, ac=ac: e.reciprocal(out=rr[:, 0:GN], in_=ac[:, 2, 0:GN]), R=[bac], W=[brr])
                    kb.op("dve", lambda e, ac=ac: e.tensor_tensor(out=ot[:, 0:GN], in0=ac[:, 0, 0:GN], in1=rr[:, 0:GN], op=ALU.mult), R=[bac, brr], W=[bot])
                    kb.op("dve", lambda e, ac=ac: e.reciprocal(out=rr[:, 0:GN], in_=ac[:, 3, 0:GN]), R=[bac], W=[brr])
                    kb.op("dve", lambda e, ac=ac: e.tensor_tensor(out=o1[:, 0:GN], in0=ac[:, 1, 0:GN], in1=rr[:, 0:GN], op=ALU.mult), R=[bac, brr], W=[bo1])
                    kb.op("dve", lambda e: e.scalar_tensor_tensor(out=ot[:, 0:GN], in0=o1[:, 0:GN], scalar=neglam, in1=ot[:, 0:GN], op0=ALU.mult, op1=ALU.add),
                          R=[bo1, bot, bsm], W=[bot])
                    kb.op("pool", lambda e: e.tensor_tensor(out=o1[:, 0:GN], in0=ot[:, 0:GN], in1=ot[:, 0:GN], op=ALU.mult), R=[bot], W=[bo1])

                    def part_b(si=si, h=h, g0=g0, GN=GN, GNq=GNq, bsc=bsc):
                        pm, bm = self.bank()
                        kb.op("pe", lambda e, pm=pm: e.matmul(pm[:, 0:GNq], lhsT=self.m128[:], rhs=o1[:, 0:GNq], start=True, stop=True), R=[bo1, self.bconst], W=bm)
                        kb.op("act", lambda e, pm=pm: e.activation(out=rr[:, 0:GN], in_=pm[:, 0:GN], func=AF.Ln, bias=self.epsb[:, 0:1]), R=bm + [self.bconst], W=[brr])
                        kb.op("act", lambda e: e.activation(out=rr[:, 0:GN], in_=rr[:, 0:GN], func=AF.Exp, scale=-0.5), R=[brr], W=[brr])
                        on, bon = onr()
                        kb.op("dve", lambda e, on=on: e.scalar_tensor_tensor(out=on[:, 0:GN], in0=ot[:, 0:GN], scalar=subg[:, 0, :], in1=rr[:, 0:GN], op0=ALU.mult, op1=ALU.mult),
                              R=[bot, brr, bsm], W=[bon])
                        kb.dma("pool", self.OT[si][h * 128:(h + 1) * 128, g0:g0 + GN], on[:, 0:GN], R=[bon], W=[bsc["o"]])
                    pending[0] = part_b
            if pending[0] is not None:
                pending[0]()
                pending[0] = None
            self.bank_list = save_banks
            kb.barrier()

    def phase_e3(self, l, j, S):
        kb = self.kb
        stage = 2 * l
        with ExitStack() as es:
            wo = self.sb(es, "wo", [128, 16, D], BF16); bwo = Buf()
            self.load_w(wo, self.hyb_w_out[j].rearrange("(k p) n -> k p n", p=128), 16, D, bwo)
            GNmax = S[0]["GN"]
            oyr = self.rot("oy", es, [128, 16, GNmax], BF16, 2)
            for si, sq in enumerate(S):
                gs, sh, gg, bbc = self.make_bc(l, 0, sq["ci"])
                src, dst = self.xio(sq, stage)
                bx = self.bscr[si]["x"]
                bsc = self.bscr[si]
                P, GN, Tq = sq["P"], sq["GN"], sq["T"]
                for g0 in range(0, Tq, GN):
                    nt = GN // P
                    t, bt = oyr()
                    kb.dma("sp", t[:, 0:8, 0:GN], self.OT[si].rearrange("(c p) t -> p c t", p=128)[:, :, g0:g0 + GN], R=[bsc["o"]], W=[bt])
                    kb.dma("sp", t[:, 8:16, 0:GN], self.YT[si].rearrange("(c p) t -> p c t", p=128)[:, :, g0:g0 + GN], R=[bsc["y"]], W=[bt])
                    for m in range(nt):
                        pp, bpp = self.pair()
                        for nh in range(2):
                            for c in range(16):
                                kb.op("pe", lambda e, c=c, nh=nh, m=m, pp=pp, t=t: e.matmul(pp[0:P, nh * 512:(nh + 1) * 512], lhsT=t[:, c, m * P:(m + 1) * P],
                                                                                          rhs=wo[:, c, nh * 512:(nh + 1) * 512], start=(c == 0), stop=(c == 15)),
                                      R=[bt, bwo], W=bpp)
                        r0 = g0 + m * P
                        self.post(pp, bpp, P, src[r0:r0 + P, :], dst[r0:r0 + P, :], gg, bbc, bx)
            kb.barrier()

    def phase_conf(self, l, j, S):
        kb = self.kb
        stage = 2 * l
        with ExitStack() as es:
            win = self.sb(es, "cwin", [128, 8, 2 * D], BF16); bwin = Buf()
            wout = self.sb(es, "cwout", [128, 8, D], BF16); bwout = Buf()
            self.load_w(win, self.conf_w_in[j].rearrange("(k p) n -> k p n", p=128), 8, 2 * D, bwin)
            self.load_w(wout, self.conf_w_out[j].rearrange("(k p) n -> k p n", p=128), 8, D, bwout)
            bsm = Buf("confsmall")
            dwf = self.sb(es, "dwf", [128, 8, 31], F32)
            self.load_fm(es, lambda c: dwf[:, c, :], self.conf_dw_w[j], 31, 8, bsm)
            bin_ = self.sb(es, "binf", [128, 16, 1], F32)
            self.load_fm(es, lambda c: bin_[:, c, :], self.conf_b_in[j:j + 1, :], 1, 16, bsm)
            vecs = self.sb(es, "cvecs", [128, 3, 8, 1], F32)
            for i, src in enumerate((self.conf_dw_b, self.conf_ln_g, self.conf_ln_b)):
                self.load_fm(es, lambda c, i=i: vecs[:, i, c, :], src[j:j + 1, :], 1, 8, bsm)
            diag = self.sb(es, "cdiag", [128, 8 * 31, 128], BF16)
            kb.op("dve", lambda e: e.tensor_tensor(out=diag[:], in0=self.identf.unsqueeze(1).to_broadcast([128, 248, 128]),
                                                   in1=dwf[:].rearrange("p c k -> p (c k)").unsqueeze(2).to_broadcast([128, 248, 128]),
                                                   op=ALU.mult), R=[bsm, self.bconst], W=[bsm])
            bor = self.sb(es, "bor", [1, D], F32)
            kb.dma("sp", bor[:], self.conf_b_out[j:j + 1, :], W=[bsm])
            borb = self.sb(es, "borb", [1, D], BF16)
            kb.op("pool", lambda e: e.tensor_copy(out=borb[:], in_=bor[:]), R=[bsm], W=[bsm])
            if CONF_STOP <= 1:
                kb.barrier()
                return
            GNmax = S[0]["GN"]
            hT = self.sb(es, "chT", [128, 8, GNmax], BF16); bhT = Buf()
            uT = self.sb(es, "uT", [128, 8, 30 + GNmax], BF16); buT = Buf()
            uF = self.sb(es, "uF", [128, 8, 32], F32); buF = Buf()
            yT = self.sb(es, "yT", [128, 8, GNmax], F32); byT = Buf()
            ynT = self.sb(es, "ynT", [128, 8, GNmax], BF16); bynT = Buf()
            sgr = self.rot("csg", es, [128, GNmax], F32, 1)
            mu = self.sb(es, "cmu", [128, GNmax], F32); bmu = Buf()
            rs = self.sb(es, "crs", [128, GNmax], F32); brs = Buf()
            kb.op("pool", lambda e: e.memset(hT[:], 0.0), W=[bhT])
            kb.op("pool", lambda e: e.memset(uT[:], 0.0), W=[buT])
            kb.op("pool", lambda e: e.memset(yT[:], 0.0), W=[byT])
            kb.op("pool", lambda e: e.memset(ynT[:], 0.0), W=[bynT])
            for si, sq in enumerate(S):
                if CONF_SEQS is not None and si not in CONF_SEQS:
                    continue
                gs, sh, gg, bbc = self.make_bc(l, 0, sq["ci"])
                src, dst = self.xio(sq, stage)
                bx = self.bscr[si]["x"]
                P, GN, Tq = sq["P"], sq["GN"], sq["T"]
                GNp = max(GN, 128)
                if sq["s"] is None:
                    kb.op("pool", lambda e: e.memset(uT[:, :, 0:30], 0.0), W=[buT])
                else:
                    self.load_fm(es, lambda c: uT[:, c, 0:30], self.st_cconv[j, sq["s"]], 30, 8, buT)
                ngr = Tq // GN
                nt = GN // P
                nl = min(30, GN)

                def stage_a(gi):
                    g0 = gi * GN
                    last = gi == ngr - 1
                    for m in range(nt):
                        self.prelude_x(src[g0 + m * P:g0 + (m + 1) * P, :], bx, P, hT, bhT, m * P, gs, sh, bbc)
                    nl = min(30, GN)
                    for c in range(8):
                        pa, ba = self.bank()
                        pg, bg = self.bank()
                        for k in range(8):
                            kb.op("pe", lambda e, k=k, pa=pa, c=c: e.matmul(pa[:, 0:GNp], lhsT=win[:, k, c * 128:(c + 1) * 128], rhs=hT[:, k, 0:GNp],
                                                                           start=(k == 0), stop=(k == 7)), R=[bwin, bhT], W=ba)
                        for k in range(8):
                            kb.op("pe", lambda e, k=k, pg=pg, c=c: e.matmul(pg[:, 0:GNp], lhsT=win[:, k, D + c * 128:D + (c + 1) * 128], rhs=hT[:, k, 0:GNp],
                                                                           start=(k == 0), stop=(k == 7)), R=[bwin, bhT], W=bg)
                        sg, bsg = sgr()
                        kb.op("act", lambda e, sg=sg, pg=pg, c=c: e.activation(out=sg[:, 0:GN], in_=pg[:, 0:GN], func=AF.Sigmoid, bias=bin_[:, 8 + c, :]),
                              R=bg + [bsm], W=[bsg])
                        kb.op("dve", lambda e, sg=sg, pa=pa, c=c: e.scalar_tensor_tensor(out=uT[:, c, 30:30 + GN], in0=pa[:, 0:GN], scalar=bin_[:, c, :],
                                                                                        in1=sg[:, 0:GN], op0=ALU.add, op1=ALU.mult),
                              R=ba + [bsg, bsm], W=[buT])
                        if last:
                            kb.op("dve", lambda e, sg=sg, pa=pa, c=c: e.scalar_tensor_tensor(out=uF[:, c, 0:nl], in0=pa[:, GN - nl:GN], scalar=bin_[:, c, :],
                                                                                            in1=sg[:, GN - nl:GN], op0=ALU.add, op1=ALU.mult),
                                  R=ba + [bsg, bsm], W=[buF])

                def stage_conv(gi):
                    for c in range(8):
                        py, by_ = self.bank()
                        for k in range(31):
                            kb.op("pe", lambda e, k=k, py=py, c=c: e.matmul(py[:, 0:GNp], lhsT=diag[:, c * 31 + k, :], rhs=uT[:, c, k:k + GNp],
                                                                           start=(k == 0), stop=(k == 30)), R=[bsm, buT], W=by_)
                        kb.op("act", lambda e, py=py, c=c: e.activation(out=yT[:, c, 0:GN], in_=py[:, 0:GN], func=AF.Identity, bias=vecs[:, 0, c, :]),
                              R=by_ + [bsm], W=[byT])

                def stage_c(gi):
                    g0 = gi * GN
                    pm, bm = self.bank()
                    for c in range(8):
                        kb.op("pe", lambda e, c=c, pm=pm: e.matmul(pm[:, 0:GNp], lhsT=self.m1024[:], rhs=yT[:, c, 0:GNp], start=(c == 0), stop=(c == 7)),
                              R=[byT, self.bconst], W=bm)
                    kb.op("act", lambda e, pm=pm: e.activation(out=mu[:, 0:GN], in_=pm[:, 0:GN], func=AF.Copy), R=bm, W=[bmu])
                    kb.op("dve", lambda e: e.tensor_tensor(out=yT[:, :, 0:GN], in0=yT[:, :, 0:GN], in1=mu[:, 0:GN].unsqueeze(1).to_broadcast([128, 8, GN]),
                                                           op=ALU.subtract), R=[byT, bmu], W=[byT])
                    kb.op("pool", lambda e: e.tensor_tensor(out=ynT[:, :, 0:GN], in0=yT[:, :, 0:GN], in1=yT[:, :, 0:GN], op=ALU.mult), R=[byT], W=[bynT])
                    pv, bv = self.bank()
                    for c in range(8):
                        kb.op("pe", lambda e, c=c, pv=pv: e.matmul(pv[:, 0:GNp], lhsT=self.onesb[:, :], rhs=ynT[:, c, 0:GNp], start=(c == 0), stop=(c == 7)),
                              R=[bynT, self.bconst], W=bv)
                    kb.op("act", lambda e, pv=pv: e.activation(out=rs[:, 0:GN], in_=pv[:, 0:GN], func=AF.Sqrt, scale=1.0 / 1024, bias=self.epsb[:, 0:1]), R=bv + [self.bconst], W=[brs])
                    kb.op("dve", lambda e: e.reciprocal(out=rs[:, 0:GN], in_=rs[:, 0:GN]), R=[brs], W=[brs])
                    kb.op("dve", lambda e: e.tensor_tensor(out=yT[:, :, 0:GN], in0=yT[:, :, 0:GN], in1=rs[:, 0:GN].unsqueeze(1).to_broadcast([128, 8, GN]),
                                                           op=ALU.mult), R=[byT, brs], W=[byT])
                    for c in range(8):
                        kb.op("act", lambda e, c=c: e.activation(out=ynT[:, c, 0:GN], in_=yT[:, c, 0:GN], func=AF.Silu, scale=vecs[:, 1, c, :], bias=vecs[:, 2, c, :]),
                              R=[byT, bsm], W=[bynT])
                    for m in range(nt):
                        pp, bpp = self.pair()
                        for nh in range(2):
                            for c in range(8):
                                kb.op("pe", lambda e, c=c, nh=nh, m=m, pp=pp: e.matmul(pp[0:P, nh * 512:(nh + 1) * 512], lhsT=ynT[:, c, m * P:(m + 1) * P],
                                                                                     rhs=wout[:, c, nh * 512:(nh + 1) * 512], start=(c == 0), stop=False),
                                      R=[bynT, bwout], W=bpp)
                            kb.op("pe", lambda e, nh=nh, pp=pp: e.matmul(pp[0:P, nh * 512:(nh + 1) * 512], lhsT=self.onesb[0:1, 0:P],
                                                                        rhs=borb[0:1, nh * 512:(nh + 1) * 512], start=False, stop=True),
                                  R=[bsm, self.bconst], W=bpp)
                        r0 = g0 + m * P
                        self.post(pp, bpp, P, src[r0:r0 + P, :], dst[r0:r0 + P, :], gg, bbc, bx)

                stage_a(0)
                for gi in range(ngr):
                    stage_conv(gi)
                    if gi + 1 < ngr:
                        kb.op("pool", lambda e: e.tensor_copy(out=uT[:, :, 0:30], in_=uT[:, :, GN:GN + 30]), R=[buT], W=[buT])
                        stage_a(gi + 1)
                    stage_c(gi)
                if CONF_STOP <= 5:
                    continue
                nl = min(30, GN)
                pp, bpp = self.pair()
                for c in range(8):
                    kb.op("pe", lambda e, c=c, pp=pp: e.transpose(out=pp[0:nl, c * 128:(c + 1) * 128], in_=uF[:, c, 0:nl], identity=self.identf[:, :]),
                          R=[buF, self.bconst], W=bpp)
                ot, bo = self.xrot()
                kb.op("act", lambda e, pp=pp, ot=ot: e.activation(out=ot[0:nl, :], in_=pp[0:nl, :], func=AF.Copy), R=bpp, W=[bo])
                if sq["s"] is None:
                    kb.dma("pool", self.cconv_p[j], ot[0:30, :], R=[bo], W=[self.bout])
                else:
                    s_ = sq["s"]
                    kb.dma("pool", self.cconv_s[j, s_, 30 - nl:30, :], ot[0:nl, :], R=[bo], W=[self.bout])
                    if nl < 30:
                        kb.dma("pool", self.cconv_s[j, s_, 0:30 - nl, :], self.st_cconv[j, s_, nl:30, :], W=[self.bout])
            kb.barrier()


_PROG = {}


def _get_prog(T, depth):
    key = (T, depth)
    if key not in _PROG:
        _PROG[key] = Prog(T=T, depth=depth)
    return _PROG[key]


def make_in_maps(inp, T, depth):
    NE, NO = (depth + 1) // 2, depth // 2
    cf, kaug, qaug, corr = _consts()
    f = lambda a: np.ascontiguousarray(np.asarray(a, dtype=np.float32))
    shared = {}
    for nm in ("ada_w", "ada_b", "norm_g", "ffn_w_up", "ffn_w_down", "hyb_w_in", "attn_subln_g", "ssm_conv_w", "ssm_conv_b",
               "ssm_dt_bias", "ssm_a_log", "ssm_d", "ssm_norm_g", "hyb_w_out", "conf_w_in", "conf_b_in", "conf_dw_w",
               "conf_dw_b", "conf_ln_g", "conf_ln_b", "conf_w_out", "conf_b_out"):
        shared[nm] = f(inp[nm])
    shared["attn_lambda"] = f(inp["attn_lambda"]).reshape(NE, 256)
    shared["cst_f"], shared["cst_kaug"], shared["cst_qaug"], shared["cst_corr"] = cf, kaug, qaug, corr
    maps = []
    for c in range(NCORES):
        m = dict(shared)
        m["x_p"] = f(inp["x_prompt"][c])
        m["x_s"] = f(inp["x_sample"][2 * c:2 * c + 2])
        m["c_all"] = f(np.concatenate([inp["c_prompt"][c:c + 1], inp["c_sample"][2 * c:2 * c + 2]], 0))
        m["cache_k"] = f(np.asarray(inp["cache_attn_k"])[:, 2 * c:2 * c + 2].reshape(NE, 2, -1, D))
        m["cache_v"] = f(np.asarray(inp["cache_attn_v"])[:, 2 * c:2 * c + 2].reshape(NE, 2, -1, D))
        m["st_sconv"] = f(np.asarray(inp["state_ssm_conv"])[:, 2 * c:2 * c + 2])
        m["st_ssm"] = f(np.asarray(inp["state_ssm"])[:, 2 * c:2 * c + 2].reshape(NE, 2, 1024, 128))
        m["st_cconv"] = f(np.asarray(inp["state_conf_conv"])[:, 2 * c:2 * c + 2])
        maps.append(m)
    return maps


def gather(res, T, depth, TS=16):
    NE, NO = (depth + 1) // 2, depth // 2
    R = res.results
    st = lambda k, ax=0: np.stack([np.asarray(r[k]) for r in R], ax)
    cat = lambda k, ax: np.concatenate([np.asarray(r[k]) for r in R], ax)
    y_p = st("y_p")
    y_s = cat("y_s", 0)
    k_p = st("k_p", 1).reshape(NE, NCORES, T, 8, 128)
    v_p = st("v_p", 1).reshape(NE, NCORES, T, 8, 128)
    sconv_p = st("sconv_p", 1)
    ssm_p = st("ssm_p", 1).reshape(NE, NCORES, 16, 64, 128)
    cconv_p = st("cconv_p", 1)
    k_s = cat("k_s", 1).reshape(NE, 2 * NCORES, TS, 8, 128)
    v_s = cat("v_s", 1).reshape(NE, 2 * NCORES, TS, 8, 128)
    sconv_s = cat("sconv_s", 1)
    ssm_s = cat("ssm_s", 1).reshape(NE, 2 * NCORES, 16, 64, 128)
    cconv_s = cat("cconv_s", 1)
    return (y_p, y_s, k_p, v_p, sconv_p, ssm_p, cconv_p, k_s, v_s, sconv_s, ssm_s, cconv_s)


def kernel(**inputs):
    T = int(np.asarray(inputs["x_prompt"]).shape[1])
    depth = int(np.asarray(inputs["ada_w"]).shape[0])
    prog = _get_prog(T, depth)
    maps = make_in_maps(inputs, T, depth)
    res = run_bass_kernel_spmd(prog.nc, maps, core_ids=list(range(NCORES)))
    outs = gather(res, T, depth)
    return tuple(np.ascontiguousarray(o, dtype=np.float32) for o in outs)
```

```python
import math
import numpy as np
import concourse.bass as bass
import concourse.mybir as mybir
from concourse.bass_utils import run_bass_kernel_spmd
from contextlib import ExitStack

F32 = mybir.dt.float32
BF16 = mybir.dt.bfloat16
AF = mybir.ActivationFunctionType
ALU = mybir.AluOpType
AX = mybir.AxisListType

D = 1024
DFF = 2816
DIN = 5648
EPS = 1e-6
NCORES = 8
SKIP = set()
DEBUG = False
CONF_STOP = 99
E1B_STOP = 99
CONF_SEQS = None


class Buf:
    __slots__ = ("name", "w", "r", "excl")

    def __init__(self, name="", excl=False):
        self.name = name
        self.w = None
        self.r = {}
        self.excl = excl


class KB:
    def __init__(self, nc, es, n_lanes=8):
        self.nc = nc
        self.eng = {"pe": nc.tensor, "act": nc.scalar, "dve": nc.vector, "pool": nc.gpsimd, "sp": nc.sync}
        self.sem, self.cnt, self.mult = {}, {}, {}
        for e in self.eng:
            self.sem[e] = es.enter_context(nc.semaphore("s_" + e))
            self.cnt[e] = 0
            self.mult[e] = 1
        self.lanes = {}
        for q in ("sp", "pool"):
            ls = []
            for i in range(n_lanes):
                key = "L%s%d" % (q, i)
                self.sem[key] = es.enter_context(nc.semaphore("s_" + key))
                self.cnt[key] = 0
                self.mult[key] = 16
                ls.append(key)
            self.lanes[q] = ls
        self.lane_rr = {q: 0 for q in self.lanes}
        self.known = {e: {} for e in self.eng}
        self.nins = 0
        self.nwait = 0

    def _need(self, deps, key, k):
        if deps.get(key, 0) < k:
            deps[key] = k

    def _collect(self, R, W, eng=None):
        deps = {}
        for b in R:
            if b.w is not None:
                self._need(deps, b.w[0], b.w[1])
            if b.excl:
                for e, k in b.r.items():
                    if e != eng:
                        self._need(deps, e, k)
        for b in W:
            if b.w is not None:
                self._need(deps, b.w[0], b.w[1])
            for e, k in b.r.items():
                self._need(deps, e, k)
        return deps

    def _emit_waits(self, e, deps):
        kn = self.known[e]
        for key, k in deps.items():
            if key == e and e == "pe":
                continue
            if kn.get(key, 0) >= k:
                continue
            self.eng[e].wait_ge(self.sem[key], k * self.mult[key])
            kn[key] = k
            self.nwait += 1

    def _mark(self, key, k, R, W):
        for b in R:
            if b.r.get(key, 0) < k:
                b.r[key] = k
        for b in W:
            b.w = (key, k)
            b.r = {}

    def op(self, e, fn, R=(), W=()):
        self._emit_waits(e, self._collect(R, W, e))
        ins = fn(self.eng[e])
        self.cnt[e] += 1
        ins.then_inc(self.sem[e], 1)
        self._mark(e, self.cnt[e], R, W)
        self.nins += 1

    def dma(self, q, out, in_, R=(), W=()):
        ls = self.lanes[q]
        lane = ls[self.lane_rr[q] % len(ls)]
        self.lane_rr[q] += 1
        deps = self._collect(R, W)
        if self.cnt[lane] > 0:
            self._need(deps, lane, self.cnt[lane])
        self._emit_waits(q, deps)
        ins = self.eng[q].dma_start(out=out, in_=in_)
        self.cnt[lane] += 1
        ins.then_inc(self.sem[lane], 16)
        self._mark(lane, self.cnt[lane], R, W)
        self.nins += 1

    def barrier(self):
        for e in self.eng:
            deps = {k2: self.cnt[k2] for k2 in self.cnt if self.cnt[k2] > 0 and k2 != e}
            self._emit_waits(e, deps)


def _consts():
    c = {}
    i = np.arange(128)
    c["ident"] = np.eye(128, dtype=np.float32)
    c["ones"] = np.ones((128, 128), np.float32)
    c["trile"] = (i[:, None] <= i[None, :]).astype(np.float32)
    c["ugt"] = (i[:, None] > i[None, :]).astype(np.float32)
    cf = np.stack([c["ident"], c["ones"], c["trile"], c["ugt"]], 0)
    pos = np.arange(8192)
    kaug = np.zeros((8, 5, 8192), np.float32)
    qaug = np.zeros((8, 5, 8192), np.float32)
    corr = np.zeros((8, 128, 128), np.float32)
    for h in range(8):
        s8 = 8.0 * 2.0 ** (-(h + 1))
        ph, pl = (pos // 128).astype(np.float32), (pos % 128).astype(np.float32)
        qaug[h, 0] = -s8 * 128.0 * ph
        qaug[h, 1] = -s8 * pl
        qaug[h, 2] = 0.0
        qaug[h, 3] = 1.0
        qaug[h, 4] = 1.0
        kaug[h, 0] = 1.0
        kaug[h, 1] = 1.0
        kaug[h, 2] = 1.0
        kaug[h, 3] = s8 * 128.0 * ph
        kaug[h, 4] = s8 * pl
        kk, qq = i[:, None], i[None, :]
        cm = np.where(kk > qq, -2.0 * s8 * (kk - qq), 0.0)
        cm = np.where((kk // 64) > (qq // 64), -240000.0, cm)
        corr[h] = cm
    return cf, kaug, qaug, corr


class Prog:
    def __init__(self, T=8192, depth=4, TS=16, PAST=1024):
        self.T, self.depth, self.TS, self.PAST = T, depth, TS, PAST
        self.NE = (depth + 1) // 2
        self.NO = depth // 2
        self.nc = bass.Bass("TRN2", target_bir_lowering=False)
        self.build()

    def din(self, name, shape, dt=F32):
        return self.nc.dram_tensor(name, list(shape), dt, kind="ExternalInput").ap()

    def dout(self, name, shape, dt=F32):
        return self.nc.dram_tensor(name, list(shape), dt, kind="ExternalOutput").ap()

    def dscr(self, name, shape, dt=F32):
        return self.nc.dram_tensor(name, list(shape), dt, kind="Internal").ap()

    def sb(self, es, name, shape, dt):
        self._uid += 1
        return es.enter_context(self.nc.sbuf_tensor("%s_%d" % (name, self._uid), list(shape), dt))

    def bank(self):
        i = self.bank_list[self.bank_rr % len(self.bank_list)]
        self.bank_rr += 1
        return self.PP[i // 2][:, (i % 2) * 512:(i % 2) * 512 + 512], [self.pb[i]]

    def pair(self):
        i = self.pair_list[self.pair_rr % len(self.pair_list)]
        self.pair_rr += 1
        return self.PP[i], [self.pb[2 * i], self.pb[2 * i + 1]]

    def rot(self, key, es, shape, dt, n=2):
        tiles = [(self.sb(es, key, shape, dt), Buf(key)) for _ in range(n)]
        st = {"i": 0}

        def nxt():
            t = tiles[st["i"] % n]
            st["i"] += 1
            return t
        return nxt

    def load_fm(self, es, dst_fn, src, R, nch, bdst, q="sp", pad=0):
        kb = self.kb
        Rp = R + pad
        for b0 in range(0, nch, 8):
            nb = min(8, nch - b0)
            st, bst = self.wstage()
            if pad:
                kb.op("pool", lambda e, st=st: e.memset(st[0:Rp, :], 0.0), W=[bst])
            kb.dma(q, st[pad:Rp, 0:nb * 128], src[:, b0 * 128:(b0 + nb) * 128], W=[bst])
            for c0 in range(0, nb, 4):
                pb, bb = self.bank()
                n4 = min(4, nb - c0)
                for c in range(c0, c0 + n4):
                    kb.op("pe", lambda e, c=c, pb=pb, st=st: e.transpose(out=pb[:, (c - c0) * 32:(c - c0) * 32 + Rp],
                                                            in_=st[0:Rp, c * 128:(c + 1) * 128],
                                                            identity=self.identf[0:Rp, 0:Rp]), R=[bst, self.bconst], W=bb)
                for c in range(c0, c0 + n4):
                    kb.op("act", lambda e, c=c, pb=pb: e.activation(out=dst_fn(b0 + c), in_=pb[:, (c - c0) * 32:(c - c0) * 32 + Rp],
                                                             func=AF.Copy), R=bb, W=[bdst])

    def load_w(self, dst, src3, nk, ncols, bdst, c0=0):
        kb = self.kb
        for k in range(nk):
            for a in range(0, ncols, 1024):
                w = min(1024, ncols - a)
                st, bst = self.wstage()
                kb.dma("sp", st[:, 0:w], src3[k, :, c0 + a:c0 + a + w], W=[bst])
                kb.op("pool", lambda e, st=st, k=k, a=a, w=w: e.tensor_copy(out=dst[:, k, a:a + w], in_=st[:, 0:w]),
                      R=[bst], W=[bdst])

    def prelude(self, xsrc, P, hT, bhT, col0, gs, sh, bbc):
        kb = self.kb
        xt, bx = self.xrot()
        kb.dma("sp", xt[0:P, :], xsrc, W=[bx])
        junk, bj = self.frot()
        ssq, bs = self.srot()
        kb.op("act", lambda e: e.activation(out=junk[0:P, :], in_=xt[0:P, :], func=AF.Square, accum_out=ssq[0:P, 0:1]),
              R=[bx], W=[bj, bs])
        kb.op("act", lambda e: e.activation(out=ssq[0:P, 1:2], in_=ssq[0:P, 0:1], func=AF.Sqrt, scale=1.0 / D, bias=self.epsb[0:P, 0:1]),
              R=[bs, self.bconst], W=[bs])
        kb.op("dve", lambda e: e.reciprocal(out=ssq[0:P, 1:2], in_=ssq[0:P, 1:2]), R=[bs], W=[bs])
        kb.op("dve", lambda e: e.scalar_tensor_tensor(out=junk[0:P, :], in0=xt[0:P, :], scalar=ssq[0:P, 1:2], in1=gs[0:P, :],
                                                       op0=ALU.mult, op1=ALU.mult), R=[bx, bs, bbc], W=[bj])
        hb, bh = self.hrot()
        kb.op("pool", lambda e: e.tensor_tensor(out=hb[0:P, :], in0=junk[0:P, :], in1=sh[0:P, :], op=ALU.add),
              R=[bj, bbc], W=[bh])
        self.transpose_to(hb, bh, P, 8, lambda k: hT[:, k, col0:col0 + P], bhT)

    def transpose_to(self, src, bsrc, P, nch, dst_fn, bdst):
        kb = self.kb
        for k0 in range(0, nch, 8):
            pb, bb = self.bank()
            pbb = pb.bitcast(BF16)
            n8 = min(8, nch - k0)
            for k in range(k0, k0 + n8):
                kb.op("pe", lambda e, k=k: e.transpose(out=pbb[:, (k - k0) * 128:(k - k0) * 128 + P],
                                                        in_=src[0:P, k * 128:(k + 1) * 128],
                                                        identity=self.identb[0:P, 0:P]), R=[bsrc, self.bconst], W=bb)
            for k in range(k0, k0 + n8):
                kb.op("act", lambda e, k=k: e.activation(out=dst_fn(k), in_=pbb[:, (k - k0) * 128:(k - k0) * 128 + P],
                                                         func=AF.Copy), R=bb, W=[bdst])

    def post(self, pp, bpp, P, xsrc, xdst, gg, bbc, bdram):
        kb = self.kb
        junk, bj = self.frot()
        ssq, bs = self.srot()
        kb.op("act", lambda e: e.activation(out=junk[0:P, :], in_=pp[0:P, :], func=AF.Square, accum_out=ssq[0:P, 0:1]),
              R=bpp, W=[bj, bs])
        kb.op("act", lambda e: e.activation(out=ssq[0:P, 1:2], in_=ssq[0:P, 0:1], func=AF.Sqrt, scale=1.0 / D, bias=self.epsb[0:P, 0:1]),
              R=[bs, self.bconst], W=[bs])
        kb.op("dve", lambda e: e.reciprocal(out=ssq[0:P, 1:2], in_=ssq[0:P, 1:2]), R=[bs], W=[bs])
        kb.op("dve", lambda e: e.scalar_tensor_tensor(out=junk[0:P, :], in0=pp[0:P, :], scalar=ssq[0:P, 1:2], in1=gg[0:P, :],
                                                       op0=ALU.mult, op1=ALU.mult), R=bpp + [bs, bbc], W=[bj])
        xt, bx = self.xrot()
        kb.dma("sp", xt[0:P, :], xsrc, R=[bdram], W=[bx])
        kb.op("dve", lambda e: e.tensor_tensor(out=xt[0:P, :], in0=xt[0:P, :], in1=junk[0:P, :], op=ALU.add),
              R=[bx, bj], W=[bx])
        kb.dma("pool", xdst, xt[0:P, :], R=[bx], W=[bdram])

    def make_bc(self, l, sub, ci):
        kb = self.kb
        gs, sh, gg = self.bc_gs, self.bc_sh, self.bc_gg
        bbc = self.bbc
        tmp, btmp = self.wstage()
        tmp2, btmp2 = self.wstage()
        base = sub * 3 * D
        mrow = lambda a: self.MOD[l, ci:ci + 1, base + a * D: base + (a + 1) * D].partition_broadcast(128)
        grow = lambda i: self.norm_g[l, i:i + 1, :].partition_broadcast(128)
        kb.dma("sp", sh[:], mrow(0), R=[self.bmod], W=[bbc])
        kb.dma("sp", gs[:], mrow(1), R=[self.bmod], W=[bbc])
        kb.dma("sp", gg[:], mrow(2), R=[self.bmod], W=[bbc])
        kb.dma("sp", tmp[:], grow(2 * sub), W=[btmp])
        kb.op("dve", lambda e: e.scalar_tensor_tensor(out=gs[:], in0=gs[:], scalar=1.0, in1=tmp[:], op0=ALU.add, op1=ALU.mult),
              R=[bbc, btmp], W=[bbc])
        kb.dma("sp", tmp2[:], grow(2 * sub + 1), W=[btmp2])
        kb.op("dve", lambda e: e.tensor_tensor(out=gg[:], in0=gg[:], in1=tmp2[:], op=ALU.mult), R=[bbc, btmp2], W=[bbc])
        return gs, sh, gg, bbc

    def seqs(self):
        T, TS = self.T, self.TS
        S = []
        S.append(dict(name="p", ci=0, T=T, P=128, GN=min(512, T), x0=self.x_p, xb=self.xb_p, y=self.y_p, s=None))
        for s in range(2):
            S.append(dict(name="s%d" % s, ci=1 + s, T=TS, P=TS, GN=TS, x0=self.x_s[s], xb=self.xb_s[s], y=self.y_s[s], s=s))
        return S

    def xio(self, sq, stage):
        act = self.active_stages
        src = sq["x0"] if stage == act[0] else sq["xb"]
        dst = sq["y"] if stage == act[-1] else sq["xb"]
        return src, dst

    def build(self):
        nc = self.nc
        T, TS, PAST, NE, NO, depth = self.T, self.TS, self.PAST, self.NE, self.NO, self.depth
        TK = PAST + TS
        self._uid = 0
        self.x_p = self.din("x_p", [T, D])
        self.x_s = self.din("x_s", [2, TS, D])
        self.c_all = self.din("c_all", [3, D])
        self.cache_k = self.din("cache_k", [NE, 2, PAST, D])
        self.cache_v = self.din("cache_v", [NE, 2, PAST, D])
        self.st_sconv = self.din("st_sconv", [NE, 2, 3, 1536])
        self.st_ssm = self.din("st_ssm", [NE, 2, 1024, 128])
        self.st_cconv = self.din("st_cconv", [max(NO, 1), 2, 30, D])
        self.ada_w = self.din("ada_w", [depth, D, 6 * D])
        self.ada_b = self.din("ada_b", [depth, 6 * D])
        self.norm_g = self.din("norm_g", [depth, 4, D])
        self.ffn_w_up = self.din("ffn_w_up", [depth, D, 2 * DFF])
        self.ffn_w_down = self.din("ffn_w_down", [depth, DFF, D])
        self.hyb_w_in = self.din("hyb_w_in", [NE, D, DIN])
        self.attn_lambda = self.din("attn_lambda", [NE, 256])
        self.attn_subln_g = self.din("attn_subln_g", [NE, 128])
        self.ssm_conv_w = self.din("ssm_conv_w", [NE, 4, 1536])
        self.ssm_conv_b = self.din("ssm_conv_b", [NE, 1536])
        self.ssm_dt_bias = self.din("ssm_dt_bias", [NE, 16])
        self.ssm_a_log = self.din("ssm_a_log", [NE, 16])
        self.ssm_d = self.din("ssm_d", [NE, 16])
        self.ssm_norm_g = self.din("ssm_norm_g", [NE, D])
        self.hyb_w_out = self.din("hyb_w_out", [NE, 2 * D, D])
        self.conf_w_in = self.din("conf_w_in", [max(NO, 1), D, 2 * D])
        self.conf_b_in = self.din("conf_b_in", [max(NO, 1), 2 * D])
        self.conf_dw_w = self.din("conf_dw_w", [max(NO, 1), 31, D])
        self.conf_dw_b = self.din("conf_dw_b", [max(NO, 1), D])
        self.conf_ln_g = self.din("conf_ln_g", [max(NO, 1), D])
        self.conf_ln_b = self.din("conf_ln_b", [max(NO, 1), D])
        self.conf_w_out = self.din("conf_w_out", [max(NO, 1), D, D])
        self.conf_b_out = self.din("conf_b_out", [max(NO, 1), D])
        self.cst_f = self.din("cst_f", [4, 128, 128])
        self.cst_kaug = self.din("cst_kaug", [8, 5, 8192])
        self.cst_qaug = self.din("cst_qaug", [8, 5, 8192])
        self.cst_corr = self.din("cst_corr", [8, 128, 128])
        self.y_p = self.dout("y_p", [T, D])
        self.y_s = self.dout("y_s", [2, TS, D])
        self.k_p = self.dout("k_p", [NE, T, D])
        self.v_p = self.dout("v_p", [NE, T, D])
        self.sconv_p = self.dout("sconv_p", [NE, 3, 1536])
        self.ssm_p = self.dout("ssm_p", [NE, 1024, 128])
        self.cconv_p = self.dout("cconv_p", [max(NO, 1), 30, D])
        self.k_s = self.dout("k_s", [NE, 2, TS, D])
        self.v_s = self.dout("v_s", [NE, 2, TS, D])
        self.sconv_s = self.dout("sconv_s", [NE, 2, 3, 1536])
        self.ssm_s = self.dout("ssm_s", [NE, 2, 1024, 128])
        self.cconv_s = self.dout("cconv_s", [max(NO, 1), 2, 30, D])
        self.xb_p = self.dscr("xb_p", [T, D])
        self.xb_s = self.dscr("xb_s", [2, TS, D])
        self.MOD = self.dscr("modrows", [depth, 3, 6 * D])
        self.QT = [self.dscr("qt_p", [8, 128, T], BF16)] + [self.dscr("qt_s%d" % s, [8, 128, TS], BF16) for s in range(2)]
        self.KT = [self.dscr("kt_p", [8, 128, T], BF16)] + [self.dscr("kt_s%d" % s, [8, 128, TK], BF16) for s in range(2)]
        self.VB = [self.dscr("vb_p", [T, D], BF16)] + [self.dscr("vb_s%d" % s, [TK, D], BF16) for s in range(2)]
        self.OT = [self.dscr("ot_p", [D, T], BF16)] + [self.dscr("ot_s%d" % s, [D, TS], BF16) for s in range(2)]
        self.YT = [self.dscr("yt_p", [D, T], BF16)] + [self.dscr("yt_s%d" % s, [D, TS], BF16) for s in range(2)]
        self.bscr = [dict(q=Buf(), k=Buf(), v=Buf(), o=Buf(), y=Buf(), x=Buf()) for _ in range(3)]
        self.bmod = Buf()
        self.bout = Buf()

        with ExitStack() as es:
            self.kb = kb = KB(nc, es)
            self.PP = [es.enter_context(nc.psum_tensor("pp%d" % i, [128, 1024], F32)) for i in range(4)]
            self.pb = [Buf("bank%d" % i, excl=True) for i in range(8)]
            self.bank_list, self.bank_rr = list(range(8)), 0
            self.pair_list, self.pair_rr = list(range(4)), 0
            self.bconst = Buf("const")
            cf = self.sb(es, "cf", [128, 4, 128], F32)
            kb.dma("sp", cf[:], self.cst_f.rearrange("a p c -> p a c"), W=[self.bconst])
            self.identf, self.onesf, self.trilef, self.ugtf = cf[:, 0, :], cf[:, 1, :], cf[:, 2, :], cf[:, 3, :]
            cb = self.sb(es, "cb", [128, 4, 128], BF16)
            kb.op("pool", lambda e: e.tensor_copy(out=cb[:], in_=cf[:]), R=[self.bconst], W=[self.bconst])
            self.identb, self.onesb, self.trileb, self.ugtb = cb[:, 0, :], cb[:, 1, :], cb[:, 2, :], cb[:, 3, :]
            self.epsb = self.sb(es, "epsb", [128, 1], F32)
            kb.op("pool", lambda e: e.memset(self.epsb[:], EPS), W=[self.bconst])
            self.m1024 = self.sb(es, "m1024", [128, 128], F32)
            kb.op("pool", lambda e: e.memset(self.m1024[:], 1.0 / 1024), W=[self.bconst])
            self.m128 = self.sb(es, "m128", [128, 128], F32)
            kb.op("pool", lambda e: e.memset(self.m128[:], 1.0 / 128), W=[self.bconst])
            self.xrot = self.rot("xt", es, [128, D], F32, 2)
            self.frot = self.rot("ft", es, [128, D], F32, 2)
            self.hrot = self.rot("hb", es, [128, D], BF16, 1)
            self.srot = self.rot("ssq", es, [128, 4], F32, 4)
            self.wstage = self.rot("wst", es, [128, 1024], F32, 2)
            self.bc_gs = self.sb(es, "bcgs", [128, D], F32)
            self.bc_sh = self.sb(es, "bcsh", [128, D], F32)
            self.bc_gg = self.sb(es, "bcgg", [128, D], F32)
            self.bbc = Buf("bc")

            self.active_stages = []
            for l in range(depth):
                if (l % 2 == 0 and "hyb" not in SKIP) or (l % 2 == 1 and "conf" not in SKIP):
                    self.active_stages.append(2 * l)
                if "ffn" not in SKIP:
                    self.active_stages.append(2 * l + 1)
            self.phase_adaln()
            S = self.seqs()
            for l in range(depth):
                j = l // 2
                if l % 2 == 0 and "hyb" not in SKIP:
                    for ph in (self.phase_e1a, self.phase_e1b, self.phase_e2, self.phase_e3):
                        if DEBUG:
                            print("phase", ph.__name__, "l", l, "next_id", self.nc.next_id(), flush=True)
                        if ph.__name__[6:] not in SKIP:
                            ph(l, j, S)
                if l % 2 == 1 and "conf" not in SKIP:
                    self.phase_conf(l, j, S)
                if "ffn" not in SKIP:
                    self.phase_ffn(l, S)
            kb.barrier()
            self.stats = (kb.nins, kb.nwait)

    def phase_adaln(self):
        kb, depth = self.kb, self.depth
        with ExitStack() as es:
            ct = self.sb(es, "ct", [4, D], F32); bct = Buf()
            kb.dma("sp", ct[0:3, :], self.c_all, W=[bct])
            kb.op("act", lambda e: e.activation(out=ct[0:3, :], in_=ct[0:3, :], func=AF.Silu), R=[bct], W=[bct])
            cT = self.sb(es, "cT", [128, 8, 4], F32); bcT = Buf()
            pb, bb = self.bank()
            for k in range(8):
                kb.op("pe", lambda e, k=k: e.transpose(out=pb[:, k * 4:k * 4 + 3], in_=ct[0:3, k * 128:(k + 1) * 128],
                                                        identity=self.identf[0:3, 0:3]), R=[bct, self.bconst], W=bb)
            for k in range(8):
                kb.op("act", lambda e, k=k: e.activation(out=cT[:, k, 0:3], in_=pb[:, k * 4:k * 4 + 3], func=AF.Copy), R=bb, W=[bcT])
            ab = self.sb(es, "ab", [4, 6 * D], F32); bab = Buf()
            mrow = self.sb(es, "mrow", [4, 6 * D], F32); bmr = Buf()
            wrot = self.rot("adaw", es, [128, 3072], F32, 3)
            for l in range(depth):
                kb.dma("sp", ab[0:3, :], self.ada_b[l:l + 1, :].partition_broadcast(3), R=[bab], W=[bab])
                for half in range(2):
                    banks = [self.bank() for _ in range(6)]
                    for k in range(8):
                        wt, bw = wrot()
                        kb.dma("sp" if k % 2 == 0 else "pool", wt[:], self.ada_w[l, k * 128:(k + 1) * 128, half * 3072:(half + 1) * 3072], W=[bw])
                        for n in range(6):
                            pbn, bbn = banks[n]
                            kb.op("pe", lambda e, k=k, n=n, pbn=pbn, wt=wt: e.matmul(pbn[0:3, :], lhsT=cT[:, k, 0:3], rhs=wt[:, n * 512:(n + 1) * 512],
                                                                                  start=(k == 0), stop=(k == 7)), R=[bcT, bw], W=bbn)
                    for n in range(6):
                        pbn, bbn = banks[n]
                        c0 = half * 3072 + n * 512
                        kb.op("dve", lambda e, pbn=pbn, c0=c0: e.tensor_tensor(out=mrow[0:3, c0:c0 + 512], in0=pbn[0:3, :], in1=ab[0:3, c0:c0 + 512], op=ALU.add),
                              R=bbn + [bab], W=[bmr])
                kb.dma("sp", self.MOD[l], mrow[0:3, :], R=[bmr], W=[self.bmod])
            kb.barrier()

    def phase_ffn(self, l, S):
        kb = self.kb
        stage = 2 * l + 1
        with ExitStack() as es:
            wup = self.sb(es, "wup", [128, 8, 2 * DFF], BF16); bwu = Buf()
            wdn = self.sb(es, "wdn", [128, 22, D], BF16); bwd = Buf()
            self.load_w(wup, self.ffn_w_up[l].rearrange("(k p) n -> k p n", p=128), 8, 2 * DFF, bwu)
            self.load_w(wdn, self.ffn_w_down[l].rearrange("(k p) n -> k p n", p=128), 22, D, bwd)
            GNmax = S[0]["GN"]
            hT = self.sb(es, "hT", [128, 8, GNmax], BF16); bhT = Buf()
            aT = self.sb(es, "aT", [128, 22, GNmax], BF16); baT = Buf()
            sgr = self.rot("sg", es, [128, GNmax], F32, 1)
            kb.op("pool", lambda e: e.memset(hT[:], 0.0), W=[bhT])
            for si, sq in enumerate(S):
                gs, sh, gg, bbc = self.make_bc(l, 1, sq["ci"])
                src, dst = self.xio(sq, stage)
                bx = self.bscr[si]["x"]
                P, GN = sq["P"], sq["GN"]
                GNp = max(GN, 128)
                for g0 in range(0, sq["T"], GN):
                    nt = GN // P
                    for m in range(nt):
                        self.prelude_x(src[g0 + m * P:g0 + (m + 1) * P, :], bx, P, hT, bhT, m * P, gs, sh, bbc)
                    for jf in range(22):
                        pg, bg = self.bank()
                        pu, bu = self.bank()
                        for k in range(8):
                            kb.op("pe", lambda e, k=k, pg=pg: e.matmul(pg[:, 0:GNp], lhsT=wup[:, k, jf * 128:(jf + 1) * 128], rhs=hT[:, k, 0:GNp],
                                                                      start=(k == 0), stop=(k == 7)), R=[bwu, bhT], W=bg)
                        for k in range(8):
                            kb.op("pe", lambda e, k=k, pu=pu: e.matmul(pu[:, 0:GNp], lhsT=wup[:, k, DFF + jf * 128:DFF + (jf + 1) * 128], rhs=hT[:, k, 0:GNp],
                                                                      start=(k == 0), stop=(k == 7)), R=[bwu, bhT], W=bu)
                        sg, bsg = sgr()
                        kb.op("act", lambda e, sg=sg, pg=pg: e.activation(out=sg[:, 0:GN], in_=pg[:, 0:GN], func=AF.Silu), R=bg, W=[bsg])
                        kb.op("dve", lambda e, sg=sg, pu=pu, jf=jf: e.tensor_tensor(out=aT[:, jf, 0:GN], in0=sg[:, 0:GN], in1=pu[:, 0:GN], op=ALU.mult),
                              R=[bsg] + bu, W=[baT])
                    for m in range(nt):
                        pp, bpp = self.pair()
                        for nh in range(2):
                            for k in range(22):
                                kb.op("pe", lambda e, k=k, nh=nh, m=m, pp=pp: e.matmul(pp[0:P, nh * 512:(nh + 1) * 512], lhsT=aT[:, k, m * P:(m + 1) * P],
                                                                                     rhs=wdn[:, k, nh * 512:(nh + 1) * 512], start=(k == 0), stop=(k == 21)),
                                      R=[baT, bwd], W=bpp)
                        r0 = g0 + m * P
                        self.post(pp, bpp, P, src[r0:r0 + P, :], dst[r0:r0 + P, :], gg, bbc, bx)
            kb.barrier()

    def prelude_x(self, xsrc, bdram, P, hT, bhT, col0, gs, sh, bbc):
        kb = self.kb
        xt, bx = self.xrot()
        kb.dma("sp", xt[0:P, :], xsrc, R=[bdram], W=[bx])
        junk, bj = self.frot()
        ssq, bs = self.srot()
        kb.op("act", lambda e: e.activation(out=junk[0:P, :], in_=xt[0:P, :], func=AF.Square, accum_out=ssq[0:P, 0:1]),
              R=[bx], W=[bj, bs])
        kb.op("act", lambda e: e.activation(out=ssq[0:P, 1:2], in_=ssq[0:P, 0:1], func=AF.Sqrt, scale=1.0 / D, bias=self.epsb[0:P, 0:1]),
              R=[bs, self.bconst], W=[bs])
        kb.op("dve", lambda e: e.reciprocal(out=ssq[0:P, 1:2], in_=ssq[0:P, 1:2]), R=[bs], W=[bs])
        kb.op("dve", lambda e: e.scalar_tensor_tensor(out=junk[0:P, :], in0=xt[0:P, :], scalar=ssq[0:P, 1:2], in1=gs[0:P, :],
                                                       op0=ALU.mult, op1=ALU.mult), R=[bx, bs, bbc], W=[bj])
        hb, bh = self.hrot()
        kb.op("pool", lambda e: e.tensor_tensor(out=hb[0:P, :], in0=junk[0:P, :], in1=sh[0:P, :], op=ALU.add),
              R=[bj, bbc], W=[bh])
        self.transpose_to(hb, bh, P, 8, lambda k: hT[:, k, col0:col0 + P], bhT)

    def evac(self, out, in_, R, W):
        self._ev = getattr(self, "_ev", 0) + 1
        if self._ev % 2 == 0:
            self.kb.op("act", lambda e: e.activation(out=out, in_=in_, func=AF.Copy), R=R, W=W)
        else:
            self.kb.op("dve", lambda e: e.tensor_copy(out=out, in_=in_), R=R, W=W)

    def phase_e1a(self, l, j, S):
        kb = self.kb
        stage = 2 * l
        PAST, TS = self.PAST, self.TS
        with ExitStack() as es:
            w = self.sb(es, "w1a", [128, 8, 3072], BF16); bw = Buf()
            self.load_w(w, self.hyb_w_in[j].rearrange("(k p) n -> k p n", p=128), 8, 3072, bw, c0=0)
            GNmax = S[0]["GN"]
            hT = self.sb(es, "ahT", [128, 8, GNmax], BF16); bhT = Buf()
            kvt = self.rot("kvt", es, [128, 2048], F32, 2)
            vbr = self.rot("vbr", es, [128, D], BF16, 2)
            fmr = self.rot("fmr", es, [128, GNmax], BF16, 3)
            kfr = self.rot("kfr", es, [128, 8, 128], BF16, 2)
            kb.op("pool", lambda e: e.memset(hT[:], 0.0), W=[bhT])
            for si, sq in enumerate(S):
                gs, sh, gg, bbc = self.make_bc(l, 0, sq["ci"])
                src, _ = self.xio(sq, stage)
                bx = self.bscr[si]["x"]
                bsc = self.bscr[si]
                P, GN, Tq = sq["P"], sq["GN"], sq["T"]
                kp0 = 0
                if sq["s"] is not None:
                    s_ = sq["s"]
                    kp0 = PAST
                    for t in range(PAST // 128):
                        ck, bck = kvt()
                        kb.dma("sp", ck[:, 0:1024], self.cache_k[j, s_, t * 128:(t + 1) * 128, :], W=[bck])
                        kb.dma("sp", ck[:, 1024:2048], self.cache_v[j, s_, t * 128:(t + 1) * 128, :], W=[bck])
                        kf, bkf = kfr()
                        for h0 in (0, 4):
                            pb, bb = self.bank()
                            for h in range(h0, h0 + 4):
                                kb.op("pe", lambda e, h=h, pb=pb, ck=ck: e.transpose(out=pb[:, (h - h0) * 128:(h - h0 + 1) * 128], in_=ck[:, h * 128:(h + 1) * 128],
                                                                                     identity=self.identf[:, :]), R=[bck, self.bconst], W=bb)
                            self.evac(kf[:, h0:h0 + 4, :], pb.rearrange("p (a b) -> p a b", a=4), bb, [bkf])
                        kb.dma("pool", self.KT[si][:, :, t * 128:(t + 1) * 128].rearrange("h r t -> r h t"), kf[:], R=[bkf], W=[bsc["k"]])
                        vb, bvb = vbr()
                        kb.op("pool", lambda e, vb=vb, ck=ck: e.tensor_copy(out=vb[:], in_=ck[:, 1024:2048]), R=[bck], W=[bvb])
                        kb.dma("pool", self.VB[si][t * 128:(t + 1) * 128, :], vb[:], R=[bvb], W=[bsc["v"]])
                for g0 in range(0, Tq, GN):
                    nt = GN // P
                    for m in range(nt):
                        self.prelude_x(src[g0 + m * P:g0 + (m + 1) * P, :], bx, P, hT, bhT, m * P, gs, sh, bbc)
                    for m in range(nt):
                        kv, bkv = kvt()
                        for n4 in range(4):
                            pb, bb = self.bank()
                            for k in range(8):
                                kb.op("pe", lambda e, k=k, pb=pb, n4=n4, m=m: e.matmul(pb[0:P, :], lhsT=hT[:, k, m * P:(m + 1) * P], rhs=w[:, k, 1024 + n4 * 512:1024 + (n4 + 1) * 512],
                                                                                     start=(k == 0), stop=(k == 7)), R=[bhT, bw], W=bb)
                            self.evac(kv[0:P, n4 * 512:(n4 + 1) * 512], pb[0:P, :], bb, [bkv])
                        r0 = g0 + m * P
                        if sq["s"] is None:
                            kb.dma("pool", self.k_p[j, r0:r0 + P, :], kv[0:P, 0:1024], R=[bkv], W=[self.bout])
                            kb.dma("pool", self.v_p[j, r0:r0 + P, :], kv[0:P, 1024:2048], R=[bkv], W=[self.bout])
                        else:
                            kb.dma("pool", self.k_s[j, sq["s"], r0:r0 + P, :], kv[0:P, 0:1024], R=[bkv], W=[self.bout])
                            kb.dma("pool", self.v_s[j, sq["s"], r0:r0 + P, :], kv[0:P, 1024:2048], R=[bkv], W=[self.bout])
                        vb, bvb = vbr()
                        kb.op("pool", lambda e, vb=vb, kv=kv: e.tensor_copy(out=vb[0:P, :], in_=kv[0:P, 1024:2048]), R=[bkv], W=[bvb])
                        kb.dma("pool", self.VB[si][kp0 + r0:kp0 + r0 + P, :], vb[0:P, :], R=[bvb], W=[bsc["v"]])
                    for c in range(16):
                        pb, bb = self.bank()
                        for k in range(8):
                            kb.op("pe", lambda e, k=k, pb=pb, c=c: e.matmul(pb[:, 0:max(GN, 128)], lhsT=w[:, k, c * 128:(c + 1) * 128], rhs=hT[:, k, 0:max(GN, 128)],
                                                                           start=(k == 0), stop=(k == 7)), R=[bhT, bw], W=bb)
                        fm, bfm = fmr()
                        self.evac(fm[:, 0:GN], pb[:, 0:GN], bb, [bfm])
                        if c < 8:
                            kb.dma("pool", self.QT[si][c, :, g0:g0 + GN], fm[:, 0:GN], R=[bfm], W=[bsc["q"]])
                        else:
                            kb.dma("pool", self.KT[si][c - 8, :, kp0 + g0:kp0 + g0 + GN], fm[:, 0:GN], R=[bfm], W=[bsc["k"]])
            kb.barrier()

    def phase_e1b(self, l, j, S):
        kb = self.kb
        stage = 2 * l
        with ExitStack() as es:
            w = self.sb(es, "w1b", [128, 8, 2576], BF16); bw = Buf()
            self.load_w(w, self.hyb_w_in[j].rearrange("(k p) n -> k p n", p=128), 8, 2576, bw, c0=3072)
            bsm = Buf("e1bsmall")
            cwf = self.sb(es, "cwf", [128, 12, 4], F32)
            self.load_fm(es, lambda c: cwf[:, c, :], self.ssm_conv_w[j], 4, 12, bsm)
            cbf = self.sb(es, "cbf", [128, 12, 1], F32)
            self.load_fm(es, lambda c: cbf[:, c, :], self.ssm_conv_b[j:j + 1, :], 1, 12, bsm)
            diag4 = self.sb(es, "diag4", [128, 48, 128], BF16)
            kb.op("dve", lambda e: e.tensor_tensor(out=diag4[:], in0=self.identf.unsqueeze(1).to_broadcast([128, 48, 128]),
                                                   in1=cwf[:].rearrange("p c k -> p (c k)").unsqueeze(2).to_broadcast([128, 48, 128]),
                                                   op=ALU.mult), R=[bsm, self.bconst], W=[bsm])
            cbr = self.sb(es, "cbr", [1, 1536], F32)
            kb.dma("sp", cbr[:], self.ssm_conv_b[j:j + 1, :], W=[bsm])
            cbrb = self.sb(es, "cbrb", [1, 1536], BF16)
            kb.op("pool", lambda e: e.tensor_copy(out=cbrb[:], in_=cbr[:]), R=[bsm], W=[bsm])
            sm = self.sb(es, "ssmsm", [128, 4, 16], F32)
            kb.dma("sp", sm[:, 0, :], self.ssm_a_log[j:j + 1, :].partition_broadcast(128), W=[bsm])
            kb.dma("sp", sm[:, 1, :], self.ssm_d[j:j + 1, :].partition_broadcast(128), W=[bsm])
            kb.dma("sp", sm[:, 2, :], self.ssm_dt_bias[j:j + 1, :].partition_broadcast(128), W=[bsm])
            kb.op("act", lambda e: e.activation(out=sm[:, 0, :], in_=sm[:, 0, :], func=AF.Exp), R=[bsm], W=[bsm])
            kb.op("dve", lambda e: e.tensor_scalar(out=sm[:, 0, :], in0=sm[:, 0, :], scalar1=-1.0, scalar2=None, op0=ALU.mult), R=[bsm], W=[bsm])
            a_b, D_b, dtb_b = sm[:, 0, :], sm[:, 1, :], sm[:, 2, :]
            ngb = self.sb(es, "ngb", [128, D], F32)
            kb.dma("sp", ngb[:], self.ssm_norm_g[j:j + 1, :].partition_broadcast(128), W=[bsm])
            if E1B_STOP <= 1:
                kb.barrier()
                return
            GNmax = S[0]["GN"]
            hT = self.sb(es, "bhT", [128, 8, GNmax], BF16); bhT = Buf()
            xbcT = self.sb(es, "xbcT", [128, 12, 4 + GNmax], BF16); bxb = Buf()
            t1 = self.sb(es, "t1", [128, D], F32); bt1 = Buf()
            t3 = self.sb(es, "t3", [128, D], F32); bt3 = Buf()
            ynb = self.sb(es, "ynb", [128, D], BF16); bynb = Buf()
            ynT = self.sb(es, "ynT", [128, 8, 128], BF16); bynT = Buf()
            hst = self.sb(es, "hst", [128, D], F32); bhst = Buf()
            hbf = self.sb(es, "hbf", [128, D], BF16); bhbf = Buf()
            xraw = self.sb(es, "xraw", [128, 1536], F32); bxr = Buf()
            TB = []
            for i in range(2):
                TB.append((self.sb(es, "szt", [128, D], F32), Buf(), self.sb(es, "xst", [128, D], F32), Buf(),
                           self.sb(es, "Rf", [128, 2048], BF16), Buf(), self.sb(es, "exf", [128, 1024], F32), Buf(),
                           self.sb(es, "scf", [128, 2048], BF16), Buf(), self.sb(es, "xdt", [128, D], BF16), Buf(),
                           self.sb(es, "xdtw", [128, D], BF16), Buf(), self.sb(es, "bct", [128, 4, 128], BF16), Buf(),
                           self.sb(es, "btok", [128, 256], BF16), Buf(), self.sb(es, "cbm", [128, 2, 128], F32), Buf(),
                           self.sb(es, "d16", [128, 12, 16], F32), Buf(), self.sb(es, "d16b", [128, 16], BF16)))
            tbi = [0]
            kb.op("pool", lambda e: e.memset(hT[:], 0.0), W=[bhT])
            kb.op("pool", lambda e: e.memset(xbcT[:], 0.0), W=[bxb])
            for si, sq in enumerate(S):
                if CONF_SEQS is not None and si not in CONF_SEQS:
                    continue
                gs, sh, gg, bbc = self.make_bc(l, 0, sq["ci"])
                src, _ = self.xio(sq, stage)
                bx = self.bscr[si]["x"]
                bsc = self.bscr[si]
                P, GN, Tq = sq["P"], sq["GN"], sq["T"]
                if sq["s"] is None:
                    kb.op("pool", lambda e: e.memset(xbcT[:, :, 0:4], 0.0), W=[bxb])
                    kb.op("pool", lambda e: e.memset(hst[:], 0.0), W=[bhst])
                    kb.op("pool", lambda e: e.memset(hbf[:], 0.0), W=[bhbf])
                else:
                    s_ = sq["s"]
                    self.load_fm(es, lambda c: xbcT[:, c, 0:4], self.st_sconv[j, s_], 3, 12, bxb, pad=1)
                    st, bst = self.wstage()
                    kb.dma("sp", st[:].rearrange("p (c n) -> p c n", c=8), self.st_ssm[j, s_].rearrange("(c p) n -> p c n", p=128), W=[bst])
                    pp, bpp = self.pair()
                    for c in range(8):
                        kb.op("pe", lambda e, c=c, pp=pp, st=st: e.transpose(out=pp[:, c * 128:(c + 1) * 128], in_=st[:, c * 128:(c + 1) * 128], identity=self.identf[:, :]),
                              R=[bst, self.bconst], W=bpp)
                    kb.op("act", lambda e, pp=pp: e.activation(out=hst[:], in_=pp[:], func=AF.Copy), R=bpp, W=[bhst])
                    kb.op("dve", lambda e: e.tensor_copy(out=hbf[:], in_=hst[:]), R=[bhst], W=[bhbf])
                ngr = Tq // GN
                for gi in range(ngr):
                    g0 = gi * GN
                    nt = GN // P
                    for m in range(nt):
                        self.prelude_x(src[g0 + m * P:g0 + (m + 1) * P, :], bx, P, hT, bhT, m * P, gs, sh, bbc)
                    for c in range(12):
                        pb, bb = self.bank()
                        for k in range(8):
                            kb.op("pe", lambda e, k=k, pb=pb, c=c: e.matmul(pb[:, 0:max(GN, 128)], lhsT=w[:, k, 1024 + c * 128:1024 + (c + 1) * 128], rhs=hT[:, k, 0:max(GN, 128)],
                                                                           start=(k == 0), stop=(k == 7)), R=[bhT, bw], W=bb)
                        self.evac(xbcT[:, c, 4:4 + GN], pb[:, 0:GN], bb, [bxb])
                    def tile_gen(m, gi=gi, g0=g0):
                        (szt, bsz, xst, bxs, Rf, bR, exf, bex, scf, bscf, xdt, bxdt, xdtw, bxdtw, bct, bbct, btok, bbtok, cbm, bcbm, d16, bd, d16b) = TB[tbi[0] % 2]
                        tbi[0] += 1
                        Rv = Rf[0:P, 0:16 * P].rearrange("p (h l) -> p h l", h=16)
                        scv = scf[0:P, 0:16 * P].rearrange("p (h l) -> p h l", h=16)
                        t0 = m * P
                        r0 = g0 + t0
                        pz, bz = self.pair()
                        for nh in range(2):
                            for k in range(8):
                                kb.op("pe", lambda e, k=k, nh=nh, pz=pz: e.matmul(pz[0:P, nh * 512:(nh + 1) * 512], lhsT=hT[:, k, t0:t0 + P], rhs=w[:, k, nh * 512:(nh + 1) * 512],
                                                                                start=(k == 0), stop=(k == 7)), R=[bhT, bw], W=bz)
                        kb.op("act", lambda e, pz=pz: e.activation(out=szt[0:P, :], in_=pz[0:P, :], func=AF.Silu), R=bz, W=[bsz])
                        pd, bpd = self.bank()
                        for k in range(8):
                            kb.op("pe", lambda e, k=k, pd=pd: e.matmul(pd[0:P, 0:128], lhsT=hT[:, k, t0:t0 + P], rhs=w[:, k, 2448:2576], start=(k == 0), stop=(k == 7)),
                                  R=[bhT, bw], W=bpd)
                        X, AXv, EX, LG, DT, DTA, WL, DTW, EXPA, EL = [d16[0:P, i, :] for i in range(10)]
                        EL = d16[:, 9, :]
                        kb.op("dve", lambda e, pd=pd: e.tensor_tensor(out=X, in0=pd[0:P, 112:128], in1=dtb_b[0:P, :], op=ALU.add), R=bpd + [bsm], W=[bd])
                        kb.op("dve", lambda e: e.scalar_tensor_tensor(out=AXv, in0=X, scalar=-1.0, in1=X, op0=ALU.mult, op1=ALU.max), R=[bd], W=[bd])
                        kb.op("act", lambda e: e.activation(out=EX, in_=AXv, func=AF.Exp, scale=-1.0), R=[bd], W=[bd])
                        kb.op("act", lambda e: e.activation(out=LG, in_=EX, func=AF.Ln, bias=1.0), R=[bd], W=[bd])
                        kb.op("dve", lambda e: e.scalar_tensor_tensor(out=DT, in0=X, scalar=0.0, in1=LG, op0=ALU.max, op1=ALU.add), R=[bd], W=[bd])
                        kb.op("dve", lambda e: e.tensor_tensor(out=DTA, in0=DT, in1=a_b[0:P, :], op=ALU.mult), R=[bd, bsm], W=[bd])
                        kb.op("dve", lambda e: e.tensor_copy(out=d16b[0:P, :], in_=DTA), R=[bd], W=[bd])
                        px, bpx = self.pair()
                        for k in range(5):
                            for c in range(8):
                                st_, sp_ = (k == 0 and c % 4 == 0), (k == 4 and c % 4 == 3)
                                if k < 4:
                                    kb.op("pe", lambda e, c=c, k=k, px=px, st_=st_, sp_=sp_: e.matmul(px[0:P, c * 128:(c + 1) * 128], lhsT=xbcT[:, c, 1 + t0 + k:1 + t0 + k + P], rhs=diag4[:, c * 4 + k, :],
                                                                                                   start=st_, stop=sp_), R=[bxb, bsm], W=bpx)
                                else:
                                    kb.op("pe", lambda e, c=c, px=px, st_=st_, sp_=sp_: e.matmul(px[0:P, c * 128:(c + 1) * 128], lhsT=self.onesb[0:1, 0:P], rhs=cbrb[0:1, c * 128:(c + 1) * 128],
                                                                                              start=st_, stop=sp_), R=[bsm, self.bconst], W=bpx)
                        kb.op("act", lambda e, px=px: e.activation(out=xst[0:P, :], in_=px[0:P, :], func=AF.Silu), R=bpx, W=[bxs])
                        pk, bpk = self.bank()
                        for k in range(5):
                            for c in (8, 9):
                                st_, sp_ = (k == 0 and c == 8), (k == 4 and c == 9)
                                if k < 4:
                                    kb.op("pe", lambda e, c=c, k=k, pk=pk, st_=st_, sp_=sp_: e.matmul(pk[0:P, (c - 8) * 128:(c - 7) * 128], lhsT=xbcT[:, c, 1 + t0 + k:1 + t0 + k + P], rhs=diag4[:, c * 4 + k, :],
                                                                                                   start=st_, stop=sp_), R=[bxb, bsm], W=bpk)
                                else:
                                    kb.op("pe", lambda e, c=c, pk=pk, st_=st_, sp_=sp_: e.matmul(pk[0:P, (c - 8) * 128:(c - 7) * 128], lhsT=self.onesb[0:1, 0:P], rhs=cbrb[0:1, c * 128:(c + 1) * 128],
                                                                                              start=st_, stop=sp_), R=[bsm, self.bconst], W=bpk)
                        kb.op("act", lambda e, pk=pk: e.activation(out=btok[0:P, :], in_=pk[0:P, 0:256], func=AF.Silu), R=bpk, W=[bbtok])
                        pf, bpf = self.bank()
                        for k in range(4):
                            for i, c in enumerate((8, 9, 10, 11)):
                                st_, sp_ = (k == 0 and i == 0), (k == 3 and i == 3)
                                kb.op("pe", lambda e, c=c, k=k, i=i, pf=pf, st_=st_, sp_=sp_: e.matmul(pf[:, i * 128:(i + 1) * 128], lhsT=diag4[:, c * 4 + k, :], rhs=xbcT[:, c, 1 + t0 + k:1 + t0 + k + 128],
                                                                                                    start=st_, stop=sp_), R=[bxb, bsm], W=bpf)
                        for i, c in enumerate((8, 9, 10, 11)):
                            kb.op("act", lambda e, c=c, i=i, pf=pf: e.activation(out=bct[:, i, 0:P], in_=pf[:, i * 128:i * 128 + P], func=AF.Silu, bias=cbf[:, c, :]),
                                  R=bpf + [bsm], W=[bbct])
                        pcb, bpcb = self.bank()
                        for g in range(2):
                            kb.op("pe", lambda e, g=g, pcb=pcb: e.matmul(pcb[0:P, g * P:(g + 1) * P], lhsT=bct[:, g, 0:P], rhs=bct[:, 2 + g, 0:P], start=True, stop=True),
                                  R=[bbct], W=bpcb)
                        kb.op("dve", lambda e, pcb=pcb: e.tensor_tensor(out=cbm[0:P, :, 0:P], in0=pcb[0:P, 0:2 * P].rearrange("p (g l) -> p g l", g=2),
                                                                       in1=self.trilef[0:P, 0:P].unsqueeze(1).to_broadcast([P, 2, P]), op=ALU.mult),
                              R=bpcb + [self.bconst], W=[bcbm])
                        kb.op("dve", lambda e: e.tensor_tensor(out=Rv, in0=self.trilef[0:P, 0:P].unsqueeze(1).to_broadcast([P, 16, P]),
                                                               in1=DTA.unsqueeze(2).to_broadcast([P, 16, P]), op=ALU.mult), R=[bd, self.bconst], W=[bR])
                        for half in range(2):
                            psg, bsg = self.pair()
                            for q4 in range(2):
                                h0 = half * 8 + q4 * 4
                                kb.op("pe", lambda e, q4=q4, h0=h0, psg=psg: e.matmul(psg[0:P, q4 * 4 * P:(q4 + 1) * 4 * P], lhsT=self.ugtb[0:P, 0:P],
                                                                                     rhs=Rf[0:P, h0 * P:(h0 + 4) * P], start=True, stop=True), R=[bR, self.bconst], W=bsg)
                            kb.op("act", lambda e, psg=psg: e.activation(out=exf[0:P, 0:8 * P], in_=psg[0:P, 0:8 * P], func=AF.Exp), R=bsg, W=[bex])
                            exv = exf[0:P, 0:8 * P].rearrange("p (h l) -> p h l", h=8)
                            kb.op("dve", lambda e, half=half, exv=exv: e.tensor_copy(out=WL[:, half * 8:(half + 1) * 8], in_=exv[:, :, P - 1]), R=[bex], W=[bd])
                            kb.op("dve", lambda e, half=half, exv=exv: e.tensor_tensor(out=scv[:, half * 8:(half + 1) * 8, :], in0=exv,
                                                                                      in1=cbm[0:P, half, 0:P].unsqueeze(1).to_broadcast([P, 8, P]), op=ALU.mult),
                                  R=[bex, bcbm], W=[bscf])
                        pa, bpa = self.bank()
                        kb.op("pe", lambda e, pa=pa: e.matmul(pa[0:P, 0:16], lhsT=self.trileb[0:P, 0:P], rhs=d16b[0:P, :], start=True, stop=True), R=[bd, self.bconst], W=bpa)
                        kb.op("pe", lambda e, pa=pa: e.matmul(pa[:, 16:32], lhsT=self.onesb[0:P, :], rhs=d16b[0:P, :], start=True, stop=True), R=[bd, self.bconst], W=bpa)
                        kb.op("act", lambda e, pa=pa: e.activation(out=EXPA, in_=pa[0:P, 0:16], func=AF.Exp), R=bpa, W=[bd])
                        kb.op("act", lambda e, pa=pa: e.activation(out=EL, in_=pa[:, 16:32], func=AF.Exp), R=bpa, W=[bd])
                        kb.op("dve", lambda e: e.tensor_tensor(out=DTW, in0=DT, in1=WL, op=ALU.mult), R=[bd], W=[bd])
                        xs3 = xst[0:P, :].rearrange("p (h q) -> p h q", h=16)
                        kb.op("dve", lambda e: e.tensor_tensor(out=xdt[0:P, :].rearrange("p (h q) -> p h q", h=16), in0=xs3, in1=DT.unsqueeze(2).to_broadcast([P, 16, 64]), op=ALU.mult),
                              R=[bxs, bd], W=[bxdt])
                        kb.op("pool", lambda e: e.tensor_tensor(out=xdtw[0:P, :].rearrange("p (h q) -> p h q", h=16), in0=xs3, in1=DTW.unsqueeze(2).to_broadcast([P, 16, 64]), op=ALU.mult),
                              R=[bxs, bd], W=[bxdtw])
                        yield
                        py, bpy = self.pair()
                        for h in range(16):
                            kb.op("pe", lambda e, h=h, py=py: e.matmul(py[0:P, h * 64:(h + 1) * 64], lhsT=scf[0:P, h * P:(h + 1) * P], rhs=xdt[0:P, h * 64:(h + 1) * 64], start=True, stop=True),
                                  R=[bscf, bxdt], W=bpy)
                        po, bpo = self.pair()
                        for g in range(2):
                            kb.op("pe", lambda e, g=g, po=po: e.matmul(po[0:P, g * 512:(g + 1) * 512], lhsT=bct[:, 2 + g, 0:P], rhs=hbf[:, g * 512:(g + 1) * 512], start=True, stop=True),
                                  R=[bbct, bhbf], W=bpo)
                        kb.op("dve", lambda e, po=po: e.tensor_tensor(out=t1[0:P, :].rearrange("p (h q) -> p h q", h=16), in0=po[0:P, :].rearrange("p (h q) -> p h q", h=16),
                                                                     in1=EXPA.unsqueeze(2).to_broadcast([P, 16, 64]), op=ALU.mult), R=bpo + [bd], W=[bt1])
                        kb.op("dve", lambda e, py=py: e.tensor_tensor(out=t1[0:P, :], in0=t1[0:P, :], in1=py[0:P, :], op=ALU.add), R=bpy + [bt1], W=[bt1])
                        kb.op("pool", lambda e: e.tensor_tensor(out=t3[0:P, :].rearrange("p (h q) -> p h q", h=16), in0=xs3, in1=D_b[0:P, :].unsqueeze(2).to_broadcast([P, 16, 64]), op=ALU.mult),
                              R=[bxs, bsm], W=[bt3])
                        kb.op("dve", lambda e: e.tensor_tensor(out=t1[0:P, :], in0=t1[0:P, :], in1=t3[0:P, :], op=ALU.add), R=[bt1, bt3], W=[bt1])
                        kb.op("dve", lambda e: e.tensor_tensor(out=t1[0:P, :], in0=t1[0:P, :], in1=szt[0:P, :], op=ALU.mult), R=[bt1, bsz], W=[bt1])
                        ssq, bs = self.srot()
                        for g in range(2):
                            kb.op("act", lambda e, g=g, ssq=ssq: e.activation(out=t3[0:P, g * 512:(g + 1) * 512], in_=t1[0:P, g * 512:(g + 1) * 512], func=AF.Square, accum_out=ssq[0:P, g:g + 1]),
                                  R=[bt1], W=[bt3, bs])
                        kb.op("act", lambda e, ssq=ssq: e.activation(out=ssq[0:P, 2:4], in_=ssq[0:P, 0:2], func=AF.Sqrt, scale=1.0 / 512, bias=self.epsb[0:P, 0:1]), R=[bs, self.bconst], W=[bs])
                        kb.op("dve", lambda e, ssq=ssq: e.reciprocal(out=ssq[0:P, 2:4], in_=ssq[0:P, 2:4]), R=[bs], W=[bs])
                        for g in range(2):
                            kb.op("dve", lambda e, g=g, ssq=ssq: e.scalar_tensor_tensor(out=ynb[0:P, g * 512:(g + 1) * 512], in0=t1[0:P, g * 512:(g + 1) * 512], scalar=ssq[0:P, 2 + g:3 + g],
                                                                                       in1=ngb[0:P, g * 512:(g + 1) * 512], op0=ALU.mult, op1=ALU.mult), R=[bt1, bs, bsm], W=[bynb])
                        self.transpose_to(ynb, bynb, P, 8, lambda k: ynT[:, k, 0:P], bynT)
                        kb.dma("pool", self.YT[si].rearrange("(c p) t -> p c t", p=128)[:, :, r0:r0 + P], ynT[:, :, 0:P], R=[bynT], W=[bsc["y"]])
                        ps2, bps2 = self.pair()
                        for g in range(2):
                            kb.op("pe", lambda e, g=g, ps2=ps2: e.matmul(ps2[:, g * 512:(g + 1) * 512], lhsT=btok[0:P, g * 128:(g + 1) * 128], rhs=xdtw[0:P, g * 512:(g + 1) * 512], start=True, stop=True),
                                  R=[bbtok, bxdtw], W=bps2)
                        kb.op("dve", lambda e: e.tensor_tensor(out=hst[:].rearrange("p (h q) -> p h q", h=16), in0=hst[:].rearrange("p (h q) -> p h q", h=16),
                                                               in1=EL.unsqueeze(2).to_broadcast([128, 16, 64]), op=ALU.mult), R=[bhst, bd], W=[bhst])
                        kb.op("dve", lambda e, ps2=ps2: e.tensor_tensor(out=hst[:], in0=hst[:], in1=ps2[:], op=ALU.add), R=bps2 + [bhst], W=[bhst])
                        kb.op("act", lambda e: e.activation(out=hbf[:], in_=hst[:], func=AF.Copy), R=[bhst], W=[bhbf])
                        if gi == ngr - 1 and m == nt - 1:
                            for n3 in range(3):
                                pb, bb = self.bank()
                                for k in range(8):
                                    kb.op("pe", lambda e, k=k, pb=pb, n3=n3: e.matmul(pb[0:P, :], lhsT=hT[:, k, t0:t0 + P], rhs=w[:, k, 1024 + n3 * 512:1024 + (n3 + 1) * 512],
                                                                                     start=(k == 0), stop=(k == 7)), R=[bhT, bw], W=bb)
                                self.evac(xraw[0:P, n3 * 512:(n3 + 1) * 512], pb[0:P, :], bb, [bxr])
                            dsto = self.sconv_p[j] if sq["s"] is None else self.sconv_s[j, sq["s"]]
                            kb.dma("pool", dsto, xraw[P - 3:P, :], R=[bxr], W=[self.bout])
                    gens = [tile_gen(m) for m in range(nt)]
                    next(gens[0])
                    for m in range(nt):
                        if m + 1 < nt:
                            next(gens[m + 1])
                        next(gens[m], None)
                    if gi < ngr - 1:
                        kb.op("pool", lambda e: e.tensor_copy(out=xbcT[:, :, 0:4], in_=xbcT[:, :, GN:GN + 4]), R=[bxb], W=[bxb])
                if E1B_STOP <= 6:
                    continue
                pp, bpp = self.pair()
                for c in range(8):
                    kb.op("pe", lambda e, c=c, pp=pp: e.transpose(out=pp[:, c * 128:(c + 1) * 128], in_=hst[:, c * 128:(c + 1) * 128], identity=self.identf[:, :]),
                          R=[bhst, self.bconst], W=bpp)
                so, bso = self.xrot()
                kb.op("act", lambda e, pp=pp, so=so: e.activation(out=so[:], in_=pp[:], func=AF.Copy), R=bpp, W=[bso])
                dsts = self.ssm_p[j] if sq["s"] is None else self.ssm_s[j, sq["s"]]
                kb.dma("pool", dsts.rearrange("(c p) n -> p c n", p=128), so[:].rearrange("p (c n) -> p c n", c=8), R=[bso], W=[self.bout])
            kb.barrier()

    def phase_e2(self, l, j, S):
        kb = self.kb
        PAST, TS, T = self.PAST, self.TS, self.T
        lam_init = 0.8 - 0.6 * math.exp(-0.3 * l)
        TKmax = max(T, PAST + TS)
        with ExitStack() as es:
            save_banks = self.bank_list
            self.bank_list = [4, 5, 6, 7]
            bsm = Buf("e2small")
            lamt = self.sb(es, "lamt", [128, 256], F32)
            kb.dma("sp", lamt[:], self.attn_lambda[j:j + 1, :].partition_broadcast(128), W=[bsm])
            lsm = self.sb(es, "lsm", [128, 8], F32)
            lpr = self.sb(es, "lpr", [128, 128], F32)
            kb.op("dve", lambda e: e.tensor_tensor(out=lpr[:, 0:64], in0=lamt[:, 0:64], in1=lamt[:, 64:128], op=ALU.mult), R=[bsm], W=[bsm])
            kb.op("dve", lambda e: e.tensor_tensor(out=lpr[:, 64:128], in0=lamt[:, 128:192], in1=lamt[:, 192:256], op=ALU.mult), R=[bsm], W=[bsm])
            kb.op("dve", lambda e: e.reduce_sum(out=lsm[:, 0:2], in_=lpr[:].rearrange("p (a b) -> p a b", a=2), axis=AX.X), R=[bsm], W=[bsm])
            kb.op("act", lambda e: e.activation(out=lsm[:, 2:4], in_=lsm[:, 0:2], func=AF.Exp), R=[bsm], W=[bsm])
            kb.op("dve", lambda e: e.scalar_tensor_tensor(out=lsm[:, 4:5], in0=lsm[:, 3:4], scalar=-lam_init, in1=lsm[:, 2:3], op0=ALU.add, op1=ALU.subtract),
                  R=[bsm], W=[bsm])
            neglam = lsm[:, 4:5]
            subg = self.sb(es, "subg", [128, 1, 1], F32)
            self.load_fm(es, lambda c: subg[:, c, :], self.attn_subln_g[j:j + 1, :], 1, 1, bsm)
            kb.op("dve", lambda e: e.tensor_scalar(out=subg[:, 0, :], in0=subg[:, 0, :], scalar1=(1.0 - lam_init), scalar2=None, op0=ALU.mult), R=[bsm], W=[bsm])
            NKT = (TKmax + 127) // 128
            kvset = []
            for i in range(2):
                kvset.append(dict(kT=[self.sb(es, "kT%d_%d" % (m, i), [69, TKmax], BF16) for m in range(2)], bkT=Buf(),
                                  vh=self.sb(es, "vh%d" % i, [128, NKT, 128], BF16), bvh=Buf(),
                                  corrb=self.sb(es, "corrb%d" % i, [128, 128], BF16), bcorr=Buf()))
            GNmax = S[0]["GN"]
            qTr = self.rot("qT", es, [69, 2, GNmax], BF16, 2)
            ptr = self.rot("pt", es, [128, GNmax], BF16, 4)
            accr = self.rot("accs", es, [128, 4, GNmax], F32, 2)
            ot = self.sb(es, "e2o", [128, GNmax], F32); bot = Buf()
            o1 = self.sb(es, "e2o1", [128, GNmax], F32); bo1 = Buf()
            rr = self.sb(es, "e2r", [128, GNmax], F32); brr = Buf()
            kb.op("pool", lambda e: e.memset(o1[:], 0.0), W=[bo1])
            onr = self.rot("e2on", es, [128, GNmax], BF16, 2)
            acc = [self.PP[0][:, 0:512], self.PP[0][:, 512:1024], self.PP[1][:, 0:512], self.PP[1][:, 512:1024]]
            bacc = [[self.pb[i]] for i in range(4)]
            for _ in range(2):
                qT, bqT = qTr()
                kb.op("pool", lambda e, qT=qT: e.memset(qT[:], 0.0), W=[bqT])

            def load_head(si, sq, h, ks):
                bsc = self.bscr[si]
                Tq = sq["T"]
                kp0 = 0 if sq["s"] is None else PAST
                Tk = kp0 + Tq
                kT, bkT, vh, bvh, corrb, bcorr = ks["kT"], ks["bkT"], ks["vh"], ks["bvh"], ks["corrb"], ks["bcorr"]
                for m in range(2):
                    kb.dma("sp", kT[m][0:64, 0:Tk], self.KT[si][h, m * 64:(m + 1) * 64, 0:Tk], R=[bsc["k"]], W=[bkT])
                for a in range(0, Tk, 1024):
                    wd = min(1024, Tk - a)
                    st, bst = self.wstage()
                    kb.dma("sp", st[64:69, 0:wd], self.cst_kaug[h, :, a:a + wd], W=[bst])
                    for m in range(2):
                        kb.op("pool", lambda e, m=m, st=st, a=a, wd=wd: e.tensor_copy(out=kT[m][64:69, a:a + wd], in_=st[64:69, 0:wd]), R=[bst], W=[bkT])
                nfull = Tk // 128
                kb.dma("sp", vh[:, 0:nfull, :], self.VB[si][0:nfull * 128, h * 128:(h + 1) * 128].rearrange("(j p) e -> p j e", p=128), R=[bsc["v"]], W=[bvh])
                if Tk % 128:
                    kb.dma("sp", vh[0:Tk % 128, nfull, :], self.VB[si][nfull * 128:Tk, h * 128:(h + 1) * 128], R=[bsc["v"]], W=[bvh])
                st, bst = self.wstage()
                kb.dma("sp", st[:, 0:128], self.cst_corr[h], W=[bst])
                kb.op("pool", lambda e, st=st: e.tensor_copy(out=corrb[:], in_=st[:, 0:128]), R=[bst], W=[bcorr])

            pending = [None]
            heads = [(si, sq, h) for si, sq in enumerate(S) for h in range(8)]
            load_head(heads[0][0], heads[0][1], heads[0][2], kvset[0])
            for hi, (si, sq, h) in enumerate(heads):
                ks = kvset[hi % 2]
                kT, bkT, vh, bvh, corrb, bcorr = ks["kT"], ks["bkT"], ks["vh"], ks["bvh"], ks["corrb"], ks["bcorr"]
                bsc = self.bscr[si]
                P, GN, Tq = sq["P"], sq["GN"], sq["T"]
                kp0 = 0 if sq["s"] is None else PAST
                Tk = kp0 + Tq
                def load_q(g0, si=si, h=h, GN=GN, kp0=kp0, bsc=bsc):
                    qT, bqT = qTr()
                    if GN < 128:
                        kb.op("pool", lambda e, qT=qT: e.memset(qT[:, :, GN:128], 0.0), W=[bqT])
                    for m in range(2):
                        kb.dma("sp", qT[0:64, m, 0:GN], self.QT[si][h, m * 64:(m + 1) * 64, g0:g0 + GN], R=[bsc["q"]], W=[bqT])
                    st, bst = self.wstage()
                    kb.dma("sp", st[64:69, 0:GN], self.cst_qaug[h, :, kp0 + g0:kp0 + g0 + GN], W=[bst])
                    for m in range(2):
                        kb.op("pool", lambda e, m=m, st=st, qT=qT: e.tensor_copy(out=qT[64:69, m, 0:GN], in_=st[64:69, 0:GN]), R=[bst], W=[bqT])
                    return qT, bqT

                glist = list(range(0, Tq, GN))
                nextq = load_q(glist[0])
                for gidx, g0 in enumerate(glist):
                    qT, bqT = nextq
                    if gidx + 1 < len(glist):
                        nextq = load_q(glist[gidx + 1])
                    if gidx == 0 and hi + 1 < len(heads):
                        load_head(heads[hi + 1][0], heads[hi + 1][1], heads[hi + 1][2], kvset[(hi + 1) % 2])
                    GNq = max(GN, 128)
                    tiles = []
                    if sq["s"] is None:
                        i0, nt = g0 // 128, GN // 128
                        for jt in range(i0 + nt):
                            if jt < i0:
                                tiles.append((jt, 128, [(0, GN, False)]))
                            else:
                                c0 = (jt - i0) * 128
                                rg = [(c0, c0 + 128, True)]
                                if c0 + 128 < GN:
                                    rg.append((c0 + 128, GN, False))
                                tiles.append((jt, 128, rg))
                    else:
                        for jt in range(PAST // 128):
                            tiles.append((jt, 128, [(0, GNq, False)]))
                        tiles.append((PAST // 128, Tq, [(0, GNq, True)]))

                    def st_exp(ti):
                        jt, nk, rg = tiles[ti]
                        k0 = jt * 128
                        c0 = rg[0][0]
                        pts = []
                        for m in range(2):
                            ps, bps = self.bank()
                            for (a, b, isd) in rg:
                                kb.op("pe", lambda e, ps=ps, m=m, a=a, b=b, isd=isd: e.matmul(ps[0:nk, a:b], lhsT=kT[m][:, k0:k0 + nk], rhs=qT[:, m, a:b],
                                                                                         start=True, stop=(not isd)), R=[bkT, bqT], W=bps)
                                if isd:
                                    kb.op("pe", lambda e, ps=ps, a=a, b=b: e.matmul(ps[0:nk, a:b], lhsT=self.identb[0:nk, 0:nk], rhs=corrb[0:nk, 0:b - a],
                                                                                 start=False, stop=True), R=[bcorr, self.bconst], W=bps)
                            pt, bpt = ptr()
                            kb.op("act", lambda e, pt=pt, ps=ps: e.activation(out=pt[0:nk, c0:GNq], in_=ps[0:nk, c0:GNq], func=AF.Exp, scale=0.125), R=bps, W=[bpt])
                            pts.append((pt, bpt))
                        return pts

                    def pv(ti, pts):
                        jt, nk, rg = tiles[ti]
                        for m in range(2):
                            pt, bpt = pts[m]
                            for ri, (a, b, isd) in enumerate(rg):
                                first = (ti == 0 and ri == 0)
                                lastm = (ti == len(tiles) - 1 and ri == len(rg) - 1)
                                kb.op("pe", lambda e, pt=pt, m=m, a=a, b=b: e.matmul(acc[m][:, a:b], lhsT=vh[0:nk, jt, :], rhs=pt[0:nk, a:b],
                                                                                         start=first, stop=lastm), R=[bvh, bpt], W=bacc[m])
                                kb.op("pe", lambda e, pt=pt, m=m, a=a, b=b: e.matmul(acc[2 + m][:, a:b], lhsT=self.onesb[0:nk, :], rhs=pt[0:nk, a:b],
                                                                                         start=first, stop=lastm), R=[self.bconst, bpt], W=bacc[2 + m])

                    cur = st_exp(0)
                    for ti in range(len(tiles)):
                        nxt = st_exp(ti + 1) if ti + 1 < len(tiles) else None
                        pv(ti, cur)
                        cur = nxt
                        if ti == 2 and pending[0] is not None:
                            pending[0]()
                            pending[0] = None
                    if pending[0] is not None:
                        pending[0]()
                        pending[0] = None
                    ac, bac = accr()
                    for i in range(4):
                        if i < 2:
                            kb.op("act", lambda e, i=i, ac=ac: e.activation(out=ac[:, i, 0:GN], in_=acc[i][:, 0:GN], func=AF.Copy), R=bacc[i], W=[bac])
                        else:
                            kb.op("dve", lambda e, i=i, ac=ac: e.tensor_copy(out=ac[:, i, 0:GN], in_=acc[i][:, 0:GN]), R=bacc[i], W=[bac])
                    kb.op("dve", lambda e, ac=ac: e.reciprocal(out=rr[:, 0:GN], in_=ac[:, 2, 0:GN]), R=[bac], W=[brr])
                    kb.op("dve", lambda e, ac=ac: e.tensor_tensor(out=ot[:, 0:GN], in0=ac[:, 0, 0:GN], in1=rr[:, 0:GN], op=ALU.mult), R=[bac, brr], W=[bot])
                    kb.op("dve", lambda e, ac=ac: e.reciprocal(out=rr[:, 0:GN], in_=ac[:, 3, 0:GN]), R=[bac], W=[brr])
                    kb.op("dve", lambda e, ac=ac: e.tensor_tensor(out=o1[:, 0:GN], in0=ac[:, 1, 0:GN], in1=rr[:, 0:GN], op=ALU.mult), R=[bac, brr], W=[bo1])
                    kb.op("dve", lambda e: e.scalar_tensor_tensor(out=ot[:, 0:GN], in0=o1[:, 0:GN], scalar=neglam, in1=ot[:, 0:GN], op0=ALU.mult, op1=ALU.add),
                          R=[bo1, bot, bsm], W=[bot])
                    kb.op("pool", lambda e: e.tensor_tensor(out=o1[:, 0:GN], in0=ot[:, 0:GN], in1=ot[:, 0:GN], op=ALU.mult), R=[bot], W=[bo1])

                    def part_b(si=si, h=h, g0=g0, GN=GN, GNq=GNq, bsc=bsc):
                        pm, bm = self.bank()
                        kb.op("pe", lambda e, pm=pm: e.matmul(pm[:, 0:GNq], lhsT=self.m128[:], rhs=o1[:, 0:GNq], start=True, stop=True), R=[bo1, self.bconst], W=bm)
                        kb.op("act", lambda e, pm=pm: e.activation(out=rr[:, 0:GN], in_=pm[:, 0:GN], func=AF.Ln, bias=self.epsb[:, 0:1]), R=bm + [self.bconst], W=[brr])
                        kb.op("act", lambda e: e.activation(out=rr[:, 0:GN], in_=rr[:, 0:GN], func=AF.Exp, scale=-0.5), R=[brr], W=[brr])
                        on, bon = onr()
                        kb.op("dve", lambda e, on=on: e.scalar_tensor_tensor(out=on[:, 0:GN], in0=ot[:, 0:GN], scalar=subg[:, 0, :], in1=rr[:, 0:GN], op0=ALU.mult, op1=ALU.mult),
                              R=[bot, brr, bsm], W=[bon])
                        kb.dma("pool", self.OT[si][h * 128:(h + 1) * 128, g0:g0 + GN], on[:, 0:GN], R=[bon], W=[bsc["o"]])
                    pending[0] = part_b
            if pending[0] is not None:
                pending[0]()
                pending[0] = None
            self.bank_list = save_banks
            kb.barrier()

    def phase_e3(self, l, j, S):
        kb = self.kb
        stage = 2 * l
        with ExitStack() as es:
            wo = self.sb(es, "wo", [128, 16, D], BF16); bwo = Buf()
            self.load_w(wo, self.hyb_w_out[j].rearrange("(k p) n -> k p n", p=128), 16, D, bwo)
            GNmax = S[0]["GN"]
            oyr = self.rot("oy", es, [128, 16, GNmax], BF16, 2)
            for si, sq in enumerate(S):
                gs, sh, gg, bbc = self.make_bc(l, 0, sq["ci"])
                src, dst = self.xio(sq, stage)
                bx = self.bscr[si]["x"]
                bsc = self.bscr[si]
                P, GN, Tq = sq["P"], sq["GN"], sq["T"]
                def load_oy(g0, si=si, GN=GN, bsc=bsc):
                    t, bt = oyr()
                    kb.dma("sp", t[:, 0:8, 0:GN], self.OT[si].rearrange("(c p) t -> p c t", p=128)[:, :, g0:g0 + GN], R=[bsc["o"]], W=[bt])
                    kb.dma("sp", t[:, 8:16, 0:GN], self.YT[si].rearrange("(c p) t -> p c t", p=128)[:, :, g0:g0 + GN], R=[bsc["y"]], W=[bt])
                    return t, bt

                glist = list(range(0, Tq, GN))
                nxt = load_oy(glist[0])
                for gidx, g0 in enumerate(glist):
                    nt = GN // P
                    t, bt = nxt
                    if gidx + 1 < len(glist):
                        nxt = load_oy(glist[gidx + 1])
                    for m in range(nt):
                        pp, bpp = self.pair()
                        for nh in range(2):
                            for c in range(16):
                                kb.op("pe", lambda e, c=c, nh=nh, m=m, pp=pp, t=t: e.matmul(pp[0:P, nh * 512:(nh + 1) * 512], lhsT=t[:, c, m * P:(m + 1) * P],
                                                                                          rhs=wo[:, c, nh * 512:(nh + 1) * 512], start=(c == 0), stop=(c == 15)),
                                      R=[bt, bwo], W=bpp)
                        r0 = g0 + m * P
                        self.post(pp, bpp, P, src[r0:r0 + P, :], dst[r0:r0 + P, :], gg, bbc, bx)
            kb.barrier()

    def phase_conf(self, l, j, S):
        kb = self.kb
        stage = 2 * l
        with ExitStack() as es:
            win = self.sb(es, "cwin", [128, 8, 2 * D], BF16); bwin = Buf()
            wout = self.sb(es, "cwout", [128, 8, D], BF16); bwout = Buf()
            self.load_w(win, self.conf_w_in[j].rearrange("(k p) n -> k p n", p=128), 8, 2 * D, bwin)
            self.load_w(wout, self.conf_w_out[j].rearrange("(k p) n -> k p n", p=128), 8, D, bwout)
            bsm = Buf("confsmall")
            dwf = self.sb(es, "dwf", [128, 8, 31], F32)
            self.load_fm(es, lambda c: dwf[:, c, :], self.conf_dw_w[j], 31, 8, bsm)
            bin_ = self.sb(es, "binf", [128, 16, 1], F32)
            self.load_fm(es, lambda c: bin_[:, c, :], self.conf_b_in[j:j + 1, :], 1, 16, bsm)
            vecs = self.sb(es, "cvecs", [128, 3, 8, 1], F32)
            for i, src in enumerate((self.conf_dw_b, self.conf_ln_g, self.conf_ln_b)):
                self.load_fm(es, lambda c, i=i: vecs[:, i, c, :], src[j:j + 1, :], 1, 8, bsm)
            diag = self.sb(es, "cdiag", [128, 8 * 31, 128], BF16)
            kb.op("dve", lambda e: e.tensor_tensor(out=diag[:], in0=self.identf.unsqueeze(1).to_broadcast([128, 248, 128]),
                                                   in1=dwf[:].rearrange("p c k -> p (c k)").unsqueeze(2).to_broadcast([128, 248, 128]),
                                                   op=ALU.mult), R=[bsm, self.bconst], W=[bsm])
            bor = self.sb(es, "bor", [1, D], F32)
            kb.dma("sp", bor[:], self.conf_b_out[j:j + 1, :], W=[bsm])
            borb = self.sb(es, "borb", [1, D], BF16)
            kb.op("pool", lambda e: e.tensor_copy(out=borb[:], in_=bor[:]), R=[bsm], W=[bsm])
            if CONF_STOP <= 1:
                kb.barrier()
                return
            GNmax = S[0]["GN"]
            hT = self.sb(es, "chT", [128, 8, GNmax], BF16); bhT = Buf()
            uT = self.sb(es, "uT", [128, 8, 30 + GNmax], BF16); buT = Buf()
            uF = self.sb(es, "uF", [128, 8, 32], F32); buF = Buf()
            yT = self.sb(es, "yT", [128, 8, GNmax], F32); byT = Buf()
            ynT = self.sb(es, "ynT", [128, 8, GNmax], BF16); bynT = Buf()
            sgr = self.rot("csg", es, [128, GNmax], F32, 1)
            mu = self.sb(es, "cmu", [128, GNmax], F32); bmu = Buf()
            rs = self.sb(es, "crs", [128, GNmax], F32); brs = Buf()
            kb.op("pool", lambda e: e.memset(hT[:], 0.0), W=[bhT])
            kb.op("pool", lambda e: e.memset(uT[:], 0.0), W=[buT])
            kb.op("pool", lambda e: e.memset(yT[:], 0.0), W=[byT])
            kb.op("pool", lambda e: e.memset(ynT[:], 0.0), W=[bynT])
            for si, sq in enumerate(S):
                if CONF_SEQS is not None and si not in CONF_SEQS:
                    continue
                gs, sh, gg, bbc = self.make_bc(l, 0, sq["ci"])
                src, dst = self.xio(sq, stage)
                bx = self.bscr[si]["x"]
                P, GN, Tq = sq["P"], sq["GN"], sq["T"]
                GNp = max(GN, 128)
                if sq["s"] is None:
                    kb.op("pool", lambda e: e.memset(uT[:, :, 0:30], 0.0), W=[buT])
                else:
                    self.load_fm(es, lambda c: uT[:, c, 0:30], self.st_cconv[j, sq["s"]], 30, 8, buT)
                ngr = Tq // GN
                nt = GN // P
                nl = min(30, GN)

                def stage_a(gi):
                    g0 = gi * GN
                    last = gi == ngr - 1
                    for m in range(nt):
                        self.prelude_x(src[g0 + m * P:g0 + (m + 1) * P, :], bx, P, hT, bhT, m * P, gs, sh, bbc)
                    nl = min(30, GN)
                    for c in range(8):
                        pa, ba = self.bank()
                        pg, bg = self.bank()
                        for k in range(8):
                            kb.op("pe", lambda e, k=k, pa=pa, c=c: e.matmul(pa[:, 0:GNp], lhsT=win[:, k, c * 128:(c + 1) * 128], rhs=hT[:, k, 0:GNp],
                                                                           start=(k == 0), stop=(k == 7)), R=[bwin, bhT], W=ba)
                        for k in range(8):
                            kb.op("pe", lambda e, k=k, pg=pg, c=c: e.matmul(pg[:, 0:GNp], lhsT=win[:, k, D + c * 128:D + (c + 1) * 128], rhs=hT[:, k, 0:GNp],
                                                                           start=(k == 0), stop=(k == 7)), R=[bwin, bhT], W=bg)
                        sg, bsg = sgr()
                        kb.op("act", lambda e, sg=sg, pg=pg, c=c: e.activation(out=sg[:, 0:GN], in_=pg[:, 0:GN], func=AF.Sigmoid, bias=bin_[:, 8 + c, :]),
                              R=bg + [bsm], W=[bsg])
                        kb.op("dve", lambda e, sg=sg, pa=pa, c=c: e.scalar_tensor_tensor(out=uT[:, c, 30:30 + GN], in0=pa[:, 0:GN], scalar=bin_[:, c, :],
                                                                                        in1=sg[:, 0:GN], op0=ALU.add, op1=ALU.mult),
                              R=ba + [bsg, bsm], W=[buT])
                        if last:
                            kb.op("dve", lambda e, sg=sg, pa=pa, c=c: e.scalar_tensor_tensor(out=uF[:, c, 0:nl], in0=pa[:, GN - nl:GN], scalar=bin_[:, c, :],
                                                                                            in1=sg[:, GN - nl:GN], op0=ALU.add, op1=ALU.mult),
                                  R=ba + [bsg, bsm], W=[buF])

                def stage_conv(gi):
                    for c in range(8):
                        py, by_ = self.bank()
                        for k in range(31):
                            kb.op("pe", lambda e, k=k, py=py, c=c: e.matmul(py[:, 0:GNp], lhsT=diag[:, c * 31 + k, :], rhs=uT[:, c, k:k + GNp],
                                                                           start=(k == 0), stop=(k == 30)), R=[bsm, buT], W=by_)
                        kb.op("act", lambda e, py=py, c=c: e.activation(out=yT[:, c, 0:GN], in_=py[:, 0:GN], func=AF.Identity, bias=vecs[:, 0, c, :]),
                              R=by_ + [bsm], W=[byT])

                def stage_c(gi):
                    g0 = gi * GN
                    pm, bm = self.bank()
                    for c in range(8):
                        kb.op("pe", lambda e, c=c, pm=pm: e.matmul(pm[:, 0:GNp], lhsT=self.m1024[:], rhs=yT[:, c, 0:GNp], start=(c == 0), stop=(c == 7)),
                              R=[byT, self.bconst], W=bm)
                    kb.op("act", lambda e, pm=pm: e.activation(out=mu[:, 0:GN], in_=pm[:, 0:GN], func=AF.Copy), R=bm, W=[bmu])
                    kb.op("dve", lambda e: e.tensor_tensor(out=yT[:, :, 0:GN], in0=yT[:, :, 0:GN], in1=mu[:, 0:GN].unsqueeze(1).to_broadcast([128, 8, GN]),
                                                           op=ALU.subtract), R=[byT, bmu], W=[byT])
                    kb.op("pool", lambda e: e.tensor_tensor(out=ynT[:, :, 0:GN], in0=yT[:, :, 0:GN], in1=yT[:, :, 0:GN], op=ALU.mult), R=[byT], W=[bynT])
                    pv, bv = self.bank()
                    for c in range(8):
                        kb.op("pe", lambda e, c=c, pv=pv: e.matmul(pv[:, 0:GNp], lhsT=self.onesb[:, :], rhs=ynT[:, c, 0:GNp], start=(c == 0), stop=(c == 7)),
                              R=[bynT, self.bconst], W=bv)
                    kb.op("act", lambda e, pv=pv: e.activation(out=rs[:, 0:GN], in_=pv[:, 0:GN], func=AF.Sqrt, scale=1.0 / 1024, bias=self.epsb[:, 0:1]), R=bv + [self.bconst], W=[brs])
                    kb.op("dve", lambda e: e.reciprocal(out=rs[:, 0:GN], in_=rs[:, 0:GN]), R=[brs], W=[brs])
                    kb.op("dve", lambda e: e.tensor_tensor(out=yT[:, :, 0:GN], in0=yT[:, :, 0:GN], in1=rs[:, 0:GN].unsqueeze(1).to_broadcast([128, 8, GN]),
                                                           op=ALU.mult), R=[byT, brs], W=[byT])
                    for c in range(8):
                        kb.op("act", lambda e, c=c: e.activation(out=ynT[:, c, 0:GN], in_=yT[:, c, 0:GN], func=AF.Silu, scale=vecs[:, 1, c, :], bias=vecs[:, 2, c, :]),
                              R=[byT, bsm], W=[bynT])
                    for m in range(nt):
                        pp, bpp = self.pair()
                        for nh in range(2):
                            for c in range(8):
                                kb.op("pe", lambda e, c=c, nh=nh, m=m, pp=pp: e.matmul(pp[0:P, nh * 512:(nh + 1) * 512], lhsT=ynT[:, c, m * P:(m + 1) * P],
                                                                                     rhs=wout[:, c, nh * 512:(nh + 1) * 512], start=(c == 0), stop=False),
                                      R=[bynT, bwout], W=bpp)
                            kb.op("pe", lambda e, nh=nh, pp=pp: e.matmul(pp[0:P, nh * 512:(nh + 1) * 512], lhsT=self.onesb[0:1, 0:P],
                                                                        rhs=borb[0:1, nh * 512:(nh + 1) * 512], start=False, stop=True),
                                  R=[bsm, self.bconst], W=bpp)
                        r0 = g0 + m * P
                        self.post(pp, bpp, P, src[r0:r0 + P, :], dst[r0:r0 + P, :], gg, bbc, bx)

                stage_a(0)
                for gi in range(ngr):
                    stage_conv(gi)
                    if gi + 1 < ngr:
                        kb.op("pool", lambda e: e.tensor_copy(out=uT[:, :, 0:30], in_=uT[:, :, GN:GN + 30]), R=[buT], W=[buT])
                        stage_a(gi + 1)
                    stage_c(gi)
                if CONF_STOP <= 5:
                    continue
                nl = min(30, GN)
                pp, bpp = self.pair()
                for c in range(8):
                    kb.op("pe", lambda e, c=c, pp=pp: e.transpose(out=pp[0:nl, c * 128:(c + 1) * 128], in_=uF[:, c, 0:nl], identity=self.identf[:, :]),
                          R=[buF, self.bconst], W=bpp)
                ot, bo = self.xrot()
                kb.op("act", lambda e, pp=pp, ot=ot: e.activation(out=ot[0:nl, :], in_=pp[0:nl, :], func=AF.Copy), R=bpp, W=[bo])
                if sq["s"] is None:
                    kb.dma("pool", self.cconv_p[j], ot[0:30, :], R=[bo], W=[self.bout])
                else:
                    s_ = sq["s"]
                    kb.dma("pool", self.cconv_s[j, s_, 30 - nl:30, :], ot[0:nl, :], R=[bo], W=[self.bout])
                    if nl < 30:
                        kb.dma("pool", self.cconv_s[j, s_, 0:30 - nl, :], self.st_cconv[j, s_, nl:30, :], W=[self.bout])
            kb.barrier()


_PROG = {}


def _get_prog(T, depth):
    key = (T, depth)
    if key not in _PROG:
        _PROG[key] = Prog(T=T, depth=depth)
    return _PROG[key]


def make_in_maps(inp, T, depth):
    NE, NO = (depth + 1) // 2, depth // 2
    cf, kaug, qaug, corr = _consts()
    f = lambda a: np.ascontiguousarray(np.asarray(a, dtype=np.float32))
    shared = {}
    for nm in ("ada_w", "ada_b", "norm_g", "ffn_w_up", "ffn_w_down", "hyb_w_in", "attn_subln_g", "ssm_conv_w", "ssm_conv_b",
               "ssm_dt_bias", "ssm_a_log", "ssm_d", "ssm_norm_g", "hyb_w_out", "conf_w_in", "conf_b_in", "conf_dw_w",
               "conf_dw_b", "conf_ln_g", "conf_ln_b", "conf_w_out", "conf_b_out"):
        shared[nm] = f(inp[nm])
    shared["attn_lambda"] = f(inp["attn_lambda"]).reshape(NE, 256)
    shared["cst_f"], shared["cst_kaug"], shared["cst_qaug"], shared["cst_corr"] = cf, kaug, qaug, corr
    maps = []
    for c in range(NCORES):
        m = dict(shared)
        m["x_p"] = f(inp["x_prompt"][c])
        m["x_s"] = f(inp["x_sample"][2 * c:2 * c + 2])
        m["c_all"] = f(np.concatenate([inp["c_prompt"][c:c + 1], inp["c_sample"][2 * c:2 * c + 2]], 0))
        m["cache_k"] = f(np.asarray(inp["cache_attn_k"])[:, 2 * c:2 * c + 2].reshape(NE, 2, -1, D))
        m["cache_v"] = f(np.asarray(inp["cache_attn_v"])[:, 2 * c:2 * c + 2].reshape(NE, 2, -1, D))
        m["st_sconv"] = f(np.asarray(inp["state_ssm_conv"])[:, 2 * c:2 * c + 2])
        m["st_ssm"] = f(np.asarray(inp["state_ssm"])[:, 2 * c:2 * c + 2].reshape(NE, 2, 1024, 128))
        m["st_cconv"] = f(np.asarray(inp["state_conf_conv"])[:, 2 * c:2 * c + 2])
        maps.append(m)
    return maps


def gather(res, T, depth, TS=16):
    NE, NO = (depth + 1) // 2, depth // 2
    R = res.results
    st = lambda k, ax=0: np.stack([np.asarray(r[k]) for r in R], ax)
    cat = lambda k, ax: np.concatenate([np.asarray(r[k]) for r in R], ax)
    y_p = st("y_p")
    y_s = cat("y_s", 0)
    k_p = st("k_p", 1).reshape(NE, NCORES, T, 8, 128)
    v_p = st("v_p", 1).reshape(NE, NCORES, T, 8, 128)
    sconv_p = st("sconv_p", 1)
    ssm_p = st("ssm_p", 1).reshape(NE, NCORES, 16, 64, 128)
    cconv_p = st("cconv_p", 1)
    k_s = cat("k_s", 1).reshape(NE, 2 * NCORES, TS, 8, 128)
    v_s = cat("v_s", 1).reshape(NE, 2 * NCORES, TS, 8, 128)
    sconv_s = cat("sconv_s", 1)
    ssm_s = cat("ssm_s", 1).reshape(NE, 2 * NCORES, 16, 64, 128)
    cconv_s = cat("cconv_s", 1)
    return (y_p, y_s, k_p, v_p, sconv_p, ssm_p, cconv_p, k_s, v_s, sconv_s, ssm_s, cconv_s)


def kernel(**inputs):
    T = int(np.asarray(inputs["x_prompt"]).shape[1])
    depth = int(np.asarray(inputs["ada_w"]).shape[0])
    prog = _get_prog(T, depth)
    maps = make_in_maps(inputs, T, depth)
    res = run_bass_kernel_spmd(prog.nc, maps, core_ids=list(range(NCORES)))
    outs = gather(res, T, depth)
    return tuple(np.ascontiguousarray(o, dtype=np.float32) for o in outs)
```

```python
import math
import numpy as np
import concourse.bass as bass
import concourse.mybir as mybir
from concourse.bass_utils import run_bass_kernel_spmd
from contextlib import ExitStack

F32 = mybir.dt.float32
BF16 = mybir.dt.bfloat16
AF = mybir.ActivationFunctionType
ALU = mybir.AluOpType
AX = mybir.AxisListType

D = 1024
DFF = 2816
DIN = 5648
EPS = 1e-6
NCORES = 8
SKIP = set()
DEBUG = False
CONF_STOP = 99
E1B_STOP = 99
CONF_SEQS = None


class Buf:
    __slots__ = ("name", "w", "r", "excl")

    def __init__(self, name="", excl=False):
        self.name = name
        self.w = None
        self.r = {}
        self.excl = excl


class KB:
    def __init__(self, nc, es, n_lanes=8):
        self.nc = nc
        self.eng = {"pe": nc.tensor, "act": nc.scalar, "dve": nc.vector, "pool": nc.gpsimd, "sp": nc.sync}
        self.sem, self.cnt, self.mult = {}, {}, {}
        for e in self.eng:
            self.sem[e] = es.enter_context(nc.semaphore("s_" + e))
            self.cnt[e] = 0
            self.mult[e] = 1
        self.lanes = {}
        for q in ("sp", "pool"):
            ls = []
            for i in range(n_lanes):
                key = "L%s%d" % (q, i)
                self.sem[key] = es.enter_context(nc.semaphore("s_" + key))
                self.cnt[key] = 0
                self.mult[key] = 16
                ls.append(key)
            self.lanes[q] = ls
        self.lane_rr = {q: 0 for q in self.lanes}
        self.known = {e: {} for e in self.eng}
        self.nins = 0
        self.nwait = 0

    def _need(self, deps, key, k):
        if deps.get(key, 0) < k:
            deps[key] = k

    def _collect(self, R, W, eng=None):
        deps = {}
        for b in R:
            if b.w is not None:
                self._need(deps, b.w[0], b.w[1])
            if b.excl:
                for e, k in b.r.items():
                    if e != eng:
                        self._need(deps, e, k)
        for b in W:
            if b.w is not None:
                self._need(deps, b.w[0], b.w[1])
            for e, k in b.r.items():
                self._need(deps, e, k)
        return deps

    def _emit_waits(self, e, deps):
        kn = self.known[e]
        for key, k in deps.items():
            if key == e and e == "pe":
                continue
            if kn.get(key, 0) >= k:
                continue
            self.eng[e].wait_ge(self.sem[key], k * self.mult[key])
            kn[key] = k
            self.nwait += 1

    def _mark(self, key, k, R, W):
        for b in R:
            if b.r.get(key, 0) < k:
                b.r[key] = k
        for b in W:
            b.w = (key, k)
            b.r = {}

    def op(self, e, fn, R=(), W=()):
        self._emit_waits(e, self._collect(R, W, e))
        ins = fn(self.eng[e])
        self.cnt[e] += 1
        ins.then_inc(self.sem[e], 1)
        self._mark(e, self.cnt[e], R, W)
        self.nins += 1

    def dma(self, q, out, in_, R=(), W=()):
        ls = self.lanes[q]
        lane = ls[self.lane_rr[q] % len(ls)]
        self.lane_rr[q] += 1
        deps = self._collect(R, W)
        if self.cnt[lane] > 0:
            self._need(deps, lane, self.cnt[lane])
        self._emit_waits(q, deps)
        ins = self.eng[q].dma_start(out=out, in_=in_)
        self.cnt[lane] += 1
        ins.then_inc(self.sem[lane], 16)
        self._mark(lane, self.cnt[lane], R, W)
        self.nins += 1

    def barrier(self):
        for e in self.eng:
            deps = {k2: self.cnt[k2] for k2 in self.cnt if self.cnt[k2] > 0 and k2 != e}
            self._emit_waits(e, deps)


def _consts():
    c = {}
    i = np.arange(128)
    c["ident"] = np.eye(128, dtype=np.float32)
    c["ones"] = np.ones((128, 128), np.float32)
    c["trile"] = (i[:, None] <= i[None, :]).astype(np.float32)
    c["ugt"] = (i[:, None] > i[None, :]).astype(np.float32)
    cf = np.stack([c["ident"], c["ones"], c["trile"], c["ugt"]], 0)
    pos = np.arange(8192)
    kaug = np.zeros((8, 5, 8192), np.float32)
    qaug = np.zeros((8, 5, 8192), np.float32)
    corr = np.zeros((8, 128, 128), np.float32)
    for h in range(8):
        s8 = 8.0 * 2.0 ** (-(h + 1))
        ph, pl = (pos // 128).astype(np.float32), (pos % 128).astype(np.float32)
        qaug[h, 0] = -s8 * 128.0 * ph
        qaug[h, 1] = -s8 * pl
        qaug[h, 2] = 0.0
        qaug[h, 3] = 1.0
        qaug[h, 4] = 1.0
        kaug[h, 0] = 1.0
        kaug[h, 1] = 1.0
        kaug[h, 2] = 1.0
        kaug[h, 3] = s8 * 128.0 * ph
        kaug[h, 4] = s8 * pl
        kk, qq = i[:, None], i[None, :]
        cm = np.where(kk > qq, -2.0 * s8 * (kk - qq), 0.0)
        cm = np.where((kk // 64) > (qq // 64), -240000.0, cm)
        corr[h] = cm
    return cf, kaug, qaug, corr


class Prog:
    def __init__(self, T=8192, depth=4, TS=16, PAST=1024):
        self.T, self.depth, self.TS, self.PAST = T, depth, TS, PAST
        self.NE = (depth + 1) // 2
        self.NO = depth // 2
        self.nc = bass.Bass("TRN2", target_bir_lowering=False)
        self.build()

    def din(self, name, shape, dt=F32):
        return self.nc.dram_tensor(name, list(shape), dt, kind="ExternalInput").ap()

    def dout(self, name, shape, dt=F32):
        return self.nc.dram_tensor(name, list(shape), dt, kind="ExternalOutput").ap()

    def dscr(self, name, shape, dt=F32):
        return self.nc.dram_tensor(name, list(shape), dt, kind="Internal").ap()

    def sb(self, es, name, shape, dt):
        self._uid += 1
        return es.enter_context(self.nc.sbuf_tensor("%s_%d" % (name, self._uid), list(shape), dt))

    def bank(self):
        i = self.bank_list[self.bank_rr % len(self.bank_list)]
        self.bank_rr += 1
        return self.PP[i // 2][:, (i % 2) * 512:(i % 2) * 512 + 512], [self.pb[i]]

    def pair(self):
        i = self.pair_list[self.pair_rr % len(self.pair_list)]
        self.pair_rr += 1
        return self.PP[i], [self.pb[2 * i], self.pb[2 * i + 1]]

    def rot(self, key, es, shape, dt, n=2):
        tiles = [(self.sb(es, key, shape, dt), Buf(key)) for _ in range(n)]
        st = {"i": 0}

        def nxt():
            t = tiles[st["i"] % n]
            st["i"] += 1
            return t
        return nxt

    def load_fm(self, es, dst_fn, src, R, nch, bdst, q="sp", pad=0):
        kb = self.kb
        Rp = R + pad
        for b0 in range(0, nch, 8):
            nb = min(8, nch - b0)
            st, bst = self.wstage()
            if pad:
                kb.op("pool", lambda e, st=st: e.memset(st[0:Rp, :], 0.0), W=[bst])
            kb.dma(q, st[pad:Rp, 0:nb * 128], src[:, b0 * 128:(b0 + nb) * 128], W=[bst])
            for c0 in range(0, nb, 4):
                pb, bb = self.bank()
                n4 = min(4, nb - c0)
                for c in range(c0, c0 + n4):
                    kb.op("pe", lambda e, c=c, pb=pb, st=st: e.transpose(out=pb[:, (c - c0) * 32:(c - c0) * 32 + Rp],
                                                            in_=st[0:Rp, c * 128:(c + 1) * 128],
                                                            identity=self.identf[0:Rp, 0:Rp]), R=[bst, self.bconst], W=bb)
                for c in range(c0, c0 + n4):
                    kb.op("act", lambda e, c=c, pb=pb: e.activation(out=dst_fn(b0 + c), in_=pb[:, (c - c0) * 32:(c - c0) * 32 + Rp],
                                                             func=AF.Copy), R=bb, W=[bdst])

    def load_w(self, dst, src3, nk, ncols, bdst, c0=0):
        kb = self.kb
        for k in range(nk):
            for a in range(0, ncols, 1024):
                w = min(1024, ncols - a)
                st, bst = self.wstage()
                kb.dma("sp", st[:, 0:w], src3[k, :, c0 + a:c0 + a + w], W=[bst])
                kb.op("pool", lambda e, st=st, k=k, a=a, w=w: e.tensor_copy(out=dst[:, k, a:a + w], in_=st[:, 0:w]),
                      R=[bst], W=[bdst])

    def prelude(self, xsrc, P, hT, bhT, col0, gs, sh, bbc):
        kb = self.kb
        xt, bx = self.xrot()
        kb.dma("sp", xt[0:P, :], xsrc, W=[bx])
        junk, bj = self.frot()
        ssq, bs = self.srot()
        kb.op("act", lambda e: e.activation(out=junk[0:P, :], in_=xt[0:P, :], func=AF.Square, accum_out=ssq[0:P, 0:1]),
              R=[bx], W=[bj, bs])
        kb.op("act", lambda e: e.activation(out=ssq[0:P, 1:2], in_=ssq[0:P, 0:1], func=AF.Sqrt, scale=1.0 / D, bias=self.epsb[0:P, 0:1]),
              R=[bs, self.bconst], W=[bs])
        kb.op("dve", lambda e: e.reciprocal(out=ssq[0:P, 1:2], in_=ssq[0:P, 1:2]), R=[bs], W=[bs])
        kb.op("dve", lambda e: e.scalar_tensor_tensor(out=junk[0:P, :], in0=xt[0:P, :], scalar=ssq[0:P, 1:2], in1=gs[0:P, :],
                                                       op0=ALU.mult, op1=ALU.mult), R=[bx, bs, bbc], W=[bj])
        hb, bh = self.hrot()
        kb.op("pool", lambda e: e.tensor_tensor(out=hb[0:P, :], in0=junk[0:P, :], in1=sh[0:P, :], op=ALU.add),
              R=[bj, bbc], W=[bh])
        self.transpose_to(hb, bh, P, 8, lambda k: hT[:, k, col0:col0 + P], bhT)

    def transpose_to(self, src, bsrc, P, nch, dst_fn, bdst):
        kb = self.kb
        for k0 in range(0, nch, 8):
            pb, bb = self.bank()
            pbb = pb.bitcast(BF16)
            n8 = min(8, nch - k0)
            for k in range(k0, k0 + n8):
                kb.op("pe", lambda e, k=k: e.transpose(out=pbb[:, (k - k0) * 128:(k - k0) * 128 + P],
                                                        in_=src[0:P, k * 128:(k + 1) * 128],
                                                        identity=self.identb[0:P, 0:P]), R=[bsrc, self.bconst], W=bb)
            for k in range(k0, k0 + n8):
                kb.op("act", lambda e, k=k: e.activation(out=dst_fn(k), in_=pbb[:, (k - k0) * 128:(k - k0) * 128 + P],
                                                         func=AF.Copy), R=bb, W=[bdst])

    def post(self, pp, bpp, P, xsrc, xdst, gg, bbc, bdram):
        kb = self.kb
        junk, bj = self.frot()
        ssq, bs = self.srot()
        kb.op("act", lambda e: e.activation(out=junk[0:P, :], in_=pp[0:P, :], func=AF.Square, accum_out=ssq[0:P, 0:1]),
              R=bpp, W=[bj, bs])
        kb.op("act", lambda e: e.activation(out=ssq[0:P, 1:2], in_=ssq[0:P, 0:1], func=AF.Sqrt, scale=1.0 / D, bias=self.epsb[0:P, 0:1]),
              R=[bs, self.bconst], W=[bs])
        kb.op("dve", lambda e: e.reciprocal(out=ssq[0:P, 1:2], in_=ssq[0:P, 1:2]), R=[bs], W=[bs])
        kb.op("dve", lambda e: e.scalar_tensor_tensor(out=junk[0:P, :], in0=pp[0:P, :], scalar=ssq[0:P, 1:2], in1=gg[0:P, :],
                                                       op0=ALU.mult, op1=ALU.mult), R=bpp + [bs, bbc], W=[bj])
        xt, bx = self.xrot()
        kb.dma("sp", xt[0:P, :], xsrc, R=[bdram], W=[bx])
        kb.op("dve", lambda e: e.tensor_tensor(out=xt[0:P, :], in0=xt[0:P, :], in1=junk[0:P, :], op=ALU.add),
              R=[bx, bj], W=[bx])
        kb.dma("pool", xdst, xt[0:P, :], R=[bx], W=[bdram])

    def make_bc(self, l, sub, ci):
        kb = self.kb
        gs, sh, gg = self.bc_gs, self.bc_sh, self.bc_gg
        bbc = self.bbc
        tmp, btmp = self.wstage()
        tmp2, btmp2 = self.wstage()
        base = sub * 3 * D
        mrow = lambda a: self.MOD[l, ci:ci + 1, base + a * D: base + (a + 1) * D].partition_broadcast(128)
        grow = lambda i: self.norm_g[l, i:i + 1, :].partition_broadcast(128)
        kb.dma("sp", sh[:], mrow(0), R=[self.bmod], W=[bbc])
        kb.dma("sp", gs[:], mrow(1), R=[self.bmod], W=[bbc])
        kb.dma("sp", gg[:], mrow(2), R=[self.bmod], W=[bbc])
        kb.dma("sp", tmp[:], grow(2 * sub), W=[btmp])
        kb.op("dve", lambda e: e.scalar_tensor_tensor(out=gs[:], in0=gs[:], scalar=1.0, in1=tmp[:], op0=ALU.add, op1=ALU.mult),
              R=[bbc, btmp], W=[bbc])
        kb.dma("sp", tmp2[:], grow(2 * sub + 1), W=[btmp2])
        kb.op("dve", lambda e: e.tensor_tensor(out=gg[:], in0=gg[:], in1=tmp2[:], op=ALU.mult), R=[bbc, btmp2], W=[bbc])
        return gs, sh, gg, bbc

    def seqs(self):
        T, TS = self.T, self.TS
        S = []
        S.append(dict(name="p", ci=0, T=T, P=128, GN=min(512, T), x0=self.x_p, xb=self.xb_p, y=self.y_p, s=None))
        for s in range(2):
            S.append(dict(name="s%d" % s, ci=1 + s, T=TS, P=TS, GN=TS, x0=self.x_s[s], xb=self.xb_s[s], y=self.y_s[s], s=s))
        return S

    def xio(self, sq, stage):
        act = self.active_stages
        src = sq["x0"] if stage == act[0] else sq["xb"]
        dst = sq["y"] if stage == act[-1] else sq["xb"]
        return src, dst

    def build(self):
        nc = self.nc
        T, TS, PAST, NE, NO, depth = self.T, self.TS, self.PAST, self.NE, self.NO, self.depth
        TK = PAST + TS
        self._uid = 0
        self.x_p = self.din("x_p", [T, D])
        self.x_s = self.din("x_s", [2, TS, D])
        self.c_all = self.din("c_all", [3, D])
        self.cache_k = self.din("cache_k", [NE, 2, PAST, D])
        self.cache_v = self.din("cache_v", [NE, 2, PAST, D])
        self.st_sconv = self.din("st_sconv", [NE, 2, 3, 1536])
        self.st_ssm = self.din("st_ssm", [NE, 2, 1024, 128])
        self.st_cconv = self.din("st_cconv", [max(NO, 1), 2, 30, D])
        self.ada_w = self.din("ada_w", [depth, D, 6 * D])
        self.ada_b = self.din("ada_b", [depth, 6 * D])
        self.norm_g = self.din("norm_g", [depth, 4, D])
        self.ffn_w_up = self.din("ffn_w_up", [depth, D, 2 * DFF])
        self.ffn_w_down = self.din("ffn_w_down", [depth, DFF, D])
        self.hyb_w_in = self.din("hyb_w_in", [NE, D, DIN])
        self.attn_lambda = self.din("attn_lambda", [NE, 256])
        self.attn_subln_g = self.din("attn_subln_g", [NE, 128])
        self.ssm_conv_w = self.din("ssm_conv_w", [NE, 4, 1536])
        self.ssm_conv_b = self.din("ssm_conv_b", [NE, 1536])
        self.ssm_dt_bias = self.din("ssm_dt_bias", [NE, 16])
        self.ssm_a_log = self.din("ssm_a_log", [NE, 16])
        self.ssm_d = self.din("ssm_d", [NE, 16])
        self.ssm_norm_g = self.din("ssm_norm_g", [NE, D])
        self.hyb_w_out = self.din("hyb_w_out", [NE, 2 * D, D])
        self.conf_w_in = self.din("conf_w_in", [max(NO, 1), D, 2 * D])
        self.conf_b_in = self.din("conf_b_in", [max(NO, 1), 2 * D])
        self.conf_dw_w = self.din("conf_dw_w", [max(NO, 1), 31, D])
        self.conf_dw_b = self.din("conf_dw_b", [max(NO, 1), D])
        self.conf_ln_g = self.din("conf_ln_g", [max(NO, 1), D])
        self.conf_ln_b = self.din("conf_ln_b", [max(NO, 1), D])
        self.conf_w_out = self.din("conf_w_out", [max(NO, 1), D, D])
        self.conf_b_out = self.din("conf_b_out", [max(NO, 1), D])
        self.cst_f = self.din("cst_f", [4, 128, 128])
        self.cst_kaug = self.din("cst_kaug", [8, 5, 8192])
        self.cst_qaug = self.din("cst_qaug", [8, 5, 8192])
        self.cst_corr = self.din("cst_corr", [8, 128, 128])
        self.y_p = self.dout("y_p", [T, D])
        self.y_s = self.dout("y_s", [2, TS, D])
        self.k_p = self.dout("k_p", [NE, T, D])
        self.v_p = self.dout("v_p", [NE, T, D])
        self.sconv_p = self.dout("sconv_p", [NE, 3, 1536])
        self.ssm_p = self.dout("ssm_p", [NE, 1024, 128])
        self.cconv_p = self.dout("cconv_p", [max(NO, 1), 30, D])
        self.k_s = self.dout("k_s", [NE, 2, TS, D])
        self.v_s = self.dout("v_s", [NE, 2, TS, D])
        self.sconv_s = self.dout("sconv_s", [NE, 2, 3, 1536])
        self.ssm_s = self.dout("ssm_s", [NE, 2, 1024, 128])
        self.cconv_s = self.dout("cconv_s", [max(NO, 1), 2, 30, D])
        self.xb_p = self.dscr("xb_p", [T, D])
        self.xb_s = self.dscr("xb_s", [2, TS, D])
        self.MOD = self.dscr("modrows", [depth, 3, 6 * D])
        self.QT = [self.dscr("qt_p", [8, 128, T], BF16)] + [self.dscr("qt_s%d" % s, [8, 128, TS], BF16) for s in range(2)]
        self.KT = [self.dscr("kt_p", [8, 128, T], BF16)] + [self.dscr("kt_s%d" % s, [8, 128, TK], BF16) for s in range(2)]
        self.VB = [self.dscr("vb_p", [T, D], BF16)] + [self.dscr("vb_s%d" % s, [TK, D], BF16) for s in range(2)]
        self.OT = [self.dscr("ot_p", [D, T], BF16)] + [self.dscr("ot_s%d" % s, [D, TS], BF16) for s in range(2)]
        self.YT = [self.dscr("yt_p", [D, T], BF16)] + [self.dscr("yt_s%d" % s, [D, TS], BF16) for s in range(2)]
        self.bscr = [dict(q=Buf(), k=Buf(), v=Buf(), o=Buf(), y=Buf(), x=Buf()) for _ in range(3)]
        self.bmod = Buf()
        self.bout = Buf()

        with ExitStack() as es:
            self.kb = kb = KB(nc, es)
            self.PP = [es.enter_context(nc.psum_tensor("pp%d" % i, [128, 1024], F32)) for i in range(4)]
            self.pb = [Buf("bank%d" % i, excl=True) for i in range(8)]
            self.bank_list, self.bank_rr = list(range(8)), 0
            self.pair_list, self.pair_rr = list(range(4)), 0
            self.bconst = Buf("const")
            cf = self.sb(es, "cf", [128, 4, 128], F32)
            kb.dma("sp", cf[:], self.cst_f.rearrange("a p c -> p a c"), W=[self.bconst])
            self.identf, self.onesf, self.trilef, self.ugtf = cf[:, 0, :], cf[:, 1, :], cf[:, 2, :], cf[:, 3, :]
            cb = self.sb(es, "cb", [128, 4, 128], BF16)
            kb.op("pool", lambda e: e.tensor_copy(out=cb[:], in_=cf[:]), R=[self.bconst], W=[self.bconst])
            self.identb, self.onesb, self.trileb, self.ugtb = cb[:, 0, :], cb[:, 1, :], cb[:, 2, :], cb[:, 3, :]
            self.epsb = self.sb(es, "epsb", [128, 1], F32)
            kb.op("pool", lambda e: e.memset(self.epsb[:], EPS), W=[self.bconst])
            self.m1024 = self.sb(es, "m1024", [128, 128], F32)
            kb.op("pool", lambda e: e.memset(self.m1024[:], 1.0 / 1024), W=[self.bconst])
            self.m128 = self.sb(es, "m128", [128, 128], F32)
            kb.op("pool", lambda e: e.memset(self.m128[:], 1.0 / 128), W=[self.bconst])
            self.xrot = self.rot("xt", es, [128, D], F32, 2)
            self.frot = self.rot("ft", es, [128, D], F32, 2)
            self.hrot = self.rot("hb", es, [128, D], BF16, 1)
            self.srot = self.rot("ssq", es, [128, 4], F32, 4)
            self.wstage = self.rot("wst", es, [128, 1024], F32, 2)
            self.bc_gs = self.sb(es, "bcgs", [128, D], F32)
            self.bc_sh = self.sb(es, "bcsh", [128, D], F32)
            self.bc_gg = self.sb(es, "bcgg", [128, D], F32)
            self.bbc = Buf("bc")

            self.active_stages = []
            for l in range(depth):
                if (l % 2 == 0 and "hyb" not in SKIP) or (l % 2 == 1 and "conf" not in SKIP):
                    self.active_stages.append(2 * l)
                if "ffn" not in SKIP:
                    self.active_stages.append(2 * l + 1)
            self.phase_adaln()
            S = self.seqs()
            for l in range(depth):
                j = l // 2
                if l % 2 == 0 and "hyb" not in SKIP:
                    for ph in (self.phase_e1a, self.phase_e1b, self.phase_e2, self.phase_e3):
                        if DEBUG:
                            print("phase", ph.__name__, "l", l, "next_id", self.nc.next_id(), flush=True)
                        if ph.__name__[6:] not in SKIP:
                            ph(l, j, S)
                if l % 2 == 1 and "conf" not in SKIP:
                    self.phase_conf(l, j, S)
                if "ffn" not in SKIP:
                    self.phase_ffn(l, S)
            kb.barrier()
            self.stats = (kb.nins, kb.nwait)

    def phase_adaln(self):
        kb, depth = self.kb, self.depth
        with ExitStack() as es:
            ct = self.sb(es, "ct", [4, D], F32); bct = Buf()
            kb.dma("sp", ct[0:3, :], self.c_all, W=[bct])
            kb.op("act", lambda e: e.activation(out=ct[0:3, :], in_=ct[0:3, :], func=AF.Silu), R=[bct], W=[bct])
            cT = self.sb(es, "cT", [128, 8, 4], F32); bcT = Buf()
            pb, bb = self.bank()
            for k in range(8):
                kb.op("pe", lambda e, k=k: e.transpose(out=pb[:, k * 4:k * 4 + 3], in_=ct[0:3, k * 128:(k + 1) * 128],
                                                        identity=self.identf[0:3, 0:3]), R=[bct, self.bconst], W=bb)
            for k in range(8):
                kb.op("act", lambda e, k=k: e.activation(out=cT[:, k, 0:3], in_=pb[:, k * 4:k * 4 + 3], func=AF.Copy), R=bb, W=[bcT])
            ab = self.sb(es, "ab", [4, 6 * D], F32); bab = Buf()
            mrow = self.sb(es, "mrow", [4, 6 * D], F32); bmr = Buf()
            wrot = self.rot("adaw", es, [128, 3072], F32, 3)
            for l in range(depth):
                kb.dma("sp", ab[0:3, :], self.ada_b[l:l + 1, :].partition_broadcast(3), R=[bab], W=[bab])
                for half in range(2):
                    banks = [self.bank() for _ in range(6)]
                    for k in range(8):
                        wt, bw = wrot()
                        kb.dma("sp" if k % 2 == 0 else "pool", wt[:], self.ada_w[l, k * 128:(k + 1) * 128, half * 3072:(half + 1) * 3072], W=[bw])
                        for n in range(6):
                            pbn, bbn = banks[n]
                            kb.op("pe", lambda e, k=k, n=n, pbn=pbn, wt=wt: e.matmul(pbn[0:3, :], lhsT=cT[:, k, 0:3], rhs=wt[:, n * 512:(n + 1) * 512],
                                                                                  start=(k == 0), stop=(k == 7)), R=[bcT, bw], W=bbn)
                    for n in range(6):
                        pbn, bbn = banks[n]
                        c0 = half * 3072 + n * 512
                        kb.op("dve", lambda e, pbn=pbn, c0=c0: e.tensor_tensor(out=mrow[0:3, c0:c0 + 512], in0=pbn[0:3, :], in1=ab[0:3, c0:c0 + 512], op=ALU.add),
                              R=bbn + [bab], W=[bmr])
                kb.dma("sp", self.MOD[l], mrow[0:3, :], R=[bmr], W=[self.bmod])
            kb.barrier()

    def phase_ffn(self, l, S):
        kb = self.kb
        stage = 2 * l + 1
        with ExitStack() as es:
            wup = self.sb(es, "wup", [128, 8, 2 * DFF], BF16); bwu = Buf()
            wdn = self.sb(es, "wdn", [128, 22, D], BF16); bwd = Buf()
            self.load_w(wup, self.ffn_w_up[l].rearrange("(k p) n -> k p n", p=128), 8, 2 * DFF, bwu)
            self.load_w(wdn, self.ffn_w_down[l].rearrange("(k p) n -> k p n", p=128), 22, D, bwd)
            GNmax = S[0]["GN"]
            hT = self.sb(es, "hT", [128, 8, GNmax], BF16); bhT = Buf()
            aT = self.sb(es, "aT", [128, 22, GNmax], BF16); baT = Buf()
            sgr = self.rot("sg", es, [128, GNmax], F32, 1)
            kb.op("pool", lambda e: e.memset(hT[:], 0.0), W=[bhT])
            for si, sq in enumerate(S):
                gs, sh, gg, bbc = self.make_bc(l, 1, sq["ci"])
                src, dst = self.xio(sq, stage)
                bx = self.bscr[si]["x"]
                P, GN = sq["P"], sq["GN"]
                GNp = max(GN, 128)
                for g0 in range(0, sq["T"], GN):
                    nt = GN // P
                    for m in range(nt):
                        self.prelude_x(src[g0 + m * P:g0 + (m + 1) * P, :], bx, P, hT, bhT, m * P, gs, sh, bbc)
                    for jf in range(22):
                        pg, bg = self.bank()
                        pu, bu = self.bank()
                        for k in range(8):
                            kb.op("pe", lambda e, k=k, pg=pg: e.matmul(pg[:, 0:GNp], lhsT=wup[:, k, jf * 128:(jf + 1) * 128], rhs=hT[:, k, 0:GNp],
                                                                      start=(k == 0), stop=(k == 7)), R=[bwu, bhT], W=bg)
                        for k in range(8):
                            kb.op("pe", lambda e, k=k, pu=pu: e.matmul(pu[:, 0:GNp], lhsT=wup[:, k, DFF + jf * 128:DFF + (jf + 1) * 128], rhs=hT[:, k, 0:GNp],
                                                                      start=(k == 0), stop=(k == 7)), R=[bwu, bhT], W=bu)
                        sg, bsg = sgr()
                        kb.op("act", lambda e, sg=sg, pg=pg: e.activation(out=sg[:, 0:GN], in_=pg[:, 0:GN], func=AF.Silu), R=bg, W=[bsg])
                        kb.op("dve", lambda e, sg=sg, pu=pu, jf=jf: e.tensor_tensor(out=aT[:, jf, 0:GN], in0=sg[:, 0:GN], in1=pu[:, 0:GN], op=ALU.mult),
                              R=[bsg] + bu, W=[baT])
                    for m in range(nt):
                        pp, bpp = self.pair()
                        for nh in range(2):
                            for k in range(22):
                                kb.op("pe", lambda e, k=k, nh=nh, m=m, pp=pp: e.matmul(pp[0:P, nh * 512:(nh + 1) * 512], lhsT=aT[:, k, m * P:(m + 1) * P],
                                                                                     rhs=wdn[:, k, nh * 512:(nh + 1) * 512], start=(k == 0), stop=(k == 21)),
                                      R=[baT, bwd], W=bpp)
                        r0 = g0 + m * P
                        self.post(pp, bpp, P, src[r0:r0 + P, :], dst[r0:r0 + P, :], gg, bbc, bx)
            kb.barrier()

    def prelude_x(self, xsrc, bdram, P, hT, bhT, col0, gs, sh, bbc):
        kb = self.kb
        xt, bx = self.xrot()
        kb.dma("sp", xt[0:P, :], xsrc, R=[bdram], W=[bx])
        junk, bj = self.frot()
        ssq, bs = self.srot()
        kb.op("act", lambda e: e.activation(out=junk[0:P, :], in_=xt[0:P, :], func=AF.Square, accum_out=ssq[0:P, 0:1]),
              R=[bx], W=[bj, bs])
        kb.op("act", lambda e: e.activation(out=ssq[0:P, 1:2], in_=ssq[0:P, 0:1], func=AF.Sqrt, scale=1.0 / D, bias=self.epsb[0:P, 0:1]),
              R=[bs, self.bconst], W=[bs])
        kb.op("dve", lambda e: e.reciprocal(out=ssq[0:P, 1:2], in_=ssq[0:P, 1:2]), R=[bs], W=[bs])
        kb.op("dve", lambda e: e.scalar_tensor_tensor(out=junk[0:P, :], in0=xt[0:P, :], scalar=ssq[0:P, 1:2], in1=gs[0:P, :],
                                                       op0=ALU.mult, op1=ALU.mult), R=[bx, bs, bbc], W=[bj])
        hb, bh = self.hrot()
        kb.op("pool", lambda e: e.tensor_tensor(out=hb[0:P, :], in0=junk[0:P, :], in1=sh[0:P, :], op=ALU.add),
              R=[bj, bbc], W=[bh])
        self.transpose_to(hb, bh, P, 8, lambda k: hT[:, k, col0:col0 + P], bhT)

    def evac(self, out, in_, R, W):
        self._ev = getattr(self, "_ev", 0) + 1
        if self._ev % 2 == 0:
            self.kb.op("act", lambda e: e.activation(out=out, in_=in_, func=AF.Copy), R=R, W=W)
        else:
            self.kb.op("dve", lambda e: e.tensor_copy(out=out, in_=in_), R=R, W=W)

    def phase_e1a(self, l, j, S):
        kb = self.kb
        stage = 2 * l
        PAST, TS = self.PAST, self.TS
        with ExitStack() as es:
            w = self.sb(es, "w1a", [128, 8, 3072], BF16); bw = Buf()
            self.load_w(w, self.hyb_w_in[j].rearrange("(k p) n -> k p n", p=128), 8, 3072, bw, c0=0)
            GNmax = S[0]["GN"]
            hT = self.sb(es, "ahT", [128, 8, GNmax], BF16); bhT = Buf()
            kvt = self.rot("kvt", es, [128, 2048], F32, 2)
            vbr = self.rot("vbr", es, [128, D], BF16, 2)
            fmr = self.rot("fmr", es, [128, GNmax], BF16, 3)
            kfr = self.rot("kfr", es, [128, 8, 128], BF16, 2)
            kb.op("pool", lambda e: e.memset(hT[:], 0.0), W=[bhT])
            for si, sq in enumerate(S):
                gs, sh, gg, bbc = self.make_bc(l, 0, sq["ci"])
                src, _ = self.xio(sq, stage)
                bx = self.bscr[si]["x"]
                bsc = self.bscr[si]
                P, GN, Tq = sq["P"], sq["GN"], sq["T"]
                kp0 = 0
                if sq["s"] is not None:
                    s_ = sq["s"]
                    kp0 = PAST
                    for t in range(PAST // 128):
                        ck, bck = kvt()
                        kb.dma("sp", ck[:, 0:1024], self.cache_k[j, s_, t * 128:(t + 1) * 128, :], W=[bck])
                        kb.dma("sp", ck[:, 1024:2048], self.cache_v[j, s_, t * 128:(t + 1) * 128, :], W=[bck])
                        kf, bkf = kfr()
                        for h0 in (0, 4):
                            pb, bb = self.bank()
                            for h in range(h0, h0 + 4):
                                kb.op("pe", lambda e, h=h, pb=pb, ck=ck: e.transpose(out=pb[:, (h - h0) * 128:(h - h0 + 1) * 128], in_=ck[:, h * 128:(h + 1) * 128],
                                                                                     identity=self.identf[:, :]), R=[bck, self.bconst], W=bb)
                            self.evac(kf[:, h0:h0 + 4, :], pb.rearrange("p (a b) -> p a b", a=4), bb, [bkf])
                        kb.dma("pool", self.KT[si][:, :, t * 128:(t + 1) * 128].rearrange("h r t -> r h t"), kf[:], R=[bkf], W=[bsc["k"]])
                        vb, bvb = vbr()
                        kb.op("pool", lambda e, vb=vb, ck=ck: e.tensor_copy(out=vb[:], in_=ck[:, 1024:2048]), R=[bck], W=[bvb])
                        kb.dma("pool", self.VB[si][t * 128:(t + 1) * 128, :], vb[:], R=[bvb], W=[bsc["v"]])
                for g0 in range(0, Tq, GN):
                    nt = GN // P
                    for m in range(nt):
                        self.prelude_x(src[g0 + m * P:g0 + (m + 1) * P, :], bx, P, hT, bhT, m * P, gs, sh, bbc)
                    for m in range(nt):
                        kv, bkv = kvt()
                        for n4 in range(4):
                            pb, bb = self.bank()
                            for k in range(8):
                                kb.op("pe", lambda e, k=k, pb=pb, n4=n4, m=m: e.matmul(pb[0:P, :], lhsT=hT[:, k, m * P:(m + 1) * P], rhs=w[:, k, 1024 + n4 * 512:1024 + (n4 + 1) * 512],
                                                                                     start=(k == 0), stop=(k == 7)), R=[bhT, bw], W=bb)
                            self.evac(kv[0:P, n4 * 512:(n4 + 1) * 512], pb[0:P, :], bb, [bkv])
                        r0 = g0 + m * P
                        if sq["s"] is None:
                            kb.dma("pool", self.k_p[j, r0:r0 + P, :], kv[0:P, 0:1024], R=[bkv], W=[self.bout])
                            kb.dma("pool", self.v_p[j, r0:r0 + P, :], kv[0:P, 1024:2048], R=[bkv], W=[self.bout])
                        else:
                            kb.dma("pool", self.k_s[j, sq["s"], r0:r0 + P, :], kv[0:P, 0:1024], R=[bkv], W=[self.bout])
                            kb.dma("pool", self.v_s[j, sq["s"], r0:r0 + P, :], kv[0:P, 1024:2048], R=[bkv], W=[self.bout])
                        vb, bvb = vbr()
                        kb.op("pool", lambda e, vb=vb, kv=kv: e.tensor_copy(out=vb[0:P, :], in_=kv[0:P, 1024:2048]), R=[bkv], W=[bvb])
                        kb.dma("pool", self.VB[si][kp0 + r0:kp0 + r0 + P, :], vb[0:P, :], R=[bvb], W=[bsc["v"]])
                    for c in range(16):
                        pb, bb = self.bank()
                        for k in range(8):
                            kb.op("pe", lambda e, k=k, pb=pb, c=c: e.matmul(pb[:, 0:max(GN, 128)], lhsT=w[:, k, c * 128:(c + 1) * 128], rhs=hT[:, k, 0:max(GN, 128)],
                                                                           start=(k == 0), stop=(k == 7)), R=[bhT, bw], W=bb)
                        fm, bfm = fmr()
                        self.evac(fm[:, 0:GN], pb[:, 0:GN], bb, [bfm])
                        if c < 8:
                            kb.dma("pool", self.QT[si][c, :, g0:g0 + GN], fm[:, 0:GN], R=[bfm], W=[bsc["q"]])
                        else:
                            kb.dma("pool", self.KT[si][c - 8, :, kp0 + g0:kp0 + g0 + GN], fm[:, 0:GN], R=[bfm], W=[bsc["k"]])
            kb.barrier()

    def phase_e1b(self, l, j, S):
        kb = self.kb
        stage = 2 * l
        with ExitStack() as es:
            w = self.sb(es, "w1b", [128, 8, 2576], BF16); bw = Buf()
            self.load_w(w, self.hyb_w_in[j].rearrange("(k p) n -> k p n", p=128), 8, 2576, bw, c0=3072)
            bsm = Buf("e1bsmall")
            cwf = self.sb(es, "cwf", [128, 12, 4], F32)
            self.load_fm(es, lambda c: cwf[:, c, :], self.ssm_conv_w[j], 4, 12, bsm)
            cbf = self.sb(es, "cbf", [128, 12, 1], F32)
            self.load_fm(es, lambda c: cbf[:, c, :], self.ssm_conv_b[j:j + 1, :], 1, 12, bsm)
            diag4 = self.sb(es, "diag4", [128, 48, 128], BF16)
            kb.op("dve", lambda e: e.tensor_tensor(out=diag4[:], in0=self.identf.unsqueeze(1).to_broadcast([128, 48, 128]),
                                                   in1=cwf[:].rearrange("p c k -> p (c k)").unsqueeze(2).to_broadcast([128, 48, 128]),
                                                   op=ALU.mult), R=[bsm, self.bconst], W=[bsm])
            cbr = self.sb(es, "cbr", [1, 1536], F32)
            kb.dma("sp", cbr[:], self.ssm_conv_b[j:j + 1, :], W=[bsm])
            cbrb = self.sb(es, "cbrb", [1, 1536], BF16)
            kb.op("pool", lambda e: e.tensor_copy(out=cbrb[:], in_=cbr[:]), R=[bsm], W=[bsm])
            sm = self.sb(es, "ssmsm", [128, 4, 16], F32)
            kb.dma("sp", sm[:, 0, :], self.ssm_a_log[j:j + 1, :].partition_broadcast(128), W=[bsm])
            kb.dma("sp", sm[:, 1, :], self.ssm_d[j:j + 1, :].partition_broadcast(128), W=[bsm])
            kb.dma("sp", sm[:, 2, :], self.ssm_dt_bias[j:j + 1, :].partition_broadcast(128), W=[bsm])
            kb.op("act", lambda e: e.activation(out=sm[:, 0, :], in_=sm[:, 0, :], func=AF.Exp), R=[bsm], W=[bsm])
            kb.op("dve", lambda e: e.tensor_scalar(out=sm[:, 0, :], in0=sm[:, 0, :], scalar1=-1.0, scalar2=None, op0=ALU.mult), R=[bsm], W=[bsm])
            a_b, D_b, dtb_b = sm[:, 0, :], sm[:, 1, :], sm[:, 2, :]
            ngb = self.sb(es, "ngb", [128, D], F32)
            kb.dma("sp", ngb[:], self.ssm_norm_g[j:j + 1, :].partition_broadcast(128), W=[bsm])
            if E1B_STOP <= 1:
                kb.barrier()
                return
            GNmax = S[0]["GN"]
            hT = self.sb(es, "bhT", [128, 8, GNmax], BF16); bhT = Buf()
            xbcT = self.sb(es, "xbcT", [128, 12, 4 + GNmax], BF16); bxb = Buf()
            t1 = self.sb(es, "t1", [128, D], F32); bt1 = Buf()
            t3 = self.sb(es, "t3", [128, D], F32); bt3 = Buf()
            ynb = self.sb(es, "ynb", [128, D], BF16); bynb = Buf()
            ynT = self.sb(es, "ynT", [128, 8, 128], BF16); bynT = Buf()
            hst = self.sb(es, "hst", [128, D], F32); bhst = Buf()
            hbf = self.sb(es, "hbf", [128, D], BF16); bhbf = Buf()
            xraw = self.sb(es, "xraw", [128, 1536], F32); bxr = Buf()
            TB = []
            for i in range(2):
                TB.append((self.sb(es, "szt", [128, D], F32), Buf(), self.sb(es, "xst", [128, D], F32), Buf(),
                           self.sb(es, "Rf", [128, 2048], BF16), Buf(), self.sb(es, "exf", [128, 1024], F32), Buf(),
                           self.sb(es, "scf", [128, 2048], BF16), Buf(), self.sb(es, "xdt", [128, D], BF16), Buf(),
                           self.sb(es, "xdtw", [128, D], BF16), Buf(), self.sb(es, "bct", [128, 4, 128], BF16), Buf(),
                           self.sb(es, "btok", [128, 256], BF16), Buf(), self.sb(es, "cbm", [128, 2, 128], F32), Buf(),
                           self.sb(es, "d16", [128, 12, 16], F32), Buf(), self.sb(es, "d16b", [128, 16], BF16)))
            tbi = [0]
            kb.op("pool", lambda e: e.memset(hT[:], 0.0), W=[bhT])
            kb.op("pool", lambda e: e.memset(xbcT[:], 0.0), W=[bxb])
            for si, sq in enumerate(S):
                if CONF_SEQS is not None and si not in CONF_SEQS:
                    continue
                gs, sh, gg, bbc = self.make_bc(l, 0, sq["ci"])
                src, _ = self.xio(sq, stage)
                bx = self.bscr[si]["x"]
                bsc = self.bscr[si]
                P, GN, Tq = sq["P"], sq["GN"], sq["T"]
                if sq["s"] is None:
                    kb.op("pool", lambda e: e.memset(xbcT[:, :, 0:4], 0.0), W=[bxb])
                    kb.op("pool", lambda e: e.memset(hst[:], 0.0), W=[bhst])
                    kb.op("pool", lambda e: e.memset(hbf[:], 0.0), W=[bhbf])
                else:
                    s_ = sq["s"]
                    self.load_fm(es, lambda c: xbcT[:, c, 0:4], self.st_sconv[j, s_], 3, 12, bxb, pad=1)
                    st, bst = self.wstage()
                    kb.dma("sp", st[:].rearrange("p (c n) -> p c n", c=8), self.st_ssm[j, s_].rearrange("(c p) n -> p c n", p=128), W=[bst])
                    pp, bpp = self.pair()
                    for c in range(8):
                        kb.op("pe", lambda e, c=c, pp=pp, st=st: e.transpose(out=pp[:, c * 128:(c + 1) * 128], in_=st[:, c * 128:(c + 1) * 128], identity=self.identf[:, :]),
                              R=[bst, self.bconst], W=bpp)
                    kb.op("act", lambda e, pp=pp: e.activation(out=hst[:], in_=pp[:], func=AF.Copy), R=bpp, W=[bhst])
                    kb.op("dve", lambda e: e.tensor_copy(out=hbf[:], in_=hst[:]), R=[bhst], W=[bhbf])
                ngr = Tq // GN
                for gi in range(ngr):
                    g0 = gi * GN
                    nt = GN // P
                    for m in range(nt):
                        self.prelude_x(src[g0 + m * P:g0 + (m + 1) * P, :], bx, P, hT, bhT, m * P, gs, sh, bbc)
                    for c in range(12):
                        pb, bb = self.bank()
                        for k in range(8):
                            kb.op("pe", lambda e, k=k, pb=pb, c=c: e.matmul(pb[:, 0:max(GN, 128)], lhsT=w[:, k, 1024 + c * 128:1024 + (c + 1) * 128], rhs=hT[:, k, 0:max(GN, 128)],
                                                                           start=(k == 0), stop=(k == 7)), R=[bhT, bw], W=bb)
                        self.evac(xbcT[:, c, 4:4 + GN], pb[:, 0:GN], bb, [bxb])
                    def tile_gen(m, gi=gi, g0=g0):
                        (szt, bsz, xst, bxs, Rf, bR, exf, bex, scf, bscf, xdt, bxdt, xdtw, bxdtw, bct, bbct, btok, bbtok, cbm, bcbm, d16, bd, d16b) = TB[tbi[0] % 2]
                        tbi[0] += 1
                        Rv = Rf[0:P, 0:16 * P].rearrange("p (h l) -> p h l", h=16)
                        scv = scf[0:P, 0:16 * P].rearrange("p (h l) -> p h l", h=16)
                        t0 = m * P
                        r0 = g0 + t0
                        pz, bz = self.pair()
                        for nh in range(2):
                            for k in range(8):
                                kb.op("pe", lambda e, k=k, nh=nh, pz=pz: e.matmul(pz[0:P, nh * 512:(nh + 1) * 512], lhsT=hT[:, k, t0:t0 + P], rhs=w[:, k, nh * 512:(nh + 1) * 512],
                                                                                start=(k == 0), stop=(k == 7)), R=[bhT, bw], W=bz)
                        kb.op("act", lambda e, pz=pz: e.activation(out=szt[0:P, :], in_=pz[0:P, :], func=AF.Silu), R=bz, W=[bsz])
                        pd, bpd = self.bank()
                        for k in range(8):
                            kb.op("pe", lambda e, k=k, pd=pd: e.matmul(pd[0:P, 0:128], lhsT=hT[:, k, t0:t0 + P], rhs=w[:, k, 2448:2576], start=(k == 0), stop=(k == 7)),
                                  R=[bhT, bw], W=bpd)
                        X, AXv, EX, LG, DT, DTA, WL, DTW, EXPA, EL = [d16[0:P, i, :] for i in range(10)]
                        EL = d16[:, 9, :]
                        kb.op("dve", lambda e, pd=pd: e.tensor_tensor(out=X, in0=pd[0:P, 112:128], in1=dtb_b[0:P, :], op=ALU.add), R=bpd + [bsm], W=[bd])
                        kb.op("dve", lambda e: e.scalar_tensor_tensor(out=AXv, in0=X, scalar=-1.0, in1=X, op0=ALU.mult, op1=ALU.max), R=[bd], W=[bd])
                        kb.op("act", lambda e: e.activation(out=EX, in_=AXv, func=AF.Exp, scale=-1.0), R=[bd], W=[bd])
                        kb.op("act", lambda e: e.activation(out=LG, in_=EX, func=AF.Ln, bias=1.0), R=[bd], W=[bd])
                        kb.op("dve", lambda e: e.scalar_tensor_tensor(out=DT, in0=X, scalar=0.0, in1=LG, op0=ALU.max, op1=ALU.add), R=[bd], W=[bd])
                        kb.op("dve", lambda e: e.tensor_tensor(out=DTA, in0=DT, in1=a_b[0:P, :], op=ALU.mult), R=[bd, bsm], W=[bd])
                        kb.op("dve", lambda e: e.tensor_copy(out=d16b[0:P, :], in_=DTA), R=[bd], W=[bd])
                        px, bpx = self.pair()
                        for k in range(5):
                            for c in range(8):
                                st_, sp_ = (k == 0 and c % 4 == 0), (k == 4 and c % 4 == 3)
                                if k < 4:
                                    kb.op("pe", lambda e, c=c, k=k, px=px, st_=st_, sp_=sp_: e.matmul(px[0:P, c * 128:(c + 1) * 128], lhsT=xbcT[:, c, 1 + t0 + k:1 + t0 + k + P], rhs=diag4[:, c * 4 + k, :],
                                                                                                   start=st_, stop=sp_), R=[bxb, bsm], W=bpx)
                                else:
                                    kb.op("pe", lambda e, c=c, px=px, st_=st_, sp_=sp_: e.matmul(px[0:P, c * 128:(c + 1) * 128], lhsT=self.onesb[0:1, 0:P], rhs=cbrb[0:1, c * 128:(c + 1) * 128],
                                                                                              start=st_, stop=sp_), R=[bsm, self.bconst], W=bpx)
                        kb.op("act", lambda e, px=px: e.activation(out=xst[0:P, :], in_=px[0:P, :], func=AF.Silu), R=bpx, W=[bxs])
                        pk, bpk = self.bank()
                        for k in range(5):
                            for c in (8, 9):
                                st_, sp_ = (k == 0 and c == 8), (k == 4 and c == 9)
                                if k < 4:
                                    kb.op("pe", lambda e, c=c, k=k, pk=pk, st_=st_, sp_=sp_: e.matmul(pk[0:P, (c - 8) * 128:(c - 7) * 128], lhsT=xbcT[:, c, 1 + t0 + k:1 + t0 + k + P], rhs=diag4[:, c * 4 + k, :],
                                                                                                   start=st_, stop=sp_), R=[bxb, bsm], W=bpk)
                                else:
                                    kb.op("pe", lambda e, c=c, pk=pk, st_=st_, sp_=sp_: e.matmul(pk[0:P, (c - 8) * 128:(c - 7) * 128], lhsT=self.onesb[0:1, 0:P], rhs=cbrb[0:1, c * 128:(c + 1) * 128],
                                                                                              start=st_, stop=sp_), R=[bsm, self.bconst], W=bpk)
                        kb.op("act", lambda e, pk=pk: e.activation(out=btok[0:P, :], in_=pk[0:P, 0:256], func=AF.Silu), R=bpk, W=[bbtok])
                        pf, bpf = self.bank()
                        for k in range(4):
                            for i, c in enumerate((8, 9, 10, 11)):
                                st_, sp_ = (k == 0 and i == 0), (k == 3 and i == 3)
                                kb.op("pe", lambda e, c=c, k=k, i=i, pf=pf, st_=st_, sp_=sp_: e.matmul(pf[:, i * 128:(i + 1) * 128], lhsT=diag4[:, c * 4 + k, :], rhs=xbcT[:, c, 1 + t0 + k:1 + t0 + k + 128],
                                                                                                    start=st_, stop=sp_), R=[bxb, bsm], W=bpf)
                        for i, c in enumerate((8, 9, 10, 11)):
                            kb.op("act", lambda e, c=c, i=i, pf=pf: e.activation(out=bct[:, i, 0:P], in_=pf[:, i * 128:i * 128 + P], func=AF.Silu, bias=cbf[:, c, :]),
                                  R=bpf + [bsm], W=[bbct])
                        pcb, bpcb = self.bank()
                        for g in range(2):
                            kb.op("pe", lambda e, g=g, pcb=pcb: e.matmul(pcb[0:P, g * P:(g + 1) * P], lhsT=bct[:, g, 0:P], rhs=bct[:, 2 + g, 0:P], start=True, stop=True),
                                  R=[bbct], W=bpcb)
                        kb.op("dve", lambda e, pcb=pcb: e.tensor_tensor(out=cbm[0:P, :, 0:P], in0=pcb[0:P, 0:2 * P].rearrange("p (g l) -> p g l", g=2),
                                                                       in1=self.trilef[0:P, 0:P].unsqueeze(1).to_broadcast([P, 2, P]), op=ALU.mult),
                              R=bpcb + [self.bconst], W=[bcbm])
                        kb.op("dve", lambda e: e.tensor_tensor(out=Rv, in0=self.trilef[0:P, 0:P].unsqueeze(1).to_broadcast([P, 16, P]),
                                                               in1=DTA.unsqueeze(2).to_broadcast([P, 16, P]), op=ALU.mult), R=[bd, self.bconst], W=[bR])
                        for half in range(2):
                            psg, bsg = self.pair()
                            for q4 in range(2):
                                h0 = half * 8 + q4 * 4
                                kb.op("pe", lambda e, q4=q4, h0=h0, psg=psg: e.matmul(psg[0:P, q4 * 4 * P:(q4 + 1) * 4 * P], lhsT=self.ugtb[0:P, 0:P],
                                                                                     rhs=Rf[0:P, h0 * P:(h0 + 4) * P], start=True, stop=True), R=[bR, self.bconst], W=bsg)
                            kb.op("act", lambda e, psg=psg: e.activation(out=exf[0:P, 0:8 * P], in_=psg[0:P, 0:8 * P], func=AF.Exp), R=bsg, W=[bex])
                            exv = exf[0:P, 0:8 * P].rearrange("p (h l) -> p h l", h=8)
                            kb.op("dve", lambda e, half=half, exv=exv: e.tensor_copy(out=WL[:, half * 8:(half + 1) * 8], in_=exv[:, :, P - 1]), R=[bex], W=[bd])
                            kb.op("dve", lambda e, half=half, exv=exv: e.tensor_tensor(out=scv[:, half * 8:(half + 1) * 8, :], in0=exv,
                                                                                      in1=cbm[0:P, half, 0:P].unsqueeze(1).to_broadcast([P, 8, P]), op=ALU.mult),
                                  R=[bex, bcbm], W=[bscf])
                        pa, bpa = self.bank()
                        kb.op("pe", lambda e, pa=pa: e.matmul(pa[0:P, 0:16], lhsT=self.trileb[0:P, 0:P], rhs=d16b[0:P, :], start=True, stop=True), R=[bd, self.bconst], W=bpa)
                        kb.op("pe", lambda e, pa=pa: e.matmul(pa[:, 16:32], lhsT=self.onesb[0:P, :], rhs=d16b[0:P, :], start=True, stop=True), R=[bd, self.bconst], W=bpa)
                        kb.op("act", lambda e, pa=pa: e.activation(out=EXPA, in_=pa[0:P, 0:16], func=AF.Exp), R=bpa, W=[bd])
                        kb.op("act", lambda e, pa=pa: e.activation(out=EL, in_=pa[:, 16:32], func=AF.Exp), R=bpa, W=[bd])
                        kb.op("dve", lambda e: e.tensor_tensor(out=DTW, in0=DT, in1=WL, op=ALU.mult), R=[bd], W=[bd])
                        xs3 = xst[0:P, :].rearrange("p (h q) -> p h q", h=16)
                        kb.op("dve", lambda e: e.tensor_tensor(out=xdt[0:P, :].rearrange("p (h q) -> p h q", h=16), in0=xs3, in1=DT.unsqueeze(2).to_broadcast([P, 16, 64]), op=ALU.mult),
                              R=[bxs, bd], W=[bxdt])
                        kb.op("pool", lambda e: e.tensor_tensor(out=xdtw[0:P, :].rearrange("p (h q) -> p h q", h=16), in0=xs3, in1=DTW.unsqueeze(2).to_broadcast([P, 16, 64]), op=ALU.mult),
                              R=[bxs, bd], W=[bxdtw])
                        yield
                        py, bpy = self.pair()
                        for h in range(16):
                            kb.op("pe", lambda e, h=h, py=py: e.matmul(py[0:P, h * 64:(h + 1) * 64], lhsT=scf[0:P, h * P:(h + 1) * P], rhs=xdt[0:P, h * 64:(h + 1) * 64], start=True, stop=True),
                                  R=[bscf, bxdt], W=bpy)
                        po, bpo = self.pair()
                        for g in range(2):
                            kb.op("pe", lambda e, g=g, po=po: e.matmul(po[0:P, g * 512:(g + 1) * 512], lhsT=bct[:, 2 + g, 0:P], rhs=hbf[:, g * 512:(g + 1) * 512], start=True, stop=True),
                                  R=[bbct, bhbf], W=bpo)
                        kb.op("dve", lambda e, po=po: e.tensor_tensor(out=t1[0:P, :].rearrange("p (h q) -> p h q", h=16), in0=po[0:P, :].rearrange("p (h q) -> p h q", h=16),
                                                                     in1=EXPA.unsqueeze(2).to_broadcast([P, 16, 64]), op=ALU.mult), R=bpo + [bd], W=[bt1])
                        kb.op("dve", lambda e, py=py: e.tensor_tensor(out=t1[0:P, :], in0=t1[0:P, :], in1=py[0:P, :], op=ALU.add), R=bpy + [bt1], W=[bt1])
                        kb.op("pool", lambda e: e.tensor_tensor(out=t3[0:P, :].rearrange("p (h q) -> p h q", h=16), in0=xs3, in1=D_b[0:P, :].unsqueeze(2).to_broadcast([P, 16, 64]), op=ALU.mult),
                              R=[bxs, bsm], W=[bt3])
                        kb.op("dve", lambda e: e.tensor_tensor(out=t1[0:P, :], in0=t1[0:P, :], in1=t3[0:P, :], op=ALU.add), R=[bt1, bt3], W=[bt1])
                        kb.op("dve", lambda e: e.tensor_tensor(out=t1[0:P, :], in0=t1[0:P, :], in1=szt[0:P, :], op=ALU.mult), R=[bt1, bsz], W=[bt1])
                        ssq, bs = self.srot()
                        for g in range(2):
                            kb.op("act", lambda e, g=g, ssq=ssq: e.activation(out=t3[0:P, g * 512:(g + 1) * 512], in_=t1[0:P, g * 512:(g + 1) * 512], func=AF.Square, accum_out=ssq[0:P, g:g + 1]),
                                  R=[bt1], W=[bt3, bs])
                        kb.op("act", lambda e, ssq=ssq: e.activation(out=ssq[0:P, 2:4], in_=ssq[0:P, 0:2], func=AF.Sqrt, scale=1.0 / 512, bias=self.epsb[0:P, 0:1]), R=[bs, self.bconst], W=[bs])
                        kb.op("dve", lambda e, ssq=ssq: e.reciprocal(out=ssq[0:P, 2:4], in_=ssq[0:P, 2:4]), R=[bs], W=[bs])
                        for g in range(2):
                            kb.op("dve", lambda e, g=g, ssq=ssq: e.scalar_tensor_tensor(out=ynb[0:P, g * 512:(g + 1) * 512], in0=t1[0:P, g * 512:(g + 1) * 512], scalar=ssq[0:P, 2 + g:3 + g],
                                                                                       in1=ngb[0:P, g * 512:(g + 1) * 512], op0=ALU.mult, op1=ALU.mult), R=[bt1, bs, bsm], W=[bynb])
                        self.transpose_to(ynb, bynb, P, 8, lambda k: ynT[:, k, 0:P], bynT)
                        kb.dma("pool", self.YT[si].rearrange("(c p) t -> p c t", p=128)[:, :, r0:r0 + P], ynT[:, :, 0:P], R=[bynT], W=[bsc["y"]])
                        ps2, bps2 = self.pair()
                        for g in range(2):
                            kb.op("pe", lambda e, g=g, ps2=ps2: e.matmul(ps2[:, g * 512:(g + 1) * 512], lhsT=btok[0:P, g * 128:(g + 1) * 128], rhs=xdtw[0:P, g * 512:(g + 1) * 512], start=True, stop=True),
                                  R=[bbtok, bxdtw], W=bps2)
                        kb.op("dve", lambda e: e.tensor_tensor(out=hst[:].rearrange("p (h q) -> p h q", h=16), in0=hst[:].rearrange("p (h q) -> p h q", h=16),
                                                               in1=EL.unsqueeze(2).to_broadcast([128, 16, 64]), op=ALU.mult), R=[bhst, bd], W=[bhst])
                        kb.op("dve", lambda e, ps2=ps2: e.tensor_tensor(out=hst[:], in0=hst[:], in1=ps2[:], op=ALU.add), R=bps2 + [bhst], W=[bhst])
                        kb.op("act", lambda e: e.activation(out=hbf[:], in_=hst[:], func=AF.Copy), R=[bhst], W=[bhbf])
                        if gi == ngr - 1 and m == nt - 1:
                            for n3 in range(3):
                                pb, bb = self.bank()
                                for k in range(8):
                                    kb.op("pe", lambda e, k=k, pb=pb, n3=n3: e.matmul(pb[0:P, :], lhsT=hT[:, k, t0:t0 + P], rhs=w[:, k, 1024 + n3 * 512:1024 + (n3 + 1) * 512],
                                                                                     start=(k == 0), stop=(k == 7)), R=[bhT, bw], W=bb)
                                self.evac(xraw[0:P, n3 * 512:(n3 + 1) * 512], pb[0:P, :], bb, [bxr])
                            dsto = self.sconv_p[j] if sq["s"] is None else self.sconv_s[j, sq["s"]]
                            kb.dma("pool", dsto, xraw[P - 3:P, :], R=[bxr], W=[self.bout])
                    gens = [tile_gen(m) for m in range(nt)]
                    next(gens[0])
                    for m in range(nt):
                        if m + 1 < nt:
                            next(gens[m + 1])
                        next(gens[m], None)
                    if gi < ngr - 1:
                        kb.op("pool", lambda e: e.tensor_copy(out=xbcT[:, :, 0:4], in_=xbcT[:, :, GN:GN + 4]), R=[bxb], W=[bxb])
                if E1B_STOP <= 6:
                    continue
                pp, bpp = self.pair()
                for c in range(8):
                    kb.op("pe", lambda e, c=c, pp=pp: e.transpose(out=pp[:, c * 128:(c + 1) * 128], in_=hst[:, c * 128:(c + 1) * 128], identity=self.identf[:, :]),
                          R=[bhst, self.bconst], W=bpp)
                so, bso = self.xrot()
                kb.op("act", lambda e, pp=pp, so=so: e.activation(out=so[:], in_=pp[:], func=AF.Copy), R=bpp, W=[bso])
                dsts = self.ssm_p[j] if sq["s"] is None else self.ssm_s[j, sq["s"]]
                kb.dma("pool", dsts.rearrange("(c p) n -> p c n", p=128), so[:].rearrange("p (c n) -> p c n", c=8), R=[bso], W=[self.bout])
            kb.barrier()

    def phase_e2(self, l, j, S):
        kb = self.kb
        PAST, TS, T = self.PAST, self.TS, self.T
        lam_init = 0.8 - 0.6 * math.exp(-0.3 * l)
        TKmax = max(T, PAST + TS)
        with ExitStack() as es:
            save_banks = self.bank_list
            self.bank_list = [4, 5, 6, 7]
            bsm = Buf("e2small")
            lamt = self.sb(es, "lamt", [128, 256], F32)
            kb.dma("sp", lamt[:], self.attn_lambda[j:j + 1, :].partition_broadcast(128), W=[bsm])
            lsm = self.sb(es, "lsm", [128, 8], F32)
            lpr = self.sb(es, "lpr", [128, 128], F32)
            kb.op("dve", lambda e: e.tensor_tensor(out=lpr[:, 0:64], in0=lamt[:, 0:64], in1=lamt[:, 64:128], op=ALU.mult), R=[bsm], W=[bsm])
            kb.op("dve", lambda e: e.tensor_tensor(out=lpr[:, 64:128], in0=lamt[:, 128:192], in1=lamt[:, 192:256], op=ALU.mult), R=[bsm], W=[bsm])
            kb.op("dve", lambda e: e.reduce_sum(out=lsm[:, 0:2], in_=lpr[:].rearrange("p (a b) -> p a b", a=2), axis=AX.X), R=[bsm], W=[bsm])
            kb.op("act", lambda e: e.activation(out=lsm[:, 2:4], in_=lsm[:, 0:2], func=AF.Exp), R=[bsm], W=[bsm])
            kb.op("dve", lambda e: e.scalar_tensor_tensor(out=lsm[:, 4:5], in0=lsm[:, 3:4], scalar=-lam_init, in1=lsm[:, 2:3], op0=ALU.add, op1=ALU.subtract),
                  R=[bsm], W=[bsm])
            neglam = lsm[:, 4:5]
            subg = self.sb(es, "subg", [128, 1, 1], F32)
            self.load_fm(es, lambda c: subg[:, c, :], self.attn_subln_g[j:j + 1, :], 1, 1, bsm)
            kb.op("dve", lambda e: e.tensor_scalar(out=subg[:, 0, :], in0=subg[:, 0, :], scalar1=(1.0 - lam_init), scalar2=None, op0=ALU.mult), R=[bsm], W=[bsm])
            NKT = (TKmax + 127) // 128
            kvset = []
            for i in range(2):
                kvset.append(dict(kT=[self.sb(es, "kT%d_%d" % (m, i), [69, TKmax], BF16) for m in range(2)], bkT=Buf(),
                                  vh=self.sb(es, "vh%d" % i, [128, NKT, 128], BF16), bvh=Buf(),
                                  corrb=self.sb(es, "corrb%d" % i, [128, 128], BF16), bcorr=Buf()))
            GNmax = S[0]["GN"]
            qTr = self.rot("qT", es, [69, 2, GNmax], BF16, 2)
            ptr = self.rot("pt", es, [128, GNmax], BF16, 4)
            accr = self.rot("accs", es, [128, 4, GNmax], F32, 2)
            ot = self.sb(es, "e2o", [128, GNmax], F32); bot = Buf()
            o1 = self.sb(es, "e2o1", [128, GNmax], F32); bo1 = Buf()
            rr = self.sb(es, "e2r", [128, GNmax], F32); brr = Buf()
            kb.op("pool", lambda e: e.memset(o1[:], 0.0), W=[bo1])
            onr = self.rot("e2on", es, [128, GNmax], BF16, 2)
            acc = [self.PP[0][:, 0:512], self.PP[0][:, 512:1024], self.PP[1][:, 0:512], self.PP[1][:, 512:1024]]
            bacc = [[self.pb[i]] for i in range(4)]
            for _ in range(2):
                qT, bqT = qTr()
                kb.op("pool", lambda e, qT=qT: e.memset(qT[:], 0.0), W=[bqT])

            def load_head(si, sq, h, ks):
                bsc = self.bscr[si]
                Tq = sq["T"]
                kp0 = 0 if sq["s"] is None else PAST
                Tk = kp0 + Tq
                kT, bkT, vh, bvh, corrb, bcorr = ks["kT"], ks["bkT"], ks["vh"], ks["bvh"], ks["corrb"], ks["bcorr"]
                for m in range(2):
                    kb.dma("sp", kT[m][0:64, 0:Tk], self.KT[si][h, m * 64:(m + 1) * 64, 0:Tk], R=[bsc["k"]], W=[bkT])
                for a in range(0, Tk, 1024):
                    wd = min(1024, Tk - a)
                    st, bst = self.wstage()
                    kb.dma("sp", st[64:69, 0:wd], self.cst_kaug[h, :, a:a + wd], W=[bst])
                    for m in range(2):
                        kb.op("pool", lambda e, m=m, st=st, a=a, wd=wd: e.tensor_copy(out=kT[m][64:69, a:a + wd], in_=st[64:69, 0:wd]), R=[bst], W=[bkT])
                nfull = Tk // 128
                kb.dma("sp", vh[:, 0:nfull, :], self.VB[si][0:nfull * 128, h * 128:(h + 1) * 128].rearrange("(j p) e -> p j e", p=128), R=[bsc["v"]], W=[bvh])
                if Tk % 128:
                    kb.dma("sp", vh[0:Tk % 128, nfull, :], self.VB[si][nfull * 128:Tk, h * 128:(h + 1) * 128], R=[bsc["v"]], W=[bvh])
                st, bst = self.wstage()
                kb.dma("sp", st[:, 0:128], self.cst_corr[h], W=[bst])
                kb.op("pool", lambda e, st=st: e.tensor_copy(out=corrb[:], in_=st[:, 0:128]), R=[bst], W=[bcorr])

            pending = [None]
            heads = [(si, sq, h) for si, sq in enumerate(S) for h in range(8)]
            load_head(heads[0][0], heads[0][1], heads[0][2], kvset[0])
            for hi, (si, sq, h) in enumerate(heads):
                ks = kvset[hi % 2]
                kT, bkT, vh, bvh, corrb, bcorr = ks["kT"], ks["bkT"], ks["vh"], ks["bvh"], ks["corrb"], ks["bcorr"]
                bsc = self.bscr[si]
                P, GN, Tq = sq["P"], sq["GN"], sq["T"]
                kp0 = 0 if sq["s"] is None else PAST
                Tk = kp0 + Tq
                def load_q(g0, si=si, h=h, GN=GN, kp0=kp0, bsc=bsc):
                    qT, bqT = qTr()
                    if GN < 128:
                        kb.op("pool", lambda e, qT=qT: e.memset(qT[:, :, GN:128], 0.0), W=[bqT])
                    for m in range(2):
                        kb.dma("sp", qT[0:64, m, 0:GN], self.QT[si][h, m * 64:(m + 1) * 64, g0:g0 + GN], R=[bsc["q"]], W=[bqT])
                    st, bst = self.wstage()
                    kb.dma("sp", st[64:69, 0:GN], self.cst_qaug[h, :, kp0 + g0:kp0 + g0 + GN], W=[bst])
                    for m in range(2):
                        kb.op("pool", lambda e, m=m, st=st, qT=qT: e.tensor_copy(out=qT[64:69, m, 0:GN], in_=st[64:69, 0:GN]), R=[bst], W=[bqT])
                    return qT, bqT

                glist = list(range(0, Tq, GN))
                nextq = load_q(glist[0])
                for gidx, g0 in enumerate(glist):
                    qT, bqT = nextq
                    if gidx + 1 < len(glist):
                        nextq = load_q(glist[gidx + 1])
                    if gidx == 0 and hi + 1 < len(heads):
                        load_head(heads[hi + 1][0], heads[hi + 1][1], heads[hi + 1][2], kvset[(hi + 1) % 2])
                    GNq = max(GN, 128)
                    tiles = []
                    if sq["s"] is None:
                        i0, nt = g0 // 128, GN // 128
                        for jt in range(i0 + nt):
                            if jt < i0:
                                tiles.append((jt, 128, [(0, GN, False)]))
                            else:
                                c0 = (jt - i0) * 128
                                rg = [(c0, c0 + 128, True)]
                                if c0 + 128 < GN:
                                    rg.append((c0 + 128, GN, False))
                                tiles.append((jt, 128, rg))
                    else:
                        for jt in range(PAST // 128):
                            tiles.append((jt, 128, [(0, GNq, False)]))
                        tiles.append((PAST // 128, Tq, [(0, GNq, True)]))

                    def st_exp(ti):
                        jt, nk, rg = tiles[ti]
                        k0 = jt * 128
                        c0 = rg[0][0]
                        pts = []
                        for m in range(2):
                            ps, bps = self.bank()
                            for (a, b, isd) in rg:
                                kb.op("pe", lambda e, ps=ps, m=m, a=a, b=b, isd=isd: e.matmul(ps[0:nk, a:b], lhsT=kT[m][:, k0:k0 + nk], rhs=qT[:, m, a:b],
                                                                                         start=True, stop=(not isd)), R=[bkT, bqT], W=bps)
                                if isd:
                                    kb.op("pe", lambda e, ps=ps, a=a, b=b: e.matmul(ps[0:nk, a:b], lhsT=self.identb[0:nk, 0:nk], rhs=corrb[0:nk, 0:b - a],
                                                                                 start=False, stop=True), R=[bcorr, self.bconst], W=bps)
                            pt, bpt = ptr()
                            kb.op("act", lambda e, pt=pt, ps=ps: e.activation(out=pt[0:nk, c0:GNq], in_=ps[0:nk, c0:GNq], func=AF.Exp, scale=0.125), R=bps, W=[bpt])
                            pts.append((pt, bpt))
                        return pts

                    def pv(ti, pts):
                        jt, nk, rg = tiles[ti]
                        for m in range(2):
                            pt, bpt = pts[m]
                            for ri, (a, b, isd) in enumerate(rg):
                                first = (ti == 0 and ri == 0)
                                lastm = (ti == len(tiles) - 1 and ri == len(rg) - 1)
                                kb.op("pe", lambda e, pt=pt, m=m, a=a, b=b: e.matmul(acc[m][:, a:b], lhsT=vh[0:nk, jt, :], rhs=pt[0:nk, a:b],
                                                                                         start=first, stop=lastm), R=[bvh, bpt], W=bacc[m])
                                kb.op("pe", lambda e, pt=pt, m=m, a=a, b=b: e.matmul(acc[2 + m][:, a:b], lhsT=self.onesb[0:nk, :], rhs=pt[0:nk, a:b],
                                                                                         start=first, stop=lastm), R=[self.bconst, bpt], W=bacc[2 + m])

                    cur = st_exp(0)
                    for ti in range(len(tiles)):
                        nxt = st_exp(ti + 1) if ti + 1 < len(tiles) else None
                        pv(ti, cur)
                        cur = nxt
                        if ti == min(10, len(tiles) - 1) and pending[0] is not None:
                            pending[0]()
                            pending[0] = None
                    if pending[0] is not None:
                        pending[0]()
                        pending[0] = None
                    ac, bac = accr()
                    for i in range(4):
                        if i < 2:
                            kb.op("act", lambda e, i=i, ac=ac: e.activation(out=ac[:, i, 0:GN], in_=acc[i][:, 0:GN], func=AF.Copy), R=bacc[i], W=[bac])
                        else:
                            kb.op("dve", lambda e, i=i, ac=ac: e.tensor_copy(out=ac[:, i, 0:GN], in_=acc[i][:, 0:GN]), R=bacc[i], W=[bac])
                    kb.op("dve", lambda e, ac=ac: e.reciprocal(out=rr[:, 0:GN], in_=ac[:, 2, 0:GN]), R=[bac], W=[brr])
                    kb.op("dve", lambda e, ac=ac: e.tensor_tensor(out=ot[:, 0:GN], in0=ac[:, 0, 0:GN], in1=rr[:, 0:GN], op=ALU.mult), R=[bac, brr], W=[bot])
                    kb.op("dve", lambda e, ac=ac: e.reciprocal(out=rr[:, 0:GN], in_=ac[:, 3, 0:GN]), R=[bac], W=[brr])
                    kb.op("dve", lambda e, ac=ac: e.tensor_tensor(out=o1[:, 0:GN], in0=ac[:, 1, 0:GN], in1=rr[:, 0:GN], op=ALU.mult), R=[bac, brr], W=[bo1])
                    kb.op("dve", lambda e: e.scalar_tensor_tensor(out=ot[:, 0:GN], in0=o1[:, 0:GN], scalar=neglam, in1=ot[:, 0:GN], op0=ALU.mult, op1=ALU.add),
                          R=[bo1, bot, bsm], W=[bot])
                    kb.op("pool", lambda e: e.tensor_tensor(out=o1[:, 0:GN], in0=ot[:, 0:GN], in1=ot[:, 0:GN], op=ALU.mult), R=[bot], W=[bo1])

                    def part_b(si=si, h=h, g0=g0, GN=GN, GNq=GNq, bsc=bsc):
                        pm, bm = self.bank()
                        kb.op("pe", lambda e, pm=pm: e.matmul(pm[:, 0:GNq], lhsT=self.m128[:], rhs=o1[:, 0:GNq], start=True, stop=True), R=[bo1, self.bconst], W=bm)
                        kb.op("act", lambda e, pm=pm: e.activation(out=rr[:, 0:GN], in_=pm[:, 0:GN], func=AF.Ln, bias=self.epsb[:, 0:1]), R=bm + [self.bconst], W=[brr])
                        kb.op("act", lambda e: e.activation(out=rr[:, 0:GN], in_=rr[:, 0:GN], func=AF.Exp, scale=-0.5), R=[brr], W=[brr])
                        on, bon = onr()
                        kb.op("dve", lambda e, on=on: e.scalar_tensor_tensor(out=on[:, 0:GN], in0=ot[:, 0:GN], scalar=subg[:, 0, :], in1=rr[:, 0:GN], op0=ALU.mult, op1=ALU.mult),
                              R=[bot, brr, bsm], W=[bon])
                        kb.dma("pool", self.OT[si][h * 128:(h + 1) * 128, g0:g0 + GN], on[:, 0:GN], R=[bon], W=[bsc["o"]])
                    pending[0] = part_b
            if pending[0] is not None:
                pending[0]()
                pending[0] = None
            self.bank_list = save_banks
            kb.barrier()

    def phase_e3(self, l, j, S):
        kb = self.kb
        stage = 2 * l
        with ExitStack() as es:
            wo = self.sb(es, "wo", [128, 16, D], BF16); bwo = Buf()
            self.load_w(wo, self.hyb_w_out[j].rearrange("(k p) n -> k p n", p=128), 16, D, bwo)
            GNmax = S[0]["GN"]
            oyr = self.rot("oy", es, [128, 16, GNmax], BF16, 2)
            for si, sq in enumerate(S):
                gs, sh, gg, bbc = self.make_bc(l, 0, sq["ci"])
                src, dst = self.xio(sq, stage)
                bx = self.bscr[si]["x"]
                bsc = self.bscr[si]
                P, GN, Tq = sq["P"], sq["GN"], sq["T"]
                def load_oy(g0, si=si, GN=GN, bsc=bsc):
                    t, bt = oyr()
                    kb.dma("sp", t[:, 0:8, 0:GN], self.OT[si].rearrange("(c p) t -> p c t", p=128)[:, :, g0:g0 + GN], R=[bsc["o"]], W=[bt])
                    kb.dma("sp", t[:, 8:16, 0:GN], self.YT[si].rearrange("(c p) t -> p c t", p=128)[:, :, g0:g0 + GN], R=[bsc["y"]], W=[bt])
                    return t, bt

                glist = list(range(0, Tq, GN))
                nxt = load_oy(glist[0])
                for gidx, g0 in enumerate(glist):
                    nt = GN // P
                    t, bt = nxt
                    if gidx + 1 < len(glist):
                        nxt = load_oy(glist[gidx + 1])
                    for m in range(nt):
                        pp, bpp = self.pair()
                        for nh in range(2):
                            for c in range(16):
                                kb.op("pe", lambda e, c=c, nh=nh, m=m, pp=pp, t=t: e.matmul(pp[0:P, nh * 512:(nh + 1) * 512], lhsT=t[:, c, m * P:(m + 1) * P],
                                                                                          rhs=wo[:, c, nh * 512:(nh + 1) * 512], start=(c == 0), stop=(c == 15)),
                                      R=[bt, bwo], W=bpp)
                        r0 = g0 + m * P
                        self.post(pp, bpp, P, src[r0:r0 + P, :], dst[r0:r0 + P, :], gg, bbc, bx)
            kb.barrier()

    def phase_conf(self, l, j, S):
        kb = self.kb
        stage = 2 * l
        with ExitStack() as es:
            win = self.sb(es, "cwin", [128, 8, 2 * D], BF16); bwin = Buf()
            wout = self.sb(es, "cwout", [128, 8, D], BF16); bwout = Buf()
            self.load_w(win, self.conf_w_in[j].rearrange("(k p) n -> k p n", p=128), 8, 2 * D, bwin)
            self.load_w(wout, self.conf_w_out[j].rearrange("(k p) n -> k p n", p=128), 8, D, bwout)
            bsm = Buf("confsmall")
            dwf = self.sb(es, "dwf", [128, 8, 31], F32)
            self.load_fm(es, lambda c: dwf[:, c, :], self.conf_dw_w[j], 31, 8, bsm)
            bin_ = self.sb(es, "binf", [128, 16, 1], F32)
            self.load_fm(es, lambda c: bin_[:, c, :], self.conf_b_in[j:j + 1, :], 1, 16, bsm)
            vecs = self.sb(es, "cvecs", [128, 3, 8, 1], F32)
            for i, src in enumerate((self.conf_dw_b, self.conf_ln_g, self.conf_ln_b)):
                self.load_fm(es, lambda c, i=i: vecs[:, i, c, :], src[j:j + 1, :], 1, 8, bsm)
            diag = self.sb(es, "cdiag", [128, 8 * 31, 128], BF16)
            kb.op("dve", lambda e: e.tensor_tensor(out=diag[:], in0=self.identf.unsqueeze(1).to_broadcast([128, 248, 128]),
                                                   in1=dwf[:].rearrange("p c k -> p (c k)").unsqueeze(2).to_broadcast([128, 248, 128]),
                                                   op=ALU.mult), R=[bsm, self.bconst], W=[bsm])
            bor = self.sb(es, "bor", [1, D], F32)
            kb.dma("sp", bor[:], self.conf_b_out[j:j + 1, :], W=[bsm])
            borb = self.sb(es, "borb", [1, D], BF16)
            kb.op("pool", lambda e: e.tensor_copy(out=borb[:], in_=bor[:]), R=[bsm], W=[bsm])
            if CONF_STOP <= 1:
                kb.barrier()
                return
            GNmax = S[0]["GN"]
            hT = self.sb(es, "chT", [128, 8, GNmax], BF16); bhT = Buf()
            uT = self.sb(es, "uT", [128, 8, 30 + GNmax], BF16); buT = Buf()
            uF = self.sb(es, "uF", [128, 8, 32], F32); buF = Buf()
            yT = self.sb(es, "yT", [128, 8, GNmax], F32); byT = Buf()
            ynT = self.sb(es, "ynT", [128, 8, GNmax], BF16); bynT = Buf()
            sgr = self.rot("csg", es, [128, GNmax], F32, 1)
            mu = self.sb(es, "cmu", [128, GNmax], F32); bmu = Buf()
            rs = self.sb(es, "crs", [128, GNmax], F32); brs = Buf()
            kb.op("pool", lambda e: e.memset(hT[:], 0.0), W=[bhT])
            kb.op("pool", lambda e: e.memset(uT[:], 0.0), W=[buT])
            kb.op("pool", lambda e: e.memset(yT[:], 0.0), W=[byT])
            kb.op("pool", lambda e: e.memset(ynT[:], 0.0), W=[bynT])
            for si, sq in enumerate(S):
                if CONF_SEQS is not None and si not in CONF_SEQS:
                    continue
                gs, sh, gg, bbc = self.make_bc(l, 0, sq["ci"])
                src, dst = self.xio(sq, stage)
                bx = self.bscr[si]["x"]
                P, GN, Tq = sq["P"], sq["GN"], sq["T"]
                GNp = max(GN, 128)
                if sq["s"] is None:
                    kb.op("pool", lambda e: e.memset(uT[:, :, 0:30], 0.0), W=[buT])
                else:
                    self.load_fm(es, lambda c: uT[:, c, 0:30], self.st_cconv[j, sq["s"]], 30, 8, buT)
                ngr = Tq // GN
                nt = GN // P
                nl = min(30, GN)

                def stage_a(gi):
                    g0 = gi * GN
                    last = gi == ngr - 1
                    for m in range(nt):
                        self.prelude_x(src[g0 + m * P:g0 + (m + 1) * P, :], bx, P, hT, bhT, m * P, gs, sh, bbc)
                    nl = min(30, GN)
                    for c in range(8):
                        pa, ba = self.bank()
                        pg, bg = self.bank()
                        for k in range(8):
                            kb.op("pe", lambda e, k=k, pa=pa, c=c: e.matmul(pa[:, 0:GNp], lhsT=win[:, k, c * 128:(c + 1) * 128], rhs=hT[:, k, 0:GNp],
                                                                           start=(k == 0), stop=(k == 7)), R=[bwin, bhT], W=ba)
                        for k in range(8):
                            kb.op("pe", lambda e, k=k, pg=pg, c=c: e.matmul(pg[:, 0:GNp], lhsT=win[:, k, D + c * 128:D + (c + 1) * 128], rhs=hT[:, k, 0:GNp],
                                                                           start=(k == 0), stop=(k == 7)), R=[bwin, bhT], W=bg)
                        sg, bsg = sgr()
                        kb.op("act", lambda e, sg=sg, pg=pg, c=c: e.activation(out=sg[:, 0:GN], in_=pg[:, 0:GN], func=AF.Sigmoid, bias=bin_[:, 8 + c, :]),
                              R=bg + [bsm], W=[bsg])
                        kb.op("dve", lambda e, sg=sg, pa=pa, c=c: e.scalar_tensor_tensor(out=uT[:, c, 30:30 + GN], in0=pa[:, 0:GN], scalar=bin_[:, c, :],
                                                                                        in1=sg[:, 0:GN], op0=ALU.add, op1=ALU.mult),
                              R=ba + [bsg, bsm], W=[buT])
                        if last:
                            kb.op("dve", lambda e, sg=sg, pa=pa, c=c: e.scalar_tensor_tensor(out=uF[:, c, 0:nl], in0=pa[:, GN - nl:GN], scalar=bin_[:, c, :],
                                                                                            in1=sg[:, GN - nl:GN], op0=ALU.add, op1=ALU.mult),
                                  R=ba + [bsg, bsm], W=[buF])

                def stage_conv(gi):
                    for c in range(8):
                        py, by_ = self.bank()
                        for k in range(31):
                            kb.op("pe", lambda e, k=k, py=py, c=c: e.matmul(py[:, 0:GNp], lhsT=diag[:, c * 31 + k, :], rhs=uT[:, c, k:k + GNp],
                                                                           start=(k == 0), stop=(k == 30)), R=[bsm, buT], W=by_)
                        kb.op("act", lambda e, py=py, c=c: e.activation(out=yT[:, c, 0:GN], in_=py[:, 0:GN], func=AF.Identity, bias=vecs[:, 0, c, :]),
                              R=by_ + [bsm], W=[byT])

                def stage_c(gi):
                    g0 = gi * GN
                    pm, bm = self.bank()
                    for c in range(8):
                        kb.op("pe", lambda e, c=c, pm=pm: e.matmul(pm[:, 0:GNp], lhsT=self.m1024[:], rhs=yT[:, c, 0:GNp], start=(c == 0), stop=(c == 7)),
                              R=[byT, self.bconst], W=bm)
                    kb.op("act", lambda e, pm=pm: e.activation(out=mu[:, 0:GN], in_=pm[:, 0:GN], func=AF.Copy), R=bm, W=[bmu])
                    kb.op("dve", lambda e: e.tensor_tensor(out=yT[:, :, 0:GN], in0=yT[:, :, 0:GN], in1=mu[:, 0:GN].unsqueeze(1).to_broadcast([128, 8, GN]),
                                                           op=ALU.subtract), R=[byT, bmu], W=[byT])
                    kb.op("pool", lambda e: e.tensor_tensor(out=ynT[:, :, 0:GN], in0=yT[:, :, 0:GN], in1=yT[:, :, 0:GN], op=ALU.mult), R=[byT], W=[bynT])
                    pv, bv = self.bank()
                    for c in range(8):
                        kb.op("pe", lambda e, c=c, pv=pv: e.matmul(pv[:, 0:GNp], lhsT=self.onesb[:, :], rhs=ynT[:, c, 0:GNp], start=(c == 0), stop=(c == 7)),
                              R=[bynT, self.bconst], W=bv)
                    kb.op("act", lambda e, pv=pv: e.activation(out=rs[:, 0:GN], in_=pv[:, 0:GN], func=AF.Sqrt, scale=1.0 / 1024, bias=self.epsb[:, 0:1]), R=bv + [self.bconst], W=[brs])
                    kb.op("dve", lambda e: e.reciprocal(out=rs[:, 0:GN], in_=rs[:, 0:GN]), R=[brs], W=[brs])
                    kb.op("dve", lambda e: e.tensor_tensor(out=yT[:, :, 0:GN], in0=yT[:, :, 0:GN], in1=rs[:, 0:GN].unsqueeze(1).to_broadcast([128, 8, GN]),
                                                           op=ALU.mult), R=[byT, brs], W=[byT])
                    for c in range(8):
                        kb.op("act", lambda e, c=c: e.activation(out=ynT[:, c, 0:GN], in_=yT[:, c, 0:GN], func=AF.Silu, scale=vecs[:, 1, c, :], bias=vecs[:, 2, c, :]),
                              R=[byT, bsm], W=[bynT])
                    for m in range(nt):
                        pp, bpp = self.pair()
                        for nh in range(2):
                            for c in range(8):
                                kb.op("pe", lambda e, c=c, nh=nh, m=m, pp=pp: e.matmul(pp[0:P, nh * 512:(nh + 1) * 512], lhsT=ynT[:, c, m * P:(m + 1) * P],
                                                                                     rhs=wout[:, c, nh * 512:(nh + 1) * 512], start=(c == 0), stop=False),
                                      R=[bynT, bwout], W=bpp)
                            kb.op("pe", lambda e, nh=nh, pp=pp: e.matmul(pp[0:P, nh * 512:(nh + 1) * 512], lhsT=self.onesb[0:1, 0:P],
                                                                        rhs=borb[0:1, nh * 512:(nh + 1) * 512], start=False, stop=True),
                                  R=[bsm, self.bconst], W=bpp)
                        r0 = g0 + m * P
                        self.post(pp, bpp, P, src[r0:r0 + P, :], dst[r0:r0 + P, :], gg, bbc, bx)

                stage_a(0)
                for gi in range(ngr):
                    stage_conv(gi)
                    if gi + 1 < ngr:
                        kb.op("pool", lambda e: e.tensor_copy(out=uT[:, :, 0:30], in_=uT[:, :, GN:GN + 30]), R=[buT], W=[buT])
                        stage_a(gi + 1)
                    stage_c(gi)
                if CONF_STOP <= 5:
                    continue
                nl = min(30, GN)
                pp, bpp = self.pair()
                for c in range(8):
                    kb.op("pe", lambda e, c=c, pp=pp: e.transpose(out=pp[0:nl, c * 128:(c + 1) * 128], in_=uF[:, c, 0:nl], identity=self.identf[:, :]),
                          R=[buF, self.bconst], W=bpp)
                ot, bo = self.xrot()
                kb.op("act", lambda e, pp=pp, ot=ot: e.activation(out=ot[0:nl, :], in_=pp[0:nl, :], func=AF.Copy), R=bpp, W=[bo])
                if sq["s"] is None:
                    kb.dma("pool", self.cconv_p[j], ot[0:30, :], R=[bo], W=[self.bout])
                else:
                    s_ = sq["s"]
                    kb.dma("pool", self.cconv_s[j, s_, 30 - nl:30, :], ot[0:nl, :], R=[bo], W=[self.bout])
                    if nl < 30:
                        kb.dma("pool", self.cconv_s[j, s_, 0:30 - nl, :], self.st_cconv[j, s_, nl:30, :], W=[self.bout])
            kb.barrier()


_PROG = {}


def _get_prog(T, depth):
    key = (T, depth)
    if key not in _PROG:
        _PROG[key] = Prog(T=T, depth=depth)
    return _PROG[key]


def make_in_maps(inp, T, depth):
    NE, NO = (depth + 1) // 2, depth // 2
    cf, kaug, qaug, corr = _consts()
    f = lambda a: np.ascontiguousarray(np.asarray(a, dtype=np.float32))
    shared = {}
    for nm in ("ada_w", "ada_b", "norm_g", "ffn_w_up", "ffn_w_down", "hyb_w_in", "attn_subln_g", "ssm_conv_w", "ssm_conv_b",
               "ssm_dt_bias", "ssm_a_log", "ssm_d", "ssm_norm_g", "hyb_w_out", "conf_w_in", "conf_b_in", "conf_dw_w",
               "conf_dw_b", "conf_ln_g", "conf_ln_b", "conf_w_out", "conf_b_out"):
        shared[nm] = f(inp[nm])
    shared["attn_lambda"] = f(inp["attn_lambda"]).reshape(NE, 256)
    shared["cst_f"], shared["cst_kaug"], shared["cst_qaug"], shared["cst_corr"] = cf, kaug, qaug, corr
    maps = []
    for c in range(NCORES):
        m = dict(shared)
        m["x_p"] = f(inp["x_prompt"][c])
        m["x_s"] = f(inp["x_sample"][2 * c:2 * c + 2])
        m["c_all"] = f(np.concatenate([inp["c_prompt"][c:c + 1], inp["c_sample"][2 * c:2 * c + 2]], 0))
        m["cache_k"] = f(np.asarray(inp["cache_attn_k"])[:, 2 * c:2 * c + 2].reshape(NE, 2, -1, D))
        m["cache_v"] = f(np.asarray(inp["cache_attn_v"])[:, 2 * c:2 * c + 2].reshape(NE, 2, -1, D))
        m["st_sconv"] = f(np.asarray(inp["state_ssm_conv"])[:, 2 * c:2 * c + 2])
        m["st_ssm"] = f(np.asarray(inp["state_ssm"])[:, 2 * c:2 * c + 2].reshape(NE, 2, 1024, 128))
        m["st_cconv"] = f(np.asarray(inp["state_conf_conv"])[:, 2 * c:2 * c + 2])
        maps.append(m)
    return maps


def gather(res, T, depth, TS=16):
    NE, NO = (depth + 1) // 2, depth // 2
    R = res.results
    st = lambda k, ax=0: np.stack([np.asarray(r[k]) for r in R], ax)
    cat = lambda k, ax: np.concatenate([np.asarray(r[k]) for r in R], ax)
    y_p = st("y_p")
    y_s = cat("y_s", 0)
    k_p = st("k_p", 1).reshape(NE, NCORES, T, 8, 128)
    v_p = st("v_p", 1).reshape(NE, NCORES, T, 8, 128)
    sconv_p = st("sconv_p", 1)
    ssm_p = st("ssm_p", 1).reshape(NE, NCORES, 16, 64, 128)
    cconv_p = st("cconv_p", 1)
    k_s = cat("k_s", 1).reshape(NE, 2 * NCORES, TS, 8, 128)
    v_s = cat("v_s", 1).reshape(NE, 2 * NCORES, TS, 8, 128)
    sconv_s = cat("sconv_s", 1)
    ssm_s = cat("ssm_s", 1).reshape(NE, 2 * NCORES, 16, 64, 128)
    cconv_s = cat("cconv_s", 1)
    return (y_p, y_s, k_p, v_p, sconv_p, ssm_p, cconv_p, k_s, v_s, sconv_s, ssm_s, cconv_s)


def kernel(**inputs):
    T = int(np.asarray(inputs["x_prompt"]).shape[1])
    depth = int(np.asarray(inputs["ada_w"]).shape[0])
    prog = _get_prog(T, depth)
    maps = make_in_maps(inputs, T, depth)
    res = run_bass_kernel_spmd(prog.nc, maps, core_ids=list(range(NCORES)))
    outs = gather(res, T, depth)
    return tuple(np.ascontiguousarray(o, dtype=np.float32) for o in outs)
```

```python
import math
import numpy as np
import concourse.bass as bass
import concourse.mybir as mybir
from concourse.bass_utils import run_bass_kernel_spmd
from contextlib import ExitStack

F32 = mybir.dt.float32
BF16 = mybir.dt.bfloat16
AF = mybir.ActivationFunctionType
ALU = mybir.AluOpType
AX = mybir.AxisListType

D = 1024
DFF = 2816
DIN = 5648
EPS = 1e-6
NCORES = 8
SKIP = set()
DEBUG = False
CONF_STOP = 99
E1B_STOP = 99
CONF_SEQS = None


class Buf:
    __slots__ = ("name", "w", "r", "excl")

    def __init__(self, name="", excl=False):
        self.name = name
        self.w = None
        self.r = {}
        self.excl = excl


class KB:
    def __init__(self, nc, es, n_lanes=8):
        self.nc = nc
        self.eng = {"pe": nc.tensor, "act": nc.scalar, "dve": nc.vector, "pool": nc.gpsimd, "sp": nc.sync}
        self.sem, self.cnt, self.mult = {}, {}, {}
        for e in self.eng:
            self.sem[e] = es.enter_context(nc.semaphore("s_" + e))
            self.cnt[e] = 0
            self.mult[e] = 1
        self.lanes = {}
        for q in ("sp", "pool"):
            ls = []
            for i in range(n_lanes):
                key = "L%s%d" % (q, i)
                self.sem[key] = es.enter_context(nc.semaphore("s_" + key))
                self.cnt[key] = 0
                self.mult[key] = 16
                ls.append(key)
            self.lanes[q] = ls
        self.lane_rr = {q: 0 for q in self.lanes}
        self.known = {e: {} for e in self.eng}
        self.nins = 0
        self.nwait = 0

    def _need(self, deps, key, k):
        if deps.get(key, 0) < k:
            deps[key] = k

    def _collect(self, R, W, eng=None):
        deps = {}
        for b in R:
            if b.w is not None:
                self._need(deps, b.w[0], b.w[1])
            if b.excl:
                for e, k in b.r.items():
                    if e != eng:
                        self._need(deps, e, k)
        for b in W:
            if b.w is not None:
                self._need(deps, b.w[0], b.w[1])
            for e, k in b.r.items():
                self._need(deps, e, k)
        return deps

    def _emit_waits(self, e, deps):
        kn = self.known[e]
        for key, k in deps.items():
            if key == e and e == "pe":
                continue
            if kn.get(key, 0) >= k:
                continue
            self.eng[e].wait_ge(self.sem[key], k * self.mult[key])
            kn[key] = k
            self.nwait += 1

    def _mark(self, key, k, R, W):
        for b in R:
            if b.r.get(key, 0) < k:
                b.r[key] = k
        for b in W:
            b.w = (key, k)
            b.r = {}

    def op(self, e, fn, R=(), W=()):
        self._emit_waits(e, self._collect(R, W, e))
        ins = fn(self.eng[e])
        self.cnt[e] += 1
        ins.then_inc(self.sem[e], 1)
        self._mark(e, self.cnt[e], R, W)
        self.nins += 1

    def dma(self, q, out, in_, R=(), W=()):
        ls = self.lanes[q]
        lane = ls[self.lane_rr[q] % len(ls)]
        self.lane_rr[q] += 1
        deps = self._collect(R, W)
        if self.cnt[lane] > 0:
            self._need(deps, lane, self.cnt[lane])
        self._emit_waits(q, deps)
        ins = self.eng[q].dma_start(out=out, in_=in_)
        self.cnt[lane] += 1
        ins.then_inc(self.sem[lane], 16)
        self._mark(lane, self.cnt[lane], R, W)
        self.nins += 1

    def barrier(self):
        for e in self.eng:
            deps = {k2: self.cnt[k2] for k2 in self.cnt if self.cnt[k2] > 0 and k2 != e}
            self._emit_waits(e, deps)


def _consts():
    c = {}
    i = np.arange(128)
    c["ident"] = np.eye(128, dtype=np.float32)
    c["ones"] = np.ones((128, 128), np.float32)
    c["trile"] = (i[:, None] <= i[None, :]).astype(np.float32)
    c["ugt"] = (i[:, None] > i[None, :]).astype(np.float32)
    cf = np.stack([c["ident"], c["ones"], c["trile"], c["ugt"]], 0)
    pos = np.arange(8192)
    kaug = np.zeros((8, 5, 8192), np.float32)
    qaug = np.zeros((8, 5, 8192), np.float32)
    corr = np.zeros((8, 128, 128), np.float32)
    for h in range(8):
        s8 = 8.0 * 2.0 ** (-(h + 1))
        ph, pl = (pos // 128).astype(np.float32), (pos % 128).astype(np.float32)
        qaug[h, 0] = -s8 * 128.0 * ph
        qaug[h, 1] = -s8 * pl
        qaug[h, 2] = 0.0
        qaug[h, 3] = 1.0
        qaug[h, 4] = 1.0
        kaug[h, 0] = 1.0
        kaug[h, 1] = 1.0
        kaug[h, 2] = 1.0
        kaug[h, 3] = s8 * 128.0 * ph
        kaug[h, 4] = s8 * pl
        kk, qq = i[:, None], i[None, :]
        cm = np.where(kk > qq, -2.0 * s8 * (kk - qq), 0.0)
        cm = np.where((kk // 64) > (qq // 64), -240000.0, cm)
        corr[h] = cm
    return cf, kaug, qaug, corr


class Prog:
    def __init__(self, T=8192, depth=4, TS=16, PAST=1024):
        self.T, self.depth, self.TS, self.PAST = T, depth, TS, PAST
        self.NE = (depth + 1) // 2
        self.NO = depth // 2
        self.nc = bass.Bass("TRN2", target_bir_lowering=False)
        self.build()

    def din(self, name, shape, dt=F32):
        return self.nc.dram_tensor(name, list(shape), dt, kind="ExternalInput").ap()

    def dout(self, name, shape, dt=F32):
        return self.nc.dram_tensor(name, list(shape), dt, kind="ExternalOutput").ap()

    def dscr(self, name, shape, dt=F32):
        return self.nc.dram_tensor(name, list(shape), dt, kind="Internal").ap()

    def sb(self, es, name, shape, dt):
        self._uid += 1
        return es.enter_context(self.nc.sbuf_tensor("%s_%d" % (name, self._uid), list(shape), dt))

    def bank(self):
        i = self.bank_list[self.bank_rr % len(self.bank_list)]
        self.bank_rr += 1
        return self.PP[i // 2][:, (i % 2) * 512:(i % 2) * 512 + 512], [self.pb[i]]

    def pair(self):
        i = self.pair_list[self.pair_rr % len(self.pair_list)]
        self.pair_rr += 1
        return self.PP[i], [self.pb[2 * i], self.pb[2 * i + 1]]

    def rot(self, key, es, shape, dt, n=2):
        tiles = [(self.sb(es, key, shape, dt), Buf(key)) for _ in range(n)]
        st = {"i": 0}

        def nxt():
            t = tiles[st["i"] % n]
            st["i"] += 1
            return t
        return nxt

    def load_fm(self, es, dst_fn, src, R, nch, bdst, q="sp", pad=0):
        kb = self.kb
        Rp = R + pad
        for b0 in range(0, nch, 8):
            nb = min(8, nch - b0)
            st, bst = self.wstage()
            if pad:
                kb.op("pool", lambda e, st=st: e.memset(st[0:Rp, :], 0.0), W=[bst])
            kb.dma(q, st[pad:Rp, 0:nb * 128], src[:, b0 * 128:(b0 + nb) * 128], W=[bst])
            for c0 in range(0, nb, 4):
                pb, bb = self.bank()
                n4 = min(4, nb - c0)
                for c in range(c0, c0 + n4):
                    kb.op("pe", lambda e, c=c, pb=pb, st=st: e.transpose(out=pb[:, (c - c0) * 32:(c - c0) * 32 + Rp],
                                                            in_=st[0:Rp, c * 128:(c + 1) * 128],
                                                            identity=self.identf[0:Rp, 0:Rp]), R=[bst, self.bconst], W=bb)
                for c in range(c0, c0 + n4):
                    kb.op("act", lambda e, c=c, pb=pb: e.activation(out=dst_fn(b0 + c), in_=pb[:, (c - c0) * 32:(c - c0) * 32 + Rp],
                                                             func=AF.Copy), R=bb, W=[bdst])

    def load_w(self, dst, src3, nk, ncols, bdst, c0=0):
        kb = self.kb
        for k in range(nk):
            for a in range(0, ncols, 1024):
                w = min(1024, ncols - a)
                st, bst = self.wstage()
                kb.dma("sp", st[:, 0:w], src3[k, :, c0 + a:c0 + a + w], W=[bst])
                kb.op("pool", lambda e, st=st, k=k, a=a, w=w: e.tensor_copy(out=dst[:, k, a:a + w], in_=st[:, 0:w]),
                      R=[bst], W=[bdst])

    def prelude(self, xsrc, P, hT, bhT, col0, gs, sh, bbc):
        kb = self.kb
        xt, bx = self.xrot()
        kb.dma("sp", xt[0:P, :], xsrc, W=[bx])
        junk, bj = self.frot()
        ssq, bs = self.srot()
        kb.op("act", lambda e: e.activation(out=junk[0:P, :], in_=xt[0:P, :], func=AF.Square, accum_out=ssq[0:P, 0:1]),
              R=[bx], W=[bj, bs])
        kb.op("act", lambda e: e.activation(out=ssq[0:P, 1:2], in_=ssq[0:P, 0:1], func=AF.Sqrt, scale=1.0 / D, bias=self.epsb[0:P, 0:1]),
              R=[bs, self.bconst], W=[bs])
        kb.op("dve", lambda e: e.reciprocal(out=ssq[0:P, 1:2], in_=ssq[0:P, 1:2]), R=[bs], W=[bs])
        kb.op("dve", lambda e: e.scalar_tensor_tensor(out=junk[0:P, :], in0=xt[0:P, :], scalar=ssq[0:P, 1:2], in1=gs[0:P, :],
                                                       op0=ALU.mult, op1=ALU.mult), R=[bx, bs, bbc], W=[bj])
        hb, bh = self.hrot()
        kb.op("dve", lambda e: e.tensor_tensor(out=hb[0:P, :], in0=junk[0:P, :], in1=sh[0:P, :], op=ALU.add),
              R=[bj, bbc], W=[bh])
        self.transpose_to(hb, bh, P, 8, lambda k: hT[:, k, col0:col0 + P], bhT, dst3=hT[:, 0:8, col0:col0 + P])

    def transpose_to(self, src, bsrc, P, nch, dst_fn, bdst, dst3=None):
        kb = self.kb
        if dst3 is not None and nch == 8:
            pb, bb = self.bank()
            pbb = pb.bitcast(BF16)
            for k in range(8):
                kb.op("pe", lambda e, k=k: e.transpose(out=pbb[:, k * 128:k * 128 + P], in_=src[0:P, k * 128:(k + 1) * 128],
                                                        identity=self.identb[0:P, 0:P]), R=[bsrc, self.bconst], W=bb)
            kb.op("act", lambda e: e.activation(out=dst3, in_=pbb.rearrange("p (k c) -> p k c", k=8)[:, :, 0:P], func=AF.Copy), R=bb, W=[bdst])
            return
        for k0 in range(0, nch, 8):
            pb, bb = self.bank()
            pbb = pb.bitcast(BF16)
            n8 = min(8, nch - k0)
            for k in range(k0, k0 + n8):
                kb.op("pe", lambda e, k=k: e.transpose(out=pbb[:, (k - k0) * 128:(k - k0) * 128 + P],
                                                        in_=src[0:P, k * 128:(k + 1) * 128],
                                                        identity=self.identb[0:P, 0:P]), R=[bsrc, self.bconst], W=bb)
            for k in range(k0, k0 + n8):
                kb.op("act", lambda e, k=k: e.activation(out=dst_fn(k), in_=pbb[:, (k - k0) * 128:(k - k0) * 128 + P],
                                                         func=AF.Copy), R=bb, W=[bdst])

    def post(self, pp, bpp, P, xsrc, xdst, gg, bbc, bdram):
        kb = self.kb
        junk, bj = self.frot()
        ssq, bs = self.srot()
        kb.op("act", lambda e: e.activation(out=junk[0:P, :], in_=pp[0:P, :], func=AF.Square, accum_out=ssq[0:P, 0:1]),
              R=bpp, W=[bj, bs])
        kb.op("act", lambda e: e.activation(out=ssq[0:P, 1:2], in_=ssq[0:P, 0:1], func=AF.Sqrt, scale=1.0 / D, bias=self.epsb[0:P, 0:1]),
              R=[bs, self.bconst], W=[bs])
        kb.op("dve", lambda e: e.reciprocal(out=ssq[0:P, 1:2], in_=ssq[0:P, 1:2]), R=[bs], W=[bs])
        kb.op("dve", lambda e: e.scalar_tensor_tensor(out=junk[0:P, :], in0=pp[0:P, :], scalar=ssq[0:P, 1:2], in1=gg[0:P, :],
                                                       op0=ALU.mult, op1=ALU.mult), R=bpp + [bs, bbc], W=[bj])
        xt, bx = self.xrot()
        kb.dma("sp", xt[0:P, :], xsrc, R=[bdram], W=[bx])
        kb.op("dve", lambda e: e.tensor_tensor(out=xt[0:P, :], in0=xt[0:P, :], in1=junk[0:P, :], op=ALU.add),
              R=[bx, bj], W=[bx])
        kb.dma("pool", xdst, xt[0:P, :], R=[bx], W=[bdram])

    def make_bc(self, l, sub, ci):
        kb = self.kb
        gs, sh, gg = self.bc_gs, self.bc_sh, self.bc_gg
        bbc = self.bbc
        tmp, btmp = self.wstage()
        tmp2, btmp2 = self.wstage()
        base = sub * 3 * D
        mrow = lambda a: self.MOD[l, ci:ci + 1, base + a * D: base + (a + 1) * D].partition_broadcast(128)
        grow = lambda i: self.norm_g[l, i:i + 1, :].partition_broadcast(128)
        kb.dma("sp", sh[:], mrow(0), R=[self.bmod], W=[bbc])
        kb.dma("sp", gs[:], mrow(1), R=[self.bmod], W=[bbc])
        kb.dma("sp", gg[:], mrow(2), R=[self.bmod], W=[bbc])
        kb.dma("sp", tmp[:], grow(2 * sub), W=[btmp])
        kb.op("dve", lambda e: e.scalar_tensor_tensor(out=gs[:], in0=gs[:], scalar=1.0, in1=tmp[:], op0=ALU.add, op1=ALU.mult),
              R=[bbc, btmp], W=[bbc])
        kb.dma("sp", tmp2[:], grow(2 * sub + 1), W=[btmp2])
        kb.op("dve", lambda e: e.tensor_tensor(out=gg[:], in0=gg[:], in1=tmp2[:], op=ALU.mult), R=[bbc, btmp2], W=[bbc])
        return gs, sh, gg, bbc

    def seqs(self):
        T, TS = self.T, self.TS
        S = []
        S.append(dict(name="p", ci=0, T=T, P=128, GN=min(512, T), x0=self.x_p, xb=self.xb_p, y=self.y_p, s=None))
        for s in range(2):
            S.append(dict(name="s%d" % s, ci=1 + s, T=TS, P=TS, GN=TS, x0=self.x_s[s], xb=self.xb_s[s], y=self.y_s[s], s=s))
        return S

    def xio(self, sq, stage):
        act = self.active_stages
        src = sq["x0"] if stage == act[0] else sq["xb"]
        dst = sq["y"] if stage == act[-1] else sq["xb"]
        return src, dst

    def build(self):
        nc = self.nc
        T, TS, PAST, NE, NO, depth = self.T, self.TS, self.PAST, self.NE, self.NO, self.depth
        TK = PAST + TS
        self._uid = 0
        self.x_p = self.din("x_p", [T, D])
        self.x_s = self.din("x_s", [2, TS, D])
        self.c_all = self.din("c_all", [3, D])
        self.cache_k = self.din("cache_k", [NE, 2, PAST, D])
        self.cache_v = self.din("cache_v", [NE, 2, PAST, D])
        self.st_sconv = self.din("st_sconv", [NE, 2, 3, 1536])
        self.st_ssm = self.din("st_ssm", [NE, 2, 1024, 128])
        self.st_cconv = self.din("st_cconv", [max(NO, 1), 2, 30, D])
        self.ada_w = self.din("ada_w", [depth, D, 6 * D])
        self.ada_b = self.din("ada_b", [depth, 6 * D])
        self.norm_g = self.din("norm_g", [depth, 4, D])
        self.ffn_w_up = self.din("ffn_w_up", [depth, D, 2 * DFF])
        self.ffn_w_down = self.din("ffn_w_down", [depth, DFF, D])
        self.hyb_w_in = self.din("hyb_w_in", [NE, D, DIN])
        self.attn_lambda = self.din("attn_lambda", [NE, 256])
        self.attn_subln_g = self.din("attn_subln_g", [NE, 128])
        self.ssm_conv_w = self.din("ssm_conv_w", [NE, 4, 1536])
        self.ssm_conv_b = self.din("ssm_conv_b", [NE, 1536])
        self.ssm_dt_bias = self.din("ssm_dt_bias", [NE, 16])
        self.ssm_a_log = self.din("ssm_a_log", [NE, 16])
        self.ssm_d = self.din("ssm_d", [NE, 16])
        self.ssm_norm_g = self.din("ssm_norm_g", [NE, D])
        self.hyb_w_out = self.din("hyb_w_out", [NE, 2 * D, D])
        self.conf_w_in = self.din("conf_w_in", [max(NO, 1), D, 2 * D])
        self.conf_b_in = self.din("conf_b_in", [max(NO, 1), 2 * D])
        self.conf_dw_w = self.din("conf_dw_w", [max(NO, 1), 31, D])
        self.conf_dw_b = self.din("conf_dw_b", [max(NO, 1), D])
        self.conf_ln_g = self.din("conf_ln_g", [max(NO, 1), D])
        self.conf_ln_b = self.din("conf_ln_b", [max(NO, 1), D])
        self.conf_w_out = self.din("conf_w_out", [max(NO, 1), D, D])
        self.conf_b_out = self.din("conf_b_out", [max(NO, 1), D])
        self.cst_f = self.din("cst_f", [4, 128, 128])
        self.cst_kaug = self.din("cst_kaug", [8, 5, 8192])
        self.cst_qaug = self.din("cst_qaug", [8, 5, 8192])
        self.cst_corr = self.din("cst_corr", [8, 128, 128])
        self.y_p = self.dout("y_p", [T, D])
        self.y_s = self.dout("y_s", [2, TS, D])
        self.k_p = self.dout("k_p", [NE, T, D])
        self.v_p = self.dout("v_p", [NE, T, D])
        self.sconv_p = self.dout("sconv_p", [NE, 3, 1536])
        self.ssm_p = self.dout("ssm_p", [NE, 1024, 128])
        self.cconv_p = self.dout("cconv_p", [max(NO, 1), 30, D])
        self.k_s = self.dout("k_s", [NE, 2, TS, D])
        self.v_s = self.dout("v_s", [NE, 2, TS, D])
        self.sconv_s = self.dout("sconv_s", [NE, 2, 3, 1536])
        self.ssm_s = self.dout("ssm_s", [NE, 2, 1024, 128])
        self.cconv_s = self.dout("cconv_s", [max(NO, 1), 2, 30, D])
        self.xb_p = self.dscr("xb_p", [T, D])
        self.xb_s = self.dscr("xb_s", [2, TS, D])
        self.MOD = self.dscr("modrows", [depth, 3, 6 * D])
        self.QT = [self.dscr("qt_p", [8, 128, T], BF16)] + [self.dscr("qt_s%d" % s, [8, 128, TS], BF16) for s in range(2)]
        self.KT = [self.dscr("kt_p", [8, 128, T], BF16)] + [self.dscr("kt_s%d" % s, [8, 128, TK], BF16) for s in range(2)]
        self.VB = [self.dscr("vb_p", [T, D], BF16)] + [self.dscr("vb_s%d" % s, [TK, D], BF16) for s in range(2)]
        self.OT = [self.dscr("ot_p", [D, T], BF16)] + [self.dscr("ot_s%d" % s, [D, TS], BF16) for s in range(2)]
        self.YT = [self.dscr("yt_p", [D, T], BF16)] + [self.dscr("yt_s%d" % s, [D, TS], BF16) for s in range(2)]
        self.bscr = [dict(q=Buf(), k=Buf(), v=Buf(), o=Buf(), y=Buf(), x=Buf()) for _ in range(3)]
        self.bmod = Buf()
        self.bout = Buf()

        with ExitStack() as es:
            self.kb = kb = KB(nc, es)
            self.PP = [es.enter_context(nc.psum_tensor("pp%d" % i, [128, 1024], F32)) for i in range(4)]
            self.pb = [Buf("bank%d" % i, excl=True) for i in range(8)]
            self.bank_list, self.bank_rr = list(range(8)), 0
            self.pair_list, self.pair_rr = list(range(4)), 0
            self.bconst = Buf("const")
            cf = self.sb(es, "cf", [128, 4, 128], F32)
            kb.dma("sp", cf[:], self.cst_f.rearrange("a p c -> p a c"), W=[self.bconst])
            self.identf, self.onesf, self.trilef, self.ugtf = cf[:, 0, :], cf[:, 1, :], cf[:, 2, :], cf[:, 3, :]
            cb = self.sb(es, "cb", [128, 4, 128], BF16)
            kb.op("pool", lambda e: e.tensor_copy(out=cb[:], in_=cf[:]), R=[self.bconst], W=[self.bconst])
            self.identb, self.onesb, self.trileb, self.ugtb = cb[:, 0, :], cb[:, 1, :], cb[:, 2, :], cb[:, 3, :]
            self.epsb = self.sb(es, "epsb", [128, 1], F32)
            kb.op("pool", lambda e: e.memset(self.epsb[:], EPS), W=[self.bconst])
            self.m1024 = self.sb(es, "m1024", [128, 128], F32)
            kb.op("pool", lambda e: e.memset(self.m1024[:], 1.0 / 1024), W=[self.bconst])
            self.m128 = self.sb(es, "m128", [128, 128], F32)
            kb.op("pool", lambda e: e.memset(self.m128[:], 1.0 / 128), W=[self.bconst])
            self.xrot = self.rot("xt", es, [128, D], F32, 2)
            self.frot = self.rot("ft", es, [128, D], F32, 2)
            self.hrot = self.rot("hb", es, [128, D], BF16, 1)
            self.srot = self.rot("ssq", es, [128, 4], F32, 4)
            self.wstage = self.rot("wst", es, [128, 1024], F32, 2)
            self.bc_gs = self.sb(es, "bcgs", [128, D], F32)
            self.bc_sh = self.sb(es, "bcsh", [128, D], F32)
            self.bc_gg = self.sb(es, "bcgg", [128, D], F32)
            self.bbc = Buf("bc")

            self.active_stages = []
            for l in range(depth):
                if (l % 2 == 0 and "hyb" not in SKIP) or (l % 2 == 1 and "conf" not in SKIP):
                    self.active_stages.append(2 * l)
                if "ffn" not in SKIP:
                    self.active_stages.append(2 * l + 1)
            self.phase_adaln()
            S = self.seqs()
            for l in range(depth):
                j = l // 2
                if l % 2 == 0 and "hyb" not in SKIP:
                    for ph in (self.phase_e1a, self.phase_e1b, self.phase_e2, self.phase_e3):
                        if DEBUG:
                            print("phase", ph.__name__, "l", l, "next_id", self.nc.next_id(), flush=True)
                        if ph.__name__[6:] not in SKIP:
                            ph(l, j, S)
                if l % 2 == 1 and "conf" not in SKIP:
                    self.phase_conf(l, j, S)
                if "ffn" not in SKIP:
                    self.phase_ffn(l, S)
            kb.barrier()
            self.stats = (kb.nins, kb.nwait)

    def phase_adaln(self):
        kb, depth = self.kb, self.depth
        with ExitStack() as es:
            ct = self.sb(es, "ct", [4, D], F32); bct = Buf()
            kb.dma("sp", ct[0:3, :], self.c_all, W=[bct])
            kb.op("act", lambda e: e.activation(out=ct[0:3, :], in_=ct[0:3, :], func=AF.Silu), R=[bct], W=[bct])
            cT = self.sb(es, "cT", [128, 8, 4], F32); bcT = Buf()
            pb, bb = self.bank()
            for k in range(8):
                kb.op("pe", lambda e, k=k: e.transpose(out=pb[:, k * 4:k * 4 + 3], in_=ct[0:3, k * 128:(k + 1) * 128],
                                                        identity=self.identf[0:3, 0:3]), R=[bct, self.bconst], W=bb)
            for k in range(8):
                kb.op("act", lambda e, k=k: e.activation(out=cT[:, k, 0:3], in_=pb[:, k * 4:k * 4 + 3], func=AF.Copy), R=bb, W=[bcT])
            ab = self.sb(es, "ab", [4, 6 * D], F32); bab = Buf()
            mrow = self.sb(es, "mrow", [4, 6 * D], F32); bmr = Buf()
            wrot = self.rot("adaw", es, [128, 3072], F32, 3)
            for l in range(depth):
                kb.dma("sp", ab[0:3, :], self.ada_b[l:l + 1, :].partition_broadcast(3), R=[bab], W=[bab])
                for half in range(2):
                    banks = [self.bank() for _ in range(6)]
                    for k in range(8):
                        wt, bw = wrot()
                        kb.dma("sp" if k % 2 == 0 else "pool", wt[:], self.ada_w[l, k * 128:(k + 1) * 128, half * 3072:(half + 1) * 3072], W=[bw])
                        for n in range(6):
                            pbn, bbn = banks[n]
                            kb.op("pe", lambda e, k=k, n=n, pbn=pbn, wt=wt: e.matmul(pbn[0:3, :], lhsT=cT[:, k, 0:3], rhs=wt[:, n * 512:(n + 1) * 512],
                                                                                  start=(k == 0), stop=(k == 7)), R=[bcT, bw], W=bbn)
                    for n in range(6):
                        pbn, bbn = banks[n]
                        c0 = half * 3072 + n * 512
                        kb.op("dve", lambda e, pbn=pbn, c0=c0: e.tensor_tensor(out=mrow[0:3, c0:c0 + 512], in0=pbn[0:3, :], in1=ab[0:3, c0:c0 + 512], op=ALU.add),
                              R=bbn + [bab], W=[bmr])
                kb.dma("sp", self.MOD[l], mrow[0:3, :], R=[bmr], W=[self.bmod])
            kb.barrier()

    def phase_ffn(self, l, S):
        kb = self.kb
        stage = 2 * l + 1
        with ExitStack() as es:
            wup = self.sb(es, "wup", [128, 8, 2 * DFF], BF16); bwu = Buf()
            wdn = self.sb(es, "wdn", [128, 22, D], BF16); bwd = Buf()
            self.load_w(wup, self.ffn_w_up[l].rearrange("(k p) n -> k p n", p=128), 8, 2 * DFF, bwu)
            self.load_w(wdn, self.ffn_w_down[l].rearrange("(k p) n -> k p n", p=128), 22, D, bwd)
            GNmax = S[0]["GN"]
            hT = self.sb(es, "hT", [128, 8, GNmax], BF16); bhT = Buf()
            aT = self.sb(es, "aT", [128, 22, GNmax], BF16); baT = Buf()
            sgr = self.rot("sg", es, [128, GNmax], F32, 1)
            kb.op("pool", lambda e: e.memset(hT[:], 0.0), W=[bhT])
            for si, sq in enumerate(S):
                gs, sh, gg, bbc = self.make_bc(l, 1, sq["ci"])
                src, dst = self.xio(sq, stage)
                bx = self.bscr[si]["x"]
                P, GN = sq["P"], sq["GN"]
                GNp = max(GN, 128)
                for g0 in range(0, sq["T"], GN):
                    nt = GN // P
                    for m in range(nt):
                        self.prelude_x(src[g0 + m * P:g0 + (m + 1) * P, :], bx, P, hT, bhT, m * P, gs, sh, bbc)
                    for jf in range(22):
                        pg, bg = self.bank()
                        pu, bu = self.bank()
                        for k in range(8):
                            kb.op("pe", lambda e, k=k, pg=pg: e.matmul(pg[:, 0:GNp], lhsT=wup[:, k, jf * 128:(jf + 1) * 128], rhs=hT[:, k, 0:GNp],
                                                                      start=(k == 0), stop=(k == 7)), R=[bwu, bhT], W=bg)
                        for k in range(8):
                            kb.op("pe", lambda e, k=k, pu=pu: e.matmul(pu[:, 0:GNp], lhsT=wup[:, k, DFF + jf * 128:DFF + (jf + 1) * 128], rhs=hT[:, k, 0:GNp],
                                                                      start=(k == 0), stop=(k == 7)), R=[bwu, bhT], W=bu)
                        sg, bsg = sgr()
                        kb.op("act", lambda e, sg=sg, pg=pg: e.activation(out=sg[:, 0:GN], in_=pg[:, 0:GN], func=AF.Silu), R=bg, W=[bsg])
                        kb.op("dve", lambda e, sg=sg, pu=pu, jf=jf: e.tensor_tensor(out=aT[:, jf, 0:GN], in0=sg[:, 0:GN], in1=pu[:, 0:GN], op=ALU.mult),
                              R=[bsg] + bu, W=[baT])
                    for m in range(nt):
                        pp, bpp = self.pair()
                        for nh in range(2):
                            for k in range(22):
                                kb.op("pe", lambda e, k=k, nh=nh, m=m, pp=pp: e.matmul(pp[0:P, nh * 512:(nh + 1) * 512], lhsT=aT[:, k, m * P:(m + 1) * P],
                                                                                     rhs=wdn[:, k, nh * 512:(nh + 1) * 512], start=(k == 0), stop=(k == 21)),
                                      R=[baT, bwd], W=bpp)
                        r0 = g0 + m * P
                        self.post(pp, bpp, P, src[r0:r0 + P, :], dst[r0:r0 + P, :], gg, bbc, bx)
            kb.barrier()

    def prelude_x(self, xsrc, bdram, P, hT, bhT, col0, gs, sh, bbc):
        kb = self.kb
        xt, bx = self.xrot()
        kb.dma("sp", xt[0:P, :], xsrc, R=[bdram], W=[bx])
        junk, bj = self.frot()
        ssq, bs = self.srot()
        kb.op("act", lambda e: e.activation(out=junk[0:P, :], in_=xt[0:P, :], func=AF.Square, accum_out=ssq[0:P, 0:1]),
              R=[bx], W=[bj, bs])
        kb.op("act", lambda e: e.activation(out=ssq[0:P, 1:2], in_=ssq[0:P, 0:1], func=AF.Sqrt, scale=1.0 / D, bias=self.epsb[0:P, 0:1]),
              R=[bs, self.bconst], W=[bs])
        kb.op("dve", lambda e: e.reciprocal(out=ssq[0:P, 1:2], in_=ssq[0:P, 1:2]), R=[bs], W=[bs])
        kb.op("dve", lambda e: e.scalar_tensor_tensor(out=junk[0:P, :], in0=xt[0:P, :], scalar=ssq[0:P, 1:2], in1=gs[0:P, :],
                                                       op0=ALU.mult, op1=ALU.mult), R=[bx, bs, bbc], W=[bj])
        hb, bh = self.hrot()
        kb.op("dve", lambda e: e.tensor_tensor(out=hb[0:P, :], in0=junk[0:P, :], in1=sh[0:P, :], op=ALU.add),
              R=[bj, bbc], W=[bh])
        self.transpose_to(hb, bh, P, 8, lambda k: hT[:, k, col0:col0 + P], bhT, dst3=hT[:, 0:8, col0:col0 + P])

    def evac(self, out, in_, R, W):
        self._ev = getattr(self, "_ev", 0) + 1
        if self._ev % 2 == 0:
            self.kb.op("act", lambda e: e.activation(out=out, in_=in_, func=AF.Copy), R=R, W=W)
        else:
            self.kb.op("dve", lambda e: e.tensor_copy(out=out, in_=in_), R=R, W=W)

    def phase_e1a(self, l, j, S):
        kb = self.kb
        stage = 2 * l
        PAST, TS = self.PAST, self.TS
        with ExitStack() as es:
            w = self.sb(es, "w1a", [128, 8, 3072], BF16); bw = Buf()
            self.load_w(w, self.hyb_w_in[j].rearrange("(k p) n -> k p n", p=128), 8, 3072, bw, c0=0)
            GNmax = S[0]["GN"]
            hT = self.sb(es, "ahT", [128, 8, GNmax], BF16); bhT = Buf()
            kvt = self.rot("kvt", es, [128, 2048], F32, 2)
            vbr = self.rot("vbr", es, [128, D], BF16, 2)
            fmr = self.rot("fmr", es, [128, GNmax], BF16, 3)
            kfr = self.rot("kfr", es, [128, 8, 128], BF16, 2)
            kb.op("pool", lambda e: e.memset(hT[:], 0.0), W=[bhT])
            for si, sq in enumerate(S):
                gs, sh, gg, bbc = self.make_bc(l, 0, sq["ci"])
                src, _ = self.xio(sq, stage)
                bx = self.bscr[si]["x"]
                bsc = self.bscr[si]
                P, GN, Tq = sq["P"], sq["GN"], sq["T"]
                kp0 = 0
                if sq["s"] is not None:
                    s_ = sq["s"]
                    kp0 = PAST
                    for t in range(PAST // 128):
                        ck, bck = kvt()
                        kb.dma("sp", ck[:, 0:1024], self.cache_k[j, s_, t * 128:(t + 1) * 128, :], W=[bck])
                        kb.dma("sp", ck[:, 1024:2048], self.cache_v[j, s_, t * 128:(t + 1) * 128, :], W=[bck])
                        kf, bkf = kfr()
                        for h0 in (0, 4):
                            pb, bb = self.bank()
                            for h in range(h0, h0 + 4):
                                kb.op("pe", lambda e, h=h, pb=pb, ck=ck: e.transpose(out=pb[:, (h - h0) * 128:(h - h0 + 1) * 128], in_=ck[:, h * 128:(h + 1) * 128],
                                                                                     identity=self.identf[:, :]), R=[bck, self.bconst], W=bb)
                            self.evac(kf[:, h0:h0 + 4, :], pb.rearrange("p (a b) -> p a b", a=4), bb, [bkf])
                        kb.dma("pool", self.KT[si][:, :, t * 128:(t + 1) * 128].rearrange("h r t -> r h t"), kf[:], R=[bkf], W=[bsc["k"]])
                        vb, bvb = vbr()
                        kb.op("pool", lambda e, vb=vb, ck=ck: e.tensor_copy(out=vb[:], in_=ck[:, 1024:2048]), R=[bck], W=[bvb])
                        kb.dma("pool", self.VB[si][t * 128:(t + 1) * 128, :], vb[:], R=[bvb], W=[bsc["v"]])
                for g0 in range(0, Tq, GN):
                    nt = GN // P
                    for m in range(nt):
                        self.prelude_x(src[g0 + m * P:g0 + (m + 1) * P, :], bx, P, hT, bhT, m * P, gs, sh, bbc)
                    for m in range(nt):
                        kv, bkv = kvt()
                        for n4 in range(4):
                            pb, bb = self.bank()
                            for k in range(8):
                                kb.op("pe", lambda e, k=k, pb=pb, n4=n4, m=m: e.matmul(pb[0:P, :], lhsT=hT[:, k, m * P:(m + 1) * P], rhs=w[:, k, 1024 + n4 * 512:1024 + (n4 + 1) * 512],
                                                                                     start=(k == 0), stop=(k == 7)), R=[bhT, bw], W=bb)
                            self.evac(kv[0:P, n4 * 512:(n4 + 1) * 512], pb[0:P, :], bb, [bkv])
                        r0 = g0 + m * P
                        if sq["s"] is None:
                            kb.dma("pool", self.k_p[j, r0:r0 + P, :], kv[0:P, 0:1024], R=[bkv], W=[self.bout])
                            kb.dma("pool", self.v_p[j, r0:r0 + P, :], kv[0:P, 1024:2048], R=[bkv], W=[self.bout])
                        else:
                            kb.dma("pool", self.k_s[j, sq["s"], r0:r0 + P, :], kv[0:P, 0:1024], R=[bkv], W=[self.bout])
                            kb.dma("pool", self.v_s[j, sq["s"], r0:r0 + P, :], kv[0:P, 1024:2048], R=[bkv], W=[self.bout])
                        vb, bvb = vbr()
                        kb.op("pool", lambda e, vb=vb, kv=kv: e.tensor_copy(out=vb[0:P, :], in_=kv[0:P, 1024:2048]), R=[bkv], W=[bvb])
                        kb.dma("pool", self.VB[si][kp0 + r0:kp0 + r0 + P, :], vb[0:P, :], R=[bvb], W=[bsc["v"]])
                    for c in range(16):
                        pb, bb = self.bank()
                        for k in range(8):
                            kb.op("pe", lambda e, k=k, pb=pb, c=c: e.matmul(pb[:, 0:max(GN, 128)], lhsT=w[:, k, c * 128:(c + 1) * 128], rhs=hT[:, k, 0:max(GN, 128)],
                                                                           start=(k == 0), stop=(k == 7)), R=[bhT, bw], W=bb)
                        fm, bfm = fmr()
                        self.evac(fm[:, 0:GN], pb[:, 0:GN], bb, [bfm])
                        if c < 8:
                            kb.dma("pool", self.QT[si][c, :, g0:g0 + GN], fm[:, 0:GN], R=[bfm], W=[bsc["q"]])
                        else:
                            kb.dma("pool", self.KT[si][c - 8, :, kp0 + g0:kp0 + g0 + GN], fm[:, 0:GN], R=[bfm], W=[bsc["k"]])
            kb.barrier()

    def phase_e1b(self, l, j, S):
        kb = self.kb
        stage = 2 * l
        with ExitStack() as es:
            w = self.sb(es, "w1b", [128, 8, 2576], BF16); bw = Buf()
            self.load_w(w, self.hyb_w_in[j].rearrange("(k p) n -> k p n", p=128), 8, 2576, bw, c0=3072)
            bsm = Buf("e1bsmall")
            cwf = self.sb(es, "cwf", [128, 12, 4], F32)
            self.load_fm(es, lambda c: cwf[:, c, :], self.ssm_conv_w[j], 4, 12, bsm)
            cbf = self.sb(es, "cbf", [128, 12, 1], F32)
            self.load_fm(es, lambda c: cbf[:, c, :], self.ssm_conv_b[j:j + 1, :], 1, 12, bsm)
            diag4 = self.sb(es, "diag4", [128, 48, 128], BF16)
            kb.op("dve", lambda e: e.tensor_tensor(out=diag4[:], in0=self.identf.unsqueeze(1).to_broadcast([128, 48, 128]),
                                                   in1=cwf[:].rearrange("p c k -> p (c k)").unsqueeze(2).to_broadcast([128, 48, 128]),
                                                   op=ALU.mult), R=[bsm, self.bconst], W=[bsm])
            cbr = self.sb(es, "cbr", [1, 1536], F32)
            kb.dma("sp", cbr[:], self.ssm_conv_b[j:j + 1, :], W=[bsm])
            cbrb = self.sb(es, "cbrb", [1, 1536], BF16)
            kb.op("pool", lambda e: e.tensor_copy(out=cbrb[:], in_=cbr[:]), R=[bsm], W=[bsm])
            sm = self.sb(es, "ssmsm", [128, 4, 16], F32)
            kb.dma("sp", sm[:, 0, :], self.ssm_a_log[j:j + 1, :].partition_broadcast(128), W=[bsm])
            kb.dma("sp", sm[:, 1, :], self.ssm_d[j:j + 1, :].partition_broadcast(128), W=[bsm])
            kb.dma("sp", sm[:, 2, :], self.ssm_dt_bias[j:j + 1, :].partition_broadcast(128), W=[bsm])
            kb.op("act", lambda e: e.activation(out=sm[:, 0, :], in_=sm[:, 0, :], func=AF.Exp), R=[bsm], W=[bsm])
            kb.op("dve", lambda e: e.tensor_scalar(out=sm[:, 0, :], in0=sm[:, 0, :], scalar1=-1.0, scalar2=None, op0=ALU.mult), R=[bsm], W=[bsm])
            a_b, D_b, dtb_b = sm[:, 0, :], sm[:, 1, :], sm[:, 2, :]
            ngb = self.sb(es, "ngb", [128, D], F32)
            kb.dma("sp", ngb[:], self.ssm_norm_g[j:j + 1, :].partition_broadcast(128), W=[bsm])
            if E1B_STOP <= 1:
                kb.barrier()
                return
            GNmax = S[0]["GN"]
            hT = self.sb(es, "bhT", [128, 8, GNmax], BF16); bhT = Buf()
            xbcT = self.sb(es, "xbcT", [128, 12, 4 + GNmax], BF16); bxb = Buf()
            t1 = self.sb(es, "t1", [128, D], F32); bt1 = Buf()
            t3 = self.sb(es, "t3", [128, D], F32); bt3 = Buf()
            ynb = self.sb(es, "ynb", [128, D], BF16); bynb = Buf()
            ynT = self.sb(es, "ynT", [128, 8, 128], BF16); bynT = Buf()
            hst = self.sb(es, "hst", [128, D], F32); bhst = Buf()
            hbf = self.sb(es, "hbf", [128, D], BF16); bhbf = Buf()
            xraw = self.sb(es, "xraw", [128, 1536], F32); bxr = Buf()
            TB = []
            for i in range(2):
                TB.append((self.sb(es, "szt", [128, D], F32), Buf(), self.sb(es, "xst", [128, D], F32), Buf(),
                           self.sb(es, "Rf", [128, 2048], BF16), Buf(), self.sb(es, "exf", [128, 1024], F32), Buf(),
                           self.sb(es, "scf", [128, 2048], BF16), Buf(), self.sb(es, "xdt", [128, D], BF16), Buf(),
                           self.sb(es, "xdtw", [128, D], BF16), Buf(), self.sb(es, "bct", [128, 4, 128], BF16), Buf(),
                           self.sb(es, "btok", [128, 256], BF16), Buf(), self.sb(es, "cbm", [128, 2, 128], F32), Buf(),
                           self.sb(es, "d16", [128, 12, 16], F32), Buf(), self.sb(es, "d16b", [128, 16], BF16)))
            tbi = [0]
            kb.op("pool", lambda e: e.memset(hT[:], 0.0), W=[bhT])
            kb.op("pool", lambda e: e.memset(xbcT[:], 0.0), W=[bxb])
            for si, sq in enumerate(S):
                if CONF_SEQS is not None and si not in CONF_SEQS:
                    continue
                gs, sh, gg, bbc = self.make_bc(l, 0, sq["ci"])
                src, _ = self.xio(sq, stage)
                bx = self.bscr[si]["x"]
                bsc = self.bscr[si]
                P, GN, Tq = sq["P"], sq["GN"], sq["T"]
                if sq["s"] is None:
                    kb.op("pool", lambda e: e.memset(xbcT[:, :, 0:4], 0.0), W=[bxb])
                    kb.op("pool", lambda e: e.memset(hst[:], 0.0), W=[bhst])
                    kb.op("pool", lambda e: e.memset(hbf[:], 0.0), W=[bhbf])
                else:
                    s_ = sq["s"]
                    self.load_fm(es, lambda c: xbcT[:, c, 0:4], self.st_sconv[j, s_], 3, 12, bxb, pad=1)
                    st, bst = self.wstage()
                    kb.dma("sp", st[:].rearrange("p (c n) -> p c n", c=8), self.st_ssm[j, s_].rearrange("(c p) n -> p c n", p=128), W=[bst])
                    pp, bpp = self.pair()
                    for c in range(8):
                        kb.op("pe", lambda e, c=c, pp=pp, st=st: e.transpose(out=pp[:, c * 128:(c + 1) * 128], in_=st[:, c * 128:(c + 1) * 128], identity=self.identf[:, :]),
                              R=[bst, self.bconst], W=bpp)
                    kb.op("act", lambda e, pp=pp: e.activation(out=hst[:], in_=pp[:], func=AF.Copy), R=bpp, W=[bhst])
                    kb.op("dve", lambda e: e.tensor_copy(out=hbf[:], in_=hst[:]), R=[bhst], W=[bhbf])
                ngr = Tq // GN
                for gi in range(ngr):
                    g0 = gi * GN
                    nt = GN // P
                    for m in range(nt):
                        self.prelude_x(src[g0 + m * P:g0 + (m + 1) * P, :], bx, P, hT, bhT, m * P, gs, sh, bbc)
                    for c in range(12):
                        pb, bb = self.bank()
                        for k in range(8):
                            kb.op("pe", lambda e, k=k, pb=pb, c=c: e.matmul(pb[:, 0:max(GN, 128)], lhsT=w[:, k, 1024 + c * 128:1024 + (c + 1) * 128], rhs=hT[:, k, 0:max(GN, 128)],
                                                                           start=(k == 0), stop=(k == 7)), R=[bhT, bw], W=bb)
                        self.evac(xbcT[:, c, 4:4 + GN], pb[:, 0:GN], bb, [bxb])
                    def tile_gen(m, gi=gi, g0=g0):
                        (szt, bsz, xst, bxs, Rf, bR, exf, bex, scf, bscf, xdt, bxdt, xdtw, bxdtw, bct, bbct, btok, bbtok, cbm, bcbm, d16, bd, d16b) = TB[tbi[0] % 2]
                        tbi[0] += 1
                        Rv = Rf[0:P, 0:16 * P].rearrange("p (h l) -> p h l", h=16)
                        scv = scf[0:P, 0:16 * P].rearrange("p (h l) -> p h l", h=16)
                        t0 = m * P
                        r0 = g0 + t0
                        pz, bz = self.pair()
                        for nh in range(2):
                            for k in range(8):
                                kb.op("pe", lambda e, k=k, nh=nh, pz=pz: e.matmul(pz[0:P, nh * 512:(nh + 1) * 512], lhsT=hT[:, k, t0:t0 + P], rhs=w[:, k, nh * 512:(nh + 1) * 512],
                                                                                start=(k == 0), stop=(k == 7)), R=[bhT, bw], W=bz)
                        kb.op("act", lambda e, pz=pz: e.activation(out=szt[0:P, :], in_=pz[0:P, :], func=AF.Silu), R=bz, W=[bsz])
                        pd, bpd = self.bank()
                        for k in range(8):
                            kb.op("pe", lambda e, k=k, pd=pd: e.matmul(pd[0:P, 0:128], lhsT=hT[:, k, t0:t0 + P], rhs=w[:, k, 2448:2576], start=(k == 0), stop=(k == 7)),
                                  R=[bhT, bw], W=bpd)
                        X, AXv, EX, LG, DT, DTA, WL, DTW, EXPA, EL = [d16[0:P, i, :] for i in range(10)]
                        EL = d16[:, 9, :]
                        kb.op("dve", lambda e, pd=pd: e.tensor_tensor(out=X, in0=pd[0:P, 112:128], in1=dtb_b[0:P, :], op=ALU.add), R=bpd + [bsm], W=[bd])
                        kb.op("dve", lambda e: e.scalar_tensor_tensor(out=AXv, in0=X, scalar=-1.0, in1=X, op0=ALU.mult, op1=ALU.max), R=[bd], W=[bd])
                        kb.op("act", lambda e: e.activation(out=EX, in_=AXv, func=AF.Exp, scale=-1.0), R=[bd], W=[bd])
                        kb.op("act", lambda e: e.activation(out=LG, in_=EX, func=AF.Ln, bias=1.0), R=[bd], W=[bd])
                        kb.op("dve", lambda e: e.scalar_tensor_tensor(out=DT, in0=X, scalar=0.0, in1=LG, op0=ALU.max, op1=ALU.add), R=[bd], W=[bd])
                        kb.op("dve", lambda e: e.tensor_tensor(out=DTA, in0=DT, in1=a_b[0:P, :], op=ALU.mult), R=[bd, bsm], W=[bd])
                        kb.op("dve", lambda e: e.tensor_copy(out=d16b[0:P, :], in_=DTA), R=[bd], W=[bd])
                        px, bpx = self.pair()
                        for k in range(5):
                            for c in range(8):
                                st_, sp_ = (k == 0 and c % 4 == 0), (k == 4 and c % 4 == 3)
                                if k < 4:
                                    kb.op("pe", lambda e, c=c, k=k, px=px, st_=st_, sp_=sp_: e.matmul(px[0:P, c * 128:(c + 1) * 128], lhsT=xbcT[:, c, 1 + t0 + k:1 + t0 + k + P], rhs=diag4[:, c * 4 + k, :],
                                                                                                   start=st_, stop=sp_), R=[bxb, bsm], W=bpx)
                                else:
                                    kb.op("pe", lambda e, c=c, px=px, st_=st_, sp_=sp_: e.matmul(px[0:P, c * 128:(c + 1) * 128], lhsT=self.onesb[0:1, 0:P], rhs=cbrb[0:1, c * 128:(c + 1) * 128],
                                                                                              start=st_, stop=sp_), R=[bsm, self.bconst], W=bpx)
                        kb.op("act", lambda e, px=px: e.activation(out=xst[0:P, :], in_=px[0:P, :], func=AF.Silu), R=bpx, W=[bxs])
                        pk, bpk = self.bank()
                        for k in range(5):
                            for c in (8, 9):
                                st_, sp_ = (k == 0 and c == 8), (k == 4 and c == 9)
                                if k < 4:
                                    kb.op("pe", lambda e, c=c, k=k, pk=pk, st_=st_, sp_=sp_: e.matmul(pk[0:P, (c - 8) * 128:(c - 7) * 128], lhsT=xbcT[:, c, 1 + t0 + k:1 + t0 + k + P], rhs=diag4[:, c * 4 + k, :],
                                                                                                   start=st_, stop=sp_), R=[bxb, bsm], W=bpk)
                                else:
                                    kb.op("pe", lambda e, c=c, pk=pk, st_=st_, sp_=sp_: e.matmul(pk[0:P, (c - 8) * 128:(c - 7) * 128], lhsT=self.onesb[0:1, 0:P], rhs=cbrb[0:1, c * 128:(c + 1) * 128],
                                                                                              start=st_, stop=sp_), R=[bsm, self.bconst], W=bpk)
                        kb.op("act", lambda e, pk=pk: e.activation(out=btok[0:P, :], in_=pk[0:P, 0:256], func=AF.Silu), R=bpk, W=[bbtok])
                        pf, bpf = self.bank()
                        for k in range(4):
                            for i, c in enumerate((8, 9, 10, 11)):
                                st_, sp_ = (k == 0 and i == 0), (k == 3 and i == 3)
                                kb.op("pe", lambda e, c=c, k=k, i=i, pf=pf, st_=st_, sp_=sp_: e.matmul(pf[:, i * 128:(i + 1) * 128], lhsT=diag4[:, c * 4 + k, :], rhs=xbcT[:, c, 1 + t0 + k:1 + t0 + k + 128],
                                                                                                    start=st_, stop=sp_), R=[bxb, bsm], W=bpf)
                        for i, c in enumerate((8, 9, 10, 11)):
                            kb.op("act", lambda e, c=c, i=i, pf=pf: e.activation(out=bct[:, i, 0:P], in_=pf[:, i * 128:i * 128 + P], func=AF.Silu, bias=cbf[:, c, :]),
                                  R=bpf + [bsm], W=[bbct])
                        pcb, bpcb = self.bank()
                        for g in range(2):
                            kb.op("pe", lambda e, g=g, pcb=pcb: e.matmul(pcb[0:P, g * P:(g + 1) * P], lhsT=bct[:, g, 0:P], rhs=bct[:, 2 + g, 0:P], start=True, stop=True),
                                  R=[bbct], W=bpcb)
                        kb.op("dve", lambda e, pcb=pcb: e.tensor_tensor(out=cbm[0:P, :, 0:P], in0=pcb[0:P, 0:2 * P].rearrange("p (g l) -> p g l", g=2),
                                                                       in1=self.trilef[0:P, 0:P].unsqueeze(1).to_broadcast([P, 2, P]), op=ALU.mult),
                              R=bpcb + [self.bconst], W=[bcbm])
                        kb.op("dve", lambda e: e.tensor_tensor(out=Rv, in0=self.trilef[0:P, 0:P].unsqueeze(1).to_broadcast([P, 16, P]),
                                                               in1=DTA.unsqueeze(2).to_broadcast([P, 16, P]), op=ALU.mult), R=[bd, self.bconst], W=[bR])
                        for half in range(2):
                            psg, bsg = self.pair()
                            for q4 in range(2):
                                h0 = half * 8 + q4 * 4
                                kb.op("pe", lambda e, q4=q4, h0=h0, psg=psg: e.matmul(psg[0:P, q4 * 4 * P:(q4 + 1) * 4 * P], lhsT=self.ugtb[0:P, 0:P],
                                                                                     rhs=Rf[0:P, h0 * P:(h0 + 4) * P], start=True, stop=True), R=[bR, self.bconst], W=bsg)
                            kb.op("act", lambda e, psg=psg: e.activation(out=exf[0:P, 0:8 * P], in_=psg[0:P, 0:8 * P], func=AF.Exp), R=bsg, W=[bex])
                            exv = exf[0:P, 0:8 * P].rearrange("p (h l) -> p h l", h=8)
                            kb.op("dve", lambda e, half=half, exv=exv: e.tensor_copy(out=WL[:, half * 8:(half + 1) * 8], in_=exv[:, :, P - 1]), R=[bex], W=[bd])
                            kb.op("dve", lambda e, half=half, exv=exv: e.tensor_tensor(out=scv[:, half * 8:(half + 1) * 8, :], in0=exv,
                                                                                      in1=cbm[0:P, half, 0:P].unsqueeze(1).to_broadcast([P, 8, P]), op=ALU.mult),
                                  R=[bex, bcbm], W=[bscf])
                        pa, bpa = self.bank()
                        kb.op("pe", lambda e, pa=pa: e.matmul(pa[0:P, 0:16], lhsT=self.trileb[0:P, 0:P], rhs=d16b[0:P, :], start=True, stop=True), R=[bd, self.bconst], W=bpa)
                        kb.op("pe", lambda e, pa=pa: e.matmul(pa[:, 16:32], lhsT=self.onesb[0:P, :], rhs=d16b[0:P, :], start=True, stop=True), R=[bd, self.bconst], W=bpa)
                        kb.op("act", lambda e, pa=pa: e.activation(out=EXPA, in_=pa[0:P, 0:16], func=AF.Exp), R=bpa, W=[bd])
                        kb.op("act", lambda e, pa=pa: e.activation(out=EL, in_=pa[:, 16:32], func=AF.Exp), R=bpa, W=[bd])
                        kb.op("dve", lambda e: e.tensor_tensor(out=DTW, in0=DT, in1=WL, op=ALU.mult), R=[bd], W=[bd])
                        xs3 = xst[0:P, :].rearrange("p (h q) -> p h q", h=16)
                        kb.op("dve", lambda e: e.tensor_tensor(out=xdt[0:P, :].rearrange("p (h q) -> p h q", h=16), in0=xs3, in1=DT.unsqueeze(2).to_broadcast([P, 16, 64]), op=ALU.mult),
                              R=[bxs, bd], W=[bxdt])
                        kb.op("pool", lambda e: e.tensor_tensor(out=xdtw[0:P, :].rearrange("p (h q) -> p h q", h=16), in0=xs3, in1=DTW.unsqueeze(2).to_broadcast([P, 16, 64]), op=ALU.mult),
                              R=[bxs, bd], W=[bxdtw])
                        yield
                        py, bpy = self.pair()
                        for h in range(16):
                            kb.op("pe", lambda e, h=h, py=py: e.matmul(py[0:P, h * 64:(h + 1) * 64], lhsT=scf[0:P, h * P:(h + 1) * P], rhs=xdt[0:P, h * 64:(h + 1) * 64], start=True, stop=True),
                                  R=[bscf, bxdt], W=bpy)
                        po, bpo = self.pair()
                        for g in range(2):
                            kb.op("pe", lambda e, g=g, po=po: e.matmul(po[0:P, g * 512:(g + 1) * 512], lhsT=bct[:, 2 + g, 0:P], rhs=hbf[:, g * 512:(g + 1) * 512], start=True, stop=True),
                                  R=[bbct, bhbf], W=bpo)
                        kb.op("dve", lambda e, po=po: e.tensor_tensor(out=t1[0:P, :].rearrange("p (h q) -> p h q", h=16), in0=po[0:P, :].rearrange("p (h q) -> p h q", h=16),
                                                                     in1=EXPA.unsqueeze(2).to_broadcast([P, 16, 64]), op=ALU.mult), R=bpo + [bd], W=[bt1])
                        kb.op("dve", lambda e, py=py: e.tensor_tensor(out=t1[0:P, :], in0=t1[0:P, :], in1=py[0:P, :], op=ALU.add), R=bpy + [bt1], W=[bt1])
                        kb.op("pool", lambda e: e.tensor_tensor(out=t3[0:P, :].rearrange("p (h q) -> p h q", h=16), in0=xs3, in1=D_b[0:P, :].unsqueeze(2).to_broadcast([P, 16, 64]), op=ALU.mult),
                              R=[bxs, bsm], W=[bt3])
                        kb.op("dve", lambda e: e.tensor_tensor(out=t1[0:P, :], in0=t1[0:P, :], in1=t3[0:P, :], op=ALU.add), R=[bt1, bt3], W=[bt1])
                        kb.op("dve", lambda e: e.tensor_tensor(out=t1[0:P, :], in0=t1[0:P, :], in1=szt[0:P, :], op=ALU.mult), R=[bt1, bsz], W=[bt1])
                        ssq, bs = self.srot()
                        for g in range(2):
                            kb.op("act", lambda e, g=g, ssq=ssq: e.activation(out=t3[0:P, g * 512:(g + 1) * 512], in_=t1[0:P, g * 512:(g + 1) * 512], func=AF.Square, accum_out=ssq[0:P, g:g + 1]),
                                  R=[bt1], W=[bt3, bs])
                        kb.op("act", lambda e, ssq=ssq: e.activation(out=ssq[0:P, 2:4], in_=ssq[0:P, 0:2], func=AF.Sqrt, scale=1.0 / 512, bias=self.epsb[0:P, 0:1]), R=[bs, self.bconst], W=[bs])
                        kb.op("dve", lambda e, ssq=ssq: e.reciprocal(out=ssq[0:P, 2:4], in_=ssq[0:P, 2:4]), R=[bs], W=[bs])
                        for g in range(2):
                            kb.op("dve", lambda e, g=g, ssq=ssq: e.scalar_tensor_tensor(out=ynb[0:P, g * 512:(g + 1) * 512], in0=t1[0:P, g * 512:(g + 1) * 512], scalar=ssq[0:P, 2 + g:3 + g],
                                                                                       in1=ngb[0:P, g * 512:(g + 1) * 512], op0=ALU.mult, op1=ALU.mult), R=[bt1, bs, bsm], W=[bynb])
                        self.transpose_to(ynb, bynb, P, 8, lambda k: ynT[:, k, 0:P], bynT, dst3=ynT[:, 0:8, 0:P])
                        kb.dma("pool", self.YT[si].rearrange("(c p) t -> p c t", p=128)[:, :, r0:r0 + P], ynT[:, :, 0:P], R=[bynT], W=[bsc["y"]])
                        ps2, bps2 = self.pair()
                        for g in range(2):
                            kb.op("pe", lambda e, g=g, ps2=ps2: e.matmul(ps2[:, g * 512:(g + 1) * 512], lhsT=btok[0:P, g * 128:(g + 1) * 128], rhs=xdtw[0:P, g * 512:(g + 1) * 512], start=True, stop=True),
                                  R=[bbtok, bxdtw], W=bps2)
                        kb.op("dve", lambda e: e.tensor_tensor(out=hst[:].rearrange("p (h q) -> p h q", h=16), in0=hst[:].rearrange("p (h q) -> p h q", h=16),
                                                               in1=EL.unsqueeze(2).to_broadcast([128, 16, 64]), op=ALU.mult), R=[bhst, bd], W=[bhst])
                        kb.op("dve", lambda e, ps2=ps2: e.tensor_tensor(out=hst[:], in0=hst[:], in1=ps2[:], op=ALU.add), R=bps2 + [bhst], W=[bhst])
                        kb.op("act", lambda e: e.activation(out=hbf[:], in_=hst[:], func=AF.Copy), R=[bhst], W=[bhbf])
                        if gi == ngr - 1 and m == nt - 1:
                            for n3 in range(3):
                                pb, bb = self.bank()
                                for k in range(8):
                                    kb.op("pe", lambda e, k=k, pb=pb, n3=n3: e.matmul(pb[0:P, :], lhsT=hT[:, k, t0:t0 + P], rhs=w[:, k, 1024 + n3 * 512:1024 + (n3 + 1) * 512],
                                                                                     start=(k == 0), stop=(k == 7)), R=[bhT, bw], W=bb)
                                self.evac(xraw[0:P, n3 * 512:(n3 + 1) * 512], pb[0:P, :], bb, [bxr])
                            dsto = self.sconv_p[j] if sq["s"] is None else self.sconv_s[j, sq["s"]]
                            kb.dma("pool", dsto, xraw[P - 3:P, :], R=[bxr], W=[self.bout])
                    gens = [tile_gen(m) for m in range(nt)]
                    next(gens[0])
                    for m in range(nt):
                        if m + 1 < nt:
                            next(gens[m + 1])
                        next(gens[m], None)
                    if gi < ngr - 1:
                        kb.op("pool", lambda e: e.tensor_copy(out=xbcT[:, :, 0:4], in_=xbcT[:, :, GN:GN + 4]), R=[bxb], W=[bxb])
                if E1B_STOP <= 6:
                    continue
                pp, bpp = self.pair()
                for c in range(8):
                    kb.op("pe", lambda e, c=c, pp=pp: e.transpose(out=pp[:, c * 128:(c + 1) * 128], in_=hst[:, c * 128:(c + 1) * 128], identity=self.identf[:, :]),
                          R=[bhst, self.bconst], W=bpp)
                so, bso = self.xrot()
                kb.op("act", lambda e, pp=pp, so=so: e.activation(out=so[:], in_=pp[:], func=AF.Copy), R=bpp, W=[bso])
                dsts = self.ssm_p[j] if sq["s"] is None else self.ssm_s[j, sq["s"]]
                kb.dma("pool", dsts.rearrange("(c p) n -> p c n", p=128), so[:].rearrange("p (c n) -> p c n", c=8), R=[bso], W=[self.bout])
            kb.barrier()

    def phase_e2(self, l, j, S):
        kb = self.kb
        PAST, TS, T = self.PAST, self.TS, self.T
        lam_init = 0.8 - 0.6 * math.exp(-0.3 * l)
        TKmax = max(T, PAST + TS)
        with ExitStack() as es:
            save_banks = self.bank_list
            self.bank_list = [4, 5, 6, 7]
            bsm = Buf("e2small")
            lamt = self.sb(es, "lamt", [128, 256], F32)
            kb.dma("sp", lamt[:], self.attn_lambda[j:j + 1, :].partition_broadcast(128), W=[bsm])
            lsm = self.sb(es, "lsm", [128, 8], F32)
            lpr = self.sb(es, "lpr", [128, 128], F32)
            kb.op("dve", lambda e: e.tensor_tensor(out=lpr[:, 0:64], in0=lamt[:, 0:64], in1=lamt[:, 64:128], op=ALU.mult), R=[bsm], W=[bsm])
            kb.op("dve", lambda e: e.tensor_tensor(out=lpr[:, 64:128], in0=lamt[:, 128:192], in1=lamt[:, 192:256], op=ALU.mult), R=[bsm], W=[bsm])
            kb.op("dve", lambda e: e.reduce_sum(out=lsm[:, 0:2], in_=lpr[:].rearrange("p (a b) -> p a b", a=2), axis=AX.X), R=[bsm], W=[bsm])
            kb.op("act", lambda e: e.activation(out=lsm[:, 2:4], in_=lsm[:, 0:2], func=AF.Exp), R=[bsm], W=[bsm])
            kb.op("dve", lambda e: e.scalar_tensor_tensor(out=lsm[:, 4:5], in0=lsm[:, 3:4], scalar=-lam_init, in1=lsm[:, 2:3], op0=ALU.add, op1=ALU.subtract),
                  R=[bsm], W=[bsm])
            neglam = lsm[:, 4:5]
            subg = self.sb(es, "subg", [128, 1, 1], F32)
            self.load_fm(es, lambda c: subg[:, c, :], self.attn_subln_g[j:j + 1, :], 1, 1, bsm)
            kb.op("dve", lambda e: e.tensor_scalar(out=subg[:, 0, :], in0=subg[:, 0, :], scalar1=(1.0 - lam_init), scalar2=None, op0=ALU.mult), R=[bsm], W=[bsm])
            NKT = (TKmax + 127) // 128
            kvset = []
            for i in range(2):
                kvset.append(dict(kT=[self.sb(es, "kT%d_%d" % (m, i), [69, TKmax], BF16) for m in range(2)], bkT=Buf(),
                                  vh=self.sb(es, "vh%d" % i, [128, NKT, 128], BF16), bvh=Buf(),
                                  corrb=self.sb(es, "corrb%d" % i, [128, 128], BF16), bcorr=Buf()))
            GNmax = S[0]["GN"]
            qTr = self.rot("qT", es, [69, 2, GNmax], BF16, 2)
            ptr = self.rot("pt", es, [128, GNmax], BF16, 4)
            accr = self.rot("accs", es, [128, 4, GNmax], F32, 2)
            ot = self.sb(es, "e2o", [128, GNmax], F32); bot = Buf()
            o1 = self.sb(es, "e2o1", [128, GNmax], F32); bo1 = Buf()
            rr = self.sb(es, "e2r", [128, GNmax], F32); brr = Buf()
            kb.op("pool", lambda e: e.memset(o1[:], 0.0), W=[bo1])
            onr = self.rot("e2on", es, [128, GNmax], BF16, 2)
            acc = [self.PP[0][:, 0:512], self.PP[0][:, 512:1024], self.PP[1][:, 0:512], self.PP[1][:, 512:1024]]
            bacc = [[self.pb[i]] for i in range(4)]
            for _ in range(2):
                qT, bqT = qTr()
                kb.op("pool", lambda e, qT=qT: e.memset(qT[:], 0.0), W=[bqT])

            def load_head(si, sq, h, ks):
                bsc = self.bscr[si]
                Tq = sq["T"]
                kp0 = 0 if sq["s"] is None else PAST
                Tk = kp0 + Tq
                kT, bkT, vh, bvh, corrb, bcorr = ks["kT"], ks["bkT"], ks["vh"], ks["bvh"], ks["corrb"], ks["bcorr"]
                for m in range(2):
                    kb.dma("sp", kT[m][0:64, 0:Tk], self.KT[si][h, m * 64:(m + 1) * 64, 0:Tk], R=[bsc["k"]], W=[bkT])
                for a in range(0, Tk, 1024):
                    wd = min(1024, Tk - a)
                    st, bst = self.wstage()
                    kb.dma("sp", st[64:69, 0:wd], self.cst_kaug[h, :, a:a + wd], W=[bst])
                    for m in range(2):
                        kb.op("pool", lambda e, m=m, st=st, a=a, wd=wd: e.tensor_copy(out=kT[m][64:69, a:a + wd], in_=st[64:69, 0:wd]), R=[bst], W=[bkT])
                nfull = Tk // 128
                kb.dma("sp", vh[:, 0:nfull, :], self.VB[si][0:nfull * 128, h * 128:(h + 1) * 128].rearrange("(j p) e -> p j e", p=128), R=[bsc["v"]], W=[bvh])
                if Tk % 128:
                    kb.dma("sp", vh[0:Tk % 128, nfull, :], self.VB[si][nfull * 128:Tk, h * 128:(h + 1) * 128], R=[bsc["v"]], W=[bvh])
                st, bst = self.wstage()
                kb.dma("sp", st[:, 0:128], self.cst_corr[h], W=[bst])
                kb.op("pool", lambda e, st=st: e.tensor_copy(out=corrb[:], in_=st[:, 0:128]), R=[bst], W=[bcorr])

            pending = [None]
            heads = [(si, sq, h) for si, sq in enumerate(S) for h in range(8)]
            load_head(heads[0][0], heads[0][1], heads[0][2], kvset[0])
            for hi, (si, sq, h) in enumerate(heads):
                ks = kvset[hi % 2]
                kT, bkT, vh, bvh, corrb, bcorr = ks["kT"], ks["bkT"], ks["vh"], ks["bvh"], ks["corrb"], ks["bcorr"]
                bsc = self.bscr[si]
                P, GN, Tq = sq["P"], sq["GN"], sq["T"]
                kp0 = 0 if sq["s"] is None else PAST
                Tk = kp0 + Tq
                def load_q(g0, si=si, h=h, GN=GN, kp0=kp0, bsc=bsc):
                    qT, bqT = qTr()
                    if GN < 128:
                        kb.op("pool", lambda e, qT=qT: e.memset(qT[:, :, GN:128], 0.0), W=[bqT])
                    for m in range(2):
                        kb.dma("sp", qT[0:64, m, 0:GN], self.QT[si][h, m * 64:(m + 1) * 64, g0:g0 + GN], R=[bsc["q"]], W=[bqT])
                    st, bst = self.wstage()
                    kb.dma("sp", st[64:69, 0:GN], self.cst_qaug[h, :, kp0 + g0:kp0 + g0 + GN], W=[bst])
                    for m in range(2):
                        kb.op("pool", lambda e, m=m, st=st, qT=qT: e.tensor_copy(out=qT[64:69, m, 0:GN], in_=st[64:69, 0:GN]), R=[bst], W=[bqT])
                    return qT, bqT

                glist = list(range(0, Tq, GN))
                nextq = load_q(glist[0])
                for gidx, g0 in enumerate(glist):
                    qT, bqT = nextq
                    if gidx + 1 < len(glist):
                        nextq = load_q(glist[gidx + 1])
                    if gidx == 0 and hi + 1 < len(heads):
                        load_head(heads[hi + 1][0], heads[hi + 1][1], heads[hi + 1][2], kvset[(hi + 1) % 2])
                    GNq = max(GN, 128)
                    tiles = []
                    if sq["s"] is None:
                        i0, nt = g0 // 128, GN // 128
                        for jt in range(i0 + nt):
                            if jt < i0:
                                tiles.append((jt, 128, [(0, GN, False)]))
                            else:
                                c0 = (jt - i0) * 128
                                rg = [(c0, c0 + 128, True)]
                                if c0 + 128 < GN:
                                    rg.append((c0 + 128, GN, False))
                                tiles.append((jt, 128, rg))
                    else:
                        for jt in range(PAST // 128):
                            tiles.append((jt, 128, [(0, GNq, False)]))
                        tiles.append((PAST // 128, Tq, [(0, GNq, True)]))

                    def st_exp(ti):
                        jt, nk, rg = tiles[ti]
                        k0 = jt * 128
                        c0 = rg[0][0]
                        pts = []
                        for m in range(2):
                            ps, bps = self.bank()
                            for (a, b, isd) in rg:
                                kb.op("pe", lambda e, ps=ps, m=m, a=a, b=b, isd=isd: e.matmul(ps[0:nk, a:b], lhsT=kT[m][:, k0:k0 + nk], rhs=qT[:, m, a:b],
                                                                                         start=True, stop=(not isd)), R=[bkT, bqT], W=bps)
                                if isd:
                                    kb.op("pe", lambda e, ps=ps, a=a, b=b: e.matmul(ps[0:nk, a:b], lhsT=self.identb[0:nk, 0:nk], rhs=corrb[0:nk, 0:b - a],
                                                                                 start=False, stop=True), R=[bcorr, self.bconst], W=bps)
                            pt, bpt = ptr()
                            kb.op("act", lambda e, pt=pt, ps=ps: e.activation(out=pt[0:nk, c0:GNq], in_=ps[0:nk, c0:GNq], func=AF.Exp, scale=0.125), R=bps, W=[bpt])
                            pts.append((pt, bpt))
                        return pts

                    def pv(ti, pts):
                        jt, nk, rg = tiles[ti]
                        for m in range(2):
                            pt, bpt = pts[m]
                            for ri, (a, b, isd) in enumerate(rg):
                                first = (ti == 0 and ri == 0)
                                lastm = (ti == len(tiles) - 1 and ri == len(rg) - 1)
                                kb.op("pe", lambda e, pt=pt, m=m, a=a, b=b: e.matmul(acc[m][:, a:b], lhsT=vh[0:nk, jt, :], rhs=pt[0:nk, a:b],
                                                                                         start=first, stop=lastm), R=[bvh, bpt], W=bacc[m])
                                kb.op("pe", lambda e, pt=pt, m=m, a=a, b=b: e.matmul(acc[2 + m][:, a:b], lhsT=self.onesb[0:nk, :], rhs=pt[0:nk, a:b],
                                                                                         start=first, stop=lastm), R=[self.bconst, bpt], W=bacc[2 + m])

                    cur = st_exp(0)
                    for ti in range(len(tiles)):
                        nxt = st_exp(ti + 1) if ti + 1 < len(tiles) else None
                        pv(ti, cur)
                        cur = nxt
                        if ti == min(10, len(tiles) - 1) and pending[0] is not None:
                            pending[0]()
                            pending[0] = None
                    if pending[0] is not None:
                        pending[0]()
                        pending[0] = None
                    ac, bac = accr()
                    for i in range(4):
                        if i < 2:
                            kb.op("act", lambda e, i=i, ac=ac: e.activation(out=ac[:, i, 0:GN], in_=acc[i][:, 0:GN], func=AF.Copy), R=bacc[i], W=[bac])
                        else:
                            kb.op("dve", lambda e, i=i, ac=ac: e.tensor_copy(out=ac[:, i, 0:GN], in_=acc[i][:, 0:GN]), R=bacc[i], W=[bac])
                    kb.op("dve", lambda e, ac=ac: e.reciprocal(out=rr[:, 0:GN], in_=ac[:, 2, 0:GN]), R=[bac], W=[brr])
                    kb.op("dve", lambda e, ac=ac: e.tensor_tensor(out=ot[:, 0:GN], in0=ac[:, 0, 0:GN], in1=rr[:, 0:GN], op=ALU.mult), R=[bac, brr], W=[bot])
                    kb.op("dve", lambda e, ac=ac: e.reciprocal(out=rr[:, 0:GN], in_=ac[:, 3, 0:GN]), R=[bac], W=[brr])
                    kb.op("dve", lambda e, ac=ac: e.tensor_tensor(out=o1[:, 0:GN], in0=ac[:, 1, 0:GN], in1=rr[:, 0:GN], op=ALU.mult), R=[bac, brr], W=[bo1])
                    kb.op("dve", lambda e: e.scalar_tensor_tensor(out=ot[:, 0:GN], in0=o1[:, 0:GN], scalar=neglam, in1=ot[:, 0:GN], op0=ALU.mult, op1=ALU.add),
                          R=[bo1, bot, bsm], W=[bot])
                    kb.op("pool", lambda e: e.tensor_tensor(out=o1[:, 0:GN], in0=ot[:, 0:GN], in1=ot[:, 0:GN], op=ALU.mult), R=[bot], W=[bo1])

                    def part_b(si=si, h=h, g0=g0, GN=GN, GNq=GNq, bsc=bsc):
                        pm, bm = self.bank()
                        kb.op("pe", lambda e, pm=pm: e.matmul(pm[:, 0:GNq], lhsT=self.m128[:], rhs=o1[:, 0:GNq], start=True, stop=True), R=[bo1, self.bconst], W=bm)
                        kb.op("act", lambda e, pm=pm: e.activation(out=rr[:, 0:GN], in_=pm[:, 0:GN], func=AF.Ln, bias=self.epsb[:, 0:1]), R=bm + [self.bconst], W=[brr])
                        kb.op("act", lambda e: e.activation(out=rr[:, 0:GN], in_=rr[:, 0:GN], func=AF.Exp, scale=-0.5), R=[brr], W=[brr])
                        on, bon = onr()
                        kb.op("dve", lambda e, on=on: e.scalar_tensor_tensor(out=on[:, 0:GN], in0=ot[:, 0:GN], scalar=subg[:, 0, :], in1=rr[:, 0:GN], op0=ALU.mult, op1=ALU.mult),
                              R=[bot, brr, bsm], W=[bon])
                        kb.dma("pool", self.OT[si][h * 128:(h + 1) * 128, g0:g0 + GN], on[:, 0:GN], R=[bon], W=[bsc["o"]])
                    pending[0] = part_b
            if pending[0] is not None:
                pending[0]()
                pending[0] = None
            self.bank_list = save_banks
            kb.barrier()

    def phase_e3(self, l, j, S):
        kb = self.kb
        stage = 2 * l
        with ExitStack() as es:
            wo = self.sb(es, "wo", [128, 16, D], BF16); bwo = Buf()
            self.load_w(wo, self.hyb_w_out[j].rearrange("(k p) n -> k p n", p=128), 16, D, bwo)
            GNmax = S[0]["GN"]
            oyr = self.rot("oy", es, [128, 16, GNmax], BF16, 2)
            for si, sq in enumerate(S):
                gs, sh, gg, bbc = self.make_bc(l, 0, sq["ci"])
                src, dst = self.xio(sq, stage)
                bx = self.bscr[si]["x"]
                bsc = self.bscr[si]
                P, GN, Tq = sq["P"], sq["GN"], sq["T"]
                def load_oy(g0, si=si, GN=GN, bsc=bsc):
                    t, bt = oyr()
                    kb.dma("sp", t[:, 0:8, 0:GN], self.OT[si].rearrange("(c p) t -> p c t", p=128)[:, :, g0:g0 + GN], R=[bsc["o"]], W=[bt])
                    kb.dma("sp", t[:, 8:16, 0:GN], self.YT[si].rearrange("(c p) t -> p c t", p=128)[:, :, g0:g0 + GN], R=[bsc["y"]], W=[bt])
                    return t, bt

                glist = list(range(0, Tq, GN))
                nxt = load_oy(glist[0])
                for gidx, g0 in enumerate(glist):
                    nt = GN // P
                    t, bt = nxt
                    if gidx + 1 < len(glist):
                        nxt = load_oy(glist[gidx + 1])
                    for m in range(nt):
                        pp, bpp = self.pair()
                        for nh in range(2):
                            for c in range(16):
                                kb.op("pe", lambda e, c=c, nh=nh, m=m, pp=pp, t=t: e.matmul(pp[0:P, nh * 512:(nh + 1) * 512], lhsT=t[:, c, m * P:(m + 1) * P],
                                                                                          rhs=wo[:, c, nh * 512:(nh + 1) * 512], start=(c == 0), stop=(c == 15)),
                                      R=[bt, bwo], W=bpp)
                        r0 = g0 + m * P
                        self.post(pp, bpp, P, src[r0:r0 + P, :], dst[r0:r0 + P, :], gg, bbc, bx)
            kb.barrier()

    def phase_conf(self, l, j, S):
        kb = self.kb
        stage = 2 * l
        with ExitStack() as es:
            win = self.sb(es, "cwin", [128, 8, 2 * D], BF16); bwin = Buf()
            wout = self.sb(es, "cwout", [128, 8, D], BF16); bwout = Buf()
            self.load_w(win, self.conf_w_in[j].rearrange("(k p) n -> k p n", p=128), 8, 2 * D, bwin)
            self.load_w(wout, self.conf_w_out[j].rearrange("(k p) n -> k p n", p=128), 8, D, bwout)
            bsm = Buf("confsmall")
            dwf = self.sb(es, "dwf", [128, 8, 31], F32)
            self.load_fm(es, lambda c: dwf[:, c, :], self.conf_dw_w[j], 31, 8, bsm)
            bin_ = self.sb(es, "binf", [128, 16, 1], F32)
            self.load_fm(es, lambda c: bin_[:, c, :], self.conf_b_in[j:j + 1, :], 1, 16, bsm)
            vecs = self.sb(es, "cvecs", [128, 3, 8, 1], F32)
            for i, src in enumerate((self.conf_dw_b, self.conf_ln_g, self.conf_ln_b)):
                self.load_fm(es, lambda c, i=i: vecs[:, i, c, :], src[j:j + 1, :], 1, 8, bsm)
            diag = self.sb(es, "cdiag", [128, 8 * 31, 128], BF16)
            kb.op("dve", lambda e: e.tensor_tensor(out=diag[:], in0=self.identf.unsqueeze(1).to_broadcast([128, 248, 128]),
                                                   in1=dwf[:].rearrange("p c k -> p (c k)").unsqueeze(2).to_broadcast([128, 248, 128]),
                                                   op=ALU.mult), R=[bsm, self.bconst], W=[bsm])
            bor = self.sb(es, "bor", [1, D], F32)
            kb.dma("sp", bor[:], self.conf_b_out[j:j + 1, :], W=[bsm])
            borb = self.sb(es, "borb", [1, D], BF16)
            kb.op("pool", lambda e: e.tensor_copy(out=borb[:], in_=bor[:]), R=[bsm], W=[bsm])
            if CONF_STOP <= 1:
                kb.barrier()
                return
            GNmax = S[0]["GN"]
            hT = self.sb(es, "chT", [128, 8, GNmax], BF16); bhT = Buf()
            uT = self.sb(es, "uT", [128, 8, 30 + GNmax], BF16); buT = Buf()
            uF = self.sb(es, "uF", [128, 8, 32], F32); buF = Buf()
            yT = self.sb(es, "yT", [128, 8, GNmax], F32); byT = Buf()
            ynT = self.sb(es, "ynT", [128, 8, GNmax], BF16); bynT = Buf()
            sgr = self.rot("csg", es, [128, GNmax], F32, 1)
            mu = self.sb(es, "cmu", [128, GNmax], F32); bmu = Buf()
            rs = self.sb(es, "crs", [128, GNmax], F32); brs = Buf()
            kb.op("pool", lambda e: e.memset(hT[:], 0.0), W=[bhT])
            kb.op("pool", lambda e: e.memset(uT[:], 0.0), W=[buT])
            kb.op("pool", lambda e: e.memset(yT[:], 0.0), W=[byT])
            kb.op("pool", lambda e: e.memset(ynT[:], 0.0), W=[bynT])
            for si, sq in enumerate(S):
                if CONF_SEQS is not None and si not in CONF_SEQS:
                    continue
                gs, sh, gg, bbc = self.make_bc(l, 0, sq["ci"])
                src, dst = self.xio(sq, stage)
                bx = self.bscr[si]["x"]
                P, GN, Tq = sq["P"], sq["GN"], sq["T"]
                GNp = max(GN, 128)
                if sq["s"] is None:
                    kb.op("pool", lambda e: e.memset(uT[:, :, 0:30], 0.0), W=[buT])
                else:
                    self.load_fm(es, lambda c: uT[:, c, 0:30], self.st_cconv[j, sq["s"]], 30, 8, buT)
                ngr = Tq // GN
                nt = GN // P
                nl = min(30, GN)

                def stage_a(gi):
                    g0 = gi * GN
                    last = gi == ngr - 1
                    for m in range(nt):
                        self.prelude_x(src[g0 + m * P:g0 + (m + 1) * P, :], bx, P, hT, bhT, m * P, gs, sh, bbc)
                    nl = min(30, GN)
                    for c in range(8):
                        pa, ba = self.bank()
                        pg, bg = self.bank()
                        for k in range(8):
                            kb.op("pe", lambda e, k=k, pa=pa, c=c: e.matmul(pa[:, 0:GNp], lhsT=win[:, k, c * 128:(c + 1) * 128], rhs=hT[:, k, 0:GNp],
                                                                           start=(k == 0), stop=(k == 7)), R=[bwin, bhT], W=ba)
                        for k in range(8):
                            kb.op("pe", lambda e, k=k, pg=pg, c=c: e.matmul(pg[:, 0:GNp], lhsT=win[:, k, D + c * 128:D + (c + 1) * 128], rhs=hT[:, k, 0:GNp],
                                                                           start=(k == 0), stop=(k == 7)), R=[bwin, bhT], W=bg)
                        sg, bsg = sgr()
                        kb.op("act", lambda e, sg=sg, pg=pg, c=c: e.activation(out=sg[:, 0:GN], in_=pg[:, 0:GN], func=AF.Sigmoid, bias=bin_[:, 8 + c, :]),
                              R=bg + [bsm], W=[bsg])
                        kb.op("dve", lambda e, sg=sg, pa=pa, c=c: e.scalar_tensor_tensor(out=uT[:, c, 30:30 + GN], in0=pa[:, 0:GN], scalar=bin_[:, c, :],
                                                                                        in1=sg[:, 0:GN], op0=ALU.add, op1=ALU.mult),
                              R=ba + [bsg, bsm], W=[buT])
                        if last:
                            kb.op("dve", lambda e, sg=sg, pa=pa, c=c: e.scalar_tensor_tensor(out=uF[:, c, 0:nl], in0=pa[:, GN - nl:GN], scalar=bin_[:, c, :],
                                                                                            in1=sg[:, GN - nl:GN], op0=ALU.add, op1=ALU.mult),
                                  R=ba + [bsg, bsm], W=[buF])

                def stage_conv(gi):
                    for c in range(8):
                        py, by_ = self.bank()
                        for k in range(31):
                            kb.op("pe", lambda e, k=k, py=py, c=c: e.matmul(py[:, 0:GNp], lhsT=diag[:, c * 31 + k, :], rhs=uT[:, c, k:k + GNp],
                                                                           start=(k == 0), stop=(k == 30)), R=[bsm, buT], W=by_)
                        kb.op("act", lambda e, py=py, c=c: e.activation(out=yT[:, c, 0:GN], in_=py[:, 0:GN], func=AF.Identity, bias=vecs[:, 0, c, :]),
                              R=by_ + [bsm], W=[byT])

                def stage_c(gi):
                    g0 = gi * GN
                    pm, bm = self.bank()
                    for c in range(8):
                        kb.op("pe", lambda e, c=c, pm=pm: e.matmul(pm[:, 0:GNp], lhsT=self.m1024[:], rhs=yT[:, c, 0:GNp], start=(c == 0), stop=(c == 7)),
                              R=[byT, self.bconst], W=bm)
                    kb.op("act", lambda e, pm=pm: e.activation(out=mu[:, 0:GN], in_=pm[:, 0:GN], func=AF.Copy), R=bm, W=[bmu])
                    kb.op("dve", lambda e: e.tensor_tensor(out=yT[:, :, 0:GN], in0=yT[:, :, 0:GN], in1=mu[:, 0:GN].unsqueeze(1).to_broadcast([128, 8, GN]),
                                                           op=ALU.subtract), R=[byT, bmu], W=[byT])
                    kb.op("pool", lambda e: e.tensor_tensor(out=ynT[:, :, 0:GN], in0=yT[:, :, 0:GN], in1=yT[:, :, 0:GN], op=ALU.mult), R=[byT], W=[bynT])
                    pv, bv = self.bank()
                    for c in range(8):
                        kb.op("pe", lambda e, c=c, pv=pv: e.matmul(pv[:, 0:GNp], lhsT=self.onesb[:, :], rhs=ynT[:, c, 0:GNp], start=(c == 0), stop=(c == 7)),
                              R=[bynT, self.bconst], W=bv)
                    kb.op("act", lambda e, pv=pv: e.activation(out=rs[:, 0:GN], in_=pv[:, 0:GN], func=AF.Sqrt, scale=1.0 / 1024, bias=self.epsb[:, 0:1]), R=bv + [self.bconst], W=[brs])
                    kb.op("dve", lambda e: e.reciprocal(out=rs[:, 0:GN], in_=rs[:, 0:GN]), R=[brs], W=[brs])
                    kb.op("dve", lambda e: e.tensor_tensor(out=yT[:, :, 0:GN], in0=yT[:, :, 0:GN], in1=rs[:, 0:GN].unsqueeze(1).to_broadcast([128, 8, GN]),
                                                           op=ALU.mult), R=[byT, brs], W=[byT])
                    for c in range(8):
                        kb.op("act", lambda e, c=c: e.activation(out=ynT[:, c, 0:GN], in_=yT[:, c, 0:GN], func=AF.Silu, scale=vecs[:, 1, c, :], bias=vecs[:, 2, c, :]),
                              R=[byT, bsm], W=[bynT])
                    for m in range(nt):
                        pp, bpp = self.pair()
                        for nh in range(2):
                            for c in range(8):
                                kb.op("pe", lambda e, c=c, nh=nh, m=m, pp=pp: e.matmul(pp[0:P, nh * 512:(nh + 1) * 512], lhsT=ynT[:, c, m * P:(m + 1) * P],
                                                                                     rhs=wout[:, c, nh * 512:(nh + 1) * 512], start=(c == 0), stop=False),
                                      R=[bynT, bwout], W=bpp)
                            kb.op("pe", lambda e, nh=nh, pp=pp: e.matmul(pp[0:P, nh * 512:(nh + 1) * 512], lhsT=self.onesb[0:1, 0:P],
                                                                        rhs=borb[0:1, nh * 512:(nh + 1) * 512], start=False, stop=True),
                                  R=[bsm, self.bconst], W=bpp)
                        r0 = g0 + m * P
                        self.post(pp, bpp, P, src[r0:r0 + P, :], dst[r0:r0 + P, :], gg, bbc, bx)

                stage_a(0)
                for gi in range(ngr):
                    stage_conv(gi)
                    if gi + 1 < ngr:
                        kb.op("pool", lambda e: e.tensor_copy(out=uT[:, :, 0:30], in_=uT[:, :, GN:GN + 30]), R=[buT], W=[buT])
                        stage_a(gi + 1)
                    stage_c(gi)
                if CONF_STOP <= 5:
                    continue
                nl = min(30, GN)
                pp, bpp = self.pair()
                for c in range(8):
                    kb.op("pe", lambda e, c=c, pp=pp: e.transpose(out=pp[0:nl, c * 128:(c + 1) * 128], in_=uF[:, c, 0:nl], identity=self.identf[:, :]),
                          R=[buF, self.bconst], W=bpp)
                ot, bo = self.xrot()
                kb.op("act", lambda e, pp=pp, ot=ot: e.activation(out=ot[0:nl, :], in_=pp[0:nl, :], func=AF.Copy), R=bpp, W=[bo])
                if sq["s"] is None:
                    kb.dma("pool", self.cconv_p[j], ot[0:30, :], R=[bo], W=[self.bout])
                else:
                    s_ = sq["s"]
                    kb.dma("pool", self.cconv_s[j, s_, 30 - nl:30, :], ot[0:nl, :], R=[bo], W=[self.bout])
                    if nl < 30:
                        kb.dma("pool", self.cconv_s[j, s_, 0:30 - nl, :], self.st_cconv[j, s_, nl:30, :], W=[self.bout])
            kb.barrier()


_PROG = {}


def _get_prog(T, depth):
    key = (T, depth)
    if key not in _PROG:
        _PROG[key] = Prog(T=T, depth=depth)
    return _PROG[key]


def make_in_maps(inp, T, depth):
    NE, NO = (depth + 1) // 2, depth // 2
    cf, kaug, qaug, corr = _consts()
    f = lambda a: np.ascontiguousarray(np.asarray(a, dtype=np.float32))
    shared = {}
    for nm in ("ada_w", "ada_b", "norm_g", "ffn_w_up", "ffn_w_down", "hyb_w_in", "attn_subln_g", "ssm_conv_w", "ssm_conv_b",
               "ssm_dt_bias", "ssm_a_log", "ssm_d", "ssm_norm_g", "hyb_w_out", "conf_w_in", "conf_b_in", "conf_dw_w",
               "conf_dw_b", "conf_ln_g", "conf_ln_b", "conf_w_out", "conf_b_out"):
        shared[nm] = f(inp[nm])
    shared["attn_lambda"] = f(inp["attn_lambda"]).reshape(NE, 256)
    shared["cst_f"], shared["cst_kaug"], shared["cst_qaug"], shared["cst_corr"] = cf, kaug, qaug, corr
    maps = []
    for c in range(NCORES):
        m = dict(shared)
        m["x_p"] = f(inp["x_prompt"][c])
        m["x_s"] = f(inp["x_sample"][2 * c:2 * c + 2])
        m["c_all"] = f(np.concatenate([inp["c_prompt"][c:c + 1], inp["c_sample"][2 * c:2 * c + 2]], 0))
        m["cache_k"] = f(np.asarray(inp["cache_attn_k"])[:, 2 * c:2 * c + 2].reshape(NE, 2, -1, D))
        m["cache_v"] = f(np.asarray(inp["cache_attn_v"])[:, 2 * c:2 * c + 2].reshape(NE, 2, -1, D))
        m["st_sconv"] = f(np.asarray(inp["state_ssm_conv"])[:, 2 * c:2 * c + 2])
        m["st_ssm"] = f(np.asarray(inp["state_ssm"])[:, 2 * c:2 * c + 2].reshape(NE, 2, 1024, 128))
        m["st_cconv"] = f(np.asarray(inp["state_conf_conv"])[:, 2 * c:2 * c + 2])
        maps.append(m)
    return maps


def gather(res, T, depth, TS=16):
    NE, NO = (depth + 1) // 2, depth // 2
    R = res.results
    st = lambda k, ax=0: np.stack([np.asarray(r[k]) for r in R], ax)
    cat = lambda k, ax: np.concatenate([np.asarray(r[k]) for r in R], ax)
    y_p = st("y_p")
    y_s = cat("y_s", 0)
    k_p = st("k_p", 1).reshape(NE, NCORES, T, 8, 128)
    v_p = st("v_p", 1).reshape(NE, NCORES, T, 8, 128)
    sconv_p = st("sconv_p", 1)
    ssm_p = st("ssm_p", 1).reshape(NE, NCORES, 16, 64, 128)
    cconv_p = st("cconv_p", 1)
    k_s = cat("k_s", 1).reshape(NE, 2 * NCORES, TS, 8, 128)
    v_s = cat("v_s", 1).reshape(NE, 2 * NCORES, TS, 8, 128)
    sconv_s = cat("sconv_s", 1)
    ssm_s = cat("ssm_s", 1).reshape(NE, 2 * NCORES, 16, 64, 128)
    cconv_s = cat("cconv_s", 1)
    return (y_p, y_s, k_p, v_p, sconv_p, ssm_p, cconv_p, k_s, v_s, sconv_s, ssm_s, cconv_s)


def kernel(**inputs):
    T = int(np.asarray(inputs["x_prompt"]).shape[1])
    depth = int(np.asarray(inputs["ada_w"]).shape[0])
    prog = _get_prog(T, depth)
    maps = make_in_maps(inputs, T, depth)
    res = run_bass_kernel_spmd(prog.nc, maps, core_ids=list(range(NCORES)))
    outs = gather(res, T, depth)
    return tuple(np.ascontiguousarray(o, dtype=np.float32) for o in outs)
```
